# Optimizing a Trainium2 kernel written in Bass

```python
import math
import jax, jax.numpy as jnp
from jax import lax
import numpy as np

D_MODEL = 1024
BATCH = 4
SEQ = 8192
DEPTH = 2

HEAD_DIM = 64
D_MIX = D_MODEL
HA = (D_MIX // 2) // (2 * HEAD_DIM)
HB = (D_MIX // 4) // HEAD_DIM
HC = (D_MIX // 4) // HEAD_DIM
WA = HA * 2 * HEAD_DIM
WB = HB * HEAD_DIM
WC = HC * HEAD_DIM
ROT_DIM = HEAD_DIM // 4
ROPE_THETA = 500000.0
Q_BLOCK = 128
LN_EPS = 1e-5
SUBLN_EPS = 1e-5
DEEPNORM_ALPHA = (2 * DEPTH) ** 0.25
DEEPNORM_BETA = (8 * DEPTH) ** -0.25
SPLIT_SIZES = (WA, WA, WA, WA, WB, WB, WB, HB, WB, WC, WC, WC, WC)
VALUE_SEGMENTS = (2, 6, 11)
D_IN = sum(SPLIT_SIZES)
SPLIT_POINTS = tuple(int(v) for v in np.cumsum(SPLIT_SIZES)[:-1])

kernel_name = 'hybrid_diff_fox_stickbreak_heads'


def _layernorm(t, g, b):
    tf = t.astype(jnp.float32)
    mu = jnp.mean(tf, -1, keepdims=True)
    var = jnp.mean(jnp.square(tf - mu), -1, keepdims=True)
    out = (tf - mu) * lax.rsqrt(var + LN_EPS) * g.astype(jnp.float32) + b.astype(jnp.float32)
    return out.astype(t.dtype)


def _rope_tables(seq):
    inv = ROPE_THETA ** (-jnp.arange(0, ROT_DIM, 2, dtype=jnp.float32) / ROT_DIM)
    ang = jnp.arange(seq, dtype=jnp.float32)[:, None] * inv[None, :]
    return jnp.cos(ang), jnp.sin(ang)


def _partial_rope(t, cos, sin):
    half = ROT_DIM // 2
    x1 = t[..., :half].astype(jnp.float32)
    x2 = t[..., half:ROT_DIM].astype(jnp.float32)
    r1 = (x1 * cos - x2 * sin).astype(t.dtype)
    r2 = (x2 * cos + x1 * sin).astype(t.dtype)
    return jnp.concatenate([r1, r2, t[..., ROT_DIM:]], axis=-1)


def _split_heads(t, n_heads):
    b_, s_, w_ = t.shape
    return t.reshape(b_, s_, n_heads, w_ // n_heads).transpose(0, 2, 1, 3)


def _to_blocks(t):
    b_, h_, s_ = t.shape[:3]
    t = t.reshape((b_, h_, s_ // Q_BLOCK, Q_BLOCK) + t.shape[3:])
    return jnp.moveaxis(t, 2, 0)


def _from_blocks(t):
    nb, b_, h_, blk, dv = t.shape
    return t.transpose(1, 0, 3, 2, 4).reshape(b_, nb * blk, h_ * dv)


def _differential_attention(q1, q2, k1, k2, v, lam):
    seq = k1.shape[2]
    kpos = jnp.arange(seq)
    starts = jnp.arange(seq // Q_BLOCK) * Q_BLOCK
    scale = HEAD_DIM ** -0.5

    def block(args):
        q1b, q2b, start = args
        qpos = start + jnp.arange(Q_BLOCK)
        causal = kpos[None, :] <= qpos[:, None]
        s1 = jnp.einsum('bhqd,bhkd->bhqk', q1b, k1).astype(jnp.float32) * scale
        s2 = jnp.einsum('bhqd,bhkd->bhqk', q2b, k2).astype(jnp.float32) * scale
        p1 = jax.nn.softmax(jnp.where(causal, s1, -jnp.inf), axis=-1)
        p2 = jax.nn.softmax(jnp.where(causal, s2, -jnp.inf), axis=-1)
        w = (p1 - lam * p2).astype(v.dtype)
        return jnp.einsum('bhqk,bhkd->bhqd', w, v)

    return lax.map(block, (_to_blocks(q1), _to_blocks(q2), starts))


def _forgetting_attention(q, k, v, cum_logf):
    seq = k.shape[2]
    kpos = jnp.arange(seq)
    starts = jnp.arange(seq // Q_BLOCK) * Q_BLOCK
    scale = HEAD_DIM ** -0.5

    def block(args):
        qb, cqb, start = args
        qpos = start + jnp.arange(Q_BLOCK)
        causal = kpos[None, :] <= qpos[:, None]
        s = jnp.einsum('bhqd,bhkd->bhqk', qb, k).astype(jnp.float32) * scale
        logits = s + cqb[..., :, None] - cum_logf[..., None, :]
        p = jax.nn.softmax(jnp.where(causal, logits, -jnp.inf), axis=-1)
        return jnp.einsum('bhqk,bhkd->bhqd', p.astype(v.dtype), v)

    return lax.map(block, (_to_blocks(q), _to_blocks(cum_logf), starts))


def _stick_breaking_attention(q, k, v):
    seq = k.shape[2]
    kpos = jnp.arange(seq)
    starts = jnp.arange(seq // Q_BLOCK) * Q_BLOCK
    scale = HEAD_DIM ** -0.5

    def block(args):
        qb, start = args
        qpos = start + jnp.arange(Q_BLOCK)
        strict = kpos[None, :] < qpos[:, None]
        z = jnp.einsum('bhqd,bhkd->bhqk', qb, k).astype(jnp.float32) * scale
        log_fail = jnp.where(strict, jax.nn.log_sigmoid(-z), 0.0)
        after = lax.cumsum(log_fail, axis=3, reverse=True) - log_fail
        weights = jnp.where(strict, jnp.exp(jax.nn.log_sigmoid(z) + after), 0.0)
        return jnp.einsum('bhqk,bhkd->bhqd', weights.astype(v.dtype), v)

    return lax.map(block, (_to_blocks(q), starts))


def setup_inputs(seed: int = 0) -> dict:
    key = jax.random.key(seed)
    ks = jax.random.split(key, 13)
    x = jax.random.normal(ks[0], (BATCH, SEQ, D_MODEL), jnp.float32)
    ln_in_g = 1.0 + 0.02 * jax.random.normal(ks[1], (D_MODEL,), jnp.float32)
    ln_in_b = 0.02 * jax.random.normal(ks[2], (D_MODEL,), jnp.float32)
    col_scale = jnp.concatenate([
        jnp.full((n,), DEEPNORM_BETA if i in VALUE_SEGMENTS else 1.0, jnp.float32)
        for i, n in enumerate(SPLIT_SIZES)])
    w_in = (jax.random.normal(ks[3], (DEPTH, D_MODEL, D_IN), jnp.float32)
            * (D_MODEL ** -0.5) * col_scale)
    b_forget = 1.0 + 0.1 * jax.random.normal(ks[4], (DEPTH, HB), jnp.float32)
    lambda_q1 = 0.1 * jax.random.normal(ks[5], (DEPTH, HEAD_DIM), jnp.float32)
    lambda_k1 = 0.1 * jax.random.normal(ks[6], (DEPTH, HEAD_DIM), jnp.float32)
    lambda_q2 = 0.1 * jax.random.normal(ks[7], (DEPTH, HEAD_DIM), jnp.float32)
    lambda_k2 = 0.1 * jax.random.normal(ks[8], (DEPTH, HEAD_DIM), jnp.float32)
    subln_g = 1.0 + 0.02 * jax.random.normal(ks[9], (DEPTH, 2 * HEAD_DIM), jnp.float32)
    w_out = (jax.random.normal(ks[10], (DEPTH, D_MIX, D_MODEL), jnp.float32)
             * (D_MIX ** -0.5) * DEEPNORM_BETA)
    ln_g = 1.0 + 0.02 * jax.random.normal(ks[11], (DEPTH, D_MODEL), jnp.float32)
    ln_b = 0.02 * jax.random.normal(ks[12], (DEPTH, D_MODEL), jnp.float32)
    return {'x': x, 'ln_in_g': ln_in_g, 'ln_in_b': ln_in_b, 'w_in': w_in,
            'b_forget': b_forget, 'lambda_q1': lambda_q1, 'lambda_k1': lambda_k1,
            'lambda_q2': lambda_q2, 'lambda_k2': lambda_k2, 'subln_g': subln_g,
            'w_out': w_out, 'ln_g': ln_g, 'ln_b': ln_b}


def reference(x, ln_in_g, ln_in_b, w_in, b_forget, lambda_q1, lambda_k1, lambda_q2,
              lambda_k2, subln_g, w_out, ln_g, ln_b):
    bsz, seq, _ = x.shape
    cos, sin = _rope_tables(seq)
    h = _layernorm(x, ln_in_g, ln_in_b)
    for layer in range(DEPTH):
        proj = jnp.einsum('bsd,de->bse', h, w_in[layer])
        (aq, ak, av, ag, bq, bk, bv, bf, bg, cq, ck, cv, cg) = jnp.split(proj, SPLIT_POINTS, axis=-1)

        qa = _partial_rope(_split_heads(aq, 2 * HA), cos, sin).reshape(bsz, HA, 2, seq, HEAD_DIM)
        ka = _partial_rope(_split_heads(ak, 2 * HA), cos, sin).reshape(bsz, HA, 2, seq, HEAD_DIM)
        va = _split_heads(av, HA)
        lam_init = 0.8 - 0.6 * math.exp(-0.3 * layer)
        lam = (jnp.exp(jnp.sum(lambda_q1[layer].astype(jnp.float32) * lambda_k1[layer].astype(jnp.float32)))
               - jnp.exp(jnp.sum(lambda_q2[layer].astype(jnp.float32) * lambda_k2[layer].astype(jnp.float32)))
               + lam_init)
        oa = _differential_attention(qa[:, :, 0], qa[:, :, 1], ka[:, :, 0], ka[:, :, 1], va, lam)
        oaf = oa.astype(jnp.float32)
        oa = (oaf * lax.rsqrt(jnp.mean(jnp.square(oaf), -1, keepdims=True) + SUBLN_EPS)
              * subln_g[layer].astype(jnp.float32) * (1.0 - lam_init)).astype(h.dtype)
        ya = _from_blocks(oa) * jax.nn.silu(ag)

        logf = jax.nn.log_sigmoid((bf + b_forget[layer]).astype(jnp.float32))
        cum_logf = jnp.cumsum(logf, axis=1).transpose(0, 2, 1)
        ob = _forgetting_attention(_split_heads(bq, HB), _split_heads(bk, HB),
                                   _split_heads(bv, HB), cum_logf)
        yb = _from_blocks(ob) * jax.nn.silu(bg)

        oc = _stick_breaking_attention(_split_heads(cq, HC), _split_heads(ck, HC), _split_heads(cv, HC))
        yc = _from_blocks(oc) * jax.nn.silu(cg)

        y = jnp.einsum('bse,ed->bsd', jnp.concatenate([ya, yb, yc], axis=-1), w_out[layer])
        h = _layernorm(DEEPNORM_ALPHA * h + y, ln_g[layer], ln_b[layer])
    return h
```

```python
import contextlib
import math

import numpy as np
import ml_dtypes

import concourse.bass as bass
import concourse.mybir as mybir
from concourse.bass_utils import run_bass_kernel_spmd

F32 = mybir.dt.float32
BF16 = mybir.dt.bfloat16
AF = mybir.ActivationFunctionType
ALU = mybir.AluOpType

S = 8192
D = 1024
OWN = 4096
NB = 16
DEPTH = 2
LN_EPS = 1e-5
SUBLN_EPS = 1e-5
ALPHA = (2 * DEPTH) ** 0.25
ROPE_THETA = 500000.0
MASKV = -30000.0
WKC = 2564
WQC = 2560

ENGS = ("pe", "act", "dve", "pool", "sp")
DEBUG = False
DUMP = False


class Op:
    __slots__ = ("eng", "fn", "deps", "marked", "sig", "dma", "dsem", "dval", "cc")

    def __init__(self, eng, fn, dma):
        self.eng = eng
        self.fn = fn
        self.deps = []
        self.marked = False
        self.sig = None
        self.dma = dma
        self.dsem = None
        self.dval = None
        self.cc = False


class _Rec:
    def __init__(self):
        self.call = None

    def __getattr__(self, name):
        def f(*a, **k):
            self.call = (name, a, k)
            return None
        return f

    def replay(self, eng):
        name, a, k = self.call
        return getattr(eng, name)(*a, **k)


class Prog:
    NDSEM = 24

    def __init__(self, nc):
        self.nc = nc
        self.ops = {e: [] for e in ENGS}
        self.last_writer = {}
        self.readers = {}
        self.dma_ops = {e: [] for e in ENGS}
        self.final = []

    def op(self, eng, fn, reads=(), writes=(), dma=False, extra=()):
        if fn is not None:
            rec = _Rec()
            fn(rec)
            assert rec.call is not None
            fn = rec.replay
        o = Op(eng, fn, dma)
        deps = []
        seen = set()

        def add(d):
            if d is None or id(d) in seen:
                return
            seen.add(id(d))
            deps.append(d)

        for b in reads:
            add(self.last_writer.get(b))
        for b in writes:
            add(self.last_writer.get(b))
            for r in self.readers.get(b, ()):
                add(r)
        for d in extra:
            add(d)
        if dma:
            lst = self.dma_ops[eng]
            n = len(lst)
            if n >= self.NDSEM:
                add(lst[n - self.NDSEM])
            o.dsem = n % self.NDSEM
            o.dval = 16 * (n // self.NDSEM + 1)
            lst.append(o)
        for d in deps:
            if d.dma:
                o.deps.append(d)
            elif d.eng == "pe" and eng == "pe" and not dma and fn is not None:
                continue
            else:
                d.marked = True
                o.deps.append(d)
        for b in reads:
            self.readers.setdefault(b, []).append(o)
        for b in writes:
            self.last_writer[b] = o
            self.readers[b] = []
        self.ops[eng].append(o)
        return o

    def barrier(self, eng, deps):
        return self.op(eng, None, extra=deps)

    def full_barrier(self):
        tails = []
        for e in ENGS:
            for o in reversed(self.ops[e]):
                if o.fn is not None and not o.dma:
                    tails.append(o)
                    break
        dmas = []
        for e in ENGS:
            dmas += self.dma_ops[e][-self.NDSEM:]
        for e in ENGS:
            self.barrier(e, tails + dmas)
        self.last_writer = {}
        self.readers = {}

    def emit(self):
        nc = self.nc
        with contextlib.ExitStack() as st:
            csem = {e: st.enter_context(nc.semaphore(f"c_{e}")) for e in ENGS}
            dsem = {
                e: [st.enter_context(nc.semaphore(f"d_{e}_{i}")) for i in range(self.NDSEM)]
                for e in ENGS if self.dma_ops[e]
            }
            for e in ENGS:
                c = 0
                for o in self.ops[e]:
                    if o.marked and not o.dma:
                        c += 1
                        o.sig = c
            block = st.enter_context(nc.Block())

            def run(e, engobj):
                seen = {}

                def wait(d):
                    if d.dma:
                        key = ("d", d.eng, d.dsem)
                        sem, val = dsem[d.eng][d.dsem], d.dval
                    else:
                        key = ("c", d.eng)
                        sem, val = csem[d.eng], d.sig
                    if seen.get(key, 0) >= val:
                        return
                    seen[key] = val
                    engobj.wait_ge(sem, val)

                for o in self.ops[e]:
                    for d in o.deps:
                        wait(d)
                    if o.fn is None:
                        continue
                    ins = o.fn(engobj)
                    if o.dma:
                        ins.then_inc(dsem[e][o.dsem], 16)
                    elif o.marked:
                        ins.then_inc(csem[e], 1)
                if e == "sp":
                    for d in self.final:
                        wait(d)

            @block.tensor
            def _(eng):
                run("pe", eng)

            @block.scalar
            def _(eng):
                run("act", eng)

            @block.vector
            def _(eng):
                run("dve", eng)

            @block.gpsimd
            def _(eng):
                run("pool", eng)

            @block.sync
            def _(eng):
                run("sp", eng)


SB_BASE = 16512
SB_TOP = 229376 - 1024


class Arena:
    def __init__(self, nc):
        self.nc = nc
        self.ptr = SB_BASE
        self.n = 0

    def mark(self):
        return self.ptr

    def reset(self, p):
        self.ptr = p

    def alloc(self, name, shape, dt):
        esz = 4 if dt == F32 else 2
        nbytes = esz
        for s in shape[1:]:
            nbytes *= s
        nbytes = (nbytes + 63) // 64 * 64
        off = self.ptr
        self.ptr += nbytes
        assert self.ptr <= SB_TOP, (name, self.ptr)
        self.n += 1
        return self.nc.alloc_sbuf_tensor_at(f"{name}_{self.n}", list(shape), dt, offset=off)


def build_program(layers, lam_inits):
    nc = bass.Bass("TRN2", target_bir_lowering=False)
    P = Prog(nc)
    A = Arena(nc)

    def din(name, shape, dt=F32):
        return nc.dram_tensor(name, list(shape), dt, kind="ExternalInput").ap()

    def dscr(name, shape, dt):
        if DEBUG:
            return nc.dram_tensor(name, list(shape), dt, kind="ExternalOutput").ap()
        return nc.dram_tensor(name, list(shape), dt).ap()

    L0 = layers[0]
    first_is_l0 = (L0 == 0)
    hin_full = din("hin_full", [S, D])
    hin_own = din("hin_own", [OWN, D])
    hin_oth = din("hin_oth", [OWN, D])
    blend = din("blend", [128, 4])
    lng_in = din("lng_in", [128, D])
    lnb_in = din("lnb_in", [128, D])
    rk_cos = din("rk_cos", [128, S])
    rk_sin = din("rk_sin", [128, S])
    rq_cos_p = [din(f"rq_cos{p}", [128, OWN]) for p in range(2)]
    rq_sin_p = [din(f"rq_sin{p}", [128, OWN]) for p in range(2)]
    selg_p = [din(f"selg{p}", [4, 2]) for p in range(2)]
    c_ident = din("c_ident", [128, 128], BF16)
    c_tneg = din("c_tneg", [128, 128], BF16)
    c_onesneg = din("c_onesneg", [128, 128], BF16)
    c_onesb = din("c_onesb", [128, 128], BF16)
    c_onesf = din("c_onesf", [128, 128])
    c_mask_ab_p = [din(f"c_mask_ab{p}", [128, 4, 256], BF16) for p in range(2)]
    c_mask_c_p = [din(f"c_mask_c{p}", [128, 4, 256], BF16) for p in range(2)]
    per_layer = {}
    for l in layers:
        per_layer[l] = dict(
            wk=din(f"wk{l}", [D, WKC]), wq=din(f"wq{l}", [D, WQC]), wo=din(f"wo{l}", [D, D]),
            lng=din(f"lng{l}", [128, D]), lnb=din(f"lnb{l}", [128, D]),
            bfg=din(f"bfg{l}", [4, 1]), lamv=din(f"lamv{l}", [128, 4, 64]), subg=din(f"subg{l}", [128, 1]),
        )
    out = nc.dram_tensor("out", [OWN, D], F32, kind="ExternalOutput").ap()

    KTA = dscr("KTA", [4, 128, S], BF16)
    KTB = dscr("KTB", [4, 70, S], BF16)
    KTC = dscr("KTC", [4, 64, S], BF16)
    VS = dscr("VS", [S, 1024], BF16)
    QTA = dscr("QTA", [4, 128, OWN], BF16)
    QTB = dscr("QTB", [4, 70, OWN], BF16)
    QTC = dscr("QTC", [4, 64, OWN], BF16)
    GT = dscr("GT", [1024, OWN], F32)
    HRES = dscr("HRES", [OWN, D], F32)
    YT = dscr("YT", [1024, OWN], BF16)
    NLF = dscr("NLF", [4, S], F32)
    CN = dscr("CN", [4, S], F32)
    HP = [dscr(f"HP{p}", [OWN, D], F32) for p in range(2)]

    if DEBUG == "C0":
        DBG_U = nc.dram_tensor("DBG_U", [128, 1024], F32, kind="ExternalOutput").ap()
        DBG_L = nc.dram_tensor("DBG_L", [128, 1024], BF16, kind="ExternalOutput").ap()
        DBG_X = nc.dram_tensor("DBG_X", [128, 1024], F32, kind="ExternalOutput").ap()
    X0 = nc.alloc_psum_tensor("X0", [128, 1024], F32)
    X1 = nc.alloc_psum_tensor("X1", [128, 1024], F32)
    X2 = nc.alloc_psum_tensor("X2", [128, 1024], F32)
    PA = nc.alloc_psum_tensor("PA", [128, 512], F32)
    PB = nc.alloc_psum_tensor("PB", [128, 512], F32)
    XS = [X0, X1, X2]

    ident = A.alloc("ident", [128, 128], BF16)
    tneg = A.alloc("tneg", [128, 128], BF16)
    onesneg = A.alloc("onesneg", [128, 128], BF16)
    onesb = A.alloc("onesb", [128, 128], BF16)
    onesf = A.alloc("onesf", [128, 128], F32)
    mask_ab = A.alloc("mask_ab", [128, 4, 256], BF16)
    mask_c = A.alloc("mask_c", [128, 4, 256], BF16)
    selg_sb = A.alloc("selg", [4, 2], F32)
    lam_t = A.alloc("lam", [128, 8], F32)
    subg_t = A.alloc("subg", [128, 2], F32)
    negb_t = A.alloc("negb", [4, 2], F32)
    for dst, src, k in ((ident, c_ident, "ident"), (tneg, c_tneg, "tneg"), (onesneg, c_onesneg, "onesneg"),
                        (onesb, c_onesb, "onesb"), (onesf, c_onesf, "onesf")):
        P.op("sp", lambda e, dst=dst, src=src: e.dma_start(out=dst[:], in_=src[:, :]), writes=[k], dma=True)
    blend_sb = A.alloc("blend", [128, 4], F32)
    P.op("sp", lambda e: e.dma_start(out=blend_sb[:], in_=blend[:, :]), writes=["blend"], dma=True)
    persist_mark = A.mark()
    CONST_KEYS = ["ident", "tneg", "onesneg", "onesb", "onesf", "mask_ab", "mask_c", "selg"]

    def reload_const_keys():
        pass

    out_stores = []

    fused = len(layers) > 1
    schedule = []
    for li, l in enumerate(layers):
        npass = 2 if (fused and li < len(layers) - 1) else 1
        for p_ in range(npass):
            schedule.append((li, l, p_))
    for (li, l, pss) in schedule:
        pl = per_layer[l]
        lam_init = lam_inits[l]
        do_ln_in = (l == 0)
        last_layer = (li == len(layers) - 1)
        do_k = (pss == 0)
        from_hp = (li > 0)
        src_own = (HP[0] if from_hp else (hin_own if pss == 0 else hin_oth))
        dest = out if last_layer else HP[pss]
        rq_cos, rq_sin = rq_cos_p[pss], rq_sin_p[pss]
        A.reset(persist_mark)
        P.op("sp", lambda e: e.dma_start(out=mask_ab[:], in_=c_mask_ab_p[pss][:, :, :]), writes=["mask_ab"], dma=True)
        P.op("sp", lambda e: e.dma_start(out=mask_c[:], in_=c_mask_c_p[pss][:, :, :]), writes=["mask_c"], dma=True)
        P.op("sp", lambda e: e.dma_start(out=selg_sb[:], in_=selg_p[pss][:, :]), writes=["selg"], dma=True)

        lamv_sb = A.alloc("lamv", [128, 4, 64], F32)
        lprod = A.alloc("lprod", [128, 2, 64], F32)
        P.op("sp", lambda e: e.dma_start(out=lamv_sb[:], in_=pl["lamv"][:, :, :]), writes=["lamv"], dma=True)
        P.op("sp", lambda e: e.dma_start(out=subg_t[:, 0:1], in_=pl["subg"][:, :]), writes=["subg0"], dma=True)
        P.op("sp", lambda e: e.dma_start(out=negb_t[:, 0:1], in_=pl["bfg"][:, :]), writes=["negb0"], dma=True)
        P.op("dve", lambda e: e.tensor_tensor(out=lprod[:, 0, :], in0=lamv_sb[:, 0, :], in1=lamv_sb[:, 1, :], op=ALU.mult),
             reads=["lamv"], writes=["lprod"])
        P.op("dve", lambda e: e.tensor_tensor(out=lprod[:, 1, :], in0=lamv_sb[:, 2, :], in1=lamv_sb[:, 3, :], op=ALU.mult),
             reads=["lamv", "lprod"], writes=["lprod"])
        P.op("dve", lambda e: e.reduce_sum(out=lam_t[:, 0:1], in_=lprod[:, 0, :], axis=mybir.AxisListType.X),
             reads=["lprod"], writes=["lam01"])
        P.op("dve", lambda e: e.reduce_sum(out=lam_t[:, 1:2], in_=lprod[:, 1, :], axis=mybir.AxisListType.X),
             reads=["lprod", "lam01"], writes=["lam01"])
        P.op("act", lambda e: e.activation(out=lam_t[:, 2:4], in_=lam_t[:, 0:2], func=AF.Exp), reads=["lam01"], writes=["lam23"])
        P.op("dve", lambda e: e.scalar_tensor_tensor(out=lam_t[:, 4:5], in0=lam_t[:, 3:4], scalar=-float(lam_init),
                                                     in1=lam_t[:, 2:3], op0=ALU.add, op1=ALU.subtract),
             reads=["lam23"], writes=["neglam"])
        P.op("dve", lambda e: e.tensor_scalar(out=subg_t[:, 1:2], in0=subg_t[:, 0:1], scalar1=float((1.0 - lam_init) * math.sqrt(128.0)),
                                              scalar2=None, op0=ALU.mult), reads=["subg0"], writes=["gsub"])
        P.op("dve", lambda e: e.tensor_scalar(out=negb_t[:, 1:2], in0=negb_t[:, 0:1], scalar1=-1.0, scalar2=None, op0=ALU.mult),
             reads=["negb0"], writes=["negb"])
        neglam = lam_t[:, 4:5]
        gsub = subg_t[:, 1:2]
        negb = negb_t[:, 1:2]
        layer_mark = A.mark()

        Wk = A.alloc("Wk", [128, 8, WKC], BF16)
        Wq = A.alloc("Wq", [128, 8, WQC], BF16)
        for c in range(8):
            P.op("pool", lambda e, c=c: e.dma_start(out=Wk[:, c, :], in_=pl["wk"][c * 128:(c + 1) * 128, :]), writes=[("Wk", c)], dma=True)
        for c in range(8):
            P.op("pool", lambda e, c=c: e.dma_start(out=Wq[:, c, :], in_=pl["wq"][c * 128:(c + 1) * 128, :]), writes=[("Wq", c)], dma=True)
        WK_KEYS = [("Wk", c) for c in range(8)]
        WQ_KEYS = [("Wq", c) for c in range(8)]
        if do_ln_in:
            g_in = A.alloc("g_in", [128, D], F32)
            b_in = A.alloc("b_in", [128, D], F32)
            P.op("sp", lambda e: e.dma_start(out=g_in[:], in_=lng_in[:, :]), writes=["g_in"], dma=True)
            P.op("sp", lambda e: e.dma_start(out=b_in[:], in_=lnb_in[:, :]), writes=["b_in"], dma=True)
        xin_t = [A.alloc("xin", [128, D], F32) for _ in range(2)]
        hf_t = [A.alloc("hf", [128, D], F32) for _ in range(2)]
        hb_t = [A.alloc("hb", [128, D], BF16) for _ in range(2)]
        hT_t = [A.alloc("hT", [128, 8, 512], BF16) for _ in range(2)]
        rc_t = [A.alloc("rc", [128, 512], F32) for _ in range(2)]
        rs_t = [A.alloc("rs", [128, 512], F32) for _ in range(2)]
        tmp_t = [A.alloc("tmp", [128, 512], F32) for _ in range(2)]
        stb_t = [A.alloc("stb", [128, 512], BF16) for _ in range(4)]
        stf_t = [A.alloc("stf", [128, 512], F32) for _ in range(2)]
        st6 = A.alloc("st6", [128, 12], F32)
        mv = A.alloc("mv", [128, 4], F32)
        accs = [(X0, 0), (X0, 512), (X1, 0), (X1, 512), (X2, 0), (X2, 512)]
        cnt = dict(tile=0, acc=0, stb=0, stf=0, tmp=0, grp=0)
        stores1 = []

        def layer_norm(src, skey, dst, dkey, gt, bt, gkeys):
            for hh in range(2):
                P.op("dve", lambda e, hh=hh: e.bn_stats(out=st6[:, hh * 6:(hh + 1) * 6], in_=src[:, hh * 512:(hh + 1) * 512]),
                     reads=[skey], writes=[("st6", hh)])
            P.op("dve", lambda e: e.bn_aggr(out=mv[:, 0:2], in_=st6[:, 0:12]), reads=[("st6", 0), ("st6", 1)], writes=["mv"])
            P.op("act", lambda e: e.activation(out=mv[:, 3:4], in_=mv[:, 1:2], func=AF.Ln, bias=float(LN_EPS), scale=1.0),
                 reads=["mv"], writes=["lnv"])
            P.op("act", lambda e: e.activation(out=mv[:, 2:3], in_=mv[:, 3:4], func=AF.Exp, scale=-0.5), reads=["lnv"], writes=["rstd"])
            P.op("dve", lambda e: e.tensor_scalar(out=dst[:], in0=src[:], scalar1=mv[:, 0:1], scalar2=mv[:, 2:3],
                                                  op0=ALU.subtract, op1=ALU.mult), reads=[skey, "mv", "rstd"], writes=[dkey])
            P.op("dve", lambda e: e.tensor_tensor(out=dst[:], in0=dst[:], in1=gt[:], op=ALU.mult), reads=[dkey, gkeys[0]], writes=[dkey])
            P.op("dve", lambda e: e.tensor_tensor(out=dst[:], in0=dst[:], in1=bt[:], op=ALU.add), reads=[dkey, gkeys[1]], writes=[dkey])

        def next_acc():
            a = accs[cnt["acc"] % len(accs)]
            key = ("acc", cnt["acc"] % len(accs))
            cnt["acc"] += 1
            return a[0], a[1], key

        def next_stb():
            i = cnt["stb"] % 4
            cnt["stb"] += 1
            return stb_t[i], ("stb", i)

        def next_stf():
            i = cnt["stf"] % 2
            cnt["stf"] += 1
            return stf_t[i], ("stf", i)

        def proj_pass(src, ngroups, mode):
            W = Wk if mode == "k" else Wq
            WKEYS = WK_KEYS if mode == "k" else WQ_KEYS
            rcos = rk_cos if mode == "k" else rq_cos
            rsin = rk_sin if mode == "k" else rq_sin
            gslot = {}

            def prep_begin(G):
                gs = cnt["grp"] % 2
                cnt["grp"] += 1
                gslot[G] = gs
                rc, rs_ = rc_t[gs], rs_t[gs]
                P.op("sp", lambda e: e.dma_start(out=rc[:], in_=rcos[:, G * 512:(G + 1) * 512]), writes=[("rc", gs)], dma=True)
                P.op("sp", lambda e: e.dma_start(out=rs_[:], in_=rsin[:, G * 512:(G + 1) * 512]), writes=[("rs", gs)], dma=True)

            tstate = {}

            def prep_load(G, tt):
                gs = gslot[G]
                ts_ = cnt["tile"] % 2
                cnt["tile"] += 1
                row0 = G * 512 + tt * 128
                xin = xin_t[ts_]
                tstate[(G, tt)] = ts_
                if src == "blend":
                    idx = row0 % OWN
                    t1 = hf_t[ts_]
                    P.op("sp", lambda e: e.dma_start(out=xin[:], in_=HP[0][idx:idx + 128, :]), writes=[("xin", ts_)], dma=True)
                    P.op("sp", lambda e: e.dma_start(out=t1[:], in_=HP[1][idx:idx + 128, :]), writes=[("hf", ts_)], dma=True)
                else:
                    P.op("sp", lambda e: e.dma_start(out=xin[:], in_=src[row0:row0 + 128, :]), writes=[("xin", ts_)], dma=True)

            def prep_norm(G, tt):
                ts_ = tstate[(G, tt)]
                row0 = G * 512 + tt * 128
                xin = xin_t[ts_]
                if src == "blend":
                    rr = row0 // OWN
                    t1 = hf_t[ts_]
                    P.op("dve", lambda e: e.tensor_scalar(out=xin[:], in0=xin[:], scalar1=blend_sb[:, 2 * rr:2 * rr + 1], scalar2=None,
                                                          op0=ALU.mult), reads=[("xin", ts_)], writes=[("xin", ts_)])
                    P.op("dve", lambda e: e.scalar_tensor_tensor(out=xin[:], in0=t1[:], scalar=blend_sb[:, 2 * rr + 1:2 * rr + 2], in1=xin[:],
                                                                 op0=ALU.mult, op1=ALU.add),
                         reads=[("xin", ts_), ("hf", ts_)], writes=[("xin", ts_)])
                if do_ln_in:
                    hf = hf_t[ts_]
                    hfk = ("hf", ts_)
                    layer_norm(xin, ("xin", ts_), hf, hfk, g_in, b_in, ["g_in", "b_in"])
                else:
                    hf, hfk = xin, ("xin", ts_)
                if mode == "q":
                    stores1.append(P.op("pool", lambda e: e.dma_start(out=HRES[row0:row0 + 128, :], in_=hf[:]), reads=[hfk], dma=True))
                hb = hb_t[ts_]
                P.op("act", lambda e: e.copy(out=hb[:], in_=hf[:]), reads=[hfk], writes=[("hb", ts_)])

            def prep_tr(G, tt):
                gs = gslot[G]
                hT = hT_t[gs]
                ts_ = tstate[(G, tt)]
                hb = hb_t[ts_]
                for half in range(2):
                    ps, o0, akey = next_acc()
                    for c4 in range(4):
                        c = half * 4 + c4
                        P.op("pe", lambda e, c4=c4, c=c: e.matmul(
                            out=ps[:, o0 + c4 * 128:o0 + (c4 + 1) * 128], lhsT=hb[:, c * 128:(c + 1) * 128], rhs=ident[:],
                            start=True, stop=True), reads=[("hb", ts_), "ident"], writes=[akey])
                    P.op("dve", lambda e: e.tensor_copy(
                        out=hT[:, half * 4:(half + 1) * 4, tt * 128:(tt + 1) * 128],
                        in_=ps[:, o0:o0 + 512].rearrange("p (c t) -> p c t", c=4)),
                        reads=[akey], writes=[("hT", gs, tt, half)])

            def chunks(G):
                gs = gslot[G]
                hT = hT_t[gs]
                hTkeys = [("hT", gs, tt_, hf_) for tt_ in range(4) for hf_ in range(2)]
                rc, rs_ = rc_t[gs], rs_t[gs]

                def fm_chunk(col0, M):
                    ps, o0, akey = next_acc()
                    for c in range(8):
                        P.op("pe", lambda e, c=c: e.matmul(
                            out=ps[0:M, o0:o0 + 512], lhsT=W[:, c, col0:col0 + M], rhs=hT[:, c, :], start=(c == 0), stop=(c == 7)),
                            reads=hTkeys + [WKEYS[c]], writes=[akey])
                    return ps, o0, akey

                tok0 = G * 512
                for i in range(4):
                    p1, o1, k1 = fm_chunk(i * 128, 128)
                    p2, o2, k2 = fm_chunk(512 + i * 128, 128)
                    t1 = tmp_t[0]
                    t2 = tmp_t[1]
                    P.op("dve", lambda e: e.tensor_tensor(out=t1[:], in0=p1[:, o1:o1 + 512], in1=rc[:], op=ALU.mult),
                         reads=[k1, ("rc", gs)], writes=[("tmp", 0)])
                    P.op("dve", lambda e: e.tensor_tensor(out=t2[:], in0=p2[:, o2:o2 + 512], in1=rs_[:], op=ALU.mult),
                         reads=[k2, ("rs", gs)], writes=[("tmp", 1)])
                    sb_, sk = next_stb()
                    P.op("dve", lambda e: e.tensor_tensor(out=sb_[:], in0=t1[:], in1=t2[:], op=ALU.add),
                         reads=[("tmp", 0), ("tmp", 1)], writes=[sk])
                    dstT = KTA if mode == "k" else QTA
                    stores1.append(P.op("pool", lambda e: e.dma_start(out=dstT[i, :, tok0:tok0 + 512], in_=sb_[:]), reads=[sk], dma=True))
                    yield
                for kind, cbase, dstT in (("B", 1024, KTB if mode == "k" else QTB), ("C", 1280, KTC if mode == "k" else QTC)):
                    for h in range(4):
                        ps, o0, akey = fm_chunk(cbase + h * 64, 64)
                        sb_, sk = next_stb()
                        if mode == "k":
                            P.op("act", lambda e: e.copy(out=sb_[0:64, :], in_=ps[0:64, o0:o0 + 512]), reads=[akey], writes=[sk])
                        else:
                            P.op("act", lambda e: e.mul(out=sb_[0:64, :], in_=ps[0:64, o0:o0 + 512], mul=0.125), reads=[akey], writes=[sk])
                        stores1.append(P.op("pool", lambda e: e.dma_start(out=dstT[h, 0:64, tok0:tok0 + 512], in_=sb_[0:64, :]),
                                            reads=[sk], dma=True))
                        yield
                if mode == "k":
                    ps, o0, akey = fm_chunk(1536, 4)
                    sf, sfk = next_stf()
                    P.op("act", lambda e: e.activation(out=sf[0:4, :], in_=ps[0:4, o0:o0 + 512], func=AF.Exp, bias=negb, scale=-1.0),
                         reads=[akey, "negb"], writes=[sfk])
                    P.op("act", lambda e: e.activation(out=sf[0:4, :], in_=sf[0:4, :], func=AF.Ln, bias=1.0, scale=1.0),
                         reads=[sfk], writes=[sfk])
                    r = G // 8
                    for bb in range(2):
                        i_blk = (G % 8) * 2 + bb
                        t0 = i_blk * 512 + r * 256
                        stores1.append(P.op("pool", lambda e, bb=bb, t0=t0: e.dma_start(
                            out=NLF[:, t0:t0 + 256], in_=sf[0:4, bb * 256:(bb + 1) * 256]), reads=[sfk], dma=True))
                    yield
                    for tt in range(4):
                        for half in range(2):
                            ps, o0, akey = next_acc()
                            for c in range(8):
                                P.op("pe", lambda e, c=c: e.matmul(
                                    out=ps[:, o0:o0 + 512], lhsT=hT[:, c, tt * 128:(tt + 1) * 128],
                                    rhs=W[:, c, 1540 + half * 512:1540 + (half + 1) * 512], start=(c == 0), stop=(c == 7)),
                                    reads=hTkeys + [WKEYS[c]], writes=[akey])
                            sb_, sk = next_stb()
                            P.op("act", lambda e: e.copy(out=sb_[:], in_=ps[:, o0:o0 + 512]), reads=[akey], writes=[sk])
                            r0 = tok0 + tt * 128
                            stores1.append(P.op("pool", lambda e: e.dma_start(
                                out=VS[r0:r0 + 128, half * 512:(half + 1) * 512], in_=sb_[:]), reads=[sk], dma=True))
                            yield
                else:
                    for gch in range(8):
                        ps, o0, akey = fm_chunk(1536 + gch * 128, 128)
                        sf, sfk = next_stf()
                        P.op("act", lambda e: e.activation(out=sf[:], in_=ps[:, o0:o0 + 512], func=AF.Silu), reads=[akey], writes=[sfk])
                        stores1.append(P.op("pool", lambda e: e.dma_start(
                            out=GT[gch * 128:(gch + 1) * 128, tok0:tok0 + 512], in_=sf[:]), reads=[sfk], dma=True))
                        yield

            prep_begin(0)
            for tt in range(4):
                prep_load(0, tt)
                prep_norm(0, tt)
                prep_tr(0, tt)
            ev_load = {0: 0, 3: 1, 8: 2, 13: 3}
            ev_norm = {1: 0, 6: 1, 11: 2, 16: 3}
            ev_tr = {5: 0, 10: 1, 15: 2, 19: 3}
            for G in range(ngroups):
                more = (G + 1 < ngroups)
                done = set()

                def fire(idx):
                    if not more:
                        return
                    for ev, fn, tag in ((ev_load, prep_load, "l"), (ev_norm, prep_norm, "n"), (ev_tr, prep_tr, "t")):
                        if idx in ev and (tag, ev[idx]) not in done:
                            done.add((tag, ev[idx]))
                            fn(G + 1, ev[idx])
                if more:
                    prep_begin(G + 1)
                fire(0)
                idx = 0
                for _ in chunks(G):
                    idx += 1
                    fire(idx)
                for k in range(idx + 1, 24):
                    fire(k)

        if do_k:
            proj_pass("blend" if from_hp else hin_full, 16, "k")
        proj_pass(src_own, 8, "q")
        P.full_barrier()
        A.reset(layer_mark)

        nlf = A.alloc("nlf", [4, S], F32)
        cn = A.alloc("cn", [4, S], F32)
        pk = [A.alloc("pk", [4, S], BF16) for _ in range(3)]
        cq = [A.alloc("cq", [4, OWN], F32) for _ in range(2)]
        pq = [A.alloc("pq", [4, OWN], BF16) for _ in range(3)]
        ones_r = A.alloc("ones_r", [4, S], BF16)
        P.op("pool", lambda e: e.memset(ones_r[:], 1.0), writes=["ones_r"])
        if do_k:
            P.op("sp", lambda e: e.dma_start(out=nlf[:], in_=NLF[:, :]), writes=["nlf"], dma=True)
            P.op("dve", lambda e: e.tensor_tensor_scan(out=cn[:], data0=nlf[:], data1=nlf[:], initial=0.0, op0=ALU.add, op1=ALU.max),
                 reads=["nlf"], writes=["cn"])
            st_cn = P.op("pool", lambda e: e.dma_start(out=CN[:, :], in_=cn[:]), reads=["cn"], dma=True)
            P.barrier("sp", [st_cn])
        CNv = CN.rearrange("h (i r t) -> h r i t", i=16, r=2, t=256)
        for r in range(2):
            if do_k:
                P.op("sp", lambda e, r=r: e.dma_start(out=nlf[:, r * OWN:(r + 1) * OWN].rearrange("h (i t) -> h i t", t=256),
                                                     in_=CNv[:, r, :, :]), reads=["cn"], writes=["nlf"], dma=True)
            P.op("sp", lambda e, r=r: e.dma_start(out=cq[r][:].rearrange("h (i t) -> h i t", t=256), in_=CNv[:, r, :, :]),
                 writes=[("cq", r)], dma=True)

        def split3(src, skey, pieces, pkey):
            for p_ in range(3):
                P.op("dve", lambda e, p_=p_: e.tensor_copy(out=pieces[p_][:], in_=src[:]), reads=[skey], writes=[(pkey, p_)])
                if p_ < 2:
                    P.op("dve", lambda e, p_=p_: e.tensor_tensor(out=src[:], in0=src[:], in1=pieces[p_][:], op=ALU.subtract),
                         reads=[skey, (pkey, p_)], writes=[skey])

        if do_k:
            split3(nlf, "nlf", pk, "pk")
        P.op("dve", lambda e: e.tensor_scalar(out=cq[0][:], in0=cq[0][:], scalar1=selg_sb[:, 0:1], scalar2=-1.0, op0=ALU.mult, op1=ALU.mult),
             reads=[("cq", 0)], writes=[("cq", 0)])
        P.op("dve", lambda e: e.tensor_scalar(out=cq[1][:], in0=cq[1][:], scalar1=selg_sb[:, 1:2], scalar2=-1.0, op0=ALU.mult, op1=ALU.mult),
             reads=[("cq", 1)], writes=[("cq", 1)])
        P.op("dve", lambda e: e.tensor_tensor(out=cq[0][:], in0=cq[0][:], in1=cq[1][:], op=ALU.add),
             reads=[("cq", 0), ("cq", 1)], writes=[("cq", 0)])
        split3(cq[0], ("cq", 0), pq, "pq")
        bst = []
        for h in range(4):
            for p_ in range(3):
                if do_k:
                    bst.append(P.op("pool", lambda e, h=h, p_=p_: e.dma_start(out=KTB[h, 67 + p_:68 + p_, :], in_=pk[p_][h:h + 1, :]),
                                    reads=[("pk", p_)], dma=True))
                bst.append(P.op("pool", lambda e, h=h, p_=p_: e.dma_start(out=QTB[h, 64 + p_:65 + p_, :], in_=pq[p_][h:h + 1, :]),
                                reads=[("pq", p_)], dma=True))
            if do_k:
                bst.append(P.op("pool", lambda e, h=h: e.dma_start(out=KTB[h, 64:67, :], in_=ones_r[0:3, :]), reads=["ones_r"], dma=True))
            bst.append(P.op("pool", lambda e, h=h: e.dma_start(out=QTB[h, 67:70, :], in_=ones_r[0:3, 0:OWN]), reads=["ones_r"], dma=True))
        P.full_barrier()
        A.reset(layer_mark)

        kt_t = [A.alloc("kt", [128, S], BF16) for _ in range(2)]
        qt_t = [A.alloc("qt", [128, OWN], BF16) for _ in range(2)]
        vt_t = [A.alloc("vt", [128, 64, 128], BF16) for _ in range(2)]
        qz_t = [A.alloc("qz", [128, 2, OWN], BF16) for _ in range(2)]
        for us_ in range(2):
            P.op("pool", lambda e: e.memset(qz_t[us_][64:128, 0, :], 0.0), writes=[("qz0", us_)])
            P.op("pool", lambda e: e.memset(qz_t[us_][0:64, 1, :], 0.0), writes=[("qz1", us_)])
        E_t = [A.alloc("E", [128, 1024], BF16) for _ in range(4)]
        U_t = [A.alloc("U", [128, 1024], F32) for _ in range(3)]
        Lp_t = [A.alloc("Lp", [128, 1024], BF16) for _ in range(3)]
        Y_t = [A.alloc("Y", [128, 1024], F32) for _ in range(3)]
        cb_t = [A.alloc("cb", [128, 256], F32) for _ in range(3)]
        gate_t = [A.alloc("gate", [128, 256], F32) for _ in range(2)]
        r_t = [A.alloc("r", [128, 256], F32) for _ in range(2)]
        o_t = [A.alloc("o", [128, 256], F32) for _ in range(3)]
        sq_t = A.alloc("sq", [128, 256], F32)
        rstd_t = A.alloc("rstd", [128, 256], F32)
        ys_t = [A.alloc("ys", [128, 256], BF16) for _ in range(2)]
        Es_t = [A.alloc("Es", [128, 1024], F32) for _ in range(2)]
        ystores = []
        ucount = [0]
        ycount = [0]

        def kcol(j, m):
            return (m // 2) * OWN + j * 256 + (m % 2) * 128

        def vch(j, m):
            return (m // 2) * 32 + j * 2 + (m % 2)

        def ktk(us):
            return [("kt", us, 0), ("kt", us, 1), ("kt", us, "z")]

        def qtk(us):
            return [("qt", us, 0), ("qt", us, 1), ("qt", us, "z"), ("qz0", us), ("qz1", us)]

        def load_unit(kind, h):
            us = ucount[0] % 2
            ucount[0] += 1
            kt, qt, vt = kt_t[us], qt_t[us], vt_t[us]
            rows = {"A": 128, "B": 70, "C": 64}[kind]
            KT = {"A": KTA, "B": KTB, "C": KTC}[kind]
            QT = {"A": QTA, "B": QTB, "C": QTC}[kind]
            if kind == "C":
                P.op("pool", lambda e: e.memset(kt[64:128, :], 0.0), writes=[("kt", us, "z")])
                P.op("pool", lambda e: e.memset(qt[64:128, :], 0.0), writes=[("qt", us, "z")])
            for r in range(2):
                P.op("sp", lambda e, r=r: e.dma_start(out=kt[0:rows, r * OWN:(r + 1) * OWN], in_=KT[h, :, r * OWN:(r + 1) * OWN]),
                     writes=[("kt", us, r)], reads=[("kt", us, "z")], dma=True)
            if kind == "A":
                qz = qz_t[us]
                P.op("sp", lambda e: e.dma_start(out=qz[0:64, 0, :], in_=QT[h, 0:64, :]), writes=[("qt", us, 0)], dma=True)
                P.op("sp", lambda e: e.dma_start(out=qz[64:128, 1, :], in_=QT[h, 64:128, :]), writes=[("qt", us, 1)], dma=True)
            else:
                P.op("sp", lambda e: e.dma_start(out=qt[0:rows, :], in_=QT[h, :, :]), writes=[("qt", us, 0)], reads=[("qt", us, "z")], dma=True)
            if kind == "A":
                c0, cw = h * 128, 128
            elif kind == "B":
                c0, cw = 512 + h * 64, 64
            else:
                c0, cw = 768 + h * 64, 64
            if kind == "B":
                P.op("pool", lambda e: e.memset(vt[:, :, 64:128], 1.0), writes=[("vt", us)])
            for q8 in range(8):
                P.op("sp", lambda e, q8=q8: e.dma_start(
                    out=vt[:, q8 * 8:(q8 + 1) * 8, 0:cw],
                    in_=VS[q8 * 1024:(q8 + 1) * 1024, c0:c0 + cw].rearrange("(c p) w -> p c w", p=128)),
                    writes=[("vt", us, q8)], reads=[("vt", us)], dma=True)
            return us

        def vkeys(us):
            return [("vt", us)] + [("vt", us, q8) for q8 in range(8)]

        def load_gate(row0, nrows, i, slot=None):
            gs = (ycount[0] % 2) if slot is None else slot
            g = gate_t[gs]
            P.op("sp", lambda e: e.dma_start(out=g[0:nrows, :], in_=GT[row0:row0 + nrows, i * 256:(i + 1) * 256]),
                 writes=[("gate", gs)], dma=True)
            return g, ("gate", gs)

        def store_y(ysrc_fn, row0, nrows, i, reads):
            ys = ys_t[ycount[0] % 2]
            yk = ("ys", ycount[0] % 2)
            ycount[0] += 1
            ysrc_fn(ys, yk)
            ystores.append(P.op("pool", lambda e: e.dma_start(out=YT[row0:row0 + nrows, i * 256:(i + 1) * 256], in_=ys[0:nrows, :]),
                                reads=[yk], dma=True))

        def attn_A(h, us):
            kt, qt, vt = kt_t[us], qt_t[us], vt_t[us]
            items = [(i, j, sub) for i in range(NB) for j in range(i + 1) for sub in (0, 1)]
            OP = [(PA, 0), (PA, 256)]
            LP = [(PB, 0), (PB, 256)]
            pending = []
            gate_of = {}
            lsb_t, rsb_t = Es_t[0], Es_t[1]
            x2r = []

            def QK(w):
                i, j, sub = items[w]
                xs = sub
                X = XS[xs]
                r0 = sub * 64
                diag = (j == i)
                for m in range(4):
                    kc = kcol(j, m)
                    P.op("pe", lambda e, m=m, kc=kc: e.matmul(
                        out=X[:, m * 256:(m + 1) * 256], lhsT=kt[:, kc:kc + 128], rhs=qz_t[us][:, sub, i * 256:(i + 1) * 256],
                        start=True, stop=not diag), reads=ktk(us) + qtk(us), writes=[("X", xs)])
                    if diag:
                        P.op("pe", lambda e, m=m: e.matmul(out=X[:, m * 256:(m + 1) * 256], lhsT=ident[:], rhs=mask_ab[:, m, :],
                                                           start=False, stop=True), writes=[("X", xs)])

            def EXP(w):
                i, j, sub = items[w]
                es = (w % 4)
                P.op("act", lambda e: e.activation(out=E_t[es][:], in_=XS[sub][:], func=AF.Exp), reads=[("X", sub)], writes=[("E", es)])

            def PV(w):
                i, j, sub = items[w]
                es = (w % 4)
                E = E_t[es]
                first, last = (j == 0), (j == i)
                if first and sub == 0:
                    gate_of[i] = load_gate(h * 128, 128, i, slot=i % 2)
                for m in range(4):
                    ch = vch(j, m)
                    ot_, oc_ = OP[sub]
                    P.op("pe", lambda e, m=m, ch=ch: e.matmul(out=ot_[:, oc_:oc_ + 256], lhsT=vt[:, ch, :], rhs=E[:, m * 256:(m + 1) * 256],
                                                              start=(first and m == 0 and sub == 0), stop=(last and m == 3),
                                                              skip_group_check=True),
                         reads=[("E", es)] + vkeys(us), writes=["bankPA"])
                    lt_, lc_ = LP[sub]
                    P.op("pe", lambda e, m=m: e.matmul(out=lt_[:, lc_:lc_ + 256], lhsT=onesb[:], rhs=E[:, m * 256:(m + 1) * 256],
                                                       start=(first and m == 0 and sub == 0), stop=(last and m == 3),
                                                       skip_group_check=True),
                         reads=[("E", es)], writes=["bankPB"])
                if last and sub == 1:
                    epi0(i)

            def epi0(i):
                for sub in (0, 1):
                    P.op("dve", lambda e, sub=sub: e.tensor_copy(out=o_t[sub][:], in_=OP[sub][0][:, OP[sub][1]:OP[sub][1] + 256]),
                         reads=["bankPA"], writes=[("o", sub)])
                P.op("dve", lambda e: e.tensor_copy(out=lsb_t[:, 0:512], in_=PB[:, 0:512]), reads=["bankPB"], writes=["lsb"])
                pending.append([2, lambda: epi1(i)])

            def epi1(i):
                P.op("act", lambda e: e.activation(out=rsb_t[:, 0:512], in_=lsb_t[:, 0:512], func=AF.Ln), reads=["lsb"], writes=["rsb"])
                P.op("act", lambda e: e.activation(out=rsb_t[:, 0:512], in_=rsb_t[:, 0:512], func=AF.Exp, scale=-1.0), reads=["rsb"], writes=["rsb"])
                for sub in (0, 1):
                    P.op("dve", lambda e, sub=sub: e.tensor_tensor(out=o_t[sub][:], in0=o_t[sub][:], in1=rsb_t[:, sub * 256:(sub + 1) * 256], op=ALU.mult),
                         reads=[("o", sub), "rsb"], writes=[("o", sub)])
                P.op("dve", lambda e: e.scalar_tensor_tensor(out=o_t[2][:], in0=o_t[1][:], scalar=neglam, in1=o_t[0][:],
                                                             op0=ALU.mult, op1=ALU.add), reads=[("o", 0), ("o", 1)], writes=[("o", 2)])
                P.op("dve", lambda e: e.tensor_tensor(out=sq_t[:], in0=o_t[2][:], in1=o_t[2][:], op=ALU.mult), reads=[("o", 2)], writes=["sq"])
                pending.append([2, lambda: epi2(i)])

            def epi2(i):
                g, gk = gate_of.pop(i)
                P.op("pe", lambda e: e.matmul(out=X2[:, 512:768], lhsT=onesf[:], rhs=sq_t[:], start=True, stop=True),
                     reads=["sq"], writes=["bankX2b"])
                x2r.append(P.op("act", lambda e: e.activation(out=rstd_t[:], in_=X2[:, 512:768], func=AF.Ln, bias=float(128.0 * SUBLN_EPS),
                                                              scale=1.0), reads=["bankX2b"], writes=["rstd2"]))
                P.op("act", lambda e: e.activation(out=rstd_t[:], in_=rstd_t[:], func=AF.Exp, scale=-0.5), reads=["rstd2"], writes=["rstd2"])
                P.op("dve", lambda e: e.scalar_tensor_tensor(out=o_t[2][:], in0=o_t[2][:], scalar=gsub, in1=rstd_t[:],
                                                             op0=ALU.mult, op1=ALU.mult), reads=[("o", 2), "rstd2"], writes=[("o", 2)])

                def fin(ys, yk):
                    P.op("dve", lambda e: e.tensor_tensor(out=ys[:], in0=o_t[2][:], in1=g[:], op=ALU.mult), reads=[("o", 2), gk], writes=[yk])
                store_y(fin, h * 128, 128, i, None)

            W = len(items)
            for w0 in range(min(2, W)):
                QK(w0)
            for w in range(W):
                EXP(w)
                PV(w)
                if w + 2 < W:
                    QK(w + 2)
                for pnd in list(pending):
                    pnd[0] -= 1
                    if pnd[0] <= 0:
                        pending.remove(pnd)
                        pnd[1]()
            while pending:
                pnd = pending.pop(0)
                pnd[1]()
            P.barrier("pe", x2r[-4:])

        def attn_B(h, us):
            kt, qt, vt = kt_t[us], qt_t[us], vt_t[us]
            items = [(i, j) for i in range(NB) for j in range(i + 1)]
            gate_of = {}

            def QK(w):
                i, j = items[w]
                xs = w % 3
                X = XS[xs]
                diag = (j == i)
                for m in range(4):
                    kc = kcol(j, m)
                    P.op("pe", lambda e, m=m, kc=kc: e.matmul(
                        out=X[:, m * 256:(m + 1) * 256], lhsT=kt[0:70, kc:kc + 128], rhs=qt[0:70, i * 256:(i + 1) * 256],
                        start=True, stop=not diag), reads=ktk(us) + qtk(us), writes=[("X", xs)])
                    if diag:
                        P.op("pe", lambda e, m=m: e.matmul(out=X[:, m * 256:(m + 1) * 256], lhsT=ident[:], rhs=mask_ab[:, m, :],
                                                           start=False, stop=True), writes=[("X", xs)])

            def EXP(w):
                es = w % 4
                P.op("act", lambda e: e.activation(out=E_t[es][:], in_=XS[w % 3][:], func=AF.Exp), reads=[("X", w % 3)], writes=[("E", es)])

            def PV(w):
                i, j = items[w]
                es = w % 4
                E = E_t[es]
                first, last = (j == 0), (j == i)
                PO = PA if i % 2 == 0 else PB
                pkey = "bankPA" if i % 2 == 0 else "bankPB"
                if first:
                    gate_of[i] = load_gate(512 + h * 64, 64, i, slot=i % 2)
                for m in range(4):
                    ch = vch(j, m)
                    P.op("pe", lambda e, m=m, ch=ch: e.matmul(out=PO[:, 0:256], lhsT=vt[:, ch, :], rhs=E[:, m * 256:(m + 1) * 256],
                                                              start=(first and m == 0), stop=(last and m == 3)),
                         reads=[("E", es)] + vkeys(us), writes=[pkey])
                if last:
                    g, gk = gate_of.pop(i)
                    P.op("dve", lambda e: e.reciprocal(out=r_t[0][0:64, :], in_=PO[64:128, 0:256]), reads=[pkey], writes=[("r", 0)])
                    P.op("dve", lambda e: e.tensor_tensor(out=o_t[0][0:64, :], in0=PO[0:64, 0:256], in1=r_t[0][0:64, :], op=ALU.mult),
                         reads=[pkey, ("r", 0)], writes=[("o", 0)])

                    def fin(ys, yk):
                        P.op("dve", lambda e: e.tensor_tensor(out=ys[0:64, :], in0=o_t[0][0:64, :], in1=g[0:64, :], op=ALU.mult),
                             reads=[("o", 0), gk], writes=[yk])
                    store_y(fin, 512 + h * 64, 64, i, None)

            W = len(items)
            for w0 in range(min(3, W)):
                QK(w0)
            for w in range(W):
                EXP(w)
                PV(w)
                if w + 3 < W:
                    QK(w + 3)

        def attn_C(h, us):
            kt, qt, vt = kt_t[us], qt_t[us], vt_t[us]
            items = [(i, j) for i in range(NB) for j in range(i, -1, -1)]
            W = len(items)

            def QK(w):
                i, j = items[w]
                xs = w % 3
                X = XS[xs]
                diag = (j == i)
                for m in range(4):
                    kc = kcol(j, m)
                    P.op("pe", lambda e, m=m, kc=kc: e.matmul(
                        out=X[:, m * 256:(m + 1) * 256], lhsT=kt[:, kc:kc + 128], rhs=qt[:, i * 256:(i + 1) * 256],
                        start=(m % 2 == 0), stop=True, skip_group_check=True), reads=ktk(us) + qtk(us), writes=[("X", xs)])
                    if diag:
                        P.op("pe", lambda e, m=m: e.matmul(out=X[:, m * 256:(m + 1) * 256], lhsT=ident[:], rhs=mask_c[:, m, :],
                                                           start=False, stop=True, skip_group_check=True), writes=[("X", xs)])

            def EA(w):
                xs = w % 3
                P.op("act", lambda e: e.activation(out=U_t[xs][:], in_=XS[xs][:], func=AF.Exp), reads=[("X", xs)], writes=[("U", xs)])
                P.op("act", lambda e: e.activation(out=Lp_t[xs][:], in_=U_t[xs][:], func=AF.Ln, bias=1.0, scale=1.0),
                     reads=[("U", xs)], writes=[("Lp", xs)])

            def TM(w):
                i, j = items[w]
                xs = w % 3
                X, Lp = XS[xs], Lp_t[xs]
                for m in range(4):
                    P.op("pe", lambda e, m=m: e.matmul(out=X[:, m * 256:(m + 1) * 256], lhsT=tneg[:], rhs=Lp[:, m * 256:(m + 1) * 256],
                                                       start=False, stop=(m == 3), skip_group_check=True),
                         reads=[("Lp", xs)], writes=[("X", xs)])
                    for m2 in range(m + 1, 4):
                        P.op("pe", lambda e, m=m, m2=m2: e.matmul(out=X[:, m * 256:(m + 1) * 256], lhsT=onesneg[:],
                                                                  rhs=Lp[:, m2 * 256:(m2 + 1) * 256], start=False, stop=True,
                                                                  skip_group_check=True), reads=[("Lp", xs)], writes=[("X", xs)])
                if j > 0:
                    for m in range(4):
                        P.op("pe", lambda e, m=m: e.matmul(out=PB[:, 0:256], lhsT=onesneg[:], rhs=Lp[:, m * 256:(m + 1) * 256],
                                                           start=(j == i and m == 0), stop=(m == 3), skip_group_check=True),
                             reads=[("Lp", xs)], writes=["bankPB"])
                    cs = w % 3
                    P.op("dve", lambda e: e.tensor_copy(out=cb_t[cs][:], in_=PB[:, 0:256]), reads=["bankPB"], writes=[("cb", cs)])

            def ADD(w):
                i, j = items[w]
                xs = w % 3
                Y = Y_t[xs]
                if j == i:
                    for hb_ in range(2):
                        P.op("dve", lambda e, hb_=hb_: e.tensor_copy(out=Y[:, hb_ * 512:(hb_ + 1) * 512], in_=XS[xs][:, hb_ * 512:(hb_ + 1) * 512]),
                             reads=[("X", xs)], writes=[("Y", xs)])
                else:
                    cs = (w - 1) % 3
                    for m in range(4):
                        P.op("dve", lambda e, m=m: e.tensor_tensor(out=Y[:, m * 256:(m + 1) * 256], in0=XS[xs][:, m * 256:(m + 1) * 256],
                                                                   in1=cb_t[cs][:], op=ALU.add),
                             reads=[("X", xs), ("cb", cs)], writes=[("Y", xs)])

            def EB(w):
                xs = w % 3
                es = w % 4
                P.op("act", lambda e: e.activation(out=E_t[es][:], in_=Y_t[xs][:], func=AF.Exp), reads=[("Y", xs)], writes=[("E", es)])

            def PV(w):
                i, j = items[w]
                es = w % 4
                E = E_t[es]
                first, last = (j == i), (j == 0)
                for m in range(4):
                    ch = vch(j, m)
                    P.op("pe", lambda e, m=m, ch=ch, E=E: e.matmul(out=PA[:, 0:256], lhsT=vt[:, ch, :], rhs=E[:, m * 256:(m + 1) * 256],
                                                                 start=(first and m == 0), stop=(last and m == 3)),
                         reads=[("E", es)] + vkeys(us), writes=["bankPA"])
                if last:
                    g, gk = load_gate(768 + h * 64, 64, i)

                    def fin(ys, yk):
                        P.op("dve", lambda e: e.tensor_tensor(out=ys[0:64, :], in0=PA[0:64, 0:256], in1=g[0:64, :], op=ALU.mult),
                             reads=["bankPA", gk], writes=[yk])
                    store_y(fin, 768 + h * 64, 64, i, None)

            for w0 in range(min(3, W)):
                QK(w0)
                EA(w0)
            TM(0)
            for w in range(W):
                ADD(w)
                EB(w)
                if w + 1 < W:
                    TM(w + 1)
                if w + 3 < W:
                    QK(w + 3)
                    EA(w + 3)
                PV(w)

        units = [("A", h) for h in range(4)] + [("B", h) for h in range(4)] + [("C", h) for h in range(4)]
        fns = {"A": attn_A, "B": attn_B, "C": attn_C}
        us_cur = load_unit(*units[0])
        for k_, (kind_, h_) in enumerate(units):
            us_next = load_unit(*units[k_ + 1]) if k_ + 1 < len(units) else None
            fns[kind_](h_, us_cur)
            us_cur = us_next
        P.full_barrier()
        A.reset(layer_mark)

        Wo = A.alloc("Wo", [128, 8, D], BF16)
        for c in range(8):
            P.op("pool", lambda e, c=c: e.dma_start(out=Wo[:, c, :], in_=pl["wo"][c * 128:(c + 1) * 128, :]), writes=[("Wo", c)], dma=True)
        g_o = A.alloc("g_o", [128, D], F32)
        b_o = A.alloc("b_o", [128, D], F32)
        P.op("sp", lambda e: e.dma_start(out=g_o[:], in_=pl["lng"][:, :]), writes=["g_o"], dma=True)
        P.op("sp", lambda e: e.dma_start(out=b_o[:], in_=pl["lnb"][:, :]), writes=["b_o"], dma=True)
        yt_t = [A.alloc("yt", [128, 8, 128], BF16) for _ in range(2)]
        hr_t = [A.alloc("hr", [128, D], F32) for _ in range(2)]
        z_t = [A.alloc("z", [128, D], F32) for _ in range(2)]
        ot_t = [A.alloc("ot", [128, D], F32) for _ in range(2)]
        st6 = A.alloc("st6b", [128, 12], F32)
        mv = A.alloc("mvb", [128, 4], F32)
        for T in range(OWN // 128):
            s_ = T % 2
            yt, hr, z, ot = yt_t[s_], hr_t[s_], z_t[s_], ot_t[s_]
            P.op("sp", lambda e, yt=yt, T=T: e.dma_start(out=yt[:], in_=YT[:, T * 128:(T + 1) * 128].rearrange("(c p) t -> p c t", p=128)),
                 writes=[("yt", s_)], dma=True)
            P.op("sp", lambda e, hr=hr, T=T: e.dma_start(out=hr[:], in_=HRES[T * 128:(T + 1) * 128, :]), writes=[("hr", s_)], dma=True)
            X = XS[s_]
            for half in range(2):
                for c in range(8):
                    P.op("pe", lambda e, X=X, half=half, c=c, yt=yt: e.matmul(
                        out=X[:, half * 512:(half + 1) * 512], lhsT=yt[:, c, :], rhs=Wo[:, c, half * 512:(half + 1) * 512],
                        start=(c == 0), stop=(c == 7)), reads=[("yt", s_), ("Wo", c)], writes=[("X", s_)])
            P.op("dve", lambda e, z=z, hr=hr, X=X: e.scalar_tensor_tensor(out=z[:], in0=hr[:], scalar=float(ALPHA), in1=X[:],
                                                                         op0=ALU.mult, op1=ALU.add),
                 reads=[("hr", s_), ("X", s_)], writes=[("z", s_)])
            layer_norm(z, ("z", s_), ot, ("ot", s_), g_o, b_o, ["g_o", "b_o"])
            o_ = P.op("pool", lambda e, ot=ot, T=T: e.dma_start(out=dest[T * 128:(T + 1) * 128, :], in_=ot[:]),
                      reads=[("ot", s_)], dma=True)
            if last_layer:
                out_stores.append(o_)
        P.full_barrier()

    P.final += out_stores
    P.emit()
    return nc


_BF = ml_dtypes.bfloat16


def _own_rows(g):
    return (np.arange(NB)[:, None] * 512 + g * 256 + np.arange(256)[None, :]).reshape(-1)


def _rope_tables(pos, scale):
    inv = ROPE_THETA ** (-np.arange(0, 16, 2, dtype=np.float32) / 16.0)
    ang = pos.astype(np.float32)[None, :] * inv[:, None].astype(np.float32)
    cos, sin = np.cos(ang), np.sin(ang)
    C = np.ones((128, len(pos)), np.float32)
    Sg = np.zeros((128, len(pos)), np.float32)
    for a in range(2):
        b0 = a * 64
        C[b0:b0 + 8] = cos
        C[b0 + 8:b0 + 16] = cos
        Sg[b0:b0 + 8] = -sin
        Sg[b0 + 8:b0 + 16] = sin
    return (C * scale).astype(np.float32), (Sg * scale).astype(np.float32)


def _swap_cols(cols512):
    idx = np.arange(512)
    out = idx.copy()
    for sh in range(8):
        b = sh * 64
        out[b:b + 8] = idx[b + 8:b + 16]
        out[b + 8:b + 16] = idx[b:b + 8]
    return cols512[out]


def _weight_layouts(w_in_l):
    o = {}
    pos = 0
    for name, n in (("Aq", 512), ("Ak", 512), ("Av", 512), ("Ag", 512), ("Bq", 256), ("Bk", 256), ("Bv", 256), ("Bf", 4),
                    ("Bg", 256), ("Cq", 256), ("Ck", 256), ("Cv", 256), ("Cg", 256)):
        o[name] = np.arange(pos, pos + n)
        pos += n
    kcols = np.concatenate([o["Ak"], _swap_cols(o["Ak"]), o["Bk"], o["Ck"], o["Bf"], o["Av"], o["Bv"], o["Cv"]])
    qcols = np.concatenate([o["Aq"], _swap_cols(o["Aq"]), o["Bq"], o["Cq"], o["Ag"], o["Bg"], o["Cg"]])
    assert len(kcols) == WKC and len(qcols) == WQC
    return np.ascontiguousarray(w_in_l[:, kcols]), np.ascontiguousarray(w_in_l[:, qcols])


def _masks(g):
    p = np.arange(128)[:, None, None]
    m = np.arange(4)[None, :, None]
    t = np.arange(256)[None, None, :]
    kpos = (m // 2) * 256 + (m % 2) * 128 + p
    qpos = g * 256 + t
    mab = np.where(kpos <= qpos, 0.0, MASKV).astype(np.float32)
    mc = np.where(kpos < qpos, 0.0, MASKV).astype(np.float32)
    return mab.astype(_BF), mc.astype(_BF)


_PROG_CACHE = {}


def _get_prog(layers):
    key = tuple(layers)
    if key not in _PROG_CACHE:
        lam_inits = {l: 0.8 - 0.6 * math.exp(-0.3 * l) for l in range(DEPTH)}
        _PROG_CACHE[key] = build_program(list(layers), lam_inits)
    return _PROG_CACHE[key]


def _consts(g):
    j = np.arange(128)[:, None]
    s_ = np.arange(128)[None, :]
    d = {}
    d["c_ident"] = np.eye(128, dtype=np.float32).astype(_BF)
    d["c_tneg"] = np.where(j >= s_, -1.0, 0.0).astype(np.float32).astype(_BF)
    d["c_onesneg"] = np.full((128, 128), -1.0, np.float32).astype(_BF)
    d["c_onesb"] = np.ones((128, 128), np.float32).astype(_BF)
    d["c_onesf"] = np.ones((128, 128), np.float32)
    gath = np.concatenate([_own_rows(0), _own_rows(1)])
    d["rk_cos"], d["rk_sin"] = _rope_tables(gath, 1.0)
    for p in range(2):
        gp = g if p == 0 else 1 - g
        d[f"c_mask_ab{p}"], d[f"c_mask_c{p}"] = _masks(gp)
        d[f"rq_cos{p}"], d[f"rq_sin{p}"] = _rope_tables(_own_rows(gp), 0.125)
        sel = np.zeros((4, 2), np.float32)
        sel[:, gp] = 1.0
        d[f"selg{p}"] = sel
    bl = np.zeros((128, 4), np.float32)
    for r in range(2):
        bl[:, 2 * r] = 1.0 if r == g else 0.0
        bl[:, 2 * r + 1] = 0.0 if r == g else 1.0
    d["blend"] = bl
    return d


def _layer_inputs(l, w_in, b_forget, lambda_q1, lambda_k1, lambda_q2, lambda_k2, subln_g, w_out, ln_g, ln_b):
    wk, wq = _weight_layouts(np.asarray(w_in[l], np.float32))
    rep = lambda v: np.ascontiguousarray(np.broadcast_to(np.asarray(v, np.float32)[None, :], (128, len(v))))
    lamv = np.stack([rep(lambda_q1[l]), rep(lambda_k1[l]), rep(lambda_q2[l]), rep(lambda_k2[l])], axis=1)
    return {
        f"wk{l}": wk, f"wq{l}": wq, f"wo{l}": np.ascontiguousarray(np.asarray(w_out[l], np.float32)),
        f"lng{l}": rep(ln_g[l]), f"lnb{l}": rep(ln_b[l]),
        f"bfg{l}": np.asarray(b_forget[l], np.float32).reshape(4, 1),
        f"lamv{l}": np.ascontiguousarray(lamv), f"subg{l}": np.asarray(subln_g[l], np.float32).reshape(128, 1),
    }


LAUNCH_PLAN = [[0, 1]]


def kernel(x, ln_in_g, ln_in_b, w_in, b_forget, lambda_q1, lambda_k1, lambda_q2, lambda_k2, subln_g, w_out, ln_g, ln_b):
    x = np.asarray(x, np.float32)
    B = x.shape[0]
    rep = lambda v: np.ascontiguousarray(np.broadcast_to(np.asarray(v, np.float32)[None, :], (128, len(v))))
    consts = [_consts(g) for g in range(2)]
    gath = np.concatenate([_own_rows(0), _own_rows(1)])
    h = x
    for layers in LAUNCH_PLAN:
        nc = _get_prog(layers)
        lay = {}
        for l in layers:
            lay.update(_layer_inputs(l, w_in, b_forget, lambda_q1, lambda_k1, lambda_q2, lambda_k2, subln_g, w_out, ln_g, ln_b))
        in_maps = []
        for core in range(8):
            b, g = core // 2, core % 2
            m = dict(consts[g])
            m.update(lay)
            m["hin_full"] = np.ascontiguousarray(h[b][gath])
            m["hin_own"] = np.ascontiguousarray(h[b][_own_rows(g)])
            m["hin_oth"] = np.ascontiguousarray(h[b][_own_rows(1 - g)])
            m["lng_in"] = rep(ln_in_g)
            m["lnb_in"] = rep(ln_in_b)
            in_maps.append(m)
        res = run_bass_kernel_spmd(nc, in_maps, core_ids=list(range(8)))
        hn = np.empty_like(x)
        for core in range(8):
            b, g = core // 2, core % 2
            hn[b][_own_rows(g)] = np.asarray(res.results[core]["out"], np.float32)
        h = hn
    return h
```

```python
import contextlib
import math

import numpy as np
import ml_dtypes

import concourse.bass as bass
import concourse.mybir as mybir
from concourse.bass_utils import run_bass_kernel_spmd

F32 = mybir.dt.float32
BF16 = mybir.dt.bfloat16
AF = mybir.ActivationFunctionType
ALU = mybir.AluOpType

S = 8192
D = 1024
OWN = 4096
NB = 16
DEPTH = 2
LN_EPS = 1e-5
SUBLN_EPS = 1e-5
ALPHA = (2 * DEPTH) ** 0.25
ROPE_THETA = 500000.0
MASKV = -30000.0
WKC = 2564
WQC = 2560

ENGS = ("pe", "act", "dve", "pool", "sp")
DEBUG = False
DUMP = False


class Op:
    __slots__ = ("eng", "fn", "deps", "marked", "sig", "dma", "dsem", "dval", "cc")

    def __init__(self, eng, fn, dma):
        self.eng = eng
        self.fn = fn
        self.deps = []
        self.marked = False
        self.sig = None
        self.dma = dma
        self.dsem = None
        self.dval = None
        self.cc = False


class _Rec:
    def __init__(self):
        self.call = None

    def __getattr__(self, name):
        def f(*a, **k):
            self.call = (name, a, k)
            return None
        return f

    def replay(self, eng):
        name, a, k = self.call
        return getattr(eng, name)(*a, **k)


class Prog:
    NDSEM = 24

    def __init__(self, nc):
        self.nc = nc
        self.ops = {e: [] for e in ENGS}
        self.last_writer = {}
        self.readers = {}
        self.dma_ops = {e: [] for e in ENGS}
        self.final = []

    def op(self, eng, fn, reads=(), writes=(), dma=False, extra=()):
        if fn is not None:
            rec = _Rec()
            fn(rec)
            assert rec.call is not None
            fn = rec.replay
        o = Op(eng, fn, dma)
        deps = []
        seen = set()

        def add(d):
            if d is None or id(d) in seen:
                return
            seen.add(id(d))
            deps.append(d)

        for b in reads:
            add(self.last_writer.get(b))
        for b in writes:
            add(self.last_writer.get(b))
            for r in self.readers.get(b, ()):
                add(r)
        for d in extra:
            add(d)
        if dma:
            lst = self.dma_ops[eng]
            n = len(lst)
            if n >= self.NDSEM:
                add(lst[n - self.NDSEM])
            o.dsem = n % self.NDSEM
            o.dval = 16 * (n // self.NDSEM + 1)
            lst.append(o)
        for d in deps:
            if d.dma:
                o.deps.append(d)
            elif d.eng == "pe" and eng == "pe" and not dma and fn is not None:
                continue
            else:
                d.marked = True
                o.deps.append(d)
        for b in reads:
            self.readers.setdefault(b, []).append(o)
        for b in writes:
            self.last_writer[b] = o
            self.readers[b] = []
        self.ops[eng].append(o)
        return o

    def barrier(self, eng, deps):
        return self.op(eng, None, extra=deps)

    def full_barrier(self):
        tails = []
        for e in ENGS:
            for o in reversed(self.ops[e]):
                if o.fn is not None and not o.dma:
                    tails.append(o)
                    break
        dmas = []
        for e in ENGS:
            dmas += self.dma_ops[e][-self.NDSEM:]
        for e in ENGS:
            self.barrier(e, tails + dmas)
        self.last_writer = {}
        self.readers = {}

    def emit(self):
        nc = self.nc
        with contextlib.ExitStack() as st:
            csem = {e: st.enter_context(nc.semaphore(f"c_{e}")) for e in ENGS}
            dsem = {
                e: [st.enter_context(nc.semaphore(f"d_{e}_{i}")) for i in range(self.NDSEM)]
                for e in ENGS if self.dma_ops[e]
            }
            for e in ENGS:
                c = 0
                for o in self.ops[e]:
                    if o.marked and not o.dma:
                        c += 1
                        o.sig = c
            block = st.enter_context(nc.Block())

            def run(e, engobj):
                seen = {}

                def wait(d):
                    if d.dma:
                        key = ("d", d.eng, d.dsem)
                        sem, val = dsem[d.eng][d.dsem], d.dval
                    else:
                        key = ("c", d.eng)
                        sem, val = csem[d.eng], d.sig
                    if seen.get(key, 0) >= val:
                        return
                    seen[key] = val
                    engobj.wait_ge(sem, val)

                for o in self.ops[e]:
                    for d in o.deps:
                        wait(d)
                    if o.fn is None:
                        continue
                    ins = o.fn(engobj)
                    if o.dma:
                        ins.then_inc(dsem[e][o.dsem], 16)
                    elif o.marked:
                        ins.then_inc(csem[e], 1)
                if e == "sp":
                    for d in self.final:
                        wait(d)

            @block.tensor
            def _(eng):
                run("pe", eng)

            @block.scalar
            def _(eng):
                run("act", eng)

            @block.vector
            def _(eng):
                run("dve", eng)

            @block.gpsimd
            def _(eng):
                run("pool", eng)

            @block.sync
            def _(eng):
                run("sp", eng)


SB_BASE = 16512
SB_TOP = 229376 - 1024


class Arena:
    def __init__(self, nc):
        self.nc = nc
        self.ptr = SB_BASE
        self.n = 0

    def mark(self):
        return self.ptr

    def reset(self, p):
        self.ptr = p

    def alloc(self, name, shape, dt):
        esz = 4 if dt == F32 else 2
        nbytes = esz
        for s in shape[1:]:
            nbytes *= s
        nbytes = (nbytes + 63) // 64 * 64
        off = self.ptr
        self.ptr += nbytes
        assert self.ptr <= SB_TOP, (name, self.ptr)
        self.n += 1
        return self.nc.alloc_sbuf_tensor_at(f"{name}_{self.n}", list(shape), dt, offset=off)


def build_program(layers, lam_inits):
    nc = bass.Bass("TRN2", target_bir_lowering=False)
    P = Prog(nc)
    A = Arena(nc)

    def din(name, shape, dt=F32):
        return nc.dram_tensor(name, list(shape), dt, kind="ExternalInput").ap()

    def dscr(name, shape, dt):
        if DEBUG:
            return nc.dram_tensor(name, list(shape), dt, kind="ExternalOutput").ap()
        return nc.dram_tensor(name, list(shape), dt).ap()

    L0 = layers[0]
    first_is_l0 = (L0 == 0)
    hin_full = din("hin_full", [S, D])
    hin_own = din("hin_own", [OWN, D])
    hin_oth = din("hin_oth", [OWN, D])
    blend = din("blend", [128, 4])
    lng_in = din("lng_in", [128, D])
    lnb_in = din("lnb_in", [128, D])
    rk_cos = din("rk_cos", [128, S])
    rk_sin = din("rk_sin", [128, S])
    rq_cos_p = [din(f"rq_cos{p}", [128, OWN]) for p in range(2)]
    rq_sin_p = [din(f"rq_sin{p}", [128, OWN]) for p in range(2)]
    selg_p = [din(f"selg{p}", [4, 2]) for p in range(2)]
    c_ident = din("c_ident", [128, 128], BF16)
    c_tneg = din("c_tneg", [128, 128], BF16)
    c_onesneg = din("c_onesneg", [128, 128], BF16)
    c_onesb = din("c_onesb", [128, 128], BF16)
    c_onesf = din("c_onesf", [128, 128])
    c_mask_ab_p = [din(f"c_mask_ab{p}", [128, 4, 256], BF16) for p in range(2)]
    c_mask_c_p = [din(f"c_mask_c{p}", [128, 4, 256], BF16) for p in range(2)]
    per_layer = {}
    for l in layers:
        per_layer[l] = dict(
            wk=din(f"wk{l}", [D, WKC]), wq=din(f"wq{l}", [D, WQC]), wo=din(f"wo{l}", [D, D]),
            lng=din(f"lng{l}", [128, D]), lnb=din(f"lnb{l}", [128, D]),
            bfg=din(f"bfg{l}", [4, 1]), lamv=din(f"lamv{l}", [128, 4, 64]), subg=din(f"subg{l}", [128, 1]),
        )
    out = nc.dram_tensor("out", [OWN, D], F32, kind="ExternalOutput").ap()

    KTA = dscr("KTA", [4, 128, S], BF16)
    KTB = dscr("KTB", [4, 70, S], BF16)
    KTC = dscr("KTC", [4, 64, S], BF16)
    VS = dscr("VS", [S, 1024], BF16)
    QTA = dscr("QTA", [4, 128, OWN], BF16)
    QTB = dscr("QTB", [4, 70, OWN], BF16)
    QTC = dscr("QTC", [4, 64, OWN], BF16)
    GT = dscr("GT", [1024, OWN], F32)
    HRES = dscr("HRES", [OWN, D], F32)
    YT = dscr("YT", [1024, OWN], BF16)
    NLF = dscr("NLF", [4, S], F32)
    CN = dscr("CN", [4, S], F32)
    HP = [dscr(f"HP{p}", [OWN, D], F32) for p in range(2)]

    if DEBUG == "C0":
        DBG_U = nc.dram_tensor("DBG_U", [128, 1024], F32, kind="ExternalOutput").ap()
        DBG_L = nc.dram_tensor("DBG_L", [128, 1024], BF16, kind="ExternalOutput").ap()
        DBG_X = nc.dram_tensor("DBG_X", [128, 1024], F32, kind="ExternalOutput").ap()
    X0 = nc.alloc_psum_tensor("X0", [128, 1024], F32)
    X1 = nc.alloc_psum_tensor("X1", [128, 1024], F32)
    X2 = nc.alloc_psum_tensor("X2", [128, 1024], F32)
    PA = nc.alloc_psum_tensor("PA", [128, 512], F32)
    PB = nc.alloc_psum_tensor("PB", [128, 512], F32)
    XS = [X0, X1, X2]

    ident = A.alloc("ident", [128, 128], BF16)
    tneg = A.alloc("tneg", [128, 128], BF16)
    onesneg = A.alloc("onesneg", [128, 128], BF16)
    onesb = A.alloc("onesb", [128, 128], BF16)
    onesf = A.alloc("onesf", [128, 128], F32)
    mask_ab = A.alloc("mask_ab", [128, 4, 256], BF16)
    mask_c = A.alloc("mask_c", [128, 4, 256], BF16)
    selg_sb = A.alloc("selg", [4, 2], F32)
    lam_t = A.alloc("lam", [128, 8], F32)
    subg_t = A.alloc("subg", [128, 2], F32)
    negb_t = A.alloc("negb", [4, 2], F32)
    for dst, src, k in ((ident, c_ident, "ident"), (tneg, c_tneg, "tneg"), (onesneg, c_onesneg, "onesneg"),
                        (onesb, c_onesb, "onesb"), (onesf, c_onesf, "onesf")):
        P.op("sp", lambda e, dst=dst, src=src: e.dma_start(out=dst[:], in_=src[:, :]), writes=[k], dma=True)
    blend_sb = A.alloc("blend", [128, 4], F32)
    P.op("sp", lambda e: e.dma_start(out=blend_sb[:], in_=blend[:, :]), writes=["blend"], dma=True)
    persist_mark = A.mark()
    CONST_KEYS = ["ident", "tneg", "onesneg", "onesb", "onesf", "mask_ab", "mask_c", "selg"]

    def reload_const_keys():
        pass

    out_stores = []

    fused = len(layers) > 1
    schedule = []
    for li, l in enumerate(layers):
        npass = 2 if (fused and li < len(layers) - 1) else 1
        for p_ in range(npass):
            schedule.append((li, l, p_))
    for (li, l, pss) in schedule:
        pl = per_layer[l]
        lam_init = lam_inits[l]
        do_ln_in = (l == 0)
        last_layer = (li == len(layers) - 1)
        do_k = (pss == 0)
        from_hp = (li > 0)
        src_own = (HP[0] if from_hp else (hin_own if pss == 0 else hin_oth))
        dest = out if last_layer else HP[pss]
        rq_cos, rq_sin = rq_cos_p[pss], rq_sin_p[pss]
        A.reset(persist_mark)
        P.op("sp", lambda e: e.dma_start(out=mask_ab[:], in_=c_mask_ab_p[pss][:, :, :]), writes=["mask_ab"], dma=True)
        P.op("sp", lambda e: e.dma_start(out=mask_c[:], in_=c_mask_c_p[pss][:, :, :]), writes=["mask_c"], dma=True)
        P.op("sp", lambda e: e.dma_start(out=selg_sb[:], in_=selg_p[pss][:, :]), writes=["selg"], dma=True)

        lamv_sb = A.alloc("lamv", [128, 4, 64], F32)
        lprod = A.alloc("lprod", [128, 2, 64], F32)
        P.op("sp", lambda e: e.dma_start(out=lamv_sb[:], in_=pl["lamv"][:, :, :]), writes=["lamv"], dma=True)
        P.op("sp", lambda e: e.dma_start(out=subg_t[:, 0:1], in_=pl["subg"][:, :]), writes=["subg0"], dma=True)
        P.op("sp", lambda e: e.dma_start(out=negb_t[:, 0:1], in_=pl["bfg"][:, :]), writes=["negb0"], dma=True)
        P.op("dve", lambda e: e.tensor_tensor(out=lprod[:, 0, :], in0=lamv_sb[:, 0, :], in1=lamv_sb[:, 1, :], op=ALU.mult),
             reads=["lamv"], writes=["lprod"])
        P.op("dve", lambda e: e.tensor_tensor(out=lprod[:, 1, :], in0=lamv_sb[:, 2, :], in1=lamv_sb[:, 3, :], op=ALU.mult),
             reads=["lamv", "lprod"], writes=["lprod"])
        P.op("dve", lambda e: e.reduce_sum(out=lam_t[:, 0:1], in_=lprod[:, 0, :], axis=mybir.AxisListType.X),
             reads=["lprod"], writes=["lam01"])
        P.op("dve", lambda e: e.reduce_sum(out=lam_t[:, 1:2], in_=lprod[:, 1, :], axis=mybir.AxisListType.X),
             reads=["lprod", "lam01"], writes=["lam01"])
        P.op("act", lambda e: e.activation(out=lam_t[:, 2:4], in_=lam_t[:, 0:2], func=AF.Exp), reads=["lam01"], writes=["lam23"])
        P.op("dve", lambda e: e.scalar_tensor_tensor(out=lam_t[:, 4:5], in0=lam_t[:, 3:4], scalar=-float(lam_init),
                                                     in1=lam_t[:, 2:3], op0=ALU.add, op1=ALU.subtract),
             reads=["lam23"], writes=["neglam"])
        P.op("dve", lambda e: e.tensor_scalar(out=subg_t[:, 1:2], in0=subg_t[:, 0:1], scalar1=float((1.0 - lam_init) * math.sqrt(128.0)),
                                              scalar2=None, op0=ALU.mult), reads=["subg0"], writes=["gsub"])
        P.op("dve", lambda e: e.tensor_scalar(out=negb_t[:, 1:2], in0=negb_t[:, 0:1], scalar1=-1.0, scalar2=None, op0=ALU.mult),
             reads=["negb0"], writes=["negb"])
        neglam = lam_t[:, 4:5]
        gsub = subg_t[:, 1:2]
        negb = negb_t[:, 1:2]
        layer_mark = A.mark()

        Wk = A.alloc("Wk", [128, 8, WKC], BF16)
        Wq = A.alloc("Wq", [128, 8, WQC], BF16)
        for c in range(8):
            P.op("pool", lambda e, c=c: e.dma_start(out=Wk[:, c, :], in_=pl["wk"][c * 128:(c + 1) * 128, :]), writes=[("Wk", c)], dma=True)
        for c in range(8):
            P.op("pool", lambda e, c=c: e.dma_start(out=Wq[:, c, :], in_=pl["wq"][c * 128:(c + 1) * 128, :]), writes=[("Wq", c)], dma=True)
        WK_KEYS = [("Wk", c) for c in range(8)]
        WQ_KEYS = [("Wq", c) for c in range(8)]
        if do_ln_in:
            g_in = A.alloc("g_in", [128, D], F32)
            b_in = A.alloc("b_in", [128, D], F32)
            P.op("sp", lambda e: e.dma_start(out=g_in[:], in_=lng_in[:, :]), writes=["g_in"], dma=True)
            P.op("sp", lambda e: e.dma_start(out=b_in[:], in_=lnb_in[:, :]), writes=["b_in"], dma=True)
        xin_t = [A.alloc("xin", [128, D], F32) for _ in range(2)]
        hf_t = [A.alloc("hf", [128, D], F32) for _ in range(2)]
        hb_t = [A.alloc("hb", [128, D], BF16) for _ in range(2)]
        hT_t = [A.alloc("hT", [128, 8, 512], BF16) for _ in range(2)]
        rc_t = [A.alloc("rc", [128, 512], F32) for _ in range(2)]
        rs_t = [A.alloc("rs", [128, 512], F32) for _ in range(2)]
        tmp_t = [A.alloc("tmp", [128, 512], F32) for _ in range(2)]
        stb_t = [A.alloc("stb", [128, 512], BF16) for _ in range(4)]
        stf_t = [A.alloc("stf", [128, 512], F32) for _ in range(2)]
        st6 = A.alloc("st6", [128, 12], F32)
        mv = A.alloc("mv", [128, 4], F32)
        accs = [(X0, 0), (X0, 512), (X1, 0), (X1, 512), (X2, 0), (X2, 512)]
        cnt = dict(tile=0, acc=0, stb=0, stf=0, tmp=0, grp=0)
        stores1 = []

        def layer_norm(src, skey, dst, dkey, gt, bt, gkeys):
            for hh in range(2):
                P.op("dve", lambda e, hh=hh: e.bn_stats(out=st6[:, hh * 6:(hh + 1) * 6], in_=src[:, hh * 512:(hh + 1) * 512]),
                     reads=[skey], writes=[("st6", hh)])
            P.op("dve", lambda e: e.bn_aggr(out=mv[:, 0:2], in_=st6[:, 0:12]), reads=[("st6", 0), ("st6", 1)], writes=["mv"])
            P.op("act", lambda e: e.activation(out=mv[:, 3:4], in_=mv[:, 1:2], func=AF.Ln, bias=float(LN_EPS), scale=1.0),
                 reads=["mv"], writes=["lnv"])
            P.op("act", lambda e: e.activation(out=mv[:, 2:3], in_=mv[:, 3:4], func=AF.Exp, scale=-0.5), reads=["lnv"], writes=["rstd"])
            P.op("dve", lambda e: e.tensor_scalar(out=dst[:], in0=src[:], scalar1=mv[:, 0:1], scalar2=mv[:, 2:3],
                                                  op0=ALU.subtract, op1=ALU.mult), reads=[skey, "mv", "rstd"], writes=[dkey])
            P.op("dve", lambda e: e.tensor_tensor(out=dst[:], in0=dst[:], in1=gt[:], op=ALU.mult), reads=[dkey, gkeys[0]], writes=[dkey])
            P.op("dve", lambda e: e.tensor_tensor(out=dst[:], in0=dst[:], in1=bt[:], op=ALU.add), reads=[dkey, gkeys[1]], writes=[dkey])

        def next_acc():
            a = accs[cnt["acc"] % len(accs)]
            key = ("acc", cnt["acc"] % len(accs))
            cnt["acc"] += 1
            return a[0], a[1], key

        def next_stb():
            i = cnt["stb"] % 4
            cnt["stb"] += 1
            return stb_t[i], ("stb", i)

        def next_stf():
            i = cnt["stf"] % 2
            cnt["stf"] += 1
            return stf_t[i], ("stf", i)

        def proj_pass(src, ngroups, mode):
            W = Wk if mode == "k" else Wq
            WKEYS = WK_KEYS if mode == "k" else WQ_KEYS
            rcos = rk_cos if mode == "k" else rq_cos
            rsin = rk_sin if mode == "k" else rq_sin
            gslot = {}

            def prep_begin(G):
                gs = cnt["grp"] % 2
                cnt["grp"] += 1
                gslot[G] = gs
                rc, rs_ = rc_t[gs], rs_t[gs]
                P.op("sp", lambda e: e.dma_start(out=rc[:], in_=rcos[:, G * 512:(G + 1) * 512]), writes=[("rc", gs)], dma=True)
                P.op("sp", lambda e: e.dma_start(out=rs_[:], in_=rsin[:, G * 512:(G + 1) * 512]), writes=[("rs", gs)], dma=True)

            tstate = {}

            def prep_load(G, tt):
                gs = gslot[G]
                ts_ = cnt["tile"] % 2
                cnt["tile"] += 1
                row0 = G * 512 + tt * 128
                xin = xin_t[ts_]
                tstate[(G, tt)] = ts_
                if src == "blend":
                    idx = row0 % OWN
                    t1 = hf_t[ts_]
                    P.op("sp", lambda e: e.dma_start(out=xin[:], in_=HP[0][idx:idx + 128, :]), writes=[("xin", ts_)], dma=True)
                    P.op("sp", lambda e: e.dma_start(out=t1[:], in_=HP[1][idx:idx + 128, :]), writes=[("hf", ts_)], dma=True)
                else:
                    P.op("sp", lambda e: e.dma_start(out=xin[:], in_=src[row0:row0 + 128, :]), writes=[("xin", ts_)], dma=True)

            def prep_norm(G, tt):
                ts_ = tstate[(G, tt)]
                row0 = G * 512 + tt * 128
                xin = xin_t[ts_]
                if src == "blend":
                    rr = row0 // OWN
                    t1 = hf_t[ts_]
                    P.op("dve", lambda e: e.tensor_scalar(out=xin[:], in0=xin[:], scalar1=blend_sb[:, 2 * rr:2 * rr + 1], scalar2=None,
                                                          op0=ALU.mult), reads=[("xin", ts_)], writes=[("xin", ts_)])
                    P.op("dve", lambda e: e.scalar_tensor_tensor(out=xin[:], in0=t1[:], scalar=blend_sb[:, 2 * rr + 1:2 * rr + 2], in1=xin[:],
                                                                 op0=ALU.mult, op1=ALU.add),
                         reads=[("xin", ts_), ("hf", ts_)], writes=[("xin", ts_)])
                if do_ln_in:
                    hf = hf_t[ts_]
                    hfk = ("hf", ts_)
                    layer_norm(xin, ("xin", ts_), hf, hfk, g_in, b_in, ["g_in", "b_in"])
                else:
                    hf, hfk = xin, ("xin", ts_)
                if mode == "q":
                    stores1.append(P.op("pool", lambda e: e.dma_start(out=HRES[row0:row0 + 128, :], in_=hf[:]), reads=[hfk], dma=True))
                hb = hb_t[ts_]
                P.op("act", lambda e: e.copy(out=hb[:], in_=hf[:]), reads=[hfk], writes=[("hb", ts_)])

            def prep_tr(G, tt):
                gs = gslot[G]
                hT = hT_t[gs]
                ts_ = tstate[(G, tt)]
                hb = hb_t[ts_]
                for half in range(2):
                    ps, o0, akey = next_acc()
                    for c4 in range(4):
                        c = half * 4 + c4
                        P.op("pe", lambda e, c4=c4, c=c: e.matmul(
                            out=ps[:, o0 + c4 * 128:o0 + (c4 + 1) * 128], lhsT=hb[:, c * 128:(c + 1) * 128], rhs=ident[:],
                            start=True, stop=True), reads=[("hb", ts_), "ident"], writes=[akey])
                    P.op("dve", lambda e: e.tensor_copy(
                        out=hT[:, half * 4:(half + 1) * 4, tt * 128:(tt + 1) * 128],
                        in_=ps[:, o0:o0 + 512].rearrange("p (c t) -> p c t", c=4)),
                        reads=[akey], writes=[("hT", gs, tt, half)])

            def chunks(G):
                gs = gslot[G]
                hT = hT_t[gs]
                hTkeys = [("hT", gs, tt_, hf_) for tt_ in range(4) for hf_ in range(2)]
                rc, rs_ = rc_t[gs], rs_t[gs]

                def fm_chunk(col0, M):
                    ps, o0, akey = next_acc()
                    for c in range(8):
                        P.op("pe", lambda e, c=c: e.matmul(
                            out=ps[0:M, o0:o0 + 512], lhsT=W[:, c, col0:col0 + M], rhs=hT[:, c, :], start=(c == 0), stop=(c == 7)),
                            reads=hTkeys + [WKEYS[c]], writes=[akey])
                    return ps, o0, akey

                tok0 = G * 512
                for i in range(4):
                    p1, o1, k1 = fm_chunk(i * 128, 128)
                    p2, o2, k2 = fm_chunk(512 + i * 128, 128)
                    t1 = tmp_t[0]
                    t2 = tmp_t[1]
                    P.op("dve", lambda e: e.tensor_tensor(out=t1[:], in0=p1[:, o1:o1 + 512], in1=rc[:], op=ALU.mult),
                         reads=[k1, ("rc", gs)], writes=[("tmp", 0)])
                    P.op("dve", lambda e: e.tensor_tensor(out=t2[:], in0=p2[:, o2:o2 + 512], in1=rs_[:], op=ALU.mult),
                         reads=[k2, ("rs", gs)], writes=[("tmp", 1)])
                    sb_, sk = next_stb()
                    P.op("dve", lambda e: e.tensor_tensor(out=sb_[:], in0=t1[:], in1=t2[:], op=ALU.add),
                         reads=[("tmp", 0), ("tmp", 1)], writes=[sk])
                    dstT = KTA if mode == "k" else QTA
                    stores1.append(P.op("pool", lambda e: e.dma_start(out=dstT[i, :, tok0:tok0 + 512], in_=sb_[:]), reads=[sk], dma=True))
                    yield
                for kind, cbase, dstT in (("B", 1024, KTB if mode == "k" else QTB), ("C", 1280, KTC if mode == "k" else QTC)):
                    for h in range(4):
                        ps, o0, akey = fm_chunk(cbase + h * 64, 64)
                        sb_, sk = next_stb()
                        if mode == "k":
                            P.op("act", lambda e: e.copy(out=sb_[0:64, :], in_=ps[0:64, o0:o0 + 512]), reads=[akey], writes=[sk])
                        else:
                            P.op("act", lambda e: e.mul(out=sb_[0:64, :], in_=ps[0:64, o0:o0 + 512], mul=0.125), reads=[akey], writes=[sk])
                        stores1.append(P.op("pool", lambda e: e.dma_start(out=dstT[h, 0:64, tok0:tok0 + 512], in_=sb_[0:64, :]),
                                            reads=[sk], dma=True))
                        yield
                if mode == "k":
                    ps, o0, akey = fm_chunk(1536, 4)
                    sf, sfk = next_stf()
                    P.op("act", lambda e: e.activation(out=sf[0:4, :], in_=ps[0:4, o0:o0 + 512], func=AF.Exp, bias=negb, scale=-1.0),
                         reads=[akey, "negb"], writes=[sfk])
                    P.op("act", lambda e: e.activation(out=sf[0:4, :], in_=sf[0:4, :], func=AF.Ln, bias=1.0, scale=1.0),
                         reads=[sfk], writes=[sfk])
                    r = G // 8
                    for bb in range(2):
                        i_blk = (G % 8) * 2 + bb
                        t0 = i_blk * 512 + r * 256
                        stores1.append(P.op("pool", lambda e, bb=bb, t0=t0: e.dma_start(
                            out=NLF[:, t0:t0 + 256], in_=sf[0:4, bb * 256:(bb + 1) * 256]), reads=[sfk], dma=True))
                    yield
                    for tt in range(4):
                        for half in range(2):
                            ps, o0, akey = next_acc()
                            for c in range(8):
                                P.op("pe", lambda e, c=c: e.matmul(
                                    out=ps[:, o0:o0 + 512], lhsT=hT[:, c, tt * 128:(tt + 1) * 128],
                                    rhs=W[:, c, 1540 + half * 512:1540 + (half + 1) * 512], start=(c == 0), stop=(c == 7)),
                                    reads=hTkeys + [WKEYS[c]], writes=[akey])
                            sb_, sk = next_stb()
                            P.op("act", lambda e: e.copy(out=sb_[:], in_=ps[:, o0:o0 + 512]), reads=[akey], writes=[sk])
                            r0 = tok0 + tt * 128
                            stores1.append(P.op("pool", lambda e: e.dma_start(
                                out=VS[r0:r0 + 128, half * 512:(half + 1) * 512], in_=sb_[:]), reads=[sk], dma=True))
                            yield
                else:
                    for gch in range(8):
                        ps, o0, akey = fm_chunk(1536 + gch * 128, 128)
                        sf, sfk = next_stf()
                        P.op("act", lambda e: e.activation(out=sf[:], in_=ps[:, o0:o0 + 512], func=AF.Silu), reads=[akey], writes=[sfk])
                        stores1.append(P.op("pool", lambda e: e.dma_start(
                            out=GT[gch * 128:(gch + 1) * 128, tok0:tok0 + 512], in_=sf[:]), reads=[sfk], dma=True))
                        yield

            prep_begin(0)
            for tt in range(4):
                prep_load(0, tt)
                prep_norm(0, tt)
                prep_tr(0, tt)
            ev_load = {0: 0, 3: 1, 8: 2, 13: 3}
            ev_norm = {1: 0, 6: 1, 11: 2, 16: 3}
            ev_tr = {5: 0, 10: 1, 15: 2, 19: 3}
            for G in range(ngroups):
                more = (G + 1 < ngroups)
                done = set()

                def fire(idx):
                    if not more:
                        return
                    for ev, fn, tag in ((ev_load, prep_load, "l"), (ev_norm, prep_norm, "n"), (ev_tr, prep_tr, "t")):
                        if idx in ev and (tag, ev[idx]) not in done:
                            done.add((tag, ev[idx]))
                            fn(G + 1, ev[idx])
                if more:
                    prep_begin(G + 1)
                fire(0)
                idx = 0
                for _ in chunks(G):
                    idx += 1
                    fire(idx)
                for k in range(idx + 1, 24):
                    fire(k)

        if do_k:
            proj_pass("blend" if from_hp else hin_full, 16, "k")
        proj_pass(src_own, 8, "q")
        P.full_barrier()
        A.reset(layer_mark)

        nlf = A.alloc("nlf", [4, S], F32)
        cn = A.alloc("cn", [4, S], F32)
        pk = [A.alloc("pk", [4, S], BF16) for _ in range(3)]
        cq = [A.alloc("cq", [4, OWN], F32) for _ in range(2)]
        pq = [A.alloc("pq", [4, OWN], BF16) for _ in range(3)]
        ones_r = A.alloc("ones_r", [4, S], BF16)
        P.op("pool", lambda e: e.memset(ones_r[:], 1.0), writes=["ones_r"])
        if do_k:
            P.op("sp", lambda e: e.dma_start(out=nlf[:], in_=NLF[:, :]), writes=["nlf"], dma=True)
            P.op("dve", lambda e: e.tensor_tensor_scan(out=cn[:], data0=nlf[:], data1=nlf[:], initial=0.0, op0=ALU.add, op1=ALU.max),
                 reads=["nlf"], writes=["cn"])
            st_cn = P.op("pool", lambda e: e.dma_start(out=CN[:, :], in_=cn[:]), reads=["cn"], dma=True)
            P.barrier("sp", [st_cn])
        CNv = CN.rearrange("h (i r t) -> h r i t", i=16, r=2, t=256)
        for r in range(2):
            if do_k:
                P.op("sp", lambda e, r=r: e.dma_start(out=nlf[:, r * OWN:(r + 1) * OWN].rearrange("h (i t) -> h i t", t=256),
                                                     in_=CNv[:, r, :, :]), reads=["cn"], writes=["nlf"], dma=True)
            P.op("sp", lambda e, r=r: e.dma_start(out=cq[r][:].rearrange("h (i t) -> h i t", t=256), in_=CNv[:, r, :, :]),
                 writes=[("cq", r)], dma=True)

        def split3(src, skey, pieces, pkey):
            for p_ in range(3):
                P.op("dve", lambda e, p_=p_: e.tensor_copy(out=pieces[p_][:], in_=src[:]), reads=[skey], writes=[(pkey, p_)])
                if p_ < 2:
                    P.op("dve", lambda e, p_=p_: e.tensor_tensor(out=src[:], in0=src[:], in1=pieces[p_][:], op=ALU.subtract),
                         reads=[skey, (pkey, p_)], writes=[skey])

        if do_k:
            split3(nlf, "nlf", pk, "pk")
        P.op("dve", lambda e: e.tensor_scalar(out=cq[0][:], in0=cq[0][:], scalar1=selg_sb[:, 0:1], scalar2=-1.0, op0=ALU.mult, op1=ALU.mult),
             reads=[("cq", 0)], writes=[("cq", 0)])
        P.op("dve", lambda e: e.tensor_scalar(out=cq[1][:], in0=cq[1][:], scalar1=selg_sb[:, 1:2], scalar2=-1.0, op0=ALU.mult, op1=ALU.mult),
             reads=[("cq", 1)], writes=[("cq", 1)])
        P.op("dve", lambda e: e.tensor_tensor(out=cq[0][:], in0=cq[0][:], in1=cq[1][:], op=ALU.add),
             reads=[("cq", 0), ("cq", 1)], writes=[("cq", 0)])
        split3(cq[0], ("cq", 0), pq, "pq")
        bst = []
        for h in range(4):
            for p_ in range(3):
                if do_k:
                    bst.append(P.op("pool", lambda e, h=h, p_=p_: e.dma_start(out=KTB[h, 67 + p_:68 + p_, :], in_=pk[p_][h:h + 1, :]),
                                    reads=[("pk", p_)], dma=True))
                bst.append(P.op("pool", lambda e, h=h, p_=p_: e.dma_start(out=QTB[h, 64 + p_:65 + p_, :], in_=pq[p_][h:h + 1, :]),
                                reads=[("pq", p_)], dma=True))
            if do_k:
                bst.append(P.op("pool", lambda e, h=h: e.dma_start(out=KTB[h, 64:67, :], in_=ones_r[0:3, :]), reads=["ones_r"], dma=True))
            bst.append(P.op("pool", lambda e, h=h: e.dma_start(out=QTB[h, 67:70, :], in_=ones_r[0:3, 0:OWN]), reads=["ones_r"], dma=True))
        P.full_barrier()
        A.reset(layer_mark)

        kt_t = [A.alloc("kt", [128, S], BF16) for _ in range(2)]
        qt_t = [A.alloc("qt", [128, OWN], BF16) for _ in range(2)]
        vt_t = [A.alloc("vt", [128, 64, 128], BF16) for _ in range(2)]
        qz_t = [A.alloc("qz", [128, 2, OWN], BF16) for _ in range(2)]
        for us_ in range(2):
            P.op("pool", lambda e: e.memset(qz_t[us_][64:128, 0, :], 0.0), writes=[("qz0", us_)])
            P.op("pool", lambda e: e.memset(qz_t[us_][0:64, 1, :], 0.0), writes=[("qz1", us_)])
        E_t = [A.alloc("E", [128, 1024], BF16) for _ in range(4)]
        U_t = [A.alloc("U", [128, 1024], F32) for _ in range(3)]
        Lp_t = [A.alloc("Lp", [128, 1024], BF16) for _ in range(3)]
        Y_t = [A.alloc("Y", [128, 1024], F32) for _ in range(3)]
        cb_t = [A.alloc("cb", [128, 256], F32) for _ in range(3)]
        gate_t = [A.alloc("gate", [128, 256], F32) for _ in range(2)]
        r_t = [A.alloc("r", [128, 256], F32) for _ in range(2)]
        o_t = [A.alloc("o", [128, 256], F32) for _ in range(3)]
        sq_t = A.alloc("sq", [128, 256], F32)
        rstd_t = A.alloc("rstd", [128, 256], F32)
        ys_t = [A.alloc("ys", [128, 256], BF16) for _ in range(2)]
        Es_t = [A.alloc("Es", [128, 1024], F32) for _ in range(2)]
        ystores = []
        ucount = [0]
        ycount = [0]

        def kcol(j, m):
            return (m // 2) * OWN + j * 256 + (m % 2) * 128

        def vch(j, m):
            return (m // 2) * 32 + j * 2 + (m % 2)

        def ktk(us):
            return [("kt", us, 0), ("kt", us, 1), ("kt", us, "z")]

        def qtk(us):
            return [("qt", us, 0), ("qt", us, 1), ("qt", us, "z"), ("qz0", us), ("qz1", us)]

        def load_unit(kind, h):
            us = ucount[0] % 2
            ucount[0] += 1
            kt, qt, vt = kt_t[us], qt_t[us], vt_t[us]
            rows = {"A": 128, "B": 70, "C": 64}[kind]
            KT = {"A": KTA, "B": KTB, "C": KTC}[kind]
            QT = {"A": QTA, "B": QTB, "C": QTC}[kind]
            if kind == "C":
                P.op("pool", lambda e: e.memset(kt[64:128, :], 0.0), writes=[("kt", us, "z")])
                P.op("pool", lambda e: e.memset(qt[64:128, :], 0.0), writes=[("qt", us, "z")])
            for r in range(2):
                P.op("pool", lambda e, r=r: e.dma_start(out=kt[0:rows, r * OWN:(r + 1) * OWN], in_=KT[h, :, r * OWN:(r + 1) * OWN]),
                     writes=[("kt", us, r)], dma=True)
            if kind == "A":
                qz = qz_t[us]
                P.op("pool", lambda e: e.dma_start(out=qz[0:64, 0, :], in_=QT[h, 0:64, :]), writes=[("qt", us, 0)], dma=True)
                P.op("pool", lambda e: e.dma_start(out=qz[64:128, 1, :], in_=QT[h, 64:128, :]), writes=[("qt", us, 1)], dma=True)
            else:
                P.op("pool", lambda e: e.dma_start(out=qt[0:rows, :], in_=QT[h, :, :]), writes=[("qt", us, 0)], dma=True)
            if kind == "A":
                c0, cw = h * 128, 128
            elif kind == "B":
                c0, cw = 512 + h * 64, 64
            else:
                c0, cw = 768 + h * 64, 64
            if kind == "B":
                P.op("pool", lambda e: e.memset(vt[:, :, 64:128], 1.0), writes=[("vt", us)])
            for q8 in range(8):
                P.op("pool", lambda e, q8=q8: e.dma_start(
                    out=vt[:, q8 * 8:(q8 + 1) * 8, 0:cw],
                    in_=VS[q8 * 1024:(q8 + 1) * 1024, c0:c0 + cw].rearrange("(c p) w -> p c w", p=128)),
                    writes=[("vt", us, q8)], reads=[("vt", us)], dma=True)
            return us

        def vkeys(us):
            return [("vt", us)] + [("vt", us, q8) for q8 in range(8)]

        def load_gate(row0, nrows, i, slot=None):
            gs = (ycount[0] % 2) if slot is None else slot
            g = gate_t[gs]
            P.op("sp", lambda e: e.dma_start(out=g[0:nrows, :], in_=GT[row0:row0 + nrows, i * 256:(i + 1) * 256]),
                 writes=[("gate", gs)], dma=True)
            return g, ("gate", gs)

        def store_y(ysrc_fn, row0, nrows, i, reads):
            ys = ys_t[ycount[0] % 2]
            yk = ("ys", ycount[0] % 2)
            ycount[0] += 1
            ysrc_fn(ys, yk)
            ystores.append(P.op("pool", lambda e: e.dma_start(out=YT[row0:row0 + nrows, i * 256:(i + 1) * 256], in_=ys[0:nrows, :]),
                                reads=[yk], dma=True))

        def attn_A(h, us):
            kt, qt, vt = kt_t[us], qt_t[us], vt_t[us]
            items = [(i, j, sub) for i in range(NB) for j in range(i + 1) for sub in (0, 1)]
            OP = [(PA, 0), (PA, 256)]
            LP = [(PB, 0), (PB, 256)]
            pending = []
            gate_of = {}
            lsb_t, rsb_t = Es_t[0], Es_t[1]
            x2r = []

            def QK(w):
                i, j, sub = items[w]
                xs = sub
                X = XS[xs]
                r0 = sub * 64
                diag = (j == i)
                for m in range(4):
                    kc = kcol(j, m)
                    P.op("pe", lambda e, m=m, kc=kc: e.matmul(
                        out=X[:, m * 256:(m + 1) * 256], lhsT=kt[:, kc:kc + 128], rhs=qz_t[us][:, sub, i * 256:(i + 1) * 256],
                        start=True, stop=not diag), reads=ktk(us) + qtk(us), writes=[("X", xs)])
                    if diag:
                        P.op("pe", lambda e, m=m: e.matmul(out=X[:, m * 256:(m + 1) * 256], lhsT=ident[:], rhs=mask_ab[:, m, :],
                                                           start=False, stop=True), writes=[("X", xs)])

            def EXP(w):
                i, j, sub = items[w]
                es = (w % 4)
                P.op("act", lambda e: e.activation(out=E_t[es][:], in_=XS[sub][:], func=AF.Exp), reads=[("X", sub)], writes=[("E", es)])

            def PV(w):
                i, j, sub = items[w]
                es = (w % 4)
                E = E_t[es]
                first, last = (j == 0), (j == i)
                if first and sub == 0:
                    gate_of[i] = load_gate(h * 128, 128, i, slot=i % 2)
                for m in range(4):
                    ch = vch(j, m)
                    ot_, oc_ = OP[sub]
                    P.op("pe", lambda e, m=m, ch=ch: e.matmul(out=ot_[:, oc_:oc_ + 256], lhsT=vt[:, ch, :], rhs=E[:, m * 256:(m + 1) * 256],
                                                              start=(first and m == 0 and sub == 0), stop=(last and m == 3),
                                                              skip_group_check=True),
                         reads=[("E", es)] + vkeys(us), writes=["bankPA"])
                    lt_, lc_ = LP[sub]
                    P.op("pe", lambda e, m=m: e.matmul(out=lt_[:, lc_:lc_ + 256], lhsT=onesb[:], rhs=E[:, m * 256:(m + 1) * 256],
                                                       start=(first and m == 0 and sub == 0), stop=(last and m == 3),
                                                       skip_group_check=True),
                         reads=[("E", es)], writes=["bankPB"])
                if last and sub == 1:
                    epi0(i)

            def epi0(i):
                for sub in (0, 1):
                    P.op("dve", lambda e, sub=sub: e.tensor_copy(out=o_t[sub][:], in_=OP[sub][0][:, OP[sub][1]:OP[sub][1] + 256]),
                         reads=["bankPA"], writes=[("o", sub)])
                P.op("dve", lambda e: e.tensor_copy(out=lsb_t[:, 0:512], in_=PB[:, 0:512]), reads=["bankPB"], writes=["lsb"])
                pending.append([2, lambda: epi1(i)])

            def epi1(i):
                P.op("act", lambda e: e.activation(out=rsb_t[:, 0:512], in_=lsb_t[:, 0:512], func=AF.Ln), reads=["lsb"], writes=["rsb"])
                P.op("act", lambda e: e.activation(out=rsb_t[:, 0:512], in_=rsb_t[:, 0:512], func=AF.Exp, scale=-1.0), reads=["rsb"], writes=["rsb"])
                for sub in (0, 1):
                    P.op("dve", lambda e, sub=sub: e.tensor_tensor(out=o_t[sub][:], in0=o_t[sub][:], in1=rsb_t[:, sub * 256:(sub + 1) * 256], op=ALU.mult),
                         reads=[("o", sub), "rsb"], writes=[("o", sub)])
                P.op("dve", lambda e: e.scalar_tensor_tensor(out=o_t[2][:], in0=o_t[1][:], scalar=neglam, in1=o_t[0][:],
                                                             op0=ALU.mult, op1=ALU.add), reads=[("o", 0), ("o", 1)], writes=[("o", 2)])
                P.op("dve", lambda e: e.tensor_tensor(out=sq_t[:], in0=o_t[2][:], in1=o_t[2][:], op=ALU.mult), reads=[("o", 2)], writes=["sq"])
                pending.append([2, lambda: epi2(i)])

            def epi2(i):
                g, gk = gate_of.pop(i)
                P.op("pe", lambda e: e.matmul(out=X2[:, 512:768], lhsT=onesf[:], rhs=sq_t[:], start=True, stop=True),
                     reads=["sq"], writes=["bankX2b"])
                x2r.append(P.op("act", lambda e: e.activation(out=rstd_t[:], in_=X2[:, 512:768], func=AF.Ln, bias=float(128.0 * SUBLN_EPS),
                                                              scale=1.0), reads=["bankX2b"], writes=["rstd2"]))
                P.op("act", lambda e: e.activation(out=rstd_t[:], in_=rstd_t[:], func=AF.Exp, scale=-0.5), reads=["rstd2"], writes=["rstd2"])
                P.op("dve", lambda e: e.scalar_tensor_tensor(out=o_t[2][:], in0=o_t[2][:], scalar=gsub, in1=rstd_t[:],
                                                             op0=ALU.mult, op1=ALU.mult), reads=[("o", 2), "rstd2"], writes=[("o", 2)])

                def fin(ys, yk):
                    P.op("dve", lambda e: e.tensor_tensor(out=ys[:], in0=o_t[2][:], in1=g[:], op=ALU.mult), reads=[("o", 2), gk], writes=[yk])
                store_y(fin, h * 128, 128, i, None)

            W = len(items)
            for w0 in range(min(2, W)):
                QK(w0)
            for w in range(W):
                EXP(w)
                PV(w)
                if w + 2 < W:
                    QK(w + 2)
                for pnd in list(pending):
                    pnd[0] -= 1
                    if pnd[0] <= 0:
                        pending.remove(pnd)
                        pnd[1]()
            while pending:
                pnd = pending.pop(0)
                pnd[1]()
            P.barrier("pe", x2r[-4:])

        def attn_B(h, us):
            kt, qt, vt = kt_t[us], qt_t[us], vt_t[us]
            items = [(i, j) for i in range(NB) for j in range(i + 1)]
            gate_of = {}

            def QK(w):
                i, j = items[w]
                xs = w % 3
                X = XS[xs]
                diag = (j == i)
                for m in range(4):
                    kc = kcol(j, m)
                    P.op("pe", lambda e, m=m, kc=kc: e.matmul(
                        out=X[:, m * 256:(m + 1) * 256], lhsT=kt[0:70, kc:kc + 128], rhs=qt[0:70, i * 256:(i + 1) * 256],
                        start=True, stop=not diag), reads=ktk(us) + qtk(us), writes=[("X", xs)])
                    if diag:
                        P.op("pe", lambda e, m=m: e.matmul(out=X[:, m * 256:(m + 1) * 256], lhsT=ident[:], rhs=mask_ab[:, m, :],
                                                           start=False, stop=True), writes=[("X", xs)])

            def EXP(w):
                es = w % 4
                P.op("act", lambda e: e.activation(out=E_t[es][:], in_=XS[w % 3][:], func=AF.Exp), reads=[("X", w % 3)], writes=[("E", es)])

            def PV(w):
                i, j = items[w]
                es = w % 4
                E = E_t[es]
                first, last = (j == 0), (j == i)
                PO = PA if i % 2 == 0 else PB
                pkey = "bankPA" if i % 2 == 0 else "bankPB"
                if first:
                    gate_of[i] = load_gate(512 + h * 64, 64, i, slot=i % 2)
                for m in range(4):
                    ch = vch(j, m)
                    P.op("pe", lambda e, m=m, ch=ch: e.matmul(out=PO[:, 0:256], lhsT=vt[:, ch, :], rhs=E[:, m * 256:(m + 1) * 256],
                                                              start=(first and m == 0), stop=(last and m == 3)),
                         reads=[("E", es)] + vkeys(us), writes=[pkey])
                if last:
                    g, gk = gate_of.pop(i)
                    P.op("dve", lambda e: e.reciprocal(out=r_t[0][0:64, :], in_=PO[64:128, 0:256]), reads=[pkey], writes=[("r", 0)])
                    P.op("dve", lambda e: e.tensor_tensor(out=o_t[0][0:64, :], in0=PO[0:64, 0:256], in1=r_t[0][0:64, :], op=ALU.mult),
                         reads=[pkey, ("r", 0)], writes=[("o", 0)])

                    def fin(ys, yk):
                        P.op("dve", lambda e: e.tensor_tensor(out=ys[0:64, :], in0=o_t[0][0:64, :], in1=g[0:64, :], op=ALU.mult),
                             reads=[("o", 0), gk], writes=[yk])
                    store_y(fin, 512 + h * 64, 64, i, None)

            W = len(items)
            for w0 in range(min(3, W)):
                QK(w0)
            for w in range(W):
                EXP(w)
                PV(w)
                if w + 3 < W:
                    QK(w + 3)

        def attn_C(h, us):
            kt, qt, vt = kt_t[us], qt_t[us], vt_t[us]
            items = [(i, j) for i in range(NB) for j in range(i, -1, -1)]
            W = len(items)

            def QK(w):
                i, j = items[w]
                xs = w % 3
                X = XS[xs]
                diag = (j == i)
                for m in range(4):
                    kc = kcol(j, m)
                    P.op("pe", lambda e, m=m, kc=kc: e.matmul(
                        out=X[:, m * 256:(m + 1) * 256], lhsT=kt[:, kc:kc + 128], rhs=qt[:, i * 256:(i + 1) * 256],
                        start=(m % 2 == 0), stop=True, skip_group_check=True), reads=ktk(us) + qtk(us), writes=[("X", xs)])
                    if diag:
                        P.op("pe", lambda e, m=m: e.matmul(out=X[:, m * 256:(m + 1) * 256], lhsT=ident[:], rhs=mask_c[:, m, :],
                                                           start=False, stop=True, skip_group_check=True), writes=[("X", xs)])

            def EA(w):
                xs = w % 3
                P.op("act", lambda e: e.activation(out=U_t[xs][:], in_=XS[xs][:], func=AF.Exp), reads=[("X", xs)], writes=[("U", xs)])
                P.op("act", lambda e: e.activation(out=Lp_t[xs][:], in_=U_t[xs][:], func=AF.Ln, bias=1.0, scale=1.0),
                     reads=[("U", xs)], writes=[("Lp", xs)])

            def TM(w):
                i, j = items[w]
                xs = w % 3
                X, Lp = XS[xs], Lp_t[xs]
                for m in range(4):
                    P.op("pe", lambda e, m=m: e.matmul(out=X[:, m * 256:(m + 1) * 256], lhsT=tneg[:], rhs=Lp[:, m * 256:(m + 1) * 256],
                                                       start=False, stop=(m == 3), skip_group_check=True),
                         reads=[("Lp", xs)], writes=[("X", xs)])
                    for m2 in range(m + 1, 4):
                        P.op("pe", lambda e, m=m, m2=m2: e.matmul(out=X[:, m * 256:(m + 1) * 256], lhsT=onesneg[:],
                                                                  rhs=Lp[:, m2 * 256:(m2 + 1) * 256], start=False, stop=True,
                                                                  skip_group_check=True), reads=[("Lp", xs)], writes=[("X", xs)])
                if j > 0:
                    for m in range(4):
                        P.op("pe", lambda e, m=m: e.matmul(out=PB[:, 0:256], lhsT=onesneg[:], rhs=Lp[:, m * 256:(m + 1) * 256],
                                                           start=(j == i and m == 0), stop=(m == 3), skip_group_check=True),
                             reads=[("Lp", xs)], writes=["bankPB"])
                    cs = w % 3
                    P.op("dve", lambda e: e.tensor_copy(out=cb_t[cs][:], in_=PB[:, 0:256]), reads=["bankPB"], writes=[("cb", cs)])

            def ADD(w):
                i, j = items[w]
                xs = w % 3
                Y = Y_t[xs]
                if j == i:
                    for hb_ in range(2):
                        P.op("dve", lambda e, hb_=hb_: e.tensor_copy(out=Y[:, hb_ * 512:(hb_ + 1) * 512], in_=XS[xs][:, hb_ * 512:(hb_ + 1) * 512]),
                             reads=[("X", xs)], writes=[("Y", xs)])
                else:
                    cs = (w - 1) % 3
                    for m in range(4):
                        P.op("dve", lambda e, m=m: e.tensor_tensor(out=Y[:, m * 256:(m + 1) * 256], in0=XS[xs][:, m * 256:(m + 1) * 256],
                                                                   in1=cb_t[cs][:], op=ALU.add),
                             reads=[("X", xs), ("cb", cs)], writes=[("Y", xs)])

            def EB(w):
                xs = w % 3
                es = w % 4
                P.op("act", lambda e: e.activation(out=E_t[es][:], in_=Y_t[xs][:], func=AF.Exp), reads=[("Y", xs)], writes=[("E", es)])

            def PV(w):
                i, j = items[w]
                es = w % 4
                E = E_t[es]
                first, last = (j == i), (j == 0)
                for m in range(4):
                    ch = vch(j, m)
                    P.op("pe", lambda e, m=m, ch=ch, E=E: e.matmul(out=PA[:, 0:256], lhsT=vt[:, ch, :], rhs=E[:, m * 256:(m + 1) * 256],
                                                                 start=(first and m == 0), stop=(last and m == 3)),
                         reads=[("E", es)] + vkeys(us), writes=["bankPA"])
                if last:
                    g, gk = load_gate(768 + h * 64, 64, i)

                    def fin(ys, yk):
                        P.op("dve", lambda e: e.tensor_tensor(out=ys[0:64, :], in0=PA[0:64, 0:256], in1=g[0:64, :], op=ALU.mult),
                             reads=["bankPA", gk], writes=[yk])
                    store_y(fin, 768 + h * 64, 64, i, None)

            for w0 in range(min(3, W)):
                QK(w0)
                EA(w0)
            TM(0)
            for w in range(W):
                ADD(w)
                EB(w)
                if w + 1 < W:
                    TM(w + 1)
                if w + 3 < W:
                    QK(w + 3)
                    EA(w + 3)
                PV(w)

        units = [("A", h) for h in range(4)] + [("B", h) for h in range(4)] + [("C", h) for h in range(4)]
        fns = {"A": attn_A, "B": attn_B, "C": attn_C}
        us_cur = load_unit(*units[0])
        for k_, (kind_, h_) in enumerate(units):
            us_next = load_unit(*units[k_ + 1]) if k_ + 1 < len(units) else None
            fns[kind_](h_, us_cur)
            us_cur = us_next
        P.full_barrier()
        A.reset(layer_mark)

        Wo = A.alloc("Wo", [128, 8, D], BF16)
        for c in range(8):
            P.op("pool", lambda e, c=c: e.dma_start(out=Wo[:, c, :], in_=pl["wo"][c * 128:(c + 1) * 128, :]), writes=[("Wo", c)], dma=True)
        g_o = A.alloc("g_o", [128, D], F32)
        b_o = A.alloc("b_o", [128, D], F32)
        P.op("sp", lambda e: e.dma_start(out=g_o[:], in_=pl["lng"][:, :]), writes=["g_o"], dma=True)
        P.op("sp", lambda e: e.dma_start(out=b_o[:], in_=pl["lnb"][:, :]), writes=["b_o"], dma=True)
        yt_t = [A.alloc("yt", [128, 8, 128], BF16) for _ in range(2)]
        hr_t = [A.alloc("hr", [128, D], F32) for _ in range(2)]
        z_t = [A.alloc("z", [128, D], F32) for _ in range(2)]
        ot_t = [A.alloc("ot", [128, D], F32) for _ in range(2)]
        st6 = A.alloc("st6b", [128, 12], F32)
        mv = A.alloc("mvb", [128, 4], F32)
        for T in range(OWN // 128):
            s_ = T % 2
            yt, hr, z, ot = yt_t[s_], hr_t[s_], z_t[s_], ot_t[s_]
            P.op("sp", lambda e, yt=yt, T=T: e.dma_start(out=yt[:], in_=YT[:, T * 128:(T + 1) * 128].rearrange("(c p) t -> p c t", p=128)),
                 writes=[("yt", s_)], dma=True)
            P.op("sp", lambda e, hr=hr, T=T: e.dma_start(out=hr[:], in_=HRES[T * 128:(T + 1) * 128, :]), writes=[("hr", s_)], dma=True)
            X = XS[s_]
            for half in range(2):
                for c in range(8):
                    P.op("pe", lambda e, X=X, half=half, c=c, yt=yt: e.matmul(
                        out=X[:, half * 512:(half + 1) * 512], lhsT=yt[:, c, :], rhs=Wo[:, c, half * 512:(half + 1) * 512],
                        start=(c == 0), stop=(c == 7)), reads=[("yt", s_), ("Wo", c)], writes=[("X", s_)])
            P.op("dve", lambda e, z=z, hr=hr, X=X: e.scalar_tensor_tensor(out=z[:], in0=hr[:], scalar=float(ALPHA), in1=X[:],
                                                                         op0=ALU.mult, op1=ALU.add),
                 reads=[("hr", s_), ("X", s_)], writes=[("z", s_)])
            layer_norm(z, ("z", s_), ot, ("ot", s_), g_o, b_o, ["g_o", "b_o"])
            o_ = P.op("pool", lambda e, ot=ot, T=T: e.dma_start(out=dest[T * 128:(T + 1) * 128, :], in_=ot[:]),
                      reads=[("ot", s_)], dma=True)
            if last_layer:
                out_stores.append(o_)
        P.full_barrier()

    P.final += out_stores
    P.emit()
    return nc


_BF = ml_dtypes.bfloat16


def _own_rows(g):
    return (np.arange(NB)[:, None] * 512 + g * 256 + np.arange(256)[None, :]).reshape(-1)


def _rope_tables(pos, scale):
    inv = ROPE_THETA ** (-np.arange(0, 16, 2, dtype=np.float32) / 16.0)
    ang = pos.astype(np.float32)[None, :] * inv[:, None].astype(np.float32)
    cos, sin = np.cos(ang), np.sin(ang)
    C = np.ones((128, len(pos)), np.float32)
    Sg = np.zeros((128, len(pos)), np.float32)
    for a in range(2):
        b0 = a * 64
        C[b0:b0 + 8] = cos
        C[b0 + 8:b0 + 16] = cos
        Sg[b0:b0 + 8] = -sin
        Sg[b0 + 8:b0 + 16] = sin
    return (C * scale).astype(np.float32), (Sg * scale).astype(np.float32)


def _swap_cols(cols512):
    idx = np.arange(512)
    out = idx.copy()
    for sh in range(8):
        b = sh * 64
        out[b:b + 8] = idx[b + 8:b + 16]
        out[b + 8:b + 16] = idx[b:b + 8]
    return cols512[out]


def _weight_layouts(w_in_l):
    o = {}
    pos = 0
    for name, n in (("Aq", 512), ("Ak", 512), ("Av", 512), ("Ag", 512), ("Bq", 256), ("Bk", 256), ("Bv", 256), ("Bf", 4),
                    ("Bg", 256), ("Cq", 256), ("Ck", 256), ("Cv", 256), ("Cg", 256)):
        o[name] = np.arange(pos, pos + n)
        pos += n
    kcols = np.concatenate([o["Ak"], _swap_cols(o["Ak"]), o["Bk"], o["Ck"], o["Bf"], o["Av"], o["Bv"], o["Cv"]])
    qcols = np.concatenate([o["Aq"], _swap_cols(o["Aq"]), o["Bq"], o["Cq"], o["Ag"], o["Bg"], o["Cg"]])
    assert len(kcols) == WKC and len(qcols) == WQC
    return np.ascontiguousarray(w_in_l[:, kcols]), np.ascontiguousarray(w_in_l[:, qcols])


def _masks(g):
    p = np.arange(128)[:, None, None]
    m = np.arange(4)[None, :, None]
    t = np.arange(256)[None, None, :]
    kpos = (m // 2) * 256 + (m % 2) * 128 + p
    qpos = g * 256 + t
    mab = np.where(kpos <= qpos, 0.0, MASKV).astype(np.float32)
    mc = np.where(kpos < qpos, 0.0, MASKV).astype(np.float32)
    return mab.astype(_BF), mc.astype(_BF)


_PROG_CACHE = {}


def _get_prog(layers):
    key = tuple(layers)
    if key not in _PROG_CACHE:
        lam_inits = {l: 0.8 - 0.6 * math.exp(-0.3 * l) for l in range(DEPTH)}
        _PROG_CACHE[key] = build_program(list(layers), lam_inits)
    return _PROG_CACHE[key]


def _consts(g):
    j = np.arange(128)[:, None]
    s_ = np.arange(128)[None, :]
    d = {}
    d["c_ident"] = np.eye(128, dtype=np.float32).astype(_BF)
    d["c_tneg"] = np.where(j >= s_, -1.0, 0.0).astype(np.float32).astype(_BF)
    d["c_onesneg"] = np.full((128, 128), -1.0, np.float32).astype(_BF)
    d["c_onesb"] = np.ones((128, 128), np.float32).astype(_BF)
    d["c_onesf"] = np.ones((128, 128), np.float32)
    gath = np.concatenate([_own_rows(0), _own_rows(1)])
    d["rk_cos"], d["rk_sin"] = _rope_tables(gath, 1.0)
    for p in range(2):
        gp = g if p == 0 else 1 - g
        d[f"c_mask_ab{p}"], d[f"c_mask_c{p}"] = _masks(gp)
        d[f"rq_cos{p}"], d[f"rq_sin{p}"] = _rope_tables(_own_rows(gp), 0.125)
        sel = np.zeros((4, 2), np.float32)
        sel[:, gp] = 1.0
        d[f"selg{p}"] = sel
    bl = np.zeros((128, 4), np.float32)
    for r in range(2):
        bl[:, 2 * r] = 1.0 if r == g else 0.0
        bl[:, 2 * r + 1] = 0.0 if r == g else 1.0
    d["blend"] = bl
    return d


def _layer_inputs(l, w_in, b_forget, lambda_q1, lambda_k1, lambda_q2, lambda_k2, subln_g, w_out, ln_g, ln_b):
    wk, wq = _weight_layouts(np.asarray(w_in[l], np.float32))
    rep = lambda v: np.ascontiguousarray(np.broadcast_to(np.asarray(v, np.float32)[None, :], (128, len(v))))
    lamv = np.stack([rep(lambda_q1[l]), rep(lambda_k1[l]), rep(lambda_q2[l]), rep(lambda_k2[l])], axis=1)
    return {
        f"wk{l}": wk, f"wq{l}": wq, f"wo{l}": np.ascontiguousarray(np.asarray(w_out[l], np.float32)),
        f"lng{l}": rep(ln_g[l]), f"lnb{l}": rep(ln_b[l]),
        f"bfg{l}": np.asarray(b_forget[l], np.float32).reshape(4, 1),
        f"lamv{l}": np.ascontiguousarray(lamv), f"subg{l}": np.asarray(subln_g[l], np.float32).reshape(128, 1),
    }


LAUNCH_PLAN = [[0, 1]]


def kernel(x, ln_in_g, ln_in_b, w_in, b_forget, lambda_q1, lambda_k1, lambda_q2, lambda_k2, subln_g, w_out, ln_g, ln_b):
    x = np.asarray(x, np.float32)
    B = x.shape[0]
    rep = lambda v: np.ascontiguousarray(np.broadcast_to(np.asarray(v, np.float32)[None, :], (128, len(v))))
    consts = [_consts(g) for g in range(2)]
    gath = np.concatenate([_own_rows(0), _own_rows(1)])
    h = x
    for layers in LAUNCH_PLAN:
        nc = _get_prog(layers)
        lay = {}
        for l in layers:
            lay.update(_layer_inputs(l, w_in, b_forget, lambda_q1, lambda_k1, lambda_q2, lambda_k2, subln_g, w_out, ln_g, ln_b))
        in_maps = []
        for core in range(8):
            b, g = core // 2, core % 2
            m = dict(consts[g])
            m.update(lay)
            m["hin_full"] = np.ascontiguousarray(h[b][gath])
            m["hin_own"] = np.ascontiguousarray(h[b][_own_rows(g)])
            m["hin_oth"] = np.ascontiguousarray(h[b][_own_rows(1 - g)])
            m["lng_in"] = rep(ln_in_g)
            m["lnb_in"] = rep(ln_in_b)
            in_maps.append(m)
        res = run_bass_kernel_spmd(nc, in_maps, core_ids=list(range(8)))
        hn = np.empty_like(x)
        for core in range(8):
            b, g = core // 2, core % 2
            hn[b][_own_rows(g)] = np.asarray(res.results[core]["out"], np.float32)
        h = hn
    return h
```

```python
import contextlib
import math

import numpy as np
import ml_dtypes

import concourse.bass as bass
import concourse.mybir as mybir
from concourse.bass_utils import run_bass_kernel_spmd

F32 = mybir.dt.float32
BF16 = mybir.dt.bfloat16
AF = mybir.ActivationFunctionType
ALU = mybir.AluOpType

S = 8192
D = 1024
OWN = 4096
NB = 16
DEPTH = 2
LN_EPS = 1e-5
SUBLN_EPS = 1e-5
ALPHA = (2 * DEPTH) ** 0.25
ROPE_THETA = 500000.0
MASKV = -30000.0
WKC = 2564
WQC = 2560

ENGS = ("pe", "act", "dve", "pool", "sp")
DEBUG = False
DUMP = False


class Op:
    __slots__ = ("eng", "fn", "deps", "marked", "sig", "dma", "dsem", "dval", "cc")

    def __init__(self, eng, fn, dma):
        self.eng = eng
        self.fn = fn
        self.deps = []
        self.marked = False
        self.sig = None
        self.dma = dma
        self.dsem = None
        self.dval = None
        self.cc = False


class _Rec:
    def __init__(self):
        self.call = None

    def __getattr__(self, name):
        def f(*a, **k):
            self.call = (name, a, k)
            return None
        return f

    def replay(self, eng):
        name, a, k = self.call
        return getattr(eng, name)(*a, **k)


class Prog:
    NDSEM = 24

    def __init__(self, nc):
        self.nc = nc
        self.ops = {e: [] for e in ENGS}
        self.last_writer = {}
        self.readers = {}
        self.dma_ops = {e: [] for e in ENGS}
        self.final = []

    def op(self, eng, fn, reads=(), writes=(), dma=False, extra=()):
        if fn is not None:
            rec = _Rec()
            fn(rec)
            assert rec.call is not None
            fn = rec.replay
        o = Op(eng, fn, dma)
        deps = []
        seen = set()

        def add(d):
            if d is None or id(d) in seen:
                return
            seen.add(id(d))
            deps.append(d)

        for b in reads:
            add(self.last_writer.get(b))
        for b in writes:
            add(self.last_writer.get(b))
            for r in self.readers.get(b, ()):
                add(r)
        for d in extra:
            add(d)
        if dma:
            lst = self.dma_ops[eng]
            n = len(lst)
            if n >= self.NDSEM:
                add(lst[n - self.NDSEM])
            o.dsem = n % self.NDSEM
            o.dval = 16 * (n // self.NDSEM + 1)
            lst.append(o)
        for d in deps:
            if d.dma:
                o.deps.append(d)
            elif d.eng == "pe" and eng == "pe" and not dma and fn is not None:
                continue
            else:
                d.marked = True
                o.deps.append(d)
        for b in reads:
            self.readers.setdefault(b, []).append(o)
        for b in writes:
            self.last_writer[b] = o
            self.readers[b] = []
        self.ops[eng].append(o)
        return o

    def barrier(self, eng, deps):
        return self.op(eng, None, extra=deps)

    def full_barrier(self):
        tails = []
        for e in ENGS:
            for o in reversed(self.ops[e]):
                if o.fn is not None and not o.dma:
                    tails.append(o)
                    break
        dmas = []
        for e in ENGS:
            dmas += self.dma_ops[e][-self.NDSEM:]
        for e in ENGS:
            self.barrier(e, tails + dmas)
        self.last_writer = {}
        self.readers = {}

    def emit(self):
        nc = self.nc
        with contextlib.ExitStack() as st:
            csem = {e: st.enter_context(nc.semaphore(f"c_{e}")) for e in ENGS}
            dsem = {
                e: [st.enter_context(nc.semaphore(f"d_{e}_{i}")) for i in range(self.NDSEM)]
                for e in ENGS if self.dma_ops[e]
            }
            for e in ENGS:
                c = 0
                for o in self.ops[e]:
                    if o.marked and not o.dma:
                        c += 1
                        o.sig = c
            block = st.enter_context(nc.Block())

            def run(e, engobj):
                seen = {}

                def wait(d):
                    if d.dma:
                        key = ("d", d.eng, d.dsem)
                        sem, val = dsem[d.eng][d.dsem], d.dval
                    else:
                        key = ("c", d.eng)
                        sem, val = csem[d.eng], d.sig
                    if seen.get(key, 0) >= val:
                        return
                    seen[key] = val
                    engobj.wait_ge(sem, val)

                for o in self.ops[e]:
                    for d in o.deps:
                        wait(d)
                    if o.fn is None:
                        continue
                    ins = o.fn(engobj)
                    if o.dma:
                        ins.then_inc(dsem[e][o.dsem], 16)
                    elif o.marked:
                        ins.then_inc(csem[e], 1)
                if e == "sp":
                    for d in self.final:
                        wait(d)

            @block.tensor
            def _(eng):
                run("pe", eng)

            @block.scalar
            def _(eng):
                run("act", eng)

            @block.vector
            def _(eng):
                run("dve", eng)

            @block.gpsimd
            def _(eng):
                run("pool", eng)

            @block.sync
            def _(eng):
                run("sp", eng)


SB_BASE = 16512
SB_TOP = 229376 - 1024


class Arena:
    def __init__(self, nc):
        self.nc = nc
        self.ptr = SB_BASE
        self.n = 0

    def mark(self):
        return self.ptr

    def reset(self, p):
        self.ptr = p

    def alloc(self, name, shape, dt):
        esz = 4 if dt == F32 else 2
        nbytes = esz
        for s in shape[1:]:
            nbytes *= s
        nbytes = (nbytes + 63) // 64 * 64
        off = self.ptr
        self.ptr += nbytes
        assert self.ptr <= SB_TOP, (name, self.ptr)
        self.n += 1
        return self.nc.alloc_sbuf_tensor_at(f"{name}_{self.n}", list(shape), dt, offset=off)


def build_program(layers, lam_inits):
    nc = bass.Bass("TRN2", target_bir_lowering=False)
    P = Prog(nc)
    A = Arena(nc)

    def din(name, shape, dt=F32):
        return nc.dram_tensor(name, list(shape), dt, kind="ExternalInput").ap()

    def dscr(name, shape, dt):
        if DEBUG:
            return nc.dram_tensor(name, list(shape), dt, kind="ExternalOutput").ap()
        return nc.dram_tensor(name, list(shape), dt).ap()

    L0 = layers[0]
    first_is_l0 = (L0 == 0)
    hin_full = din("hin_full", [S, D])
    hin_own = din("hin_own", [OWN, D])
    hin_oth = din("hin_oth", [OWN, D])
    blend = din("blend", [128, 4])
    lng_in = din("lng_in", [128, D])
    lnb_in = din("lnb_in", [128, D])
    rk_cos = din("rk_cos", [128, S])
    rk_sin = din("rk_sin", [128, S])
    rq_cos_p = [din(f"rq_cos{p}", [128, OWN]) for p in range(2)]
    rq_sin_p = [din(f"rq_sin{p}", [128, OWN]) for p in range(2)]
    selg_p = [din(f"selg{p}", [4, 2]) for p in range(2)]
    c_ident = din("c_ident", [128, 128], BF16)
    c_tneg = din("c_tneg", [128, 128], BF16)
    c_onesneg = din("c_onesneg", [128, 128], BF16)
    c_onesb = din("c_onesb", [128, 128], BF16)
    c_onesf = din("c_onesf", [128, 128])
    c_mask_ab_p = [din(f"c_mask_ab{p}", [128, 4, 256], BF16) for p in range(2)]
    c_mask_c_p = [din(f"c_mask_c{p}", [128, 4, 256], BF16) for p in range(2)]
    per_layer = {}
    for l in layers:
        per_layer[l] = dict(
            wk=din(f"wk{l}", [D, WKC]), wq=din(f"wq{l}", [D, WQC]), wo=din(f"wo{l}", [D, D]),
            lng=din(f"lng{l}", [128, D]), lnb=din(f"lnb{l}", [128, D]),
            bfg=din(f"bfg{l}", [4, 1]), lamv=din(f"lamv{l}", [128, 4, 64]), subg=din(f"subg{l}", [128, 1]),
        )
    out = nc.dram_tensor("out", [OWN, D], F32, kind="ExternalOutput").ap()

    KTA = dscr("KTA", [4, 128, S], BF16)
    KTB = dscr("KTB", [4, 70, S], BF16)
    KTC = dscr("KTC", [4, 64, S], BF16)
    VS = dscr("VS", [S, 1024], BF16)
    QTA = dscr("QTA", [4, 128, OWN], BF16)
    QTB = dscr("QTB", [4, 70, OWN], BF16)
    QTC = dscr("QTC", [4, 64, OWN], BF16)
    GT = dscr("GT", [1024, OWN], F32)
    HRES = dscr("HRES", [OWN, D], F32)
    YT = dscr("YT", [1024, OWN], BF16)
    NLF = dscr("NLF", [4, S], F32)
    CN = dscr("CN", [4, S], F32)
    HP = [dscr(f"HP{p}", [OWN, D], F32) for p in range(2)]

    if DEBUG == "C0":
        DBG_U = nc.dram_tensor("DBG_U", [128, 1024], F32, kind="ExternalOutput").ap()
        DBG_L = nc.dram_tensor("DBG_L", [128, 1024], BF16, kind="ExternalOutput").ap()
        DBG_X = nc.dram_tensor("DBG_X", [128, 1024], F32, kind="ExternalOutput").ap()
    X0 = nc.alloc_psum_tensor("X0", [128, 1024], F32)
    X1 = nc.alloc_psum_tensor("X1", [128, 1024], F32)
    X2 = nc.alloc_psum_tensor("X2", [128, 1024], F32)
    PA = nc.alloc_psum_tensor("PA", [128, 512], F32)
    PB = nc.alloc_psum_tensor("PB", [128, 512], F32)
    XS = [X0, X1, X2]

    ident = A.alloc("ident", [128, 128], BF16)
    tneg = A.alloc("tneg", [128, 128], BF16)
    onesneg = A.alloc("onesneg", [128, 128], BF16)
    onesb = A.alloc("onesb", [128, 128], BF16)
    onesf = A.alloc("onesf", [128, 128], F32)
    mask_ab = A.alloc("mask_ab", [128, 4, 256], BF16)
    mask_c = A.alloc("mask_c", [128, 4, 256], BF16)
    selg_sb = A.alloc("selg", [4, 2], F32)
    lam_t = A.alloc("lam", [128, 8], F32)
    subg_t = A.alloc("subg", [128, 2], F32)
    negb_t = A.alloc("negb", [4, 2], F32)
    for dst, src, k in ((ident, c_ident, "ident"), (tneg, c_tneg, "tneg"), (onesneg, c_onesneg, "onesneg"),
                        (onesb, c_onesb, "onesb"), (onesf, c_onesf, "onesf")):
        P.op("sp", lambda e, dst=dst, src=src: e.dma_start(out=dst[:], in_=src[:, :]), writes=[k], dma=True)
    blend_sb = A.alloc("blend", [128, 4], F32)
    P.op("sp", lambda e: e.dma_start(out=blend_sb[:], in_=blend[:, :]), writes=["blend"], dma=True)
    persist_mark = A.mark()
    CONST_KEYS = ["ident", "tneg", "onesneg", "onesb", "onesf", "mask_ab", "mask_c", "selg"]

    def reload_const_keys():
        pass

    out_stores = []

    fused = len(layers) > 1
    schedule = []
    for li, l in enumerate(layers):
        npass = 2 if (fused and li < len(layers) - 1) else 1
        for p_ in range(npass):
            schedule.append((li, l, p_))
    for (li, l, pss) in schedule:
        pl = per_layer[l]
        lam_init = lam_inits[l]
        do_ln_in = (l == 0)
        last_layer = (li == len(layers) - 1)
        do_k = (pss == 0)
        from_hp = (li > 0)
        src_own = (HP[0] if from_hp else (hin_own if pss == 0 else hin_oth))
        dest = out if last_layer else HP[pss]
        rq_cos, rq_sin = rq_cos_p[pss], rq_sin_p[pss]
        A.reset(persist_mark)
        P.op("sp", lambda e: e.dma_start(out=mask_ab[:], in_=c_mask_ab_p[pss][:, :, :]), writes=["mask_ab"], dma=True)
        P.op("sp", lambda e: e.dma_start(out=mask_c[:], in_=c_mask_c_p[pss][:, :, :]), writes=["mask_c"], dma=True)
        P.op("sp", lambda e: e.dma_start(out=selg_sb[:], in_=selg_p[pss][:, :]), writes=["selg"], dma=True)

        lamv_sb = A.alloc("lamv", [128, 4, 64], F32)
        lprod = A.alloc("lprod", [128, 2, 64], F32)
        P.op("sp", lambda e: e.dma_start(out=lamv_sb[:], in_=pl["lamv"][:, :, :]), writes=["lamv"], dma=True)
        P.op("sp", lambda e: e.dma_start(out=subg_t[:, 0:1], in_=pl["subg"][:, :]), writes=["subg0"], dma=True)
        P.op("sp", lambda e: e.dma_start(out=negb_t[:, 0:1], in_=pl["bfg"][:, :]), writes=["negb0"], dma=True)
        P.op("dve", lambda e: e.tensor_tensor(out=lprod[:, 0, :], in0=lamv_sb[:, 0, :], in1=lamv_sb[:, 1, :], op=ALU.mult),
             reads=["lamv"], writes=["lprod"])
        P.op("dve", lambda e: e.tensor_tensor(out=lprod[:, 1, :], in0=lamv_sb[:, 2, :], in1=lamv_sb[:, 3, :], op=ALU.mult),
             reads=["lamv", "lprod"], writes=["lprod"])
        P.op("dve", lambda e: e.reduce_sum(out=lam_t[:, 0:1], in_=lprod[:, 0, :], axis=mybir.AxisListType.X),
             reads=["lprod"], writes=["lam01"])
        P.op("dve", lambda e: e.reduce_sum(out=lam_t[:, 1:2], in_=lprod[:, 1, :], axis=mybir.AxisListType.X),
             reads=["lprod", "lam01"], writes=["lam01"])
        P.op("act", lambda e: e.activation(out=lam_t[:, 2:4], in_=lam_t[:, 0:2], func=AF.Exp), reads=["lam01"], writes=["lam23"])
        P.op("dve", lambda e: e.scalar_tensor_tensor(out=lam_t[:, 4:5], in0=lam_t[:, 3:4], scalar=-float(lam_init),
                                                     in1=lam_t[:, 2:3], op0=ALU.add, op1=ALU.subtract),
             reads=["lam23"], writes=["neglam"])
        P.op("dve", lambda e: e.tensor_scalar(out=subg_t[:, 1:2], in0=subg_t[:, 0:1], scalar1=float((1.0 - lam_init) * math.sqrt(128.0)),
                                              scalar2=None, op0=ALU.mult), reads=["subg0"], writes=["gsub"])
        P.op("dve", lambda e: e.tensor_scalar(out=negb_t[:, 1:2], in0=negb_t[:, 0:1], scalar1=-1.0, scalar2=None, op0=ALU.mult),
             reads=["negb0"], writes=["negb"])
        neglam = lam_t[:, 4:5]
        gsub = subg_t[:, 1:2]
        negb = negb_t[:, 1:2]
        layer_mark = A.mark()

        Wk = A.alloc("Wk", [128, 8, WKC], BF16)
        Wq = A.alloc("Wq", [128, 8, WQC], BF16)
        for c in range(8):
            P.op("pool", lambda e, c=c: e.dma_start(out=Wk[:, c, :], in_=pl["wk"][c * 128:(c + 1) * 128, :]), writes=[("Wk", c)], dma=True)
        for c in range(8):
            P.op("pool", lambda e, c=c: e.dma_start(out=Wq[:, c, :], in_=pl["wq"][c * 128:(c + 1) * 128, :]), writes=[("Wq", c)], dma=True)
        WK_KEYS = [("Wk", c) for c in range(8)]
        WQ_KEYS = [("Wq", c) for c in range(8)]
        if do_ln_in:
            g_in = A.alloc("g_in", [128, D], F32)
            b_in = A.alloc("b_in", [128, D], F32)
            P.op("sp", lambda e: e.dma_start(out=g_in[:], in_=lng_in[:, :]), writes=["g_in"], dma=True)
            P.op("sp", lambda e: e.dma_start(out=b_in[:], in_=lnb_in[:, :]), writes=["b_in"], dma=True)
        xin_t = [A.alloc("xin", [128, D], F32) for _ in range(2)]
        hf_t = [A.alloc("hf", [128, D], F32) for _ in range(2)]
        hb_t = [A.alloc("hb", [128, D], BF16) for _ in range(2)]
        hT_t = [A.alloc("hT", [128, 8, 512], BF16) for _ in range(2)]
        rc_t = [A.alloc("rc", [128, 512], F32) for _ in range(2)]
        rs_t = [A.alloc("rs", [128, 512], F32) for _ in range(2)]
        tmp_t = [A.alloc("tmp", [128, 512], F32) for _ in range(2)]
        stb_t = [A.alloc("stb", [128, 512], BF16) for _ in range(4)]
        stf_t = [A.alloc("stf", [128, 512], F32) for _ in range(2)]
        st6 = A.alloc("st6", [128, 12], F32)
        mv = A.alloc("mv", [128, 4], F32)
        accs = [(X0, 0), (X0, 512), (X1, 0), (X1, 512), (X2, 0), (X2, 512)]
        cnt = dict(tile=0, acc=0, stb=0, stf=0, tmp=0, grp=0)
        stores1 = []

        def layer_norm(src, skey, dst, dkey, gt, bt, gkeys):
            for hh in range(2):
                P.op("dve", lambda e, hh=hh: e.bn_stats(out=st6[:, hh * 6:(hh + 1) * 6], in_=src[:, hh * 512:(hh + 1) * 512]),
                     reads=[skey], writes=[("st6", hh)])
            P.op("dve", lambda e: e.bn_aggr(out=mv[:, 0:2], in_=st6[:, 0:12]), reads=[("st6", 0), ("st6", 1)], writes=["mv"])
            P.op("act", lambda e: e.activation(out=mv[:, 3:4], in_=mv[:, 1:2], func=AF.Ln, bias=float(LN_EPS), scale=1.0),
                 reads=["mv"], writes=["lnv"])
            P.op("act", lambda e: e.activation(out=mv[:, 2:3], in_=mv[:, 3:4], func=AF.Exp, scale=-0.5), reads=["lnv"], writes=["rstd"])
            P.op("dve", lambda e: e.tensor_scalar(out=dst[:], in0=src[:], scalar1=mv[:, 0:1], scalar2=mv[:, 2:3],
                                                  op0=ALU.subtract, op1=ALU.mult), reads=[skey, "mv", "rstd"], writes=[dkey])
            P.op("dve", lambda e: e.tensor_tensor(out=dst[:], in0=dst[:], in1=gt[:], op=ALU.mult), reads=[dkey, gkeys[0]], writes=[dkey])
            P.op("dve", lambda e: e.tensor_tensor(out=dst[:], in0=dst[:], in1=bt[:], op=ALU.add), reads=[dkey, gkeys[1]], writes=[dkey])

        def next_acc():
            a = accs[cnt["acc"] % len(accs)]
            key = ("acc", cnt["acc"] % len(accs))
            cnt["acc"] += 1
            return a[0], a[1], key

        def next_stb():
            i = cnt["stb"] % 4
            cnt["stb"] += 1
            return stb_t[i], ("stb", i)

        def next_stf():
            i = cnt["stf"] % 2
            cnt["stf"] += 1
            return stf_t[i], ("stf", i)

        def proj_pass(src, ngroups, mode):
            W = Wk if mode == "k" else Wq
            WKEYS = WK_KEYS if mode == "k" else WQ_KEYS
            rcos = rk_cos if mode == "k" else rq_cos
            rsin = rk_sin if mode == "k" else rq_sin
            gslot = {}

            def prep_begin(G):
                gs = cnt["grp"] % 2
                cnt["grp"] += 1
                gslot[G] = gs
                rc, rs_ = rc_t[gs], rs_t[gs]
                P.op("sp", lambda e: e.dma_start(out=rc[:], in_=rcos[:, G * 512:(G + 1) * 512]), writes=[("rc", gs)], dma=True)
                P.op("sp", lambda e: e.dma_start(out=rs_[:], in_=rsin[:, G * 512:(G + 1) * 512]), writes=[("rs", gs)], dma=True)

            tstate = {}

            def prep_load(G, tt):
                gs = gslot[G]
                ts_ = cnt["tile"] % 2
                cnt["tile"] += 1
                row0 = G * 512 + tt * 128
                xin = xin_t[ts_]
                tstate[(G, tt)] = ts_
                if src == "blend":
                    idx = row0 % OWN
                    t1 = hf_t[ts_]
                    P.op("sp", lambda e: e.dma_start(out=xin[:], in_=HP[0][idx:idx + 128, :]), writes=[("xin", ts_)], dma=True)
                    P.op("sp", lambda e: e.dma_start(out=t1[:], in_=HP[1][idx:idx + 128, :]), writes=[("hf", ts_)], dma=True)
                else:
                    P.op("sp", lambda e: e.dma_start(out=xin[:], in_=src[row0:row0 + 128, :]), writes=[("xin", ts_)], dma=True)

            def prep_norm(G, tt):
                ts_ = tstate[(G, tt)]
                row0 = G * 512 + tt * 128
                xin = xin_t[ts_]
                if src == "blend":
                    rr = row0 // OWN
                    t1 = hf_t[ts_]
                    P.op("dve", lambda e: e.tensor_scalar(out=xin[:], in0=xin[:], scalar1=blend_sb[:, 2 * rr:2 * rr + 1], scalar2=None,
                                                          op0=ALU.mult), reads=[("xin", ts_)], writes=[("xin", ts_)])
                    P.op("dve", lambda e: e.scalar_tensor_tensor(out=xin[:], in0=t1[:], scalar=blend_sb[:, 2 * rr + 1:2 * rr + 2], in1=xin[:],
                                                                 op0=ALU.mult, op1=ALU.add),
                         reads=[("xin", ts_), ("hf", ts_)], writes=[("xin", ts_)])
                if do_ln_in:
                    hf = hf_t[ts_]
                    hfk = ("hf", ts_)
                    layer_norm(xin, ("xin", ts_), hf, hfk, g_in, b_in, ["g_in", "b_in"])
                else:
                    hf, hfk = xin, ("xin", ts_)
                if mode == "q":
                    stores1.append(P.op("pool", lambda e: e.dma_start(out=HRES[row0:row0 + 128, :], in_=hf[:]), reads=[hfk], dma=True))
                hb = hb_t[ts_]
                P.op("act", lambda e: e.copy(out=hb[:], in_=hf[:]), reads=[hfk], writes=[("hb", ts_)])

            def prep_tr(G, tt):
                gs = gslot[G]
                hT = hT_t[gs]
                ts_ = tstate[(G, tt)]
                hb = hb_t[ts_]
                for half in range(2):
                    ps, o0, akey = next_acc()
                    for c4 in range(4):
                        c = half * 4 + c4
                        P.op("pe", lambda e, c4=c4, c=c: e.matmul(
                            out=ps[:, o0 + c4 * 128:o0 + (c4 + 1) * 128], lhsT=hb[:, c * 128:(c + 1) * 128], rhs=ident[:],
                            start=True, stop=True), reads=[("hb", ts_), "ident"], writes=[akey])
                    P.op("dve", lambda e: e.tensor_copy(
                        out=hT[:, half * 4:(half + 1) * 4, tt * 128:(tt + 1) * 128],
                        in_=ps[:, o0:o0 + 512].rearrange("p (c t) -> p c t", c=4)),
                        reads=[akey], writes=[("hT", gs, tt, half)])

            def chunks(G):
                gs = gslot[G]
                hT = hT_t[gs]
                hTkeys = [("hT", gs, tt_, hf_) for tt_ in range(4) for hf_ in range(2)]
                rc, rs_ = rc_t[gs], rs_t[gs]

                def fm_chunk(col0, M):
                    ps, o0, akey = next_acc()
                    for c in range(8):
                        P.op("pe", lambda e, c=c: e.matmul(
                            out=ps[0:M, o0:o0 + 512], lhsT=W[:, c, col0:col0 + M], rhs=hT[:, c, :], start=(c == 0), stop=(c == 7)),
                            reads=hTkeys + [WKEYS[c]], writes=[akey])
                    return ps, o0, akey

                tok0 = G * 512
                for i in range(4):
                    p1, o1, k1 = fm_chunk(i * 128, 128)
                    p2, o2, k2 = fm_chunk(512 + i * 128, 128)
                    t1 = tmp_t[0]
                    t2 = tmp_t[1]
                    P.op("dve", lambda e: e.tensor_tensor(out=t1[:], in0=p1[:, o1:o1 + 512], in1=rc[:], op=ALU.mult),
                         reads=[k1, ("rc", gs)], writes=[("tmp", 0)])
                    P.op("dve", lambda e: e.tensor_tensor(out=t2[:], in0=p2[:, o2:o2 + 512], in1=rs_[:], op=ALU.mult),
                         reads=[k2, ("rs", gs)], writes=[("tmp", 1)])
                    sb_, sk = next_stb()
                    P.op("dve", lambda e: e.tensor_tensor(out=sb_[:], in0=t1[:], in1=t2[:], op=ALU.add),
                         reads=[("tmp", 0), ("tmp", 1)], writes=[sk])
                    dstT = KTA if mode == "k" else QTA
                    stores1.append(P.op("pool", lambda e: e.dma_start(out=dstT[i, :, tok0:tok0 + 512], in_=sb_[:]), reads=[sk], dma=True))
                    yield
                for kind, cbase, dstT in (("B", 1024, KTB if mode == "k" else QTB), ("C", 1280, KTC if mode == "k" else QTC)):
                    for h in range(4):
                        ps, o0, akey = fm_chunk(cbase + h * 64, 64)
                        sb_, sk = next_stb()
                        if mode == "k":
                            P.op("act", lambda e: e.copy(out=sb_[0:64, :], in_=ps[0:64, o0:o0 + 512]), reads=[akey], writes=[sk])
                        else:
                            P.op("act", lambda e: e.mul(out=sb_[0:64, :], in_=ps[0:64, o0:o0 + 512], mul=0.125), reads=[akey], writes=[sk])
                        stores1.append(P.op("pool", lambda e: e.dma_start(out=dstT[h, 0:64, tok0:tok0 + 512], in_=sb_[0:64, :]),
                                            reads=[sk], dma=True))
                        yield
                if mode == "k":
                    ps, o0, akey = fm_chunk(1536, 4)
                    sf, sfk = next_stf()
                    P.op("act", lambda e: e.activation(out=sf[0:4, :], in_=ps[0:4, o0:o0 + 512], func=AF.Exp, bias=negb, scale=-1.0),
                         reads=[akey, "negb"], writes=[sfk])
                    P.op("act", lambda e: e.activation(out=sf[0:4, :], in_=sf[0:4, :], func=AF.Ln, bias=1.0, scale=1.0),
                         reads=[sfk], writes=[sfk])
                    r = G // 8
                    for bb in range(2):
                        i_blk = (G % 8) * 2 + bb
                        t0 = i_blk * 512 + r * 256
                        stores1.append(P.op("pool", lambda e, bb=bb, t0=t0: e.dma_start(
                            out=NLF[:, t0:t0 + 256], in_=sf[0:4, bb * 256:(bb + 1) * 256]), reads=[sfk], dma=True))
                    yield
                    for tt in range(4):
                        for half in range(2):
                            ps, o0, akey = next_acc()
                            for c in range(8):
                                P.op("pe", lambda e, c=c: e.matmul(
                                    out=ps[:, o0:o0 + 512], lhsT=hT[:, c, tt * 128:(tt + 1) * 128],
                                    rhs=W[:, c, 1540 + half * 512:1540 + (half + 1) * 512], start=(c == 0), stop=(c == 7)),
                                    reads=hTkeys + [WKEYS[c]], writes=[akey])
                            sb_, sk = next_stb()
                            P.op("act", lambda e: e.copy(out=sb_[:], in_=ps[:, o0:o0 + 512]), reads=[akey], writes=[sk])
                            r0 = tok0 + tt * 128
                            stores1.append(P.op("pool", lambda e: e.dma_start(
                                out=VS[r0:r0 + 128, half * 512:(half + 1) * 512], in_=sb_[:]), reads=[sk], dma=True))
                            yield
                else:
                    for gch in range(8):
                        ps, o0, akey = fm_chunk(1536 + gch * 128, 128)
                        sf, sfk = next_stf()
                        P.op("act", lambda e: e.activation(out=sf[:], in_=ps[:, o0:o0 + 512], func=AF.Silu), reads=[akey], writes=[sfk])
                        stores1.append(P.op("pool", lambda e: e.dma_start(
                            out=GT[gch * 128:(gch + 1) * 128, tok0:tok0 + 512], in_=sf[:]), reads=[sfk], dma=True))
                        yield

            prep_begin(0)
            for tt in range(4):
                prep_load(0, tt)
                prep_norm(0, tt)
                prep_tr(0, tt)
            ev_load = {0: 0, 3: 1, 8: 2, 13: 3}
            ev_norm = {1: 0, 6: 1, 11: 2, 16: 3}
            ev_tr = {5: 0, 10: 1, 15: 2, 19: 3}
            for G in range(ngroups):
                more = (G + 1 < ngroups)
                done = set()

                def fire(idx):
                    if not more:
                        return
                    for ev, fn, tag in ((ev_load, prep_load, "l"), (ev_norm, prep_norm, "n"), (ev_tr, prep_tr, "t")):
                        if idx in ev and (tag, ev[idx]) not in done:
                            done.add((tag, ev[idx]))
                            fn(G + 1, ev[idx])
                if more:
                    prep_begin(G + 1)
                fire(0)
                idx = 0
                for _ in chunks(G):
                    idx += 1
                    fire(idx)
                for k in range(idx + 1, 24):
                    fire(k)

        if do_k:
            proj_pass("blend" if from_hp else hin_full, 16, "k")
        proj_pass(src_own, 8, "q")
        P.full_barrier()
        A.reset(layer_mark)

        nlf = A.alloc("nlf", [4, S], F32)
        cn = A.alloc("cn", [4, S], F32)
        pk = [A.alloc("pk", [4, S], BF16) for _ in range(3)]
        cq = [A.alloc("cq", [4, OWN], F32) for _ in range(2)]
        pq = [A.alloc("pq", [4, OWN], BF16) for _ in range(3)]
        ones_r = A.alloc("ones_r", [4, S], BF16)
        P.op("pool", lambda e: e.memset(ones_r[:], 1.0), writes=["ones_r"])
        if do_k:
            P.op("sp", lambda e: e.dma_start(out=nlf[:], in_=NLF[:, :]), writes=["nlf"], dma=True)
            P.op("dve", lambda e: e.tensor_tensor_scan(out=cn[:], data0=nlf[:], data1=nlf[:], initial=0.0, op0=ALU.add, op1=ALU.max),
                 reads=["nlf"], writes=["cn"])
            st_cn = P.op("pool", lambda e: e.dma_start(out=CN[:, :], in_=cn[:]), reads=["cn"], dma=True)
            P.barrier("sp", [st_cn])
        CNv = CN.rearrange("h (i r t) -> h r i t", i=16, r=2, t=256)
        for r in range(2):
            if do_k:
                P.op("sp", lambda e, r=r: e.dma_start(out=nlf[:, r * OWN:(r + 1) * OWN].rearrange("h (i t) -> h i t", t=256),
                                                     in_=CNv[:, r, :, :]), reads=["cn"], writes=["nlf"], dma=True)
            P.op("sp", lambda e, r=r: e.dma_start(out=cq[r][:].rearrange("h (i t) -> h i t", t=256), in_=CNv[:, r, :, :]),
                 writes=[("cq", r)], dma=True)

        def split3(src, skey, pieces, pkey):
            for p_ in range(3):
                P.op("dve", lambda e, p_=p_: e.tensor_copy(out=pieces[p_][:], in_=src[:]), reads=[skey], writes=[(pkey, p_)])
                if p_ < 2:
                    P.op("dve", lambda e, p_=p_: e.tensor_tensor(out=src[:], in0=src[:], in1=pieces[p_][:], op=ALU.subtract),
                         reads=[skey, (pkey, p_)], writes=[skey])

        if do_k:
            split3(nlf, "nlf", pk, "pk")
        P.op("dve", lambda e: e.tensor_scalar(out=cq[0][:], in0=cq[0][:], scalar1=selg_sb[:, 0:1], scalar2=-1.0, op0=ALU.mult, op1=ALU.mult),
             reads=[("cq", 0)], writes=[("cq", 0)])
        P.op("dve", lambda e: e.tensor_scalar(out=cq[1][:], in0=cq[1][:], scalar1=selg_sb[:, 1:2], scalar2=-1.0, op0=ALU.mult, op1=ALU.mult),
             reads=[("cq", 1)], writes=[("cq", 1)])
        P.op("dve", lambda e: e.tensor_tensor(out=cq[0][:], in0=cq[0][:], in1=cq[1][:], op=ALU.add),
             reads=[("cq", 0), ("cq", 1)], writes=[("cq", 0)])
        split3(cq[0], ("cq", 0), pq, "pq")
        bst = []
        for h in range(4):
            for p_ in range(3):
                if do_k:
                    bst.append(P.op("pool", lambda e, h=h, p_=p_: e.dma_start(out=KTB[h, 67 + p_:68 + p_, :], in_=pk[p_][h:h + 1, :]),
                                    reads=[("pk", p_)], dma=True))
                bst.append(P.op("pool", lambda e, h=h, p_=p_: e.dma_start(out=QTB[h, 64 + p_:65 + p_, :], in_=pq[p_][h:h + 1, :]),
                                reads=[("pq", p_)], dma=True))
            if do_k:
                bst.append(P.op("pool", lambda e, h=h: e.dma_start(out=KTB[h, 64:67, :], in_=ones_r[0:3, :]), reads=["ones_r"], dma=True))
            bst.append(P.op("pool", lambda e, h=h: e.dma_start(out=QTB[h, 67:70, :], in_=ones_r[0:3, 0:OWN]), reads=["ones_r"], dma=True))
        P.full_barrier()
        A.reset(layer_mark)

        kt_t = [A.alloc("kt", [128, S], BF16) for _ in range(2)]
        qt_t = [A.alloc("qt", [128, OWN], BF16) for _ in range(2)]
        vt_t = [A.alloc("vt", [128, 64, 128], BF16) for _ in range(2)]
        qz_t = [A.alloc("qz", [128, 2, OWN], BF16) for _ in range(2)]
        for us_ in range(2):
            P.op("pool", lambda e: e.memset(qz_t[us_][64:128, 0, :], 0.0), writes=[("qz0", us_)])
            P.op("pool", lambda e: e.memset(qz_t[us_][0:64, 1, :], 0.0), writes=[("qz1", us_)])
        E_t = [A.alloc("E", [128, 1024], BF16) for _ in range(4)]
        U_t = [A.alloc("U", [128, 1024], F32) for _ in range(3)]
        Lp_t = [A.alloc("Lp", [128, 1024], BF16) for _ in range(3)]
        Y_t = [A.alloc("Y", [128, 1024], F32) for _ in range(3)]
        cb_t = [A.alloc("cb", [128, 256], F32) for _ in range(3)]
        gate_t = [A.alloc("gate", [128, 256], F32) for _ in range(2)]
        r_t = [A.alloc("r", [128, 256], F32) for _ in range(2)]
        o_t = [A.alloc("o", [128, 256], F32) for _ in range(3)]
        sq_t = A.alloc("sq", [128, 256], F32)
        rstd_t = A.alloc("rstd", [128, 256], F32)
        ys_t = [A.alloc("ys", [128, 256], BF16) for _ in range(2)]
        Es_t = [A.alloc("Es", [128, 1024], F32) for _ in range(2)]
        ystores = []
        ucount = [0]
        ycount = [0]

        def kcol(j, m):
            return (m // 2) * OWN + j * 256 + (m % 2) * 128

        def vch(j, m):
            return (m // 2) * 32 + j * 2 + (m % 2)

        def ktk(us):
            return [("kt", us, 0), ("kt", us, 1), ("kt", us, "z")]

        def qtk(us):
            return [("qt", us, 0), ("qt", us, 1), ("qt", us, "z"), ("qz0", us), ("qz1", us)]

        def load_unit(kind, h):
            us = ucount[0] % 2
            ucount[0] += 1
            kt, qt, vt = kt_t[us], qt_t[us], vt_t[us]
            rows = {"A": 128, "B": 70, "C": 64}[kind]
            KT = {"A": KTA, "B": KTB, "C": KTC}[kind]
            QT = {"A": QTA, "B": QTB, "C": QTC}[kind]
            if kind == "C":
                P.op("pool", lambda e: e.memset(kt[64:128, :], 0.0), writes=[("kt", us, "z")])
                P.op("pool", lambda e: e.memset(qt[64:128, :], 0.0), writes=[("qt", us, "z")])
            for r in range(2):
                P.op("pool", lambda e, r=r: e.dma_start(out=kt[0:rows, r * OWN:(r + 1) * OWN], in_=KT[h, :, r * OWN:(r + 1) * OWN]),
                     writes=[("kt", us, r)], dma=True)
            if kind == "A":
                qz = qz_t[us]
                P.op("pool", lambda e: e.dma_start(out=qz[0:64, 0, :], in_=QT[h, 0:64, :]), writes=[("qt", us, 0)], dma=True)
                P.op("pool", lambda e: e.dma_start(out=qz[64:128, 1, :], in_=QT[h, 64:128, :]), writes=[("qt", us, 1)], dma=True)
            else:
                P.op("pool", lambda e: e.dma_start(out=qt[0:rows, :], in_=QT[h, :, :]), writes=[("qt", us, 0)], dma=True)
            if kind == "A":
                c0, cw = h * 128, 128
            elif kind == "B":
                c0, cw = 512 + h * 64, 64
            else:
                c0, cw = 768 + h * 64, 64
            if kind == "B":
                P.op("pool", lambda e: e.memset(vt[:, :, 64:128], 1.0), writes=[("vt", us)])
            for q8 in range(8):
                P.op("pool", lambda e, q8=q8: e.dma_start(
                    out=vt[:, q8 * 8:(q8 + 1) * 8, 0:cw],
                    in_=VS[q8 * 1024:(q8 + 1) * 1024, c0:c0 + cw].rearrange("(c p) w -> p c w", p=128)),
                    writes=[("vt", us, q8)], reads=[("vt", us)], dma=True)
            return us

        def vkeys(us):
            return [("vt", us)] + [("vt", us, q8) for q8 in range(8)]

        def load_gate(row0, nrows, i, slot=None):
            gs = (ycount[0] % 2) if slot is None else slot
            g = gate_t[gs]
            P.op("sp", lambda e: e.dma_start(out=g[0:nrows, :], in_=GT[row0:row0 + nrows, i * 256:(i + 1) * 256]),
                 writes=[("gate", gs)], dma=True)
            return g, ("gate", gs)

        def store_y(ysrc_fn, row0, nrows, i, reads):
            ys = ys_t[ycount[0] % 2]
            yk = ("ys", ycount[0] % 2)
            ycount[0] += 1
            ysrc_fn(ys, yk)
            ystores.append(P.op("sp", lambda e: e.dma_start(out=YT[row0:row0 + nrows, i * 256:(i + 1) * 256], in_=ys[0:nrows, :]),
                                reads=[yk], dma=True))

        def attn_A(h, us):
            kt, qt, vt = kt_t[us], qt_t[us], vt_t[us]
            items = [(i, j, sub) for i in range(NB) for j in range(i + 1) for sub in (0, 1)]
            OP = [(PA, 0), (PA, 256)]
            LP = [(PB, 0), (PB, 256)]
            pending = []
            gate_of = {}
            lsb_t, rsb_t = Es_t[0], Es_t[1]
            x2r = []

            def QK(w):
                i, j, sub = items[w]
                xs = sub
                X = XS[xs]
                r0 = sub * 64
                diag = (j == i)
                for m in range(4):
                    kc = kcol(j, m)
                    P.op("pe", lambda e, m=m, kc=kc: e.matmul(
                        out=X[:, m * 256:(m + 1) * 256], lhsT=kt[:, kc:kc + 128], rhs=qz_t[us][:, sub, i * 256:(i + 1) * 256],
                        start=True, stop=not diag), reads=ktk(us) + qtk(us), writes=[("X", xs)])
                    if diag:
                        P.op("pe", lambda e, m=m: e.matmul(out=X[:, m * 256:(m + 1) * 256], lhsT=ident[:], rhs=mask_ab[:, m, :],
                                                           start=False, stop=True), writes=[("X", xs)])

            def EXP(w):
                i, j, sub = items[w]
                es = (w % 4)
                P.op("act", lambda e: e.activation(out=E_t[es][:], in_=XS[sub][:], func=AF.Exp), reads=[("X", sub)], writes=[("E", es)])

            def PV(w):
                i, j, sub = items[w]
                es = (w % 4)
                E = E_t[es]
                first, last = (j == 0), (j == i)
                if first and sub == 0:
                    gate_of[i] = load_gate(h * 128, 128, i, slot=i % 2)
                for m in range(4):
                    ch = vch(j, m)
                    ot_, oc_ = OP[sub]
                    P.op("pe", lambda e, m=m, ch=ch: e.matmul(out=ot_[:, oc_:oc_ + 256], lhsT=vt[:, ch, :], rhs=E[:, m * 256:(m + 1) * 256],
                                                              start=(first and m == 0 and sub == 0), stop=(last and m == 3),
                                                              skip_group_check=True),
                         reads=[("E", es)] + vkeys(us), writes=["bankPA"])
                    lt_, lc_ = LP[sub]
                    P.op("pe", lambda e, m=m: e.matmul(out=lt_[:, lc_:lc_ + 256], lhsT=onesb[:], rhs=E[:, m * 256:(m + 1) * 256],
                                                       start=(first and m == 0 and sub == 0), stop=(last and m == 3),
                                                       skip_group_check=True),
                         reads=[("E", es)], writes=["bankPB"])
                if last and sub == 1:
                    epi0(i)

            def epi0(i):
                for sub in (0, 1):
                    P.op("dve", lambda e, sub=sub: e.tensor_copy(out=o_t[sub][:], in_=OP[sub][0][:, OP[sub][1]:OP[sub][1] + 256]),
                         reads=["bankPA"], writes=[("o", sub)])
                P.op("dve", lambda e: e.tensor_copy(out=lsb_t[:, 0:512], in_=PB[:, 0:512]), reads=["bankPB"], writes=["lsb"])
                pending.append([2, lambda: epi1(i)])

            def epi1(i):
                P.op("act", lambda e: e.activation(out=rsb_t[:, 0:512], in_=lsb_t[:, 0:512], func=AF.Ln), reads=["lsb"], writes=["rsb"])
                P.op("act", lambda e: e.activation(out=rsb_t[:, 0:512], in_=rsb_t[:, 0:512], func=AF.Exp, scale=-1.0), reads=["rsb"], writes=["rsb"])
                for sub in (0, 1):
                    P.op("dve", lambda e, sub=sub: e.tensor_tensor(out=o_t[sub][:], in0=o_t[sub][:], in1=rsb_t[:, sub * 256:(sub + 1) * 256], op=ALU.mult),
                         reads=[("o", sub), "rsb"], writes=[("o", sub)])
                P.op("dve", lambda e: e.scalar_tensor_tensor(out=o_t[2][:], in0=o_t[1][:], scalar=neglam, in1=o_t[0][:],
                                                             op0=ALU.mult, op1=ALU.add), reads=[("o", 0), ("o", 1)], writes=[("o", 2)])
                P.op("dve", lambda e: e.tensor_tensor(out=sq_t[:], in0=o_t[2][:], in1=o_t[2][:], op=ALU.mult), reads=[("o", 2)], writes=["sq"])
                pending.append([2, lambda: epi2(i)])

            def epi2(i):
                g, gk = gate_of.pop(i)
                P.op("pe", lambda e: e.matmul(out=X2[:, 512:768], lhsT=onesf[:], rhs=sq_t[:], start=True, stop=True),
                     reads=["sq"], writes=["bankX2b"])
                x2r.append(P.op("act", lambda e: e.activation(out=rstd_t[:], in_=X2[:, 512:768], func=AF.Ln, bias=float(128.0 * SUBLN_EPS),
                                                              scale=1.0), reads=["bankX2b"], writes=["rstd2"]))
                P.op("act", lambda e: e.activation(out=rstd_t[:], in_=rstd_t[:], func=AF.Exp, scale=-0.5), reads=["rstd2"], writes=["rstd2"])
                P.op("dve", lambda e: e.scalar_tensor_tensor(out=o_t[2][:], in0=o_t[2][:], scalar=gsub, in1=rstd_t[:],
                                                             op0=ALU.mult, op1=ALU.mult), reads=[("o", 2), "rstd2"], writes=[("o", 2)])

                def fin(ys, yk):
                    P.op("dve", lambda e: e.tensor_tensor(out=ys[:], in0=o_t[2][:], in1=g[:], op=ALU.mult), reads=[("o", 2), gk], writes=[yk])
                store_y(fin, h * 128, 128, i, None)

            W = len(items)
            for w0 in range(min(2, W)):
                QK(w0)
            for w in range(W):
                EXP(w)
                PV(w)
                if w + 2 < W:
                    QK(w + 2)
                for pnd in list(pending):
                    pnd[0] -= 1
                    if pnd[0] <= 0:
                        pending.remove(pnd)
                        pnd[1]()
            while pending:
                pnd = pending.pop(0)
                pnd[1]()
            P.barrier("pe", x2r[-4:])

        def attn_B(h, us):
            kt, qt, vt = kt_t[us], qt_t[us], vt_t[us]
            items = [(i, j) for i in range(NB) for j in range(i + 1)]
            gate_of = {}

            def QK(w):
                i, j = items[w]
                xs = w % 3
                X = XS[xs]
                diag = (j == i)
                for m in range(4):
                    kc = kcol(j, m)
                    P.op("pe", lambda e, m=m, kc=kc: e.matmul(
                        out=X[:, m * 256:(m + 1) * 256], lhsT=kt[0:70, kc:kc + 128], rhs=qt[0:70, i * 256:(i + 1) * 256],
                        start=True, stop=not diag), reads=ktk(us) + qtk(us), writes=[("X", xs)])
                    if diag:
                        P.op("pe", lambda e, m=m: e.matmul(out=X[:, m * 256:(m + 1) * 256], lhsT=ident[:], rhs=mask_ab[:, m, :],
                                                           start=False, stop=True), writes=[("X", xs)])

            def EXP(w):
                es = w % 4
                P.op("act", lambda e: e.activation(out=E_t[es][:], in_=XS[w % 3][:], func=AF.Exp), reads=[("X", w % 3)], writes=[("E", es)])

            def PV(w):
                i, j = items[w]
                es = w % 4
                E = E_t[es]
                first, last = (j == 0), (j == i)
                PO = PA if i % 2 == 0 else PB
                pkey = "bankPA" if i % 2 == 0 else "bankPB"
                if first:
                    gate_of[i] = load_gate(512 + h * 64, 64, i, slot=i % 2)
                for m in range(4):
                    ch = vch(j, m)
                    P.op("pe", lambda e, m=m, ch=ch: e.matmul(out=PO[:, 0:256], lhsT=vt[:, ch, :], rhs=E[:, m * 256:(m + 1) * 256],
                                                              start=(first and m == 0), stop=(last and m == 3)),
                         reads=[("E", es)] + vkeys(us), writes=[pkey])
                if last:
                    g, gk = gate_of.pop(i)
                    P.op("dve", lambda e: e.reciprocal(out=r_t[0][0:64, :], in_=PO[64:128, 0:256]), reads=[pkey], writes=[("r", 0)])
                    P.op("dve", lambda e: e.tensor_tensor(out=o_t[0][0:64, :], in0=PO[0:64, 0:256], in1=r_t[0][0:64, :], op=ALU.mult),
                         reads=[pkey, ("r", 0)], writes=[("o", 0)])

                    def fin(ys, yk):
                        P.op("dve", lambda e: e.tensor_tensor(out=ys[0:64, :], in0=o_t[0][0:64, :], in1=g[0:64, :], op=ALU.mult),
                             reads=[("o", 0), gk], writes=[yk])
                    store_y(fin, 512 + h * 64, 64, i, None)

            W = len(items)
            for w0 in range(min(3, W)):
                QK(w0)
            for w in range(W):
                EXP(w)
                PV(w)
                if w + 3 < W:
                    QK(w + 3)

        def attn_C(h, us):
            kt, qt, vt = kt_t[us], qt_t[us], vt_t[us]
            items = [(i, j) for i in range(NB) for j in range(i, -1, -1)]
            W = len(items)

            def QK(w):
                i, j = items[w]
                xs = w % 3
                X = XS[xs]
                diag = (j == i)
                for m in range(4):
                    kc = kcol(j, m)
                    P.op("pe", lambda e, m=m, kc=kc: e.matmul(
                        out=X[:, m * 256:(m + 1) * 256], lhsT=kt[:, kc:kc + 128], rhs=qt[:, i * 256:(i + 1) * 256],
                        start=(m % 2 == 0), stop=True, skip_group_check=True), reads=ktk(us) + qtk(us), writes=[("X", xs)])
                    if diag:
                        P.op("pe", lambda e, m=m: e.matmul(out=X[:, m * 256:(m + 1) * 256], lhsT=ident[:], rhs=mask_c[:, m, :],
                                                           start=False, stop=True, skip_group_check=True), writes=[("X", xs)])

            def EA(w):
                xs = w % 3
                P.op("act", lambda e: e.activation(out=U_t[xs][:], in_=XS[xs][:], func=AF.Exp), reads=[("X", xs)], writes=[("U", xs)])
                P.op("act", lambda e: e.activation(out=Lp_t[xs][:], in_=U_t[xs][:], func=AF.Ln, bias=1.0, scale=1.0),
                     reads=[("U", xs)], writes=[("Lp", xs)])

            def TM(w):
                i, j = items[w]
                xs = w % 3
                X, Lp = XS[xs], Lp_t[xs]
                for m in range(4):
                    P.op("pe", lambda e, m=m: e.matmul(out=X[:, m * 256:(m + 1) * 256], lhsT=tneg[:], rhs=Lp[:, m * 256:(m + 1) * 256],
                                                       start=False, stop=(m == 3), skip_group_check=True),
                         reads=[("Lp", xs)], writes=[("X", xs)])
                    for m2 in range(m + 1, 4):
                        P.op("pe", lambda e, m=m, m2=m2: e.matmul(out=X[:, m * 256:(m + 1) * 256], lhsT=onesneg[:],
                                                                  rhs=Lp[:, m2 * 256:(m2 + 1) * 256], start=False, stop=True,
                                                                  skip_group_check=True), reads=[("Lp", xs)], writes=[("X", xs)])
                if j > 0:
                    for m in range(4):
                        P.op("pe", lambda e, m=m: e.matmul(out=PB[:, 0:256], lhsT=onesneg[:], rhs=Lp[:, m * 256:(m + 1) * 256],
                                                           start=(j == i and m == 0), stop=(m == 3), skip_group_check=True),
                             reads=[("Lp", xs)], writes=["bankPB"])
                    cs = w % 3
                    P.op("dve", lambda e: e.tensor_copy(out=cb_t[cs][:], in_=PB[:, 0:256]), reads=["bankPB"], writes=[("cb", cs)])

            def ADD(w):
                i, j = items[w]
                xs = w % 3
                Y = Y_t[xs]
                if j == i:
                    for hb_ in range(2):
                        P.op("dve", lambda e, hb_=hb_: e.tensor_copy(out=Y[:, hb_ * 512:(hb_ + 1) * 512], in_=XS[xs][:, hb_ * 512:(hb_ + 1) * 512]),
                             reads=[("X", xs)], writes=[("Y", xs)])
                else:
                    cs = (w - 1) % 3
                    for m in range(4):
                        P.op("dve", lambda e, m=m: e.tensor_tensor(out=Y[:, m * 256:(m + 1) * 256], in0=XS[xs][:, m * 256:(m + 1) * 256],
                                                                   in1=cb_t[cs][:], op=ALU.add),
                             reads=[("X", xs), ("cb", cs)], writes=[("Y", xs)])

            def EB(w):
                xs = w % 3
                es = w % 4
                P.op("act", lambda e: e.activation(out=E_t[es][:], in_=Y_t[xs][:], func=AF.Exp), reads=[("Y", xs)], writes=[("E", es)])

            def PV(w):
                i, j = items[w]
                es = w % 4
                E = E_t[es]
                first, last = (j == i), (j == 0)
                for m in range(4):
                    ch = vch(j, m)
                    P.op("pe", lambda e, m=m, ch=ch, E=E: e.matmul(out=PA[:, 0:256], lhsT=vt[:, ch, :], rhs=E[:, m * 256:(m + 1) * 256],
                                                                 start=(first and m == 0), stop=(last and m == 3)),
                         reads=[("E", es)] + vkeys(us), writes=["bankPA"])
                if last:
                    g, gk = load_gate(768 + h * 64, 64, i)

                    def fin(ys, yk):
                        P.op("dve", lambda e: e.tensor_tensor(out=ys[0:64, :], in0=PA[0:64, 0:256], in1=g[0:64, :], op=ALU.mult),
                             reads=["bankPA", gk], writes=[yk])
                    store_y(fin, 768 + h * 64, 64, i, None)

            for w0 in range(min(3, W)):
                QK(w0)
                EA(w0)
            TM(0)
            for w in range(W):
                ADD(w)
                EB(w)
                if w + 1 < W:
                    TM(w + 1)
                if w + 3 < W:
                    QK(w + 3)
                    EA(w + 3)
                PV(w)

        units = [("A", h) for h in range(4)] + [("B", h) for h in range(4)] + [("C", h) for h in range(4)]
        fns = {"A": attn_A, "B": attn_B, "C": attn_C}
        us_cur = load_unit(*units[0])
        for k_, (kind_, h_) in enumerate(units):
            us_next = load_unit(*units[k_ + 1]) if k_ + 1 < len(units) else None
            fns[kind_](h_, us_cur)
            us_cur = us_next
        P.full_barrier()
        A.reset(layer_mark)

        Wo = A.alloc("Wo", [128, 8, D], BF16)
        for c in range(8):
            P.op("pool", lambda e, c=c: e.dma_start(out=Wo[:, c, :], in_=pl["wo"][c * 128:(c + 1) * 128, :]), writes=[("Wo", c)], dma=True)
        g_o = A.alloc("g_o", [128, D], F32)
        b_o = A.alloc("b_o", [128, D], F32)
        P.op("sp", lambda e: e.dma_start(out=g_o[:], in_=pl["lng"][:, :]), writes=["g_o"], dma=True)
        P.op("sp", lambda e: e.dma_start(out=b_o[:], in_=pl["lnb"][:, :]), writes=["b_o"], dma=True)
        yt_t = [A.alloc("yt", [128, 8, 128], BF16) for _ in range(2)]
        hr_t = [A.alloc("hr", [128, D], F32) for _ in range(2)]
        z_t = [A.alloc("z", [128, D], F32) for _ in range(2)]
        ot_t = [A.alloc("ot", [128, D], F32) for _ in range(2)]
        st6 = A.alloc("st6b", [128, 12], F32)
        mv = A.alloc("mvb", [128, 4], F32)
        for T in range(OWN // 128):
            s_ = T % 2
            yt, hr, z, ot = yt_t[s_], hr_t[s_], z_t[s_], ot_t[s_]
            P.op("sp", lambda e, yt=yt, T=T: e.dma_start(out=yt[:], in_=YT[:, T * 128:(T + 1) * 128].rearrange("(c p) t -> p c t", p=128)),
                 writes=[("yt", s_)], dma=True)
            P.op("sp", lambda e, hr=hr, T=T: e.dma_start(out=hr[:], in_=HRES[T * 128:(T + 1) * 128, :]), writes=[("hr", s_)], dma=True)
            X = XS[s_]
            for half in range(2):
                for c in range(8):
                    P.op("pe", lambda e, X=X, half=half, c=c, yt=yt: e.matmul(
                        out=X[:, half * 512:(half + 1) * 512], lhsT=yt[:, c, :], rhs=Wo[:, c, half * 512:(half + 1) * 512],
                        start=(c == 0), stop=(c == 7)), reads=[("yt", s_), ("Wo", c)], writes=[("X", s_)])
            P.op("dve", lambda e, z=z, hr=hr, X=X: e.scalar_tensor_tensor(out=z[:], in0=hr[:], scalar=float(ALPHA), in1=X[:],
                                                                         op0=ALU.mult, op1=ALU.add),
                 reads=[("hr", s_), ("X", s_)], writes=[("z", s_)])
            layer_norm(z, ("z", s_), ot, ("ot", s_), g_o, b_o, ["g_o", "b_o"])
            o_ = P.op("pool", lambda e, ot=ot, T=T: e.dma_start(out=dest[T * 128:(T + 1) * 128, :], in_=ot[:]),
                      reads=[("ot", s_)], dma=True)
            if last_layer:
                out_stores.append(o_)
        P.full_barrier()

    P.final += out_stores
    P.emit()
    return nc


_BF = ml_dtypes.bfloat16


def _own_rows(g):
    return (np.arange(NB)[:, None] * 512 + g * 256 + np.arange(256)[None, :]).reshape(-1)


def _rope_tables(pos, scale):
    inv = ROPE_THETA ** (-np.arange(0, 16, 2, dtype=np.float32) / 16.0)
    ang = pos.astype(np.float32)[None, :] * inv[:, None].astype(np.float32)
    cos, sin = np.cos(ang), np.sin(ang)
    C = np.ones((128, len(pos)), np.float32)
    Sg = np.zeros((128, len(pos)), np.float32)
    for a in range(2):
        b0 = a * 64
        C[b0:b0 + 8] = cos
        C[b0 + 8:b0 + 16] = cos
        Sg[b0:b0 + 8] = -sin
        Sg[b0 + 8:b0 + 16] = sin
    return (C * scale).astype(np.float32), (Sg * scale).astype(np.float32)


def _swap_cols(cols512):
    idx = np.arange(512)
    out = idx.copy()
    for sh in range(8):
        b = sh * 64
        out[b:b + 8] = idx[b + 8:b + 16]
        out[b + 8:b + 16] = idx[b:b + 8]
    return cols512[out]


def _weight_layouts(w_in_l):
    o = {}
    pos = 0
    for name, n in (("Aq", 512), ("Ak", 512), ("Av", 512), ("Ag", 512), ("Bq", 256), ("Bk", 256), ("Bv", 256), ("Bf", 4),
                    ("Bg", 256), ("Cq", 256), ("Ck", 256), ("Cv", 256), ("Cg", 256)):
        o[name] = np.arange(pos, pos + n)
        pos += n
    kcols = np.concatenate([o["Ak"], _swap_cols(o["Ak"]), o["Bk"], o["Ck"], o["Bf"], o["Av"], o["Bv"], o["Cv"]])
    qcols = np.concatenate([o["Aq"], _swap_cols(o["Aq"]), o["Bq"], o["Cq"], o["Ag"], o["Bg"], o["Cg"]])
    assert len(kcols) == WKC and len(qcols) == WQC
    return np.ascontiguousarray(w_in_l[:, kcols]), np.ascontiguousarray(w_in_l[:, qcols])


def _masks(g):
    p = np.arange(128)[:, None, None]
    m = np.arange(4)[None, :, None]
    t = np.arange(256)[None, None, :]
    kpos = (m // 2) * 256 + (m % 2) * 128 + p
    qpos = g * 256 + t
    mab = np.where(kpos <= qpos, 0.0, MASKV).astype(np.float32)
    mc = np.where(kpos < qpos, 0.0, MASKV).astype(np.float32)
    return mab.astype(_BF), mc.astype(_BF)


_PROG_CACHE = {}


def _get_prog(layers):
    key = tuple(layers)
    if key not in _PROG_CACHE:
        lam_inits = {l: 0.8 - 0.6 * math.exp(-0.3 * l) for l in range(DEPTH)}
        _PROG_CACHE[key] = build_program(list(layers), lam_inits)
    return _PROG_CACHE[key]


def _consts(g):
    j = np.arange(128)[:, None]
    s_ = np.arange(128)[None, :]
    d = {}
    d["c_ident"] = np.eye(128, dtype=np.float32).astype(_BF)
    d["c_tneg"] = np.where(j >= s_, -1.0, 0.0).astype(np.float32).astype(_BF)
    d["c_onesneg"] = np.full((128, 128), -1.0, np.float32).astype(_BF)
    d["c_onesb"] = np.ones((128, 128), np.float32).astype(_BF)
    d["c_onesf"] = np.ones((128, 128), np.float32)
    gath = np.concatenate([_own_rows(0), _own_rows(1)])
    d["rk_cos"], d["rk_sin"] = _rope_tables(gath, 1.0)
    for p in range(2):
        gp = g if p == 0 else 1 - g
        d[f"c_mask_ab{p}"], d[f"c_mask_c{p}"] = _masks(gp)
        d[f"rq_cos{p}"], d[f"rq_sin{p}"] = _rope_tables(_own_rows(gp), 0.125)
        sel = np.zeros((4, 2), np.float32)
        sel[:, gp] = 1.0
        d[f"selg{p}"] = sel
    bl = np.zeros((128, 4), np.float32)
    for r in range(2):
        bl[:, 2 * r] = 1.0 if r == g else 0.0
        bl[:, 2 * r + 1] = 0.0 if r == g else 1.0
    d["blend"] = bl
    return d


def _layer_inputs(l, w_in, b_forget, lambda_q1, lambda_k1, lambda_q2, lambda_k2, subln_g, w_out, ln_g, ln_b):
    wk, wq = _weight_layouts(np.asarray(w_in[l], np.float32))
    rep = lambda v: np.ascontiguousarray(np.broadcast_to(np.asarray(v, np.float32)[None, :], (128, len(v))))
    lamv = np.stack([rep(lambda_q1[l]), rep(lambda_k1[l]), rep(lambda_q2[l]), rep(lambda_k2[l])], axis=1)
    return {
        f"wk{l}": wk, f"wq{l}": wq, f"wo{l}": np.ascontiguousarray(np.asarray(w_out[l], np.float32)),
        f"lng{l}": rep(ln_g[l]), f"lnb{l}": rep(ln_b[l]),
        f"bfg{l}": np.asarray(b_forget[l], np.float32).reshape(4, 1),
        f"lamv{l}": np.ascontiguousarray(lamv), f"subg{l}": np.asarray(subln_g[l], np.float32).reshape(128, 1),
    }


LAUNCH_PLAN = [[0, 1]]


def kernel(x, ln_in_g, ln_in_b, w_in, b_forget, lambda_q1, lambda_k1, lambda_q2, lambda_k2, subln_g, w_out, ln_g, ln_b):
    x = np.asarray(x, np.float32)
    B = x.shape[0]
    rep = lambda v: np.ascontiguousarray(np.broadcast_to(np.asarray(v, np.float32)[None, :], (128, len(v))))
    consts = [_consts(g) for g in range(2)]
    gath = np.concatenate([_own_rows(0), _own_rows(1)])
    h = x
    for layers in LAUNCH_PLAN:
        nc = _get_prog(layers)
        lay = {}
        for l in layers:
            lay.update(_layer_inputs(l, w_in, b_forget, lambda_q1, lambda_k1, lambda_q2, lambda_k2, subln_g, w_out, ln_g, ln_b))
        in_maps = []
        for core in range(8):
            b, g = core // 2, core % 2
            m = dict(consts[g])
            m.update(lay)
            m["hin_full"] = np.ascontiguousarray(h[b][gath])
            m["hin_own"] = np.ascontiguousarray(h[b][_own_rows(g)])
            m["hin_oth"] = np.ascontiguousarray(h[b][_own_rows(1 - g)])
            m["lng_in"] = rep(ln_in_g)
            m["lnb_in"] = rep(ln_in_b)
            in_maps.append(m)
        res = run_bass_kernel_spmd(nc, in_maps, core_ids=list(range(8)))
        hn = np.empty_like(x)
        for core in range(8):
            b, g = core // 2, core % 2
            hn[b][_own_rows(g)] = np.asarray(res.results[core]["out"], np.float32)
        h = hn
    return h
```

```python
import contextlib
import math

import numpy as np
import ml_dtypes

import concourse.bass as bass
import concourse.mybir as mybir
from concourse.bass_utils import run_bass_kernel_spmd

F32 = mybir.dt.float32
BF16 = mybir.dt.bfloat16
AF = mybir.ActivationFunctionType
ALU = mybir.AluOpType

S = 8192
D = 1024
OWN = 4096
NB = 16
DEPTH = 2
LN_EPS = 1e-5
SUBLN_EPS = 1e-5
ALPHA = (2 * DEPTH) ** 0.25
ROPE_THETA = 500000.0
MASKV = -30000.0
WKC = 2564
WQC = 2560

ENGS = ("pe", "act", "dve", "pool", "sp")
DEBUG = False
DUMP = False


class Op:
    __slots__ = ("eng", "fn", "deps", "marked", "sig", "dma", "dsem", "dval", "cc")

    def __init__(self, eng, fn, dma):
        self.eng = eng
        self.fn = fn
        self.deps = []
        self.marked = False
        self.sig = None
        self.dma = dma
        self.dsem = None
        self.dval = None
        self.cc = False


class _Rec:
    def __init__(self):
        self.call = None

    def __getattr__(self, name):
        def f(*a, **k):
            self.call = (name, a, k)
            return None
        return f

    def replay(self, eng):
        name, a, k = self.call
        return getattr(eng, name)(*a, **k)


class Prog:
    NDSEM = 24

    def __init__(self, nc):
        self.nc = nc
        self.ops = {e: [] for e in ENGS}
        self.last_writer = {}
        self.readers = {}
        self.dma_ops = {e: [] for e in ENGS}
        self.final = []

    def op(self, eng, fn, reads=(), writes=(), dma=False, extra=()):
        if fn is not None:
            rec = _Rec()
            fn(rec)
            assert rec.call is not None
            fn = rec.replay
        o = Op(eng, fn, dma)
        deps = []
        seen = set()

        def add(d):
            if d is None or id(d) in seen:
                return
            seen.add(id(d))
            deps.append(d)

        for b in reads:
            add(self.last_writer.get(b))
        for b in writes:
            add(self.last_writer.get(b))
            for r in self.readers.get(b, ()):
                add(r)
        for d in extra:
            add(d)
        if dma:
            lst = self.dma_ops[eng]
            n = len(lst)
            if n >= self.NDSEM:
                add(lst[n - self.NDSEM])
            o.dsem = n % self.NDSEM
            o.dval = 16 * (n // self.NDSEM + 1)
            lst.append(o)
        for d in deps:
            if d.dma:
                o.deps.append(d)
            elif d.eng == "pe" and eng == "pe" and not dma and fn is not None:
                continue
            else:
                d.marked = True
                o.deps.append(d)
        for b in reads:
            self.readers.setdefault(b, []).append(o)
        for b in writes:
            self.last_writer[b] = o
            self.readers[b] = []
        self.ops[eng].append(o)
        return o

    def barrier(self, eng, deps):
        return self.op(eng, None, extra=deps)

    def full_barrier(self):
        tails = []
        for e in ENGS:
            for o in reversed(self.ops[e]):
                if o.fn is not None and not o.dma:
                    tails.append(o)
                    break
        dmas = []
        for e in ENGS:
            dmas += self.dma_ops[e][-self.NDSEM:]
        for e in ENGS:
            self.barrier(e, tails + dmas)
        self.last_writer = {}
        self.readers = {}

    def emit(self):
        nc = self.nc
        with contextlib.ExitStack() as st:
            csem = {e: st.enter_context(nc.semaphore(f"c_{e}")) for e in ENGS}
            dsem = {
                e: [st.enter_context(nc.semaphore(f"d_{e}_{i}")) for i in range(self.NDSEM)]
                for e in ENGS if self.dma_ops[e]
            }
            for e in ENGS:
                c = 0
                for o in self.ops[e]:
                    if o.marked and not o.dma:
                        c += 1
                        o.sig = c
            block = st.enter_context(nc.Block())

            def run(e, engobj):
                seen = {}

                def wait(d):
                    if d.dma:
                        key = ("d", d.eng, d.dsem)
                        sem, val = dsem[d.eng][d.dsem], d.dval
                    else:
                        key = ("c", d.eng)
                        sem, val = csem[d.eng], d.sig
                    if seen.get(key, 0) >= val:
                        return
                    seen[key] = val
                    engobj.wait_ge(sem, val)

                for o in self.ops[e]:
                    for d in o.deps:
                        wait(d)
                    if o.fn is None:
                        continue
                    ins = o.fn(engobj)
                    if o.dma:
                        ins.then_inc(dsem[e][o.dsem], 16)
                    elif o.marked:
                        ins.then_inc(csem[e], 1)
                if e == "sp":
                    for d in self.final:
                        wait(d)

            @block.tensor
            def _(eng):
                run("pe", eng)

            @block.scalar
            def _(eng):
                run("act", eng)

            @block.vector
            def _(eng):
                run("dve", eng)

            @block.gpsimd
            def _(eng):
                run("pool", eng)

            @block.sync
            def _(eng):
                run("sp", eng)


SB_BASE = 16512
SB_TOP = 229376 - 1024


class Arena:
    def __init__(self, nc):
        self.nc = nc
        self.ptr = SB_BASE
        self.n = 0

    def mark(self):
        return self.ptr

    def reset(self, p):
        self.ptr = p

    def alloc(self, name, shape, dt):
        esz = 4 if dt == F32 else 2
        nbytes = esz
        for s in shape[1:]:
            nbytes *= s
        nbytes = (nbytes + 63) // 64 * 64
        off = self.ptr
        self.ptr += nbytes
        assert self.ptr <= SB_TOP, (name, self.ptr)
        self.n += 1
        return self.nc.alloc_sbuf_tensor_at(f"{name}_{self.n}", list(shape), dt, offset=off)


def build_program(layers, lam_inits):
    nc = bass.Bass("TRN2", target_bir_lowering=False)
    P = Prog(nc)
    A = Arena(nc)

    def din(name, shape, dt=F32):
        return nc.dram_tensor(name, list(shape), dt, kind="ExternalInput").ap()

    def dscr(name, shape, dt):
        if DEBUG:
            return nc.dram_tensor(name, list(shape), dt, kind="ExternalOutput").ap()
        return nc.dram_tensor(name, list(shape), dt).ap()

    L0 = layers[0]
    first_is_l0 = (L0 == 0)
    hin_full = din("hin_full", [S, D])
    hin_own = din("hin_own", [OWN, D])
    hin_oth = din("hin_oth", [OWN, D])
    blend = din("blend", [128, 4])
    lng_in = din("lng_in", [128, D])
    lnb_in = din("lnb_in", [128, D])
    rk_cos = din("rk_cos", [128, S])
    rk_sin = din("rk_sin", [128, S])
    rq_cos_p = [din(f"rq_cos{p}", [128, OWN]) for p in range(2)]
    rq_sin_p = [din(f"rq_sin{p}", [128, OWN]) for p in range(2)]
    selg_p = [din(f"selg{p}", [4, 2]) for p in range(2)]
    c_ident = din("c_ident", [128, 128], BF16)
    c_tneg = din("c_tneg", [128, 128], BF16)
    c_onesneg = din("c_onesneg", [128, 128], BF16)
    c_onesb = din("c_onesb", [128, 128], BF16)
    c_onesf = din("c_onesf", [128, 128])
    c_mask_ab_p = [din(f"c_mask_ab{p}", [128, 4, 256], BF16) for p in range(2)]
    c_mask_c_p = [din(f"c_mask_c{p}", [128, 4, 256], BF16) for p in range(2)]
    per_layer = {}
    for l in layers:
        per_layer[l] = dict(
            wk=din(f"wk{l}", [D, WKC]), wq=din(f"wq{l}", [D, WQC]), wo=din(f"wo{l}", [D, D]),
            lng=din(f"lng{l}", [128, D]), lnb=din(f"lnb{l}", [128, D]),
            bfg=din(f"bfg{l}", [4, 1]), lamv=din(f"lamv{l}", [128, 4, 64]), subg=din(f"subg{l}", [128, 1]),
        )
    out = nc.dram_tensor("out", [OWN, D], F32, kind="ExternalOutput").ap()

    KTA = dscr("KTA", [4, 128, S], BF16)
    KTB = dscr("KTB", [4, 70, S], BF16)
    KTC = dscr("KTC", [4, 64, S], BF16)
    VS = dscr("VS", [S, 1024], BF16)
    QTA = dscr("QTA", [4, 128, OWN], BF16)
    QTB = dscr("QTB", [4, 70, OWN], BF16)
    QTC = dscr("QTC", [4, 64, OWN], BF16)
    GT = dscr("GT", [1024, OWN], F32)
    HRES = dscr("HRES", [OWN, D], F32)
    YT = dscr("YT", [1024, OWN], BF16)
    NLF = dscr("NLF", [4, S], F32)
    CN = dscr("CN", [4, S], F32)
    HP = [dscr(f"HP{p}", [OWN, D], F32) for p in range(2)]

    if DEBUG == "C0":
        DBG_U = nc.dram_tensor("DBG_U", [128, 1024], F32, kind="ExternalOutput").ap()
        DBG_L = nc.dram_tensor("DBG_L", [128, 1024], BF16, kind="ExternalOutput").ap()
        DBG_X = nc.dram_tensor("DBG_X", [128, 1024], F32, kind="ExternalOutput").ap()
    X0 = nc.alloc_psum_tensor("X0", [128, 1024], F32)
    X1 = nc.alloc_psum_tensor("X1", [128, 1024], F32)
    X2 = nc.alloc_psum_tensor("X2", [128, 1024], F32)
    PA = nc.alloc_psum_tensor("PA", [128, 512], F32)
    PB = nc.alloc_psum_tensor("PB", [128, 512], F32)
    XS = [X0, X1, X2]

    ident = A.alloc("ident", [128, 128], BF16)
    tneg = A.alloc("tneg", [128, 128], BF16)
    onesneg = A.alloc("onesneg", [128, 128], BF16)
    onesb = A.alloc("onesb", [128, 128], BF16)
    onesf = A.alloc("onesf", [128, 128], F32)
    mask_ab = A.alloc("mask_ab", [128, 4, 256], BF16)
    mask_c = A.alloc("mask_c", [128, 4, 256], BF16)
    selg_sb = A.alloc("selg", [4, 2], F32)
    lam_t = A.alloc("lam", [128, 8], F32)
    subg_t = A.alloc("subg", [128, 2], F32)
    negb_t = A.alloc("negb", [4, 2], F32)
    for dst, src, k in ((ident, c_ident, "ident"), (tneg, c_tneg, "tneg"), (onesneg, c_onesneg, "onesneg"),
                        (onesb, c_onesb, "onesb"), (onesf, c_onesf, "onesf")):
        P.op("sp", lambda e, dst=dst, src=src: e.dma_start(out=dst[:], in_=src[:, :]), writes=[k], dma=True)
    blend_sb = A.alloc("blend", [128, 4], F32)
    P.op("sp", lambda e: e.dma_start(out=blend_sb[:], in_=blend[:, :]), writes=["blend"], dma=True)
    persist_mark = A.mark()
    CONST_KEYS = ["ident", "tneg", "onesneg", "onesb", "onesf", "mask_ab", "mask_c", "selg"]

    def reload_const_keys():
        pass

    out_stores = []

    fused = len(layers) > 1
    schedule = []
    for li, l in enumerate(layers):
        npass = 2 if (fused and li < len(layers) - 1) else 1
        for p_ in range(npass):
            schedule.append((li, l, p_))
    for (li, l, pss) in schedule:
        pl = per_layer[l]
        lam_init = lam_inits[l]
        do_ln_in = (l == 0)
        last_layer = (li == len(layers) - 1)
        do_k = (pss == 0)
        from_hp = (li > 0)
        src_own = (HP[0] if from_hp else (hin_own if pss == 0 else hin_oth))
        dest = out if last_layer else HP[pss]
        rq_cos, rq_sin = rq_cos_p[pss], rq_sin_p[pss]
        A.reset(persist_mark)
        P.op("sp", lambda e: e.dma_start(out=mask_ab[:], in_=c_mask_ab_p[pss][:, :, :]), writes=["mask_ab"], dma=True)
        P.op("sp", lambda e: e.dma_start(out=mask_c[:], in_=c_mask_c_p[pss][:, :, :]), writes=["mask_c"], dma=True)
        P.op("sp", lambda e: e.dma_start(out=selg_sb[:], in_=selg_p[pss][:, :]), writes=["selg"], dma=True)

        lamv_sb = A.alloc("lamv", [128, 4, 64], F32)
        lprod = A.alloc("lprod", [128, 2, 64], F32)
        P.op("sp", lambda e: e.dma_start(out=lamv_sb[:], in_=pl["lamv"][:, :, :]), writes=["lamv"], dma=True)
        P.op("sp", lambda e: e.dma_start(out=subg_t[:, 0:1], in_=pl["subg"][:, :]), writes=["subg0"], dma=True)
        P.op("sp", lambda e: e.dma_start(out=negb_t[:, 0:1], in_=pl["bfg"][:, :]), writes=["negb0"], dma=True)
        P.op("dve", lambda e: e.tensor_tensor(out=lprod[:, 0, :], in0=lamv_sb[:, 0, :], in1=lamv_sb[:, 1, :], op=ALU.mult),
             reads=["lamv"], writes=["lprod"])
        P.op("dve", lambda e: e.tensor_tensor(out=lprod[:, 1, :], in0=lamv_sb[:, 2, :], in1=lamv_sb[:, 3, :], op=ALU.mult),
             reads=["lamv", "lprod"], writes=["lprod"])
        P.op("dve", lambda e: e.reduce_sum(out=lam_t[:, 0:1], in_=lprod[:, 0, :], axis=mybir.AxisListType.X),
             reads=["lprod"], writes=["lam01"])
        P.op("dve", lambda e: e.reduce_sum(out=lam_t[:, 1:2], in_=lprod[:, 1, :], axis=mybir.AxisListType.X),
             reads=["lprod", "lam01"], writes=["lam01"])
        P.op("act", lambda e: e.activation(out=lam_t[:, 2:4], in_=lam_t[:, 0:2], func=AF.Exp), reads=["lam01"], writes=["lam23"])
        P.op("dve", lambda e: e.scalar_tensor_tensor(out=lam_t[:, 4:5], in0=lam_t[:, 3:4], scalar=-float(lam_init),
                                                     in1=lam_t[:, 2:3], op0=ALU.add, op1=ALU.subtract),
             reads=["lam23"], writes=["neglam"])
        P.op("dve", lambda e: e.tensor_scalar(out=subg_t[:, 1:2], in0=subg_t[:, 0:1], scalar1=float((1.0 - lam_init) * math.sqrt(128.0)),
                                              scalar2=None, op0=ALU.mult), reads=["subg0"], writes=["gsub"])
        P.op("dve", lambda e: e.tensor_scalar(out=negb_t[:, 1:2], in0=negb_t[:, 0:1], scalar1=-1.0, scalar2=None, op0=ALU.mult),
             reads=["negb0"], writes=["negb"])
        neglam = lam_t[:, 4:5]
        gsub = subg_t[:, 1:2]
        negb = negb_t[:, 1:2]
        layer_mark = A.mark()

        Wk = A.alloc("Wk", [128, 8, WKC], BF16)
        Wq = A.alloc("Wq", [128, 8, WQC], BF16)
        for c in range(8):
            P.op("pool", lambda e, c=c: e.dma_start(out=Wk[:, c, :], in_=pl["wk"][c * 128:(c + 1) * 128, :]), writes=[("Wk", c)], dma=True)
        for c in range(8):
            P.op("pool", lambda e, c=c: e.dma_start(out=Wq[:, c, :], in_=pl["wq"][c * 128:(c + 1) * 128, :]), writes=[("Wq", c)], dma=True)
        WK_KEYS = [("Wk", c) for c in range(8)]
        WQ_KEYS = [("Wq", c) for c in range(8)]
        if do_ln_in:
            g_in = A.alloc("g_in", [128, D], F32)
            b_in = A.alloc("b_in", [128, D], F32)
            P.op("sp", lambda e: e.dma_start(out=g_in[:], in_=lng_in[:, :]), writes=["g_in"], dma=True)
            P.op("sp", lambda e: e.dma_start(out=b_in[:], in_=lnb_in[:, :]), writes=["b_in"], dma=True)
        xin_t = [A.alloc("xin", [128, D], F32) for _ in range(2)]
        hf_t = [A.alloc("hf", [128, D], F32) for _ in range(2)]
        hb_t = [A.alloc("hb", [128, D], BF16) for _ in range(2)]
        hT_t = [A.alloc("hT", [128, 8, 512], BF16) for _ in range(2)]
        rc_t = [A.alloc("rc", [128, 512], F32) for _ in range(2)]
        rs_t = [A.alloc("rs", [128, 512], F32) for _ in range(2)]
        tmp_t = [A.alloc("tmp", [128, 512], F32) for _ in range(2)]
        stb_t = [A.alloc("stb", [128, 512], BF16) for _ in range(4)]
        stf_t = [A.alloc("stf", [128, 512], F32) for _ in range(2)]
        st6 = A.alloc("st6", [128, 12], F32)
        mv = A.alloc("mv", [128, 4], F32)
        accs = [(X0, 0), (X0, 512), (X1, 0), (X1, 512), (X2, 0), (X2, 512)]
        cnt = dict(tile=0, acc=0, stb=0, stf=0, tmp=0, grp=0)
        stores1 = []

        def layer_norm(src, skey, dst, dkey, gt, bt, gkeys):
            for hh in range(2):
                P.op("dve", lambda e, hh=hh: e.bn_stats(out=st6[:, hh * 6:(hh + 1) * 6], in_=src[:, hh * 512:(hh + 1) * 512]),
                     reads=[skey], writes=[("st6", hh)])
            P.op("dve", lambda e: e.bn_aggr(out=mv[:, 0:2], in_=st6[:, 0:12]), reads=[("st6", 0), ("st6", 1)], writes=["mv"])
            P.op("act", lambda e: e.activation(out=mv[:, 3:4], in_=mv[:, 1:2], func=AF.Ln, bias=float(LN_EPS), scale=1.0),
                 reads=["mv"], writes=["lnv"])
            P.op("act", lambda e: e.activation(out=mv[:, 2:3], in_=mv[:, 3:4], func=AF.Exp, scale=-0.5), reads=["lnv"], writes=["rstd"])
            P.op("dve", lambda e: e.tensor_scalar(out=dst[:], in0=src[:], scalar1=mv[:, 0:1], scalar2=mv[:, 2:3],
                                                  op0=ALU.subtract, op1=ALU.mult), reads=[skey, "mv", "rstd"], writes=[dkey])
            P.op("dve", lambda e: e.tensor_tensor(out=dst[:], in0=dst[:], in1=gt[:], op=ALU.mult), reads=[dkey, gkeys[0]], writes=[dkey])
            P.op("dve", lambda e: e.tensor_tensor(out=dst[:], in0=dst[:], in1=bt[:], op=ALU.add), reads=[dkey, gkeys[1]], writes=[dkey])

        def next_acc():
            a = accs[cnt["acc"] % len(accs)]
            key = ("acc", cnt["acc"] % len(accs))
            cnt["acc"] += 1
            return a[0], a[1], key

        def next_stb():
            i = cnt["stb"] % 4
            cnt["stb"] += 1
            return stb_t[i], ("stb", i)

        def next_stf():
            i = cnt["stf"] % 2
            cnt["stf"] += 1
            return stf_t[i], ("stf", i)

        def proj_pass(src, ngroups, mode):
            W = Wk if mode == "k" else Wq
            WKEYS = WK_KEYS if mode == "k" else WQ_KEYS
            rcos = rk_cos if mode == "k" else rq_cos
            rsin = rk_sin if mode == "k" else rq_sin
            gslot = {}

            def prep_begin(G):
                gs = cnt["grp"] % 2
                cnt["grp"] += 1
                gslot[G] = gs
                rc, rs_ = rc_t[gs], rs_t[gs]
                P.op("sp", lambda e: e.dma_start(out=rc[:], in_=rcos[:, G * 512:(G + 1) * 512]), writes=[("rc", gs)], dma=True)
                P.op("sp", lambda e: e.dma_start(out=rs_[:], in_=rsin[:, G * 512:(G + 1) * 512]), writes=[("rs", gs)], dma=True)

            tstate = {}

            def prep_load(G, tt):
                gs = gslot[G]
                ts_ = cnt["tile"] % 2
                cnt["tile"] += 1
                row0 = G * 512 + tt * 128
                xin = xin_t[ts_]
                tstate[(G, tt)] = ts_
                if src == "blend":
                    idx = row0 % OWN
                    t1 = hf_t[ts_]
                    P.op("sp", lambda e: e.dma_start(out=xin[:], in_=HP[0][idx:idx + 128, :]), writes=[("xin", ts_)], dma=True)
                    P.op("sp", lambda e: e.dma_start(out=t1[:], in_=HP[1][idx:idx + 128, :]), writes=[("hf", ts_)], dma=True)
                else:
                    P.op("sp", lambda e: e.dma_start(out=xin[:], in_=src[row0:row0 + 128, :]), writes=[("xin", ts_)], dma=True)

            def prep_norm(G, tt):
                ts_ = tstate[(G, tt)]
                row0 = G * 512 + tt * 128
                xin = xin_t[ts_]
                if src == "blend":
                    rr = row0 // OWN
                    t1 = hf_t[ts_]
                    P.op("dve", lambda e: e.tensor_scalar(out=xin[:], in0=xin[:], scalar1=blend_sb[:, 2 * rr:2 * rr + 1], scalar2=None,
                                                          op0=ALU.mult), reads=[("xin", ts_)], writes=[("xin", ts_)])
                    P.op("dve", lambda e: e.scalar_tensor_tensor(out=xin[:], in0=t1[:], scalar=blend_sb[:, 2 * rr + 1:2 * rr + 2], in1=xin[:],
                                                                 op0=ALU.mult, op1=ALU.add),
                         reads=[("xin", ts_), ("hf", ts_)], writes=[("xin", ts_)])
                if do_ln_in:
                    hf = hf_t[ts_]
                    hfk = ("hf", ts_)
                    layer_norm(xin, ("xin", ts_), hf, hfk, g_in, b_in, ["g_in", "b_in"])
                else:
                    hf, hfk = xin, ("xin", ts_)
                if mode == "q":
                    stores1.append(P.op("pool", lambda e: e.dma_start(out=HRES[row0:row0 + 128, :], in_=hf[:]), reads=[hfk], dma=True))
                hb = hb_t[ts_]
                P.op("act", lambda e: e.copy(out=hb[:], in_=hf[:]), reads=[hfk], writes=[("hb", ts_)])

            def prep_tr(G, tt):
                gs = gslot[G]
                hT = hT_t[gs]
                ts_ = tstate[(G, tt)]
                hb = hb_t[ts_]
                for half in range(2):
                    ps, o0, akey = next_acc()
                    for c4 in range(4):
                        c = half * 4 + c4
                        P.op("pe", lambda e, c4=c4, c=c: e.matmul(
                            out=ps[:, o0 + c4 * 128:o0 + (c4 + 1) * 128], lhsT=hb[:, c * 128:(c + 1) * 128], rhs=ident[:],
                            start=True, stop=True), reads=[("hb", ts_), "ident"], writes=[akey])
                    P.op("dve", lambda e: e.tensor_copy(
                        out=hT[:, half * 4:(half + 1) * 4, tt * 128:(tt + 1) * 128],
                        in_=ps[:, o0:o0 + 512].rearrange("p (c t) -> p c t", c=4)),
                        reads=[akey], writes=[("hT", gs, tt, half)])

            def chunks(G):
                gs = gslot[G]
                hT = hT_t[gs]
                hTkeys = [("hT", gs, tt_, hf_) for tt_ in range(4) for hf_ in range(2)]
                rc, rs_ = rc_t[gs], rs_t[gs]

                def fm_chunk(col0, M):
                    ps, o0, akey = next_acc()
                    for c in range(8):
                        P.op("pe", lambda e, c=c: e.matmul(
                            out=ps[0:M, o0:o0 + 512], lhsT=W[:, c, col0:col0 + M], rhs=hT[:, c, :], start=(c == 0), stop=(c == 7)),
                            reads=hTkeys + [WKEYS[c]], writes=[akey])
                    return ps, o0, akey

                tok0 = G * 512
                for i in range(4):
                    p1, o1, k1 = fm_chunk(i * 128, 128)
                    p2, o2, k2 = fm_chunk(512 + i * 128, 128)
                    t1 = tmp_t[0]
                    t2 = tmp_t[1]
                    P.op("dve", lambda e: e.tensor_tensor(out=t1[:], in0=p1[:, o1:o1 + 512], in1=rc[:], op=ALU.mult),
                         reads=[k1, ("rc", gs)], writes=[("tmp", 0)])
                    P.op("dve", lambda e: e.tensor_tensor(out=t2[:], in0=p2[:, o2:o2 + 512], in1=rs_[:], op=ALU.mult),
                         reads=[k2, ("rs", gs)], writes=[("tmp", 1)])
                    sb_, sk = next_stb()
                    P.op("dve", lambda e: e.tensor_tensor(out=sb_[:], in0=t1[:], in1=t2[:], op=ALU.add),
                         reads=[("tmp", 0), ("tmp", 1)], writes=[sk])
                    dstT = KTA if mode == "k" else QTA
                    stores1.append(P.op("pool", lambda e: e.dma_start(out=dstT[i, :, tok0:tok0 + 512], in_=sb_[:]), reads=[sk], dma=True))
                    yield
                for kind, cbase, dstT in (("B", 1024, KTB if mode == "k" else QTB), ("C", 1280, KTC if mode == "k" else QTC)):
                    for h in range(4):
                        ps, o0, akey = fm_chunk(cbase + h * 64, 64)
                        sb_, sk = next_stb()
                        if mode == "k":
                            P.op("act", lambda e: e.copy(out=sb_[0:64, :], in_=ps[0:64, o0:o0 + 512]), reads=[akey], writes=[sk])
                        else:
                            P.op("act", lambda e: e.mul(out=sb_[0:64, :], in_=ps[0:64, o0:o0 + 512], mul=0.125), reads=[akey], writes=[sk])
                        stores1.append(P.op("pool", lambda e: e.dma_start(out=dstT[h, 0:64, tok0:tok0 + 512], in_=sb_[0:64, :]),
                                            reads=[sk], dma=True))
                        yield
                if mode == "k":
                    ps, o0, akey = fm_chunk(1536, 4)
                    sf, sfk = next_stf()
                    P.op("act", lambda e: e.activation(out=sf[0:4, :], in_=ps[0:4, o0:o0 + 512], func=AF.Exp, bias=negb, scale=-1.0),
                         reads=[akey, "negb"], writes=[sfk])
                    P.op("act", lambda e: e.activation(out=sf[0:4, :], in_=sf[0:4, :], func=AF.Ln, bias=1.0, scale=1.0),
                         reads=[sfk], writes=[sfk])
                    r = G // 8
                    for bb in range(2):
                        i_blk = (G % 8) * 2 + bb
                        t0 = i_blk * 512 + r * 256
                        stores1.append(P.op("pool", lambda e, bb=bb, t0=t0: e.dma_start(
                            out=NLF[:, t0:t0 + 256], in_=sf[0:4, bb * 256:(bb + 1) * 256]), reads=[sfk], dma=True))
                    yield
                    for tt in range(4):
                        for half in range(2):
                            ps, o0, akey = next_acc()
                            for c in range(8):
                                P.op("pe", lambda e, c=c: e.matmul(
                                    out=ps[:, o0:o0 + 512], lhsT=hT[:, c, tt * 128:(tt + 1) * 128],
                                    rhs=W[:, c, 1540 + half * 512:1540 + (half + 1) * 512], start=(c == 0), stop=(c == 7)),
                                    reads=hTkeys + [WKEYS[c]], writes=[akey])
                            sb_, sk = next_stb()
                            P.op("act", lambda e: e.copy(out=sb_[:], in_=ps[:, o0:o0 + 512]), reads=[akey], writes=[sk])
                            r0 = tok0 + tt * 128
                            stores1.append(P.op("pool", lambda e: e.dma_start(
                                out=VS[r0:r0 + 128, half * 512:(half + 1) * 512], in_=sb_[:]), reads=[sk], dma=True))
                            yield
                else:
                    for gch in range(8):
                        ps, o0, akey = fm_chunk(1536 + gch * 128, 128)
                        sf, sfk = next_stf()
                        P.op("act", lambda e: e.activation(out=sf[:], in_=ps[:, o0:o0 + 512], func=AF.Silu), reads=[akey], writes=[sfk])
                        stores1.append(P.op("pool", lambda e: e.dma_start(
                            out=GT[gch * 128:(gch + 1) * 128, tok0:tok0 + 512], in_=sf[:]), reads=[sfk], dma=True))
                        yield

            prep_begin(0)
            for tt in range(4):
                prep_load(0, tt)
                prep_norm(0, tt)
                prep_tr(0, tt)
            ev_load = {0: 0, 3: 1, 8: 2, 13: 3}
            ev_norm = {1: 0, 6: 1, 11: 2, 16: 3}
            ev_tr = {5: 0, 10: 1, 15: 2, 19: 3}
            for G in range(ngroups):
                more = (G + 1 < ngroups)
                done = set()

                def fire(idx):
                    if not more:
                        return
                    for ev, fn, tag in ((ev_load, prep_load, "l"), (ev_norm, prep_norm, "n"), (ev_tr, prep_tr, "t")):
                        if idx in ev and (tag, ev[idx]) not in done:
                            done.add((tag, ev[idx]))
                            fn(G + 1, ev[idx])
                if more:
                    prep_begin(G + 1)
                fire(0)
                idx = 0
                for _ in chunks(G):
                    idx += 1
                    fire(idx)
                for k in range(idx + 1, 24):
                    fire(k)

        if do_k:
            proj_pass("blend" if from_hp else hin_full, 16, "k")
        proj_pass(src_own, 8, "q")
        P.full_barrier()
        A.reset(layer_mark)

        nlf = A.alloc("nlf", [4, S], F32)
        cn = A.alloc("cn", [4, S], F32)
        pk = [A.alloc("pk", [4, S], BF16) for _ in range(3)]
        cq = [A.alloc("cq", [4, OWN], F32) for _ in range(2)]
        pq = [A.alloc("pq", [4, OWN], BF16) for _ in range(3)]
        ones_r = A.alloc("ones_r", [4, S], BF16)
        P.op("pool", lambda e: e.memset(ones_r[:], 1.0), writes=["ones_r"])
        if do_k:
            P.op("sp", lambda e: e.dma_start(out=nlf[:], in_=NLF[:, :]), writes=["nlf"], dma=True)
            P.op("dve", lambda e: e.tensor_tensor_scan(out=cn[:], data0=nlf[:], data1=nlf[:], initial=0.0, op0=ALU.add, op1=ALU.max),
                 reads=["nlf"], writes=["cn"])
            st_cn = P.op("sp", lambda e: e.dma_start(out=CN[:, :], in_=cn[:]), reads=["cn"], dma=True)
            P.barrier("sp", [st_cn])
        CNv = CN.rearrange("h (i r t) -> h r i t", i=16, r=2, t=256)
        for r in range(2):
            if do_k:
                P.op("sp", lambda e, r=r: e.dma_start(out=nlf[:, r * OWN:(r + 1) * OWN].rearrange("h (i t) -> h i t", t=256),
                                                     in_=CNv[:, r, :, :]), reads=["cn"], writes=["nlf"], dma=True)
            P.op("sp", lambda e, r=r: e.dma_start(out=cq[r][:].rearrange("h (i t) -> h i t", t=256), in_=CNv[:, r, :, :]),
                 writes=[("cq", r)], dma=True)

        def split3(src, skey, pieces, pkey):
            for p_ in range(3):
                P.op("dve", lambda e, p_=p_: e.tensor_copy(out=pieces[p_][:], in_=src[:]), reads=[skey], writes=[(pkey, p_)])
                if p_ < 2:
                    P.op("dve", lambda e, p_=p_: e.tensor_tensor(out=src[:], in0=src[:], in1=pieces[p_][:], op=ALU.subtract),
                         reads=[skey, (pkey, p_)], writes=[skey])

        if do_k:
            split3(nlf, "nlf", pk, "pk")
        P.op("dve", lambda e: e.tensor_scalar(out=cq[0][:], in0=cq[0][:], scalar1=selg_sb[:, 0:1], scalar2=-1.0, op0=ALU.mult, op1=ALU.mult),
             reads=[("cq", 0)], writes=[("cq", 0)])
        P.op("dve", lambda e: e.tensor_scalar(out=cq[1][:], in0=cq[1][:], scalar1=selg_sb[:, 1:2], scalar2=-1.0, op0=ALU.mult, op1=ALU.mult),
             reads=[("cq", 1)], writes=[("cq", 1)])
        P.op("dve", lambda e: e.tensor_tensor(out=cq[0][:], in0=cq[0][:], in1=cq[1][:], op=ALU.add),
             reads=[("cq", 0), ("cq", 1)], writes=[("cq", 0)])
        split3(cq[0], ("cq", 0), pq, "pq")
        bst = []
        for h in range(4):
            for p_ in range(3):
                if do_k:
                    bst.append(P.op("sp", lambda e, h=h, p_=p_: e.dma_start(out=KTB[h, 67 + p_:68 + p_, :], in_=pk[p_][h:h + 1, :]),
                                    reads=[("pk", p_)], dma=True))
                bst.append(P.op("sp", lambda e, h=h, p_=p_: e.dma_start(out=QTB[h, 64 + p_:65 + p_, :], in_=pq[p_][h:h + 1, :]),
                                reads=[("pq", p_)], dma=True))
            if do_k:
                bst.append(P.op("sp", lambda e, h=h: e.dma_start(out=KTB[h, 64:67, :], in_=ones_r[0:3, :]), reads=["ones_r"], dma=True))
            bst.append(P.op("sp", lambda e, h=h: e.dma_start(out=QTB[h, 67:70, :], in_=ones_r[0:3, 0:OWN]), reads=["ones_r"], dma=True))
        P.full_barrier()
        A.reset(layer_mark)

        kt_t = [A.alloc("kt", [128, S], BF16) for _ in range(2)]
        qt_t = [A.alloc("qt", [128, OWN], BF16) for _ in range(2)]
        vt_t = [A.alloc("vt", [128, 64, 128], BF16) for _ in range(2)]
        qz_t = [A.alloc("qz", [128, 2, OWN], BF16) for _ in range(2)]
        for us_ in range(2):
            P.op("pool", lambda e: e.memset(qz_t[us_][64:128, 0, :], 0.0), writes=[("qz0", us_)])
            P.op("pool", lambda e: e.memset(qz_t[us_][0:64, 1, :], 0.0), writes=[("qz1", us_)])
        E_t = [A.alloc("E", [128, 1024], BF16) for _ in range(4)]
        U_t = [A.alloc("U", [128, 1024], F32) for _ in range(3)]
        Lp_t = [A.alloc("Lp", [128, 1024], BF16) for _ in range(3)]
        Y_t = [A.alloc("Y", [128, 1024], F32) for _ in range(3)]
        cb_t = [A.alloc("cb", [128, 256], F32) for _ in range(3)]
        gate_t = [A.alloc("gate", [128, 256], F32) for _ in range(2)]
        r_t = [A.alloc("r", [128, 256], F32) for _ in range(2)]
        o_t = [A.alloc("o", [128, 256], F32) for _ in range(3)]
        sq_t = A.alloc("sq", [128, 256], F32)
        rstd_t = A.alloc("rstd", [128, 256], F32)
        ys_t = [A.alloc("ys", [128, 256], BF16) for _ in range(2)]
        Es_t = [A.alloc("Es", [128, 1024], F32) for _ in range(2)]
        ystores = []
        ucount = [0]
        ycount = [0]

        def kcol(j, m):
            return (m // 2) * OWN + j * 256 + (m % 2) * 128

        def vch(j, m):
            return (m // 2) * 32 + j * 2 + (m % 2)

        def ktk(us):
            return [("kt", us, 0), ("kt", us, 1), ("kt", us, "z")]

        def qtk(us):
            return [("qt", us, 0), ("qt", us, 1), ("qt", us, "z"), ("qz0", us), ("qz1", us)]

        def load_unit(kind, h):
            us = ucount[0] % 2
            ucount[0] += 1
            kt, qt, vt = kt_t[us], qt_t[us], vt_t[us]
            rows = {"A": 128, "B": 70, "C": 64}[kind]
            KT = {"A": KTA, "B": KTB, "C": KTC}[kind]
            QT = {"A": QTA, "B": QTB, "C": QTC}[kind]
            if kind == "C":
                P.op("pool", lambda e: e.memset(kt[64:128, :], 0.0), writes=[("kt", us, "z")])
                P.op("pool", lambda e: e.memset(qt[64:128, :], 0.0), writes=[("qt", us, "z")])
            for r in range(2):
                P.op("pool", lambda e, r=r: e.dma_start(out=kt[0:rows, r * OWN:(r + 1) * OWN], in_=KT[h, :, r * OWN:(r + 1) * OWN]),
                     writes=[("kt", us, r)], dma=True)
            if kind == "A":
                qz = qz_t[us]
                P.op("pool", lambda e: e.dma_start(out=qz[0:64, 0, :], in_=QT[h, 0:64, :]), writes=[("qt", us, 0)], dma=True)
                P.op("pool", lambda e: e.dma_start(out=qz[64:128, 1, :], in_=QT[h, 64:128, :]), writes=[("qt", us, 1)], dma=True)
            else:
                P.op("pool", lambda e: e.dma_start(out=qt[0:rows, :], in_=QT[h, :, :]), writes=[("qt", us, 0)], dma=True)
            if kind == "A":
                c0, cw = h * 128, 128
            elif kind == "B":
                c0, cw = 512 + h * 64, 64
            else:
                c0, cw = 768 + h * 64, 64
            if kind == "B":
                P.op("pool", lambda e: e.memset(vt[:, :, 64:128], 1.0), writes=[("vt", us)])
            for q8 in range(8):
                P.op("pool", lambda e, q8=q8: e.dma_start(
                    out=vt[:, q8 * 8:(q8 + 1) * 8, 0:cw],
                    in_=VS[q8 * 1024:(q8 + 1) * 1024, c0:c0 + cw].rearrange("(c p) w -> p c w", p=128)),
                    writes=[("vt", us, q8)], reads=[("vt", us)], dma=True)
            return us

        def vkeys(us):
            return [("vt", us)] + [("vt", us, q8) for q8 in range(8)]

        def load_gate(row0, nrows, i, slot=None):
            gs = (ycount[0] % 2) if slot is None else slot
            g = gate_t[gs]
            P.op("sp", lambda e: e.dma_start(out=g[0:nrows, :], in_=GT[row0:row0 + nrows, i * 256:(i + 1) * 256]),
                 writes=[("gate", gs)], dma=True)
            return g, ("gate", gs)

        def store_y(ysrc_fn, row0, nrows, i, reads):
            ys = ys_t[ycount[0] % 2]
            yk = ("ys", ycount[0] % 2)
            ycount[0] += 1
            ysrc_fn(ys, yk)
            ystores.append(P.op("sp", lambda e: e.dma_start(out=YT[row0:row0 + nrows, i * 256:(i + 1) * 256], in_=ys[0:nrows, :]),
                                reads=[yk], dma=True))

        def attn_A(h, us):
            kt, qt, vt = kt_t[us], qt_t[us], vt_t[us]
            items = [(i, j, sub) for i in range(NB) for j in range(i + 1) for sub in (0, 1)]
            OP = [(PA, 0), (PA, 256)]
            LP = [(PB, 0), (PB, 256)]
            pending = []
            gate_of = {}
            lsb_t, rsb_t = Es_t[0], Es_t[1]
            x2r = []

            def QK(w):
                i, j, sub = items[w]
                xs = sub
                X = XS[xs]
                r0 = sub * 64
                diag = (j == i)
                for m in range(4):
                    kc = kcol(j, m)
                    P.op("pe", lambda e, m=m, kc=kc: e.matmul(
                        out=X[:, m * 256:(m + 1) * 256], lhsT=kt[:, kc:kc + 128], rhs=qz_t[us][:, sub, i * 256:(i + 1) * 256],
                        start=True, stop=not diag), reads=ktk(us) + qtk(us), writes=[("X", xs)])
                    if diag:
                        P.op("pe", lambda e, m=m: e.matmul(out=X[:, m * 256:(m + 1) * 256], lhsT=ident[:], rhs=mask_ab[:, m, :],
                                                           start=False, stop=True), writes=[("X", xs)])

            def EXP(w):
                i, j, sub = items[w]
                es = (w % 4)
                P.op("act", lambda e: e.activation(out=E_t[es][:], in_=XS[sub][:], func=AF.Exp), reads=[("X", sub)], writes=[("E", es)])

            def PV(w):
                i, j, sub = items[w]
                es = (w % 4)
                E = E_t[es]
                first, last = (j == 0), (j == i)
                if first and sub == 0:
                    gate_of[i] = load_gate(h * 128, 128, i, slot=i % 2)
                for m in range(4):
                    ch = vch(j, m)
                    ot_, oc_ = OP[sub]
                    P.op("pe", lambda e, m=m, ch=ch: e.matmul(out=ot_[:, oc_:oc_ + 256], lhsT=vt[:, ch, :], rhs=E[:, m * 256:(m + 1) * 256],
                                                              start=(first and m == 0 and sub == 0), stop=(last and m == 3),
                                                              skip_group_check=True),
                         reads=[("E", es)] + vkeys(us), writes=["bankPA"])
                    lt_, lc_ = LP[sub]
                    P.op("pe", lambda e, m=m: e.matmul(out=lt_[:, lc_:lc_ + 256], lhsT=onesb[:], rhs=E[:, m * 256:(m + 1) * 256],
                                                       start=(first and m == 0 and sub == 0), stop=(last and m == 3),
                                                       skip_group_check=True),
                         reads=[("E", es)], writes=["bankPB"])
                if last and sub == 1:
                    epi0(i)

            def epi0(i):
                for sub in (0, 1):
                    P.op("dve", lambda e, sub=sub: e.tensor_copy(out=o_t[sub][:], in_=OP[sub][0][:, OP[sub][1]:OP[sub][1] + 256]),
                         reads=["bankPA"], writes=[("o", sub)])
                P.op("dve", lambda e: e.tensor_copy(out=lsb_t[:, 0:512], in_=PB[:, 0:512]), reads=["bankPB"], writes=["lsb"])
                pending.append([2, lambda: epi1(i)])

            def epi1(i):
                P.op("act", lambda e: e.activation(out=rsb_t[:, 0:512], in_=lsb_t[:, 0:512], func=AF.Ln), reads=["lsb"], writes=["rsb"])
                P.op("act", lambda e: e.activation(out=rsb_t[:, 0:512], in_=rsb_t[:, 0:512], func=AF.Exp, scale=-1.0), reads=["rsb"], writes=["rsb"])
                for sub in (0, 1):
                    P.op("dve", lambda e, sub=sub: e.tensor_tensor(out=o_t[sub][:], in0=o_t[sub][:], in1=rsb_t[:, sub * 256:(sub + 1) * 256], op=ALU.mult),
                         reads=[("o", sub), "rsb"], writes=[("o", sub)])
                P.op("dve", lambda e: e.scalar_tensor_tensor(out=o_t[2][:], in0=o_t[1][:], scalar=neglam, in1=o_t[0][:],
                                                             op0=ALU.mult, op1=ALU.add), reads=[("o", 0), ("o", 1)], writes=[("o", 2)])
                P.op("dve", lambda e: e.tensor_tensor(out=sq_t[:], in0=o_t[2][:], in1=o_t[2][:], op=ALU.mult), reads=[("o", 2)], writes=["sq"])
                pending.append([2, lambda: epi2(i)])

            def epi2(i):
                g, gk = gate_of.pop(i)
                P.op("pe", lambda e: e.matmul(out=X2[:, 512:768], lhsT=onesf[:], rhs=sq_t[:], start=True, stop=True),
                     reads=["sq"], writes=["bankX2b"])
                x2r.append(P.op("act", lambda e: e.activation(out=rstd_t[:], in_=X2[:, 512:768], func=AF.Ln, bias=float(128.0 * SUBLN_EPS),
                                                              scale=1.0), reads=["bankX2b"], writes=["rstd2"]))
                P.op("act", lambda e: e.activation(out=rstd_t[:], in_=rstd_t[:], func=AF.Exp, scale=-0.5), reads=["rstd2"], writes=["rstd2"])
                P.op("dve", lambda e: e.scalar_tensor_tensor(out=o_t[2][:], in0=o_t[2][:], scalar=gsub, in1=rstd_t[:],
                                                             op0=ALU.mult, op1=ALU.mult), reads=[("o", 2), "rstd2"], writes=[("o", 2)])

                def fin(ys, yk):
                    P.op("dve", lambda e: e.tensor_tensor(out=ys[:], in0=o_t[2][:], in1=g[:], op=ALU.mult), reads=[("o", 2), gk], writes=[yk])
                store_y(fin, h * 128, 128, i, None)

            W = len(items)
            for w0 in range(min(2, W)):
                QK(w0)
            for w in range(W):
                EXP(w)
                PV(w)
                if w + 2 < W:
                    QK(w + 2)
                for pnd in list(pending):
                    pnd[0] -= 1
                    if pnd[0] <= 0:
                        pending.remove(pnd)
                        pnd[1]()
            while pending:
                pnd = pending.pop(0)
                pnd[1]()
            P.barrier("pe", x2r[-4:])

        def attn_B(h, us):
            kt, qt, vt = kt_t[us], qt_t[us], vt_t[us]
            items = [(i, j) for i in range(NB) for j in range(i + 1)]
            gate_of = {}

            def QK(w):
                i, j = items[w]
                xs = w % 3
                X = XS[xs]
                diag = (j == i)
                for m in range(4):
                    kc = kcol(j, m)
                    P.op("pe", lambda e, m=m, kc=kc: e.matmul(
                        out=X[:, m * 256:(m + 1) * 256], lhsT=kt[0:70, kc:kc + 128], rhs=qt[0:70, i * 256:(i + 1) * 256],
                        start=True, stop=not diag), reads=ktk(us) + qtk(us), writes=[("X", xs)])
                    if diag:
                        P.op("pe", lambda e, m=m: e.matmul(out=X[:, m * 256:(m + 1) * 256], lhsT=ident[:], rhs=mask_ab[:, m, :],
                                                           start=False, stop=True), writes=[("X", xs)])

            def EXP(w):
                es = w % 4
                P.op("act", lambda e: e.activation(out=E_t[es][:], in_=XS[w % 3][:], func=AF.Exp), reads=[("X", w % 3)], writes=[("E", es)])

            def PV(w):
                i, j = items[w]
                es = w % 4
                E = E_t[es]
                first, last = (j == 0), (j == i)
                PO = PA if i % 2 == 0 else PB
                pkey = "bankPA" if i % 2 == 0 else "bankPB"
                if first:
                    gate_of[i] = load_gate(512 + h * 64, 64, i, slot=i % 2)
                for m in range(4):
                    ch = vch(j, m)
                    P.op("pe", lambda e, m=m, ch=ch: e.matmul(out=PO[:, 0:256], lhsT=vt[:, ch, :], rhs=E[:, m * 256:(m + 1) * 256],
                                                              start=(first and m == 0), stop=(last and m == 3)),
                         reads=[("E", es)] + vkeys(us), writes=[pkey])
                if last:
                    g, gk = gate_of.pop(i)
                    P.op("dve", lambda e: e.reciprocal(out=r_t[0][0:64, :], in_=PO[64:128, 0:256]), reads=[pkey], writes=[("r", 0)])
                    P.op("dve", lambda e: e.tensor_tensor(out=o_t[0][0:64, :], in0=PO[0:64, 0:256], in1=r_t[0][0:64, :], op=ALU.mult),
                         reads=[pkey, ("r", 0)], writes=[("o", 0)])

                    def fin(ys, yk):
                        P.op("dve", lambda e: e.tensor_tensor(out=ys[0:64, :], in0=o_t[0][0:64, :], in1=g[0:64, :], op=ALU.mult),
                             reads=[("o", 0), gk], writes=[yk])
                    store_y(fin, 512 + h * 64, 64, i, None)

            W = len(items)
            for w0 in range(min(3, W)):
                QK(w0)
            for w in range(W):
                EXP(w)
                PV(w)
                if w + 3 < W:
                    QK(w + 3)

        def attn_C(h, us):
            kt, qt, vt = kt_t[us], qt_t[us], vt_t[us]
            items = [(i, j) for i in range(NB) for j in range(i, -1, -1)]
            W = len(items)

            def QK(w):
                i, j = items[w]
                xs = w % 3
                X = XS[xs]
                diag = (j == i)
                for m in range(4):
                    kc = kcol(j, m)
                    P.op("pe", lambda e, m=m, kc=kc: e.matmul(
                        out=X[:, m * 256:(m + 1) * 256], lhsT=kt[:, kc:kc + 128], rhs=qt[:, i * 256:(i + 1) * 256],
                        start=(m % 2 == 0), stop=True, skip_group_check=True), reads=ktk(us) + qtk(us), writes=[("X", xs)])
                    if diag:
                        P.op("pe", lambda e, m=m: e.matmul(out=X[:, m * 256:(m + 1) * 256], lhsT=ident[:], rhs=mask_c[:, m, :],
                                                           start=False, stop=True, skip_group_check=True), writes=[("X", xs)])

            def EA(w):
                xs = w % 3
                P.op("act", lambda e: e.activation(out=U_t[xs][:], in_=XS[xs][:], func=AF.Exp), reads=[("X", xs)], writes=[("U", xs)])
                P.op("act", lambda e: e.activation(out=Lp_t[xs][:], in_=U_t[xs][:], func=AF.Ln, bias=1.0, scale=1.0),
                     reads=[("U", xs)], writes=[("Lp", xs)])

            def TM(w):
                i, j = items[w]
                xs = w % 3
                X, Lp = XS[xs], Lp_t[xs]
                for m in range(4):
                    P.op("pe", lambda e, m=m: e.matmul(out=X[:, m * 256:(m + 1) * 256], lhsT=tneg[:], rhs=Lp[:, m * 256:(m + 1) * 256],
                                                       start=False, stop=(m == 3), skip_group_check=True),
                         reads=[("Lp", xs)], writes=[("X", xs)])
                    for m2 in range(m + 1, 4):
                        P.op("pe", lambda e, m=m, m2=m2: e.matmul(out=X[:, m * 256:(m + 1) * 256], lhsT=onesneg[:],
                                                                  rhs=Lp[:, m2 * 256:(m2 + 1) * 256], start=False, stop=True,
                                                                  skip_group_check=True), reads=[("Lp", xs)], writes=[("X", xs)])
                if j > 0:
                    for m in range(4):
                        P.op("pe", lambda e, m=m: e.matmul(out=PB[:, 0:256], lhsT=onesneg[:], rhs=Lp[:, m * 256:(m + 1) * 256],
                                                           start=(j == i and m == 0), stop=(m == 3), skip_group_check=True),
                             reads=[("Lp", xs)], writes=["bankPB"])
                    cs = w % 3
                    P.op("dve", lambda e: e.tensor_copy(out=cb_t[cs][:], in_=PB[:, 0:256]), reads=["bankPB"], writes=[("cb", cs)])

            def ADD(w):
                i, j = items[w]
                xs = w % 3
                Y = Y_t[xs]
                if j == i:
                    for hb_ in range(2):
                        P.op("dve", lambda e, hb_=hb_: e.tensor_copy(out=Y[:, hb_ * 512:(hb_ + 1) * 512], in_=XS[xs][:, hb_ * 512:(hb_ + 1) * 512]),
                             reads=[("X", xs)], writes=[("Y", xs)])
                else:
                    cs = (w - 1) % 3
                    for m in range(4):
                        P.op("dve", lambda e, m=m: e.tensor_tensor(out=Y[:, m * 256:(m + 1) * 256], in0=XS[xs][:, m * 256:(m + 1) * 256],
                                                                   in1=cb_t[cs][:], op=ALU.add),
                             reads=[("X", xs), ("cb", cs)], writes=[("Y", xs)])

            def EB(w):
                xs = w % 3
                es = w % 4
                P.op("act", lambda e: e.activation(out=E_t[es][:], in_=Y_t[xs][:], func=AF.Exp), reads=[("Y", xs)], writes=[("E", es)])

            def PV(w):
                i, j = items[w]
                es = w % 4
                E = E_t[es]
                first, last = (j == i), (j == 0)
                for m in range(4):
                    ch = vch(j, m)
                    P.op("pe", lambda e, m=m, ch=ch, E=E: e.matmul(out=PA[:, 0:256], lhsT=vt[:, ch, :], rhs=E[:, m * 256:(m + 1) * 256],
                                                                 start=(first and m == 0), stop=(last and m == 3)),
                         reads=[("E", es)] + vkeys(us), writes=["bankPA"])
                if last:
                    g, gk = load_gate(768 + h * 64, 64, i)

                    def fin(ys, yk):
                        P.op("dve", lambda e: e.tensor_tensor(out=ys[0:64, :], in0=PA[0:64, 0:256], in1=g[0:64, :], op=ALU.mult),
                             reads=["bankPA", gk], writes=[yk])
                    store_y(fin, 768 + h * 64, 64, i, None)

            for w0 in range(min(3, W)):
                QK(w0)
                EA(w0)
            TM(0)
            for w in range(W):
                ADD(w)
                EB(w)
                if w + 1 < W:
                    TM(w + 1)
                if w + 3 < W:
                    QK(w + 3)
                    EA(w + 3)
                PV(w)

        units = [("A", h) for h in range(4)] + [("B", h) for h in range(4)] + [("C", h) for h in range(4)]
        fns = {"A": attn_A, "B": attn_B, "C": attn_C}
        us_cur = load_unit(*units[0])
        for k_, (kind_, h_) in enumerate(units):
            us_next = load_unit(*units[k_ + 1]) if k_ + 1 < len(units) else None
            fns[kind_](h_, us_cur)
            us_cur = us_next
        P.full_barrier()
        A.reset(layer_mark)

        Wo = A.alloc("Wo", [128, 8, D], BF16)
        for c in range(8):
            P.op("pool", lambda e, c=c: e.dma_start(out=Wo[:, c, :], in_=pl["wo"][c * 128:(c + 1) * 128, :]), writes=[("Wo", c)], dma=True)
        g_o = A.alloc("g_o", [128, D], F32)
        b_o = A.alloc("b_o", [128, D], F32)
        P.op("sp", lambda e: e.dma_start(out=g_o[:], in_=pl["lng"][:, :]), writes=["g_o"], dma=True)
        P.op("sp", lambda e: e.dma_start(out=b_o[:], in_=pl["lnb"][:, :]), writes=["b_o"], dma=True)
        yt_t = [A.alloc("yt", [128, 8, 128], BF16) for _ in range(2)]
        hr_t = [A.alloc("hr", [128, D], F32) for _ in range(2)]
        z_t = [A.alloc("z", [128, D], F32) for _ in range(2)]
        ot_t = [A.alloc("ot", [128, D], F32) for _ in range(2)]
        st6 = A.alloc("st6b", [128, 12], F32)
        mv = A.alloc("mvb", [128, 4], F32)
        for T in range(OWN // 128):
            s_ = T % 2
            yt, hr, z, ot = yt_t[s_], hr_t[s_], z_t[s_], ot_t[s_]
            P.op("sp", lambda e, yt=yt, T=T: e.dma_start(out=yt[:], in_=YT[:, T * 128:(T + 1) * 128].rearrange("(c p) t -> p c t", p=128)),
                 writes=[("yt", s_)], dma=True)
            P.op("sp", lambda e, hr=hr, T=T: e.dma_start(out=hr[:], in_=HRES[T * 128:(T + 1) * 128, :]), writes=[("hr", s_)], dma=True)
            X = XS[s_]
            for half in range(2):
                for c in range(8):
                    P.op("pe", lambda e, X=X, half=half, c=c, yt=yt: e.matmul(
                        out=X[:, half * 512:(half + 1) * 512], lhsT=yt[:, c, :], rhs=Wo[:, c, half * 512:(half + 1) * 512],
                        start=(c == 0), stop=(c == 7)), reads=[("yt", s_), ("Wo", c)], writes=[("X", s_)])
            P.op("dve", lambda e, z=z, hr=hr, X=X: e.scalar_tensor_tensor(out=z[:], in0=hr[:], scalar=float(ALPHA), in1=X[:],
                                                                         op0=ALU.mult, op1=ALU.add),
                 reads=[("hr", s_), ("X", s_)], writes=[("z", s_)])
            layer_norm(z, ("z", s_), ot, ("ot", s_), g_o, b_o, ["g_o", "b_o"])
            o_ = P.op("pool", lambda e, ot=ot, T=T: e.dma_start(out=dest[T * 128:(T + 1) * 128, :], in_=ot[:]),
                      reads=[("ot", s_)], dma=True)
            if last_layer:
                out_stores.append(o_)
        P.full_barrier()

    P.final += out_stores
    P.emit()
    return nc


_BF = ml_dtypes.bfloat16


def _own_rows(g):
    return (np.arange(NB)[:, None] * 512 + g * 256 + np.arange(256)[None, :]).reshape(-1)


def _rope_tables(pos, scale):
    inv = ROPE_THETA ** (-np.arange(0, 16, 2, dtype=np.float32) / 16.0)
    ang = pos.astype(np.float32)[None, :] * inv[:, None].astype(np.float32)
    cos, sin = np.cos(ang), np.sin(ang)
    C = np.ones((128, len(pos)), np.float32)
    Sg = np.zeros((128, len(pos)), np.float32)
    for a in range(2):
        b0 = a * 64
        C[b0:b0 + 8] = cos
        C[b0 + 8:b0 + 16] = cos
        Sg[b0:b0 + 8] = -sin
        Sg[b0 + 8:b0 + 16] = sin
    return (C * scale).astype(np.float32), (Sg * scale).astype(np.float32)


def _swap_cols(cols512):
    idx = np.arange(512)
    out = idx.copy()
    for sh in range(8):
        b = sh * 64
        out[b:b + 8] = idx[b + 8:b + 16]
        out[b + 8:b + 16] = idx[b:b + 8]
    return cols512[out]


def _weight_layouts(w_in_l):
    o = {}
    pos = 0
    for name, n in (("Aq", 512), ("Ak", 512), ("Av", 512), ("Ag", 512), ("Bq", 256), ("Bk", 256), ("Bv", 256), ("Bf", 4),
                    ("Bg", 256), ("Cq", 256), ("Ck", 256), ("Cv", 256), ("Cg", 256)):
        o[name] = np.arange(pos, pos + n)
        pos += n
    kcols = np.concatenate([o["Ak"], _swap_cols(o["Ak"]), o["Bk"], o["Ck"], o["Bf"], o["Av"], o["Bv"], o["Cv"]])
    qcols = np.concatenate([o["Aq"], _swap_cols(o["Aq"]), o["Bq"], o["Cq"], o["Ag"], o["Bg"], o["Cg"]])
    assert len(kcols) == WKC and len(qcols) == WQC
    return np.ascontiguousarray(w_in_l[:, kcols]), np.ascontiguousarray(w_in_l[:, qcols])


def _masks(g):
    p = np.arange(128)[:, None, None]
    m = np.arange(4)[None, :, None]
    t = np.arange(256)[None, None, :]
    kpos = (m // 2) * 256 + (m % 2) * 128 + p
    qpos = g * 256 + t
    mab = np.where(kpos <= qpos, 0.0, MASKV).astype(np.float32)
    mc = np.where(kpos < qpos, 0.0, MASKV).astype(np.float32)
    return mab.astype(_BF), mc.astype(_BF)


_PROG_CACHE = {}


def _get_prog(layers):
    key = tuple(layers)
    if key not in _PROG_CACHE:
        lam_inits = {l: 0.8 - 0.6 * math.exp(-0.3 * l) for l in range(DEPTH)}
        _PROG_CACHE[key] = build_program(list(layers), lam_inits)
    return _PROG_CACHE[key]


def _consts(g):
    j = np.arange(128)[:, None]
    s_ = np.arange(128)[None, :]
    d = {}
    d["c_ident"] = np.eye(128, dtype=np.float32).astype(_BF)
    d["c_tneg"] = np.where(j >= s_, -1.0, 0.0).astype(np.float32).astype(_BF)
    d["c_onesneg"] = np.full((128, 128), -1.0, np.float32).astype(_BF)
    d["c_onesb"] = np.ones((128, 128), np.float32).astype(_BF)
    d["c_onesf"] = np.ones((128, 128), np.float32)
    gath = np.concatenate([_own_rows(0), _own_rows(1)])
    d["rk_cos"], d["rk_sin"] = _rope_tables(gath, 1.0)
    for p in range(2):
        gp = g if p == 0 else 1 - g
        d[f"c_mask_ab{p}"], d[f"c_mask_c{p}"] = _masks(gp)
        d[f"rq_cos{p}"], d[f"rq_sin{p}"] = _rope_tables(_own_rows(gp), 0.125)
        sel = np.zeros((4, 2), np.float32)
        sel[:, gp] = 1.0
        d[f"selg{p}"] = sel
    bl = np.zeros((128, 4), np.float32)
    for r in range(2):
        bl[:, 2 * r] = 1.0 if r == g else 0.0
        bl[:, 2 * r + 1] = 0.0 if r == g else 1.0
    d["blend"] = bl
    return d


def _layer_inputs(l, w_in, b_forget, lambda_q1, lambda_k1, lambda_q2, lambda_k2, subln_g, w_out, ln_g, ln_b):
    wk, wq = _weight_layouts(np.asarray(w_in[l], np.float32))
    rep = lambda v: np.ascontiguousarray(np.broadcast_to(np.asarray(v, np.float32)[None, :], (128, len(v))))
    lamv = np.stack([rep(lambda_q1[l]), rep(lambda_k1[l]), rep(lambda_q2[l]), rep(lambda_k2[l])], axis=1)
    return {
        f"wk{l}": wk, f"wq{l}": wq, f"wo{l}": np.ascontiguousarray(np.asarray(w_out[l], np.float32)),
        f"lng{l}": rep(ln_g[l]), f"lnb{l}": rep(ln_b[l]),
        f"bfg{l}": np.asarray(b_forget[l], np.float32).reshape(4, 1),
        f"lamv{l}": np.ascontiguousarray(lamv), f"subg{l}": np.asarray(subln_g[l], np.float32).reshape(128, 1),
    }


LAUNCH_PLAN = [[0, 1]]


def kernel(x, ln_in_g, ln_in_b, w_in, b_forget, lambda_q1, lambda_k1, lambda_q2, lambda_k2, subln_g, w_out, ln_g, ln_b):
    x = np.asarray(x, np.float32)
    B = x.shape[0]
    rep = lambda v: np.ascontiguousarray(np.broadcast_to(np.asarray(v, np.float32)[None, :], (128, len(v))))
    consts = [_consts(g) for g in range(2)]
    gath = np.concatenate([_own_rows(0), _own_rows(1)])
    h = x
    for layers in LAUNCH_PLAN:
        nc = _get_prog(layers)
        lay = {}
        for l in layers:
            lay.update(_layer_inputs(l, w_in, b_forget, lambda_q1, lambda_k1, lambda_q2, lambda_k2, subln_g, w_out, ln_g, ln_b))
        in_maps = []
        for core in range(8):
            b, g = core // 2, core % 2
            m = dict(consts[g])
            m.update(lay)
            m["hin_full"] = np.ascontiguousarray(h[b][gath])
            m["hin_own"] = np.ascontiguousarray(h[b][_own_rows(g)])
            m["hin_oth"] = np.ascontiguousarray(h[b][_own_rows(1 - g)])
            m["lng_in"] = rep(ln_in_g)
            m["lnb_in"] = rep(ln_in_b)
            in_maps.append(m)
        res = run_bass_kernel_spmd(nc, in_maps, core_ids=list(range(8)))
        hn = np.empty_like(x)
        for core in range(8):
            b, g = core // 2, core % 2
            hn[b][_own_rows(g)] = np.asarray(res.results[core]["out"], np.float32)
        h = hn
    return h
```

```python
import contextlib
import math

import numpy as np
import ml_dtypes

import concourse.bass as bass
import concourse.mybir as mybir
from concourse.bass_utils import run_bass_kernel_spmd

F32 = mybir.dt.float32
BF16 = mybir.dt.bfloat16
AF = mybir.ActivationFunctionType
ALU = mybir.AluOpType

S = 8192
D = 1024
OWN = 4096
NB = 16
DEPTH = 2
LN_EPS = 1e-5
SUBLN_EPS = 1e-5
ALPHA = (2 * DEPTH) ** 0.25
ROPE_THETA = 500000.0
MASKV = -30000.0
WKC = 2564
WQC = 2560

ENGS = ("pe", "act", "dve", "pool", "sp")
DEBUG = False
DUMP = False


class Op:
    __slots__ = ("eng", "fn", "deps", "marked", "sig", "dma", "dsem", "dval", "cc")

    def __init__(self, eng, fn, dma):
        self.eng = eng
        self.fn = fn
        self.deps = []
        self.marked = False
        self.sig = None
        self.dma = dma
        self.dsem = None
        self.dval = None
        self.cc = False


class _Rec:
    def __init__(self):
        self.call = None

    def __getattr__(self, name):
        def f(*a, **k):
            self.call = (name, a, k)
            return None
        return f

    def replay(self, eng):
        name, a, k = self.call
        return getattr(eng, name)(*a, **k)


class Prog:
    NDSEM = 24

    def __init__(self, nc):
        self.nc = nc
        self.ops = {e: [] for e in ENGS}
        self.last_writer = {}
        self.readers = {}
        self.dma_ops = {e: [] for e in ENGS}
        self.final = []

    def op(self, eng, fn, reads=(), writes=(), dma=False, extra=()):
        if fn is not None:
            rec = _Rec()
            fn(rec)
            assert rec.call is not None
            fn = rec.replay
        o = Op(eng, fn, dma)
        deps = []
        seen = set()

        def add(d):
            if d is None or id(d) in seen:
                return
            seen.add(id(d))
            deps.append(d)

        for b in reads:
            add(self.last_writer.get(b))
        for b in writes:
            add(self.last_writer.get(b))
            for r in self.readers.get(b, ()):
                add(r)
        for d in extra:
            add(d)
        if dma:
            lst = self.dma_ops[eng]
            n = len(lst)
            if n >= self.NDSEM:
                add(lst[n - self.NDSEM])
            o.dsem = n % self.NDSEM
            o.dval = 16 * (n // self.NDSEM + 1)
            lst.append(o)
        for d in deps:
            if d.dma:
                o.deps.append(d)
            elif d.eng == "pe" and eng == "pe" and not dma and fn is not None:
                continue
            else:
                d.marked = True
                o.deps.append(d)
        for b in reads:
            self.readers.setdefault(b, []).append(o)
        for b in writes:
            self.last_writer[b] = o
            self.readers[b] = []
        self.ops[eng].append(o)
        return o

    def barrier(self, eng, deps):
        return self.op(eng, None, extra=deps)

    def full_barrier(self):
        tails = []
        for e in ENGS:
            for o in reversed(self.ops[e]):
                if o.fn is not None and not o.dma:
                    tails.append(o)
                    break
        dmas = []
        for e in ENGS:
            dmas += self.dma_ops[e][-self.NDSEM:]
        for e in ENGS:
            self.barrier(e, tails + dmas)
        self.last_writer = {}
        self.readers = {}

    def emit(self):
        nc = self.nc
        with contextlib.ExitStack() as st:
            csem = {e: st.enter_context(nc.semaphore(f"c_{e}")) for e in ENGS}
            dsem = {
                e: [st.enter_context(nc.semaphore(f"d_{e}_{i}")) for i in range(self.NDSEM)]
                for e in ENGS if self.dma_ops[e]
            }
            for e in ENGS:
                c = 0
                for o in self.ops[e]:
                    if o.marked and not o.dma:
                        c += 1
                        o.sig = c
            block = st.enter_context(nc.Block())

            def run(e, engobj):
                seen = {}

                def wait(d):
                    if d.dma:
                        key = ("d", d.eng, d.dsem)
                        sem, val = dsem[d.eng][d.dsem], d.dval
                    else:
                        key = ("c", d.eng)
                        sem, val = csem[d.eng], d.sig
                    if seen.get(key, 0) >= val:
                        return
                    seen[key] = val
                    engobj.wait_ge(sem, val)

                for o in self.ops[e]:
                    for d in o.deps:
                        wait(d)
                    if o.fn is None:
                        continue
                    ins = o.fn(engobj)
                    if o.dma:
                        ins.then_inc(dsem[e][o.dsem], 16)
                    elif o.marked:
                        ins.then_inc(csem[e], 1)
                if e == "sp":
                    for d in self.final:
                        wait(d)

            @block.tensor
            def _(eng):
                run("pe", eng)

            @block.scalar
            def _(eng):
                run("act", eng)

            @block.vector
            def _(eng):
                run("dve", eng)

            @block.gpsimd
            def _(eng):
                run("pool", eng)

            @block.sync
            def _(eng):
                run("sp", eng)


SB_BASE = 16512
SB_TOP = 229376 - 1024


class Arena:
    def __init__(self, nc):
        self.nc = nc
        self.ptr = SB_BASE
        self.n = 0

    def mark(self):
        return self.ptr

    def reset(self, p):
        self.ptr = p

    def alloc(self, name, shape, dt):
        esz = 4 if dt == F32 else 2
        nbytes = esz
        for s in shape[1:]:
            nbytes *= s
        nbytes = (nbytes + 63) // 64 * 64
        off = self.ptr
        self.ptr += nbytes
        assert self.ptr <= SB_TOP, (name, self.ptr)
        self.n += 1
        return self.nc.alloc_sbuf_tensor_at(f"{name}_{self.n}", list(shape), dt, offset=off)


def build_program(layers, lam_inits):
    nc = bass.Bass("TRN2", target_bir_lowering=False)
    P = Prog(nc)
    A = Arena(nc)

    def din(name, shape, dt=F32):
        return nc.dram_tensor(name, list(shape), dt, kind="ExternalInput").ap()

    def dscr(name, shape, dt):
        if DEBUG:
            return nc.dram_tensor(name, list(shape), dt, kind="ExternalOutput").ap()
        return nc.dram_tensor(name, list(shape), dt).ap()

    L0 = layers[0]
    first_is_l0 = (L0 == 0)
    hin_full = din("hin_full", [S, D])
    hin_own = din("hin_own", [OWN, D])
    hin_oth = din("hin_oth", [OWN, D])
    blend = din("blend", [128, 4])
    lng_in = din("lng_in", [128, D])
    lnb_in = din("lnb_in", [128, D])
    rk_cos = din("rk_cos", [128, S])
    rk_sin = din("rk_sin", [128, S])
    rq_cos_p = [din(f"rq_cos{p}", [128, OWN]) for p in range(2)]
    rq_sin_p = [din(f"rq_sin{p}", [128, OWN]) for p in range(2)]
    selg_p = [din(f"selg{p}", [4, 2]) for p in range(2)]
    selw_p = [din(f"selw{p}", [128, 2]) for p in range(2)]
    c_ident = din("c_ident", [128, 128], BF16)
    c_tneg = din("c_tneg", [128, 128], BF16)
    c_onesneg = din("c_onesneg", [128, 128], BF16)
    c_onesb = din("c_onesb", [128, 128], BF16)
    c_onesf = din("c_onesf", [128, 128])
    c_mask_ab_p = [din(f"c_mask_ab{p}", [128, 4, 256], BF16) for p in range(2)]
    c_mask_c_p = [din(f"c_mask_c{p}", [128, 4, 256], BF16) for p in range(2)]
    per_layer = {}
    for l in layers:
        per_layer[l] = dict(
            wk=din(f"wk{l}", [D, WKC]), wq=din(f"wq{l}", [D, WQC]), wo=din(f"wo{l}", [D, D]),
            lng=din(f"lng{l}", [128, D]), lnb=din(f"lnb{l}", [128, D]),
            bfg=din(f"bfg{l}", [4, 1]), lamv=din(f"lamv{l}", [128, 4, 64]), subg=din(f"subg{l}", [128, 1]),
        )
    out = nc.dram_tensor("out", [OWN, D], F32, kind="ExternalOutput").ap()

    KTA = dscr("KTA", [4, 128, S], BF16)
    KTB = dscr("KTB", [4, 70, S], BF16)
    KTC = dscr("KTC", [4, 64, S], BF16)
    VS = dscr("VS", [S, 1024], BF16)
    QTA = dscr("QTA", [4, 128, OWN], BF16)
    QTB = dscr("QTB", [4, 70, OWN], BF16)
    QTC = dscr("QTC", [4, 64, OWN], BF16)
    GT = dscr("GT", [1024, OWN], F32)
    HRES = dscr("HRES", [OWN, D], F32)
    YT = dscr("YT", [1024, OWN], BF16)
    NLF = dscr("NLF", [4, S], F32)
    CN = dscr("CN", [4, S], F32)
    HP = [dscr(f"HP{p}", [OWN, D], F32) for p in range(2)]

    if DEBUG == "C0":
        DBG_U = nc.dram_tensor("DBG_U", [128, 1024], F32, kind="ExternalOutput").ap()
        DBG_L = nc.dram_tensor("DBG_L", [128, 1024], BF16, kind="ExternalOutput").ap()
        DBG_X = nc.dram_tensor("DBG_X", [128, 1024], F32, kind="ExternalOutput").ap()
    X0 = nc.alloc_psum_tensor("X0", [128, 1024], F32)
    X1 = nc.alloc_psum_tensor("X1", [128, 1024], F32)
    X2 = nc.alloc_psum_tensor("X2", [128, 1024], F32)
    PA = nc.alloc_psum_tensor("PA", [128, 512], F32)
    PB = nc.alloc_psum_tensor("PB", [128, 512], F32)
    XS = [X0, X1, X2]

    ident = A.alloc("ident", [128, 128], BF16)
    tneg = A.alloc("tneg", [128, 128], BF16)
    onesneg = A.alloc("onesneg", [128, 128], BF16)
    onesb = A.alloc("onesb", [128, 128], BF16)
    onesf = A.alloc("onesf", [128, 128], F32)
    mask_ab = A.alloc("mask_ab", [128, 4, 256], BF16)
    mask_c = A.alloc("mask_c", [128, 4, 256], BF16)
    selg_sb = A.alloc("selg", [4, 2], F32)
    lam_t = A.alloc("lam", [128, 8], F32)
    subg_t = A.alloc("subg", [128, 2], F32)
    negb_t = A.alloc("negb", [4, 2], F32)
    for dst, src, k in ((ident, c_ident, "ident"), (tneg, c_tneg, "tneg"), (onesneg, c_onesneg, "onesneg"),
                        (onesb, c_onesb, "onesb"), (onesf, c_onesf, "onesf")):
        P.op("sp", lambda e, dst=dst, src=src: e.dma_start(out=dst[:], in_=src[:, :]), writes=[k], dma=True)
    blend_sb = A.alloc("blend", [128, 4], F32)
    P.op("sp", lambda e: e.dma_start(out=blend_sb[:], in_=blend[:, :]), writes=["blend"], dma=True)
    persist_mark = A.mark()
    CONST_KEYS = ["ident", "tneg", "onesneg", "onesb", "onesf", "mask_ab", "mask_c", "selg"]

    def reload_const_keys():
        pass

    out_stores = []

    fused = len(layers) > 1
    schedule = []
    for li, l in enumerate(layers):
        npass = 2 if (fused and li < len(layers) - 1) else 1
        for p_ in range(npass):
            schedule.append((li, l, p_))
    for (li, l, pss) in schedule:
        pl = per_layer[l]
        lam_init = lam_inits[l]
        do_ln_in = (l == 0)
        last_layer = (li == len(layers) - 1)
        do_k = (pss == 0)
        from_hp = (li > 0)
        src_own = (HP[0] if from_hp else (hin_own if pss == 0 else hin_oth))
        dest = out if last_layer else HP[pss]
        rq_cos, rq_sin = rq_cos_p[pss], rq_sin_p[pss]
        A.reset(persist_mark)
        P.op("sp", lambda e: e.dma_start(out=mask_ab[:], in_=c_mask_ab_p[pss][:, :, :]), writes=["mask_ab"], dma=True)
        P.op("sp", lambda e: e.dma_start(out=mask_c[:], in_=c_mask_c_p[pss][:, :, :]), writes=["mask_c"], dma=True)
        P.op("sp", lambda e: e.dma_start(out=selg_sb[:], in_=selg_p[pss][:, :]), writes=["selg"], dma=True)

        lamv_sb = A.alloc("lamv", [128, 4, 64], F32)
        lprod = A.alloc("lprod", [128, 2, 64], F32)
        P.op("sp", lambda e: e.dma_start(out=lamv_sb[:], in_=pl["lamv"][:, :, :]), writes=["lamv"], dma=True)
        P.op("sp", lambda e: e.dma_start(out=subg_t[:, 0:1], in_=pl["subg"][:, :]), writes=["subg0"], dma=True)
        P.op("sp", lambda e: e.dma_start(out=negb_t[:, 0:1], in_=pl["bfg"][:, :]), writes=["negb0"], dma=True)
        P.op("dve", lambda e: e.tensor_tensor(out=lprod[:, 0, :], in0=lamv_sb[:, 0, :], in1=lamv_sb[:, 1, :], op=ALU.mult),
             reads=["lamv"], writes=["lprod"])
        P.op("dve", lambda e: e.tensor_tensor(out=lprod[:, 1, :], in0=lamv_sb[:, 2, :], in1=lamv_sb[:, 3, :], op=ALU.mult),
             reads=["lamv", "lprod"], writes=["lprod"])
        P.op("dve", lambda e: e.reduce_sum(out=lam_t[:, 0:1], in_=lprod[:, 0, :], axis=mybir.AxisListType.X),
             reads=["lprod"], writes=["lam01"])
        P.op("dve", lambda e: e.reduce_sum(out=lam_t[:, 1:2], in_=lprod[:, 1, :], axis=mybir.AxisListType.X),
             reads=["lprod", "lam01"], writes=["lam01"])
        P.op("act", lambda e: e.activation(out=lam_t[:, 2:4], in_=lam_t[:, 0:2], func=AF.Exp), reads=["lam01"], writes=["lam23"])
        P.op("dve", lambda e: e.scalar_tensor_tensor(out=lam_t[:, 4:5], in0=lam_t[:, 3:4], scalar=-float(lam_init),
                                                     in1=lam_t[:, 2:3], op0=ALU.add, op1=ALU.subtract),
             reads=["lam23"], writes=["neglam"])
        P.op("dve", lambda e: e.tensor_scalar(out=subg_t[:, 1:2], in0=subg_t[:, 0:1], scalar1=float((1.0 - lam_init) * math.sqrt(128.0)),
                                              scalar2=None, op0=ALU.mult), reads=["subg0"], writes=["gsub"])
        P.op("dve", lambda e: e.tensor_scalar(out=negb_t[:, 1:2], in0=negb_t[:, 0:1], scalar1=-1.0, scalar2=None, op0=ALU.mult),
             reads=["negb0"], writes=["negb"])
        neglam = lam_t[:, 4:5]
        gsub = subg_t[:, 1:2]
        negb = negb_t[:, 1:2]
        layer_mark = A.mark()

        Wk = A.alloc("Wk", [128, 8, WKC], BF16)
        Wq = A.alloc("Wq", [128, 8, WQC], BF16)
        for c in range(8):
            P.op("pool", lambda e, c=c: e.dma_start(out=Wk[:, c, :], in_=pl["wk"][c * 128:(c + 1) * 128, :]), writes=[("Wk", c)], dma=True)
        for c in range(8):
            P.op("pool", lambda e, c=c: e.dma_start(out=Wq[:, c, :], in_=pl["wq"][c * 128:(c + 1) * 128, :]), writes=[("Wq", c)], dma=True)
        WK_KEYS = [("Wk", c) for c in range(8)]
        WQ_KEYS = [("Wq", c) for c in range(8)]
        if do_ln_in:
            g_in = A.alloc("g_in", [128, D], F32)
            b_in = A.alloc("b_in", [128, D], F32)
            P.op("sp", lambda e: e.dma_start(out=g_in[:], in_=lng_in[:, :]), writes=["g_in"], dma=True)
            P.op("sp", lambda e: e.dma_start(out=b_in[:], in_=lnb_in[:, :]), writes=["b_in"], dma=True)
        xin_t = [A.alloc("xin", [128, D], F32) for _ in range(2)]
        hf_t = [A.alloc("hf", [128, D], F32) for _ in range(2)]
        hb_t = [A.alloc("hb", [128, D], BF16) for _ in range(2)]
        hT_t = [A.alloc("hT", [128, 8, 512], BF16) for _ in range(2)]
        rc_t = [A.alloc("rc", [128, 512], F32) for _ in range(2)]
        rs_t = [A.alloc("rs", [128, 512], F32) for _ in range(2)]
        tmp_t = [A.alloc("tmp", [128, 512], F32) for _ in range(2)]
        stb_t = [A.alloc("stb", [128, 512], BF16) for _ in range(4)]
        stf_t = [A.alloc("stf", [128, 512], F32) for _ in range(2)]
        st6 = A.alloc("st6", [128, 12], F32)
        mv = A.alloc("mv", [128, 4], F32)
        accs = [(X0, 0), (X0, 512), (X1, 0), (X1, 512), (X2, 0), (X2, 512)]
        cnt = dict(tile=0, acc=0, stb=0, stf=0, tmp=0, grp=0)
        stores1 = []

        def layer_norm(src, skey, dst, dkey, gt, bt, gkeys):
            for hh in range(2):
                P.op("dve", lambda e, hh=hh: e.bn_stats(out=st6[:, hh * 6:(hh + 1) * 6], in_=src[:, hh * 512:(hh + 1) * 512]),
                     reads=[skey], writes=[("st6", hh)])
            P.op("dve", lambda e: e.bn_aggr(out=mv[:, 0:2], in_=st6[:, 0:12]), reads=[("st6", 0), ("st6", 1)], writes=["mv"])
            P.op("act", lambda e: e.activation(out=mv[:, 3:4], in_=mv[:, 1:2], func=AF.Ln, bias=float(LN_EPS), scale=1.0),
                 reads=["mv"], writes=["lnv"])
            P.op("act", lambda e: e.activation(out=mv[:, 2:3], in_=mv[:, 3:4], func=AF.Exp, scale=-0.5), reads=["lnv"], writes=["rstd"])
            P.op("dve", lambda e: e.tensor_scalar(out=dst[:], in0=src[:], scalar1=mv[:, 0:1], scalar2=mv[:, 2:3],
                                                  op0=ALU.subtract, op1=ALU.mult), reads=[skey, "mv", "rstd"], writes=[dkey])
            P.op("dve", lambda e: e.tensor_tensor(out=dst[:], in0=dst[:], in1=gt[:], op=ALU.mult), reads=[dkey, gkeys[0]], writes=[dkey])
            P.op("dve", lambda e: e.tensor_tensor(out=dst[:], in0=dst[:], in1=bt[:], op=ALU.add), reads=[dkey, gkeys[1]], writes=[dkey])

        def next_acc():
            a = accs[cnt["acc"] % len(accs)]
            key = ("acc", cnt["acc"] % len(accs))
            cnt["acc"] += 1
            return a[0], a[1], key

        def next_stb():
            i = cnt["stb"] % 4
            cnt["stb"] += 1
            return stb_t[i], ("stb", i)

        def next_stf():
            i = cnt["stf"] % 2
            cnt["stf"] += 1
            return stf_t[i], ("stf", i)

        def proj_pass(src, ngroups, mode):
            W = Wk if mode == "k" else Wq
            WKEYS = WK_KEYS if mode == "k" else WQ_KEYS
            rcos = rk_cos if mode == "k" else rq_cos
            rsin = rk_sin if mode == "k" else rq_sin
            gslot = {}

            def prep_begin(G):
                gs = cnt["grp"] % 2
                cnt["grp"] += 1
                gslot[G] = gs
                rc, rs_ = rc_t[gs], rs_t[gs]
                P.op("sp", lambda e: e.dma_start(out=rc[:], in_=rcos[:, G * 512:(G + 1) * 512]), writes=[("rc", gs)], dma=True)
                P.op("sp", lambda e: e.dma_start(out=rs_[:], in_=rsin[:, G * 512:(G + 1) * 512]), writes=[("rs", gs)], dma=True)

            tstate = {}

            def prep_load(G, tt):
                gs = gslot[G]
                ts_ = cnt["tile"] % 2
                cnt["tile"] += 1
                row0 = G * 512 + tt * 128
                xin = xin_t[ts_]
                tstate[(G, tt)] = ts_
                if src == "blend":
                    idx = row0 % OWN
                    t1 = hf_t[ts_]
                    P.op("sp", lambda e: e.dma_start(out=xin[:], in_=HP[0][idx:idx + 128, :]), writes=[("xin", ts_)], dma=True)
                    P.op("sp", lambda e: e.dma_start(out=t1[:], in_=HP[1][idx:idx + 128, :]), writes=[("hf", ts_)], dma=True)
                else:
                    P.op("sp", lambda e: e.dma_start(out=xin[:], in_=src[row0:row0 + 128, :]), writes=[("xin", ts_)], dma=True)

            def prep_norm(G, tt):
                ts_ = tstate[(G, tt)]
                row0 = G * 512 + tt * 128
                xin = xin_t[ts_]
                if src == "blend":
                    rr = row0 // OWN
                    t1 = hf_t[ts_]
                    P.op("dve", lambda e: e.tensor_scalar(out=xin[:], in0=xin[:], scalar1=blend_sb[:, 2 * rr:2 * rr + 1], scalar2=None,
                                                          op0=ALU.mult), reads=[("xin", ts_)], writes=[("xin", ts_)])
                    P.op("dve", lambda e: e.scalar_tensor_tensor(out=xin[:], in0=t1[:], scalar=blend_sb[:, 2 * rr + 1:2 * rr + 2], in1=xin[:],
                                                                 op0=ALU.mult, op1=ALU.add),
                         reads=[("xin", ts_), ("hf", ts_)], writes=[("xin", ts_)])
                if do_ln_in:
                    hf = hf_t[ts_]
                    hfk = ("hf", ts_)
                    layer_norm(xin, ("xin", ts_), hf, hfk, g_in, b_in, ["g_in", "b_in"])
                else:
                    hf, hfk = xin, ("xin", ts_)
                if mode == "q":
                    stores1.append(P.op("pool", lambda e: e.dma_start(out=HRES[row0:row0 + 128, :], in_=hf[:]), reads=[hfk], dma=True))
                hb = hb_t[ts_]
                P.op("act", lambda e: e.copy(out=hb[:], in_=hf[:]), reads=[hfk], writes=[("hb", ts_)])

            def prep_tr(G, tt):
                gs = gslot[G]
                hT = hT_t[gs]
                ts_ = tstate[(G, tt)]
                hb = hb_t[ts_]
                for half in range(2):
                    ps, o0, akey = next_acc()
                    for c4 in range(4):
                        c = half * 4 + c4
                        P.op("pe", lambda e, c4=c4, c=c: e.matmul(
                            out=ps[:, o0 + c4 * 128:o0 + (c4 + 1) * 128], lhsT=hb[:, c * 128:(c + 1) * 128], rhs=ident[:],
                            start=True, stop=True), reads=[("hb", ts_), "ident"], writes=[akey])
                    P.op("dve", lambda e: e.tensor_copy(
                        out=hT[:, half * 4:(half + 1) * 4, tt * 128:(tt + 1) * 128],
                        in_=ps[:, o0:o0 + 512].rearrange("p (c t) -> p c t", c=4)),
                        reads=[akey], writes=[("hT", gs, tt, half)])

            def chunks(G):
                gs = gslot[G]
                hT = hT_t[gs]
                hTkeys = [("hT", gs, tt_, hf_) for tt_ in range(4) for hf_ in range(2)]
                rc, rs_ = rc_t[gs], rs_t[gs]

                def fm_chunk(col0, M):
                    ps, o0, akey = next_acc()
                    for c in range(8):
                        P.op("pe", lambda e, c=c: e.matmul(
                            out=ps[0:M, o0:o0 + 512], lhsT=W[:, c, col0:col0 + M], rhs=hT[:, c, :], start=(c == 0), stop=(c == 7)),
                            reads=hTkeys + [WKEYS[c]], writes=[akey])
                    return ps, o0, akey

                tok0 = G * 512
                for i in range(4):
                    p1, o1, k1 = fm_chunk(i * 128, 128)
                    p2, o2, k2 = fm_chunk(512 + i * 128, 128)
                    t1 = tmp_t[0]
                    t2 = tmp_t[1]
                    P.op("dve", lambda e: e.tensor_tensor(out=t1[:], in0=p1[:, o1:o1 + 512], in1=rc[:], op=ALU.mult),
                         reads=[k1, ("rc", gs)], writes=[("tmp", 0)])
                    P.op("dve", lambda e: e.tensor_tensor(out=t2[:], in0=p2[:, o2:o2 + 512], in1=rs_[:], op=ALU.mult),
                         reads=[k2, ("rs", gs)], writes=[("tmp", 1)])
                    sb_, sk = next_stb()
                    P.op("dve", lambda e: e.tensor_tensor(out=sb_[:], in0=t1[:], in1=t2[:], op=ALU.add),
                         reads=[("tmp", 0), ("tmp", 1)], writes=[sk])
                    dstT = KTA if mode == "k" else QTA
                    stores1.append(P.op("pool", lambda e: e.dma_start(out=dstT[i, :, tok0:tok0 + 512], in_=sb_[:]), reads=[sk], dma=True))
                    yield
                for kind, cbase, dstT in (("B", 1024, KTB if mode == "k" else QTB), ("C", 1280, KTC if mode == "k" else QTC)):
                    for h in range(4):
                        ps, o0, akey = fm_chunk(cbase + h * 64, 64)
                        sb_, sk = next_stb()
                        if mode == "k":
                            P.op("act", lambda e: e.copy(out=sb_[0:64, :], in_=ps[0:64, o0:o0 + 512]), reads=[akey], writes=[sk])
                        else:
                            P.op("act", lambda e: e.mul(out=sb_[0:64, :], in_=ps[0:64, o0:o0 + 512], mul=0.125), reads=[akey], writes=[sk])
                        stores1.append(P.op("pool", lambda e: e.dma_start(out=dstT[h, 0:64, tok0:tok0 + 512], in_=sb_[0:64, :]),
                                            reads=[sk], dma=True))
                        yield
                if mode == "k":
                    ps, o0, akey = fm_chunk(1536, 4)
                    sf, sfk = next_stf()
                    P.op("act", lambda e: e.activation(out=sf[0:4, :], in_=ps[0:4, o0:o0 + 512], func=AF.Exp, bias=negb, scale=-1.0),
                         reads=[akey, "negb"], writes=[sfk])
                    P.op("act", lambda e: e.activation(out=sf[0:4, :], in_=sf[0:4, :], func=AF.Ln, bias=1.0, scale=1.0),
                         reads=[sfk], writes=[sfk])
                    r = G // 8
                    for bb in range(2):
                        i_blk = (G % 8) * 2 + bb
                        t0 = i_blk * 512 + r * 256
                        stores1.append(P.op("pool", lambda e, bb=bb, t0=t0: e.dma_start(
                            out=NLF[:, t0:t0 + 256], in_=sf[0:4, bb * 256:(bb + 1) * 256]), reads=[sfk], dma=True))
                    yield
                    for tt in range(4):
                        for half in range(2):
                            ps, o0, akey = next_acc()
                            for c in range(8):
                                P.op("pe", lambda e, c=c: e.matmul(
                                    out=ps[:, o0:o0 + 512], lhsT=hT[:, c, tt * 128:(tt + 1) * 128],
                                    rhs=W[:, c, 1540 + half * 512:1540 + (half + 1) * 512], start=(c == 0), stop=(c == 7)),
                                    reads=hTkeys + [WKEYS[c]], writes=[akey])
                            sb_, sk = next_stb()
                            P.op("act", lambda e: e.copy(out=sb_[:], in_=ps[:, o0:o0 + 512]), reads=[akey], writes=[sk])
                            r0 = tok0 + tt * 128
                            stores1.append(P.op("pool", lambda e: e.dma_start(
                                out=VS[r0:r0 + 128, half * 512:(half + 1) * 512], in_=sb_[:]), reads=[sk], dma=True))
                            yield
                else:
                    for gch in range(8):
                        ps, o0, akey = fm_chunk(1536 + gch * 128, 128)
                        sf, sfk = next_stf()
                        P.op("act", lambda e: e.activation(out=sf[:], in_=ps[:, o0:o0 + 512], func=AF.Silu), reads=[akey], writes=[sfk])
                        stores1.append(P.op("pool", lambda e: e.dma_start(
                            out=GT[gch * 128:(gch + 1) * 128, tok0:tok0 + 512], in_=sf[:]), reads=[sfk], dma=True))
                        yield

            prep_begin(0)
            for tt in range(4):
                prep_load(0, tt)
                prep_norm(0, tt)
                prep_tr(0, tt)
            ev_load = {0: 0, 3: 1, 8: 2, 13: 3}
            ev_norm = {1: 0, 6: 1, 11: 2, 16: 3}
            ev_tr = {5: 0, 10: 1, 15: 2, 19: 3}
            for G in range(ngroups):
                more = (G + 1 < ngroups)
                done = set()

                def fire(idx):
                    if not more:
                        return
                    for ev, fn, tag in ((ev_load, prep_load, "l"), (ev_norm, prep_norm, "n"), (ev_tr, prep_tr, "t")):
                        if idx in ev and (tag, ev[idx]) not in done:
                            done.add((tag, ev[idx]))
                            fn(G + 1, ev[idx])
                if more:
                    prep_begin(G + 1)
                fire(0)
                idx = 0
                for _ in chunks(G):
                    idx += 1
                    fire(idx)
                for k in range(idx + 1, 24):
                    fire(k)

        if do_k:
            proj_pass("blend" if from_hp else hin_full, 16, "k")
        proj_pass(src_own, 8, "q")
        P.full_barrier()
        A.reset(layer_mark)

        nlf = A.alloc("nlf", [4, S], F32)
        cn = A.alloc("cn", [4, S], F32)
        kk = A.alloc("kk", [128, 256], F32)
        pkw = [A.alloc("pkw", [128, 256], BF16) for _ in range(3)]
        cqw = [A.alloc("cqw", [64, 256], F32) for _ in range(2)]
        pqw = [A.alloc("pqw", [64, 256], BF16) for _ in range(3)]
        ones_w = A.alloc("ones_w", [128, 256], BF16)
        selw = A.alloc("selw", [128, 2], F32)
        first_pass = (li == 0 and pss == 0)
        P.op("sp", lambda e: e.dma_start(out=selw[:], in_=selw_p[pss][:, :]), writes=["selw"], dma=True)
        if first_pass:
            P.op("pool", lambda e: e.memset(ones_w[:], 1.0), writes=["ones_w"])
        if do_k:
            P.op("sp", lambda e: e.dma_start(out=nlf[:], in_=NLF[:, :]), writes=["nlf"], dma=True)
            P.op("dve", lambda e: e.tensor_tensor_scan(out=cn[:], data0=nlf[:], data1=nlf[:], initial=0.0, op0=ALU.add, op1=ALU.max),
                 reads=["nlf"], writes=["cn"])
            st_cn = P.op("sp", lambda e: e.dma_start(out=CN[:, :], in_=cn[:]), reads=["cn"], dma=True)
            P.barrier("sp", [st_cn])
        CNv = CN.rearrange("h (i r t) -> h r i t", i=16, r=2, t=256)
        for h in range(4):
            for r in range(2):
                if do_k:
                    P.op("sp", lambda e, h=h, r=r: e.dma_start(out=kk[h * 32 + r * 16:h * 32 + (r + 1) * 16, :], in_=CNv[h, r, :, :]),
                         writes=[("kk", h, r)], dma=True)
                P.op("sp", lambda e, h=h, r=r: e.dma_start(out=cqw[r][h * 16:(h + 1) * 16, :], in_=CNv[h, r, :, :]),
                     writes=[("cqw", r, h)], dma=True)
        KK_KEYS = [("kk", h, r) for h in range(4) for r in range(2)]

        def split3(src, skeys, pieces, pkey):
            for p_ in range(3):
                P.op("dve", lambda e, p_=p_: e.tensor_copy(out=pieces[p_][:], in_=src[:]), reads=skeys, writes=[(pkey, p_)])
                if p_ < 2:
                    P.op("dve", lambda e, p_=p_: e.tensor_tensor(out=src[:], in0=src[:], in1=pieces[p_][:], op=ALU.subtract),
                         reads=skeys + [(pkey, p_)], writes=[skeys[0]])

        if do_k:
            split3(kk, KK_KEYS, pkw, "pkw")
        CQ0 = [("cqw", 0, h) for h in range(4)]
        CQ1 = [("cqw", 1, h) for h in range(4)]
        P.op("dve", lambda e: e.tensor_scalar(out=cqw[0][:], in0=cqw[0][:], scalar1=selw[0:64, 0:1], scalar2=-1.0, op0=ALU.mult, op1=ALU.mult),
             reads=CQ0 + ["selw"], writes=[CQ0[0]])
        P.op("dve", lambda e: e.tensor_scalar(out=cqw[1][:], in0=cqw[1][:], scalar1=selw[0:64, 1:2], scalar2=-1.0, op0=ALU.mult, op1=ALU.mult),
             reads=CQ1 + ["selw"], writes=[CQ1[0]])
        P.op("dve", lambda e: e.tensor_tensor(out=cqw[0][:], in0=cqw[0][:], in1=cqw[1][:], op=ALU.add),
             reads=CQ0 + CQ1, writes=[CQ0[0]])
        split3(cqw[0], CQ0, pqw, "pqw")
        bst = []
        for h in range(4):
            for p_ in range(3):
                if do_k:
                    bst.append(P.op("sp", lambda e, h=h, p_=p_: e.dma_start(
                        out=KTB[h, 67 + p_, :].rearrange("(b t) -> b t", t=256), in_=pkw[p_][h * 32:(h + 1) * 32, :]),
                        reads=[("pkw", p_)], dma=True))
                bst.append(P.op("sp", lambda e, h=h, p_=p_: e.dma_start(
                    out=QTB[h, 64 + p_, :].rearrange("(b t) -> b t", t=256), in_=pqw[p_][h * 16:(h + 1) * 16, :]),
                    reads=[("pqw", p_)], dma=True))
                if first_pass:
                    bst.append(P.op("sp", lambda e, h=h, p_=p_: e.dma_start(
                        out=KTB[h, 64 + p_, :].rearrange("(b t) -> b t", t=256), in_=ones_w[0:32, :]), reads=["ones_w"], dma=True))
                    bst.append(P.op("sp", lambda e, h=h, p_=p_: e.dma_start(
                        out=QTB[h, 67 + p_, :].rearrange("(b t) -> b t", t=256), in_=ones_w[0:16, :]), reads=["ones_w"], dma=True))
        P.full_barrier()
        A.reset(layer_mark)

        kt_t = [A.alloc("kt", [128, S], BF16) for _ in range(2)]
        qt_t = [A.alloc("qt", [128, OWN], BF16) for _ in range(2)]
        vt_t = [A.alloc("vt", [128, 64, 128], BF16) for _ in range(2)]
        qz_t = [A.alloc("qz", [128, 2, OWN], BF16) for _ in range(2)]
        for us_ in range(2):
            P.op("pool", lambda e: e.memset(qz_t[us_][64:128, 0, :], 0.0), writes=[("qz0", us_)])
            P.op("pool", lambda e: e.memset(qz_t[us_][0:64, 1, :], 0.0), writes=[("qz1", us_)])
        E_t = [A.alloc("E", [128, 1024], BF16) for _ in range(4)]
        U_t = [A.alloc("U", [128, 1024], F32) for _ in range(3)]
        Lp_t = [A.alloc("Lp", [128, 1024], BF16) for _ in range(3)]
        Y_t = [A.alloc("Y", [128, 1024], F32) for _ in range(3)]
        cb_t = [A.alloc("cb", [128, 256], F32) for _ in range(3)]
        gate_t = [A.alloc("gate", [128, 256], F32) for _ in range(2)]
        r_t = [A.alloc("r", [128, 256], F32) for _ in range(2)]
        o_t = [A.alloc("o", [128, 256], F32) for _ in range(3)]
        sq_t = A.alloc("sq", [128, 256], F32)
        rstd_t = A.alloc("rstd", [128, 256], F32)
        ys_t = [A.alloc("ys", [128, 256], BF16) for _ in range(2)]
        Es_t = [A.alloc("Es", [128, 1024], F32) for _ in range(2)]
        ystores = []
        ucount = [0]
        ycount = [0]

        def kcol(j, m):
            return (m // 2) * OWN + j * 256 + (m % 2) * 128

        def vch(j, m):
            return (m // 2) * 32 + j * 2 + (m % 2)

        def ktk(us):
            return [("kt", us, 0), ("kt", us, 1), ("kt", us, "z")]

        def qtk(us):
            return [("qt", us, 0), ("qt", us, 1), ("qt", us, "z"), ("qz0", us), ("qz1", us)]

        def load_unit(kind, h):
            us = ucount[0] % 2
            ucount[0] += 1
            kt, qt, vt = kt_t[us], qt_t[us], vt_t[us]
            rows = {"A": 128, "B": 70, "C": 64}[kind]
            KT = {"A": KTA, "B": KTB, "C": KTC}[kind]
            QT = {"A": QTA, "B": QTB, "C": QTC}[kind]
            if kind == "C":
                P.op("pool", lambda e: e.memset(kt[64:128, :], 0.0), writes=[("kt", us, "z")])
                P.op("pool", lambda e: e.memset(qt[64:128, :], 0.0), writes=[("qt", us, "z")])
            for r in range(2):
                P.op("pool", lambda e, r=r: e.dma_start(out=kt[0:rows, r * OWN:(r + 1) * OWN], in_=KT[h, :, r * OWN:(r + 1) * OWN]),
                     writes=[("kt", us, r)], dma=True)
            if kind == "A":
                qz = qz_t[us]
                P.op("pool", lambda e: e.dma_start(out=qz[0:64, 0, :], in_=QT[h, 0:64, :]), writes=[("qt", us, 0)], dma=True)
                P.op("pool", lambda e: e.dma_start(out=qz[64:128, 1, :], in_=QT[h, 64:128, :]), writes=[("qt", us, 1)], dma=True)
            else:
                P.op("pool", lambda e: e.dma_start(out=qt[0:rows, :], in_=QT[h, :, :]), writes=[("qt", us, 0)], dma=True)
            if kind == "A":
                c0, cw = h * 128, 128
            elif kind == "B":
                c0, cw = 512 + h * 64, 64
            else:
                c0, cw = 768 + h * 64, 64
            if kind == "B":
                P.op("pool", lambda e: e.memset(vt[:, :, 64:128], 1.0), writes=[("vt", us)])
            for q8 in range(8):
                P.op("pool", lambda e, q8=q8: e.dma_start(
                    out=vt[:, q8 * 8:(q8 + 1) * 8, 0:cw],
                    in_=VS[q8 * 1024:(q8 + 1) * 1024, c0:c0 + cw].rearrange("(c p) w -> p c w", p=128)),
                    writes=[("vt", us, q8)], reads=[("vt", us)], dma=True)
            return us

        def vkeys(us):
            return [("vt", us)] + [("vt", us, q8) for q8 in range(8)]

        def load_gate(row0, nrows, i, slot=None):
            gs = (ycount[0] % 2) if slot is None else slot
            g = gate_t[gs]
            P.op("sp", lambda e: e.dma_start(out=g[0:nrows, :], in_=GT[row0:row0 + nrows, i * 256:(i + 1) * 256]),
                 writes=[("gate", gs)], dma=True)
            return g, ("gate", gs)

        def store_y(ysrc_fn, row0, nrows, i, reads):
            ys = ys_t[ycount[0] % 2]
            yk = ("ys", ycount[0] % 2)
            ycount[0] += 1
            ysrc_fn(ys, yk)
            ystores.append(P.op("sp", lambda e: e.dma_start(out=YT[row0:row0 + nrows, i * 256:(i + 1) * 256], in_=ys[0:nrows, :]),
                                reads=[yk], dma=True))

        def attn_A(h, us):
            kt, qt, vt = kt_t[us], qt_t[us], vt_t[us]
            items = [(i, j, sub) for i in range(NB) for j in range(i + 1) for sub in (0, 1)]
            OP = [(PA, 0), (PA, 256)]
            LP = [(PB, 0), (PB, 256)]
            pending = []
            gate_of = {}
            lsb_t, rsb_t = Es_t[0], Es_t[1]
            x2r = []

            def QK(w):
                i, j, sub = items[w]
                xs = sub
                X = XS[xs]
                r0 = sub * 64
                diag = (j == i)
                for m in range(4):
                    kc = kcol(j, m)
                    P.op("pe", lambda e, m=m, kc=kc: e.matmul(
                        out=X[:, m * 256:(m + 1) * 256], lhsT=kt[:, kc:kc + 128], rhs=qz_t[us][:, sub, i * 256:(i + 1) * 256],
                        start=True, stop=not diag), reads=ktk(us) + qtk(us), writes=[("X", xs)])
                    if diag:
                        P.op("pe", lambda e, m=m: e.matmul(out=X[:, m * 256:(m + 1) * 256], lhsT=ident[:], rhs=mask_ab[:, m, :],
                                                           start=False, stop=True), writes=[("X", xs)])

            def EXP(w):
                i, j, sub = items[w]
                es = (w % 4)
                P.op("act", lambda e: e.activation(out=E_t[es][:], in_=XS[sub][:], func=AF.Exp), reads=[("X", sub)], writes=[("E", es)])

            def PV(w):
                i, j, sub = items[w]
                es = (w % 4)
                E = E_t[es]
                first, last = (j == 0), (j == i)
                if first and sub == 0:
                    gate_of[i] = load_gate(h * 128, 128, i, slot=i % 2)
                for m in range(4):
                    ch = vch(j, m)
                    ot_, oc_ = OP[sub]
                    P.op("pe", lambda e, m=m, ch=ch: e.matmul(out=ot_[:, oc_:oc_ + 256], lhsT=vt[:, ch, :], rhs=E[:, m * 256:(m + 1) * 256],
                                                              start=(first and m == 0 and sub == 0), stop=(last and m == 3),
                                                              skip_group_check=True),
                         reads=[("E", es)] + vkeys(us), writes=["bankPA"])
                    lt_, lc_ = LP[sub]
                    P.op("pe", lambda e, m=m: e.matmul(out=lt_[:, lc_:lc_ + 256], lhsT=onesb[:], rhs=E[:, m * 256:(m + 1) * 256],
                                                       start=(first and m == 0 and sub == 0), stop=(last and m == 3),
                                                       skip_group_check=True),
                         reads=[("E", es)], writes=["bankPB"])
                if last and sub == 1:
                    epi0(i)

            def epi0(i):
                for sub in (0, 1):
                    P.op("dve", lambda e, sub=sub: e.tensor_copy(out=o_t[sub][:], in_=OP[sub][0][:, OP[sub][1]:OP[sub][1] + 256]),
                         reads=["bankPA"], writes=[("o", sub)])
                P.op("dve", lambda e: e.tensor_copy(out=lsb_t[:, 0:512], in_=PB[:, 0:512]), reads=["bankPB"], writes=["lsb"])
                pending.append([2, lambda: epi1(i)])

            def epi1(i):
                P.op("act", lambda e: e.activation(out=rsb_t[:, 0:512], in_=lsb_t[:, 0:512], func=AF.Ln), reads=["lsb"], writes=["rsb"])
                P.op("act", lambda e: e.activation(out=rsb_t[:, 0:512], in_=rsb_t[:, 0:512], func=AF.Exp, scale=-1.0), reads=["rsb"], writes=["rsb"])
                for sub in (0, 1):
                    P.op("dve", lambda e, sub=sub: e.tensor_tensor(out=o_t[sub][:], in0=o_t[sub][:], in1=rsb_t[:, sub * 256:(sub + 1) * 256], op=ALU.mult),
                         reads=[("o", sub), "rsb"], writes=[("o", sub)])
                P.op("dve", lambda e: e.scalar_tensor_tensor(out=o_t[2][:], in0=o_t[1][:], scalar=neglam, in1=o_t[0][:],
                                                             op0=ALU.mult, op1=ALU.add), reads=[("o", 0), ("o", 1)], writes=[("o", 2)])
                P.op("dve", lambda e: e.tensor_tensor(out=sq_t[:], in0=o_t[2][:], in1=o_t[2][:], op=ALU.mult), reads=[("o", 2)], writes=["sq"])
                pending.append([2, lambda: epi2(i)])

            def epi2(i):
                g, gk = gate_of.pop(i)
                P.op("pe", lambda e: e.matmul(out=X2[:, 512:768], lhsT=onesf[:], rhs=sq_t[:], start=True, stop=True),
                     reads=["sq"], writes=["bankX2b"])
                x2r.append(P.op("act", lambda e: e.activation(out=rstd_t[:], in_=X2[:, 512:768], func=AF.Ln, bias=float(128.0 * SUBLN_EPS),
                                                              scale=1.0), reads=["bankX2b"], writes=["rstd2"]))
                P.op("act", lambda e: e.activation(out=rstd_t[:], in_=rstd_t[:], func=AF.Exp, scale=-0.5), reads=["rstd2"], writes=["rstd2"])
                P.op("dve", lambda e: e.scalar_tensor_tensor(out=o_t[2][:], in0=o_t[2][:], scalar=gsub, in1=rstd_t[:],
                                                             op0=ALU.mult, op1=ALU.mult), reads=[("o", 2), "rstd2"], writes=[("o", 2)])

                def fin(ys, yk):
                    P.op("dve", lambda e: e.tensor_tensor(out=ys[:], in0=o_t[2][:], in1=g[:], op=ALU.mult), reads=[("o", 2), gk], writes=[yk])
                store_y(fin, h * 128, 128, i, None)

            W = len(items)
            for w0 in range(min(2, W)):
                QK(w0)
            for w in range(W):
                EXP(w)
                PV(w)
                if w + 2 < W:
                    QK(w + 2)
                for pnd in list(pending):
                    pnd[0] -= 1
                    if pnd[0] <= 0:
                        pending.remove(pnd)
                        pnd[1]()
            while pending:
                pnd = pending.pop(0)
                pnd[1]()
            P.barrier("pe", x2r[-4:])

        def attn_B(h, us):
            kt, qt, vt = kt_t[us], qt_t[us], vt_t[us]
            items = [(i, j) for i in range(NB) for j in range(i + 1)]
            gate_of = {}

            def QK(w):
                i, j = items[w]
                xs = w % 3
                X = XS[xs]
                diag = (j == i)
                for m in range(4):
                    kc = kcol(j, m)
                    P.op("pe", lambda e, m=m, kc=kc: e.matmul(
                        out=X[:, m * 256:(m + 1) * 256], lhsT=kt[0:70, kc:kc + 128], rhs=qt[0:70, i * 256:(i + 1) * 256],
                        start=True, stop=not diag), reads=ktk(us) + qtk(us), writes=[("X", xs)])
                    if diag:
                        P.op("pe", lambda e, m=m: e.matmul(out=X[:, m * 256:(m + 1) * 256], lhsT=ident[:], rhs=mask_ab[:, m, :],
                                                           start=False, stop=True), writes=[("X", xs)])

            def EXP(w):
                es = w % 4
                P.op("act", lambda e: e.activation(out=E_t[es][:], in_=XS[w % 3][:], func=AF.Exp), reads=[("X", w % 3)], writes=[("E", es)])

            def PV(w):
                i, j = items[w]
                es = w % 4
                E = E_t[es]
                first, last = (j == 0), (j == i)
                PO = PA if i % 2 == 0 else PB
                pkey = "bankPA" if i % 2 == 0 else "bankPB"
                if first:
                    gate_of[i] = load_gate(512 + h * 64, 64, i, slot=i % 2)
                for m in range(4):
                    ch = vch(j, m)
                    P.op("pe", lambda e, m=m, ch=ch: e.matmul(out=PO[:, 0:256], lhsT=vt[:, ch, :], rhs=E[:, m * 256:(m + 1) * 256],
                                                              start=(first and m == 0), stop=(last and m == 3)),
                         reads=[("E", es)] + vkeys(us), writes=[pkey])
                if last:
                    g, gk = gate_of.pop(i)
                    P.op("dve", lambda e: e.reciprocal(out=r_t[0][0:64, :], in_=PO[64:128, 0:256]), reads=[pkey], writes=[("r", 0)])
                    P.op("dve", lambda e: e.tensor_tensor(out=o_t[0][0:64, :], in0=PO[0:64, 0:256], in1=r_t[0][0:64, :], op=ALU.mult),
                         reads=[pkey, ("r", 0)], writes=[("o", 0)])

                    def fin(ys, yk):
                        P.op("dve", lambda e: e.tensor_tensor(out=ys[0:64, :], in0=o_t[0][0:64, :], in1=g[0:64, :], op=ALU.mult),
                             reads=[("o", 0), gk], writes=[yk])
                    store_y(fin, 512 + h * 64, 64, i, None)

            W = len(items)
            for w0 in range(min(3, W)):
                QK(w0)
            for w in range(W):
                EXP(w)
                PV(w)
                if w + 3 < W:
                    QK(w + 3)

        def attn_C(h, us):
            kt, qt, vt = kt_t[us], qt_t[us], vt_t[us]
            items = [(i, j) for i in range(NB) for j in range(i, -1, -1)]
            W = len(items)

            def QK(w):
                i, j = items[w]
                xs = w % 3
                X = XS[xs]
                diag = (j == i)
                for m in range(4):
                    kc = kcol(j, m)
                    P.op("pe", lambda e, m=m, kc=kc: e.matmul(
                        out=X[:, m * 256:(m + 1) * 256], lhsT=kt[:, kc:kc + 128], rhs=qt[:, i * 256:(i + 1) * 256],
                        start=(m % 2 == 0), stop=True, skip_group_check=True), reads=ktk(us) + qtk(us), writes=[("X", xs)])
                    if diag:
                        P.op("pe", lambda e, m=m: e.matmul(out=X[:, m * 256:(m + 1) * 256], lhsT=ident[:], rhs=mask_c[:, m, :],
                                                           start=False, stop=True, skip_group_check=True), writes=[("X", xs)])

            def EA(w):
                xs = w % 3
                P.op("act", lambda e: e.activation(out=U_t[xs][:], in_=XS[xs][:], func=AF.Exp), reads=[("X", xs)], writes=[("U", xs)])
                P.op("act", lambda e: e.activation(out=Lp_t[xs][:], in_=U_t[xs][:], func=AF.Ln, bias=1.0, scale=1.0),
                     reads=[("U", xs)], writes=[("Lp", xs)])

            def TM(w):
                i, j = items[w]
                xs = w % 3
                X, Lp = XS[xs], Lp_t[xs]
                for m in range(4):
                    P.op("pe", lambda e, m=m: e.matmul(out=X[:, m * 256:(m + 1) * 256], lhsT=tneg[:], rhs=Lp[:, m * 256:(m + 1) * 256],
                                                       start=False, stop=(m == 3), skip_group_check=True),
                         reads=[("Lp", xs)], writes=[("X", xs)])
                    for m2 in range(m + 1, 4):
                        P.op("pe", lambda e, m=m, m2=m2: e.matmul(out=X[:, m * 256:(m + 1) * 256], lhsT=onesneg[:],
                                                                  rhs=Lp[:, m2 * 256:(m2 + 1) * 256], start=False, stop=True,
                                                                  skip_group_check=True), reads=[("Lp", xs)], writes=[("X", xs)])
                if j > 0:
                    for m in range(4):
                        P.op("pe", lambda e, m=m: e.matmul(out=PB[:, 0:256], lhsT=onesneg[:], rhs=Lp[:, m * 256:(m + 1) * 256],
                                                           start=(j == i and m == 0), stop=(m == 3), skip_group_check=True),
                             reads=[("Lp", xs)], writes=["bankPB"])
                    cs = w % 3
                    P.op("dve", lambda e: e.tensor_copy(out=cb_t[cs][:], in_=PB[:, 0:256]), reads=["bankPB"], writes=[("cb", cs)])

            def ADD(w):
                i, j = items[w]
                xs = w % 3
                Y = Y_t[xs]
                if j == i:
                    for hb_ in range(2):
                        P.op("dve", lambda e, hb_=hb_: e.tensor_copy(out=Y[:, hb_ * 512:(hb_ + 1) * 512], in_=XS[xs][:, hb_ * 512:(hb_ + 1) * 512]),
                             reads=[("X", xs)], writes=[("Y", xs)])
                else:
                    cs = (w - 1) % 3
                    for m in range(4):
                        P.op("dve", lambda e, m=m: e.tensor_tensor(out=Y[:, m * 256:(m + 1) * 256], in0=XS[xs][:, m * 256:(m + 1) * 256],
                                                                   in1=cb_t[cs][:], op=ALU.add),
                             reads=[("X", xs), ("cb", cs)], writes=[("Y", xs)])

            def EB(w):
                xs = w % 3
                es = w % 4
                P.op("act", lambda e: e.activation(out=E_t[es][:], in_=Y_t[xs][:], func=AF.Exp), reads=[("Y", xs)], writes=[("E", es)])

            def PV(w):
                i, j = items[w]
                es = w % 4
                E = E_t[es]
                first, last = (j == i), (j == 0)
                for m in range(4):
                    ch = vch(j, m)
                    P.op("pe", lambda e, m=m, ch=ch, E=E: e.matmul(out=PA[:, 0:256], lhsT=vt[:, ch, :], rhs=E[:, m * 256:(m + 1) * 256],
                                                                 start=(first and m == 0), stop=(last and m == 3)),
                         reads=[("E", es)] + vkeys(us), writes=["bankPA"])
                if last:
                    g, gk = load_gate(768 + h * 64, 64, i)

                    def fin(ys, yk):
                        P.op("dve", lambda e: e.tensor_tensor(out=ys[0:64, :], in0=PA[0:64, 0:256], in1=g[0:64, :], op=ALU.mult),
                             reads=["bankPA", gk], writes=[yk])
                    store_y(fin, 768 + h * 64, 64, i, None)

            for w0 in range(min(3, W)):
                QK(w0)
                EA(w0)
            TM(0)
            for w in range(W):
                ADD(w)
                EB(w)
                if w + 1 < W:
                    TM(w + 1)
                if w + 3 < W:
                    QK(w + 3)
                    EA(w + 3)
                PV(w)

        units = [("A", h) for h in range(4)] + [("B", h) for h in range(4)] + [("C", h) for h in range(4)]
        fns = {"A": attn_A, "B": attn_B, "C": attn_C}
        us_cur = load_unit(*units[0])
        for k_, (kind_, h_) in enumerate(units):
            us_next = load_unit(*units[k_ + 1]) if k_ + 1 < len(units) else None
            fns[kind_](h_, us_cur)
            us_cur = us_next
        P.full_barrier()
        A.reset(layer_mark)

        Wo = A.alloc("Wo", [128, 8, D], BF16)
        for c in range(8):
            P.op("pool", lambda e, c=c: e.dma_start(out=Wo[:, c, :], in_=pl["wo"][c * 128:(c + 1) * 128, :]), writes=[("Wo", c)], dma=True)
        g_o = A.alloc("g_o", [128, D], F32)
        b_o = A.alloc("b_o", [128, D], F32)
        P.op("sp", lambda e: e.dma_start(out=g_o[:], in_=pl["lng"][:, :]), writes=["g_o"], dma=True)
        P.op("sp", lambda e: e.dma_start(out=b_o[:], in_=pl["lnb"][:, :]), writes=["b_o"], dma=True)
        yt_t = [A.alloc("yt", [128, 8, 128], BF16) for _ in range(2)]
        hr_t = [A.alloc("hr", [128, D], F32) for _ in range(2)]
        z_t = [A.alloc("z", [128, D], F32) for _ in range(2)]
        ot_t = [A.alloc("ot", [128, D], F32) for _ in range(2)]
        st6 = A.alloc("st6b", [128, 12], F32)
        mv = A.alloc("mvb", [128, 4], F32)
        for T in range(OWN // 128):
            s_ = T % 2
            yt, hr, z, ot = yt_t[s_], hr_t[s_], z_t[s_], ot_t[s_]
            P.op("sp", lambda e, yt=yt, T=T: e.dma_start(out=yt[:], in_=YT[:, T * 128:(T + 1) * 128].rearrange("(c p) t -> p c t", p=128)),
                 writes=[("yt", s_)], dma=True)
            P.op("sp", lambda e, hr=hr, T=T: e.dma_start(out=hr[:], in_=HRES[T * 128:(T + 1) * 128, :]), writes=[("hr", s_)], dma=True)
            X = XS[s_]
            for half in range(2):
                for c in range(8):
                    P.op("pe", lambda e, X=X, half=half, c=c, yt=yt: e.matmul(
                        out=X[:, half * 512:(half + 1) * 512], lhsT=yt[:, c, :], rhs=Wo[:, c, half * 512:(half + 1) * 512],
                        start=(c == 0), stop=(c == 7)), reads=[("yt", s_), ("Wo", c)], writes=[("X", s_)])
            P.op("dve", lambda e, z=z, hr=hr, X=X: e.scalar_tensor_tensor(out=z[:], in0=hr[:], scalar=float(ALPHA), in1=X[:],
                                                                         op0=ALU.mult, op1=ALU.add),
                 reads=[("hr", s_), ("X", s_)], writes=[("z", s_)])
            layer_norm(z, ("z", s_), ot, ("ot", s_), g_o, b_o, ["g_o", "b_o"])
            o_ = P.op("pool", lambda e, ot=ot, T=T: e.dma_start(out=dest[T * 128:(T + 1) * 128, :], in_=ot[:]),
                      reads=[("ot", s_)], dma=True)
            if last_layer:
                out_stores.append(o_)
        P.full_barrier()

    P.final += out_stores
    P.emit()
    return nc


_BF = ml_dtypes.bfloat16


def _own_rows(g):
    return (np.arange(NB)[:, None] * 512 + g * 256 + np.arange(256)[None, :]).reshape(-1)


def _rope_tables(pos, scale):
    inv = ROPE_THETA ** (-np.arange(0, 16, 2, dtype=np.float32) / 16.0)
    ang = pos.astype(np.float32)[None, :] * inv[:, None].astype(np.float32)
    cos, sin = np.cos(ang), np.sin(ang)
    C = np.ones((128, len(pos)), np.float32)
    Sg = np.zeros((128, len(pos)), np.float32)
    for a in range(2):
        b0 = a * 64
        C[b0:b0 + 8] = cos
        C[b0 + 8:b0 + 16] = cos
        Sg[b0:b0 + 8] = -sin
        Sg[b0 + 8:b0 + 16] = sin
    return (C * scale).astype(np.float32), (Sg * scale).astype(np.float32)


def _swap_cols(cols512):
    idx = np.arange(512)
    out = idx.copy()
    for sh in range(8):
        b = sh * 64
        out[b:b + 8] = idx[b + 8:b + 16]
        out[b + 8:b + 16] = idx[b:b + 8]
    return cols512[out]


def _weight_layouts(w_in_l):
    o = {}
    pos = 0
    for name, n in (("Aq", 512), ("Ak", 512), ("Av", 512), ("Ag", 512), ("Bq", 256), ("Bk", 256), ("Bv", 256), ("Bf", 4),
                    ("Bg", 256), ("Cq", 256), ("Ck", 256), ("Cv", 256), ("Cg", 256)):
        o[name] = np.arange(pos, pos + n)
        pos += n
    kcols = np.concatenate([o["Ak"], _swap_cols(o["Ak"]), o["Bk"], o["Ck"], o["Bf"], o["Av"], o["Bv"], o["Cv"]])
    qcols = np.concatenate([o["Aq"], _swap_cols(o["Aq"]), o["Bq"], o["Cq"], o["Ag"], o["Bg"], o["Cg"]])
    assert len(kcols) == WKC and len(qcols) == WQC
    return np.ascontiguousarray(w_in_l[:, kcols]), np.ascontiguousarray(w_in_l[:, qcols])


def _masks(g):
    p = np.arange(128)[:, None, None]
    m = np.arange(4)[None, :, None]
    t = np.arange(256)[None, None, :]
    kpos = (m // 2) * 256 + (m % 2) * 128 + p
    qpos = g * 256 + t
    mab = np.where(kpos <= qpos, 0.0, MASKV).astype(np.float32)
    mc = np.where(kpos < qpos, 0.0, MASKV).astype(np.float32)
    return mab.astype(_BF), mc.astype(_BF)


_PROG_CACHE = {}


def _get_prog(layers):
    key = tuple(layers)
    if key not in _PROG_CACHE:
        lam_inits = {l: 0.8 - 0.6 * math.exp(-0.3 * l) for l in range(DEPTH)}
        _PROG_CACHE[key] = build_program(list(layers), lam_inits)
    return _PROG_CACHE[key]


def _consts(g):
    j = np.arange(128)[:, None]
    s_ = np.arange(128)[None, :]
    d = {}
    d["c_ident"] = np.eye(128, dtype=np.float32).astype(_BF)
    d["c_tneg"] = np.where(j >= s_, -1.0, 0.0).astype(np.float32).astype(_BF)
    d["c_onesneg"] = np.full((128, 128), -1.0, np.float32).astype(_BF)
    d["c_onesb"] = np.ones((128, 128), np.float32).astype(_BF)
    d["c_onesf"] = np.ones((128, 128), np.float32)
    gath = np.concatenate([_own_rows(0), _own_rows(1)])
    d["rk_cos"], d["rk_sin"] = _rope_tables(gath, 1.0)
    for p in range(2):
        gp = g if p == 0 else 1 - g
        d[f"c_mask_ab{p}"], d[f"c_mask_c{p}"] = _masks(gp)
        d[f"rq_cos{p}"], d[f"rq_sin{p}"] = _rope_tables(_own_rows(gp), 0.125)
        sel = np.zeros((4, 2), np.float32)
        sel[:, gp] = 1.0
        d[f"selg{p}"] = sel
        selw = np.zeros((128, 2), np.float32)
        selw[:, gp] = 1.0
        d[f"selw{p}"] = selw
    bl = np.zeros((128, 4), np.float32)
    for r in range(2):
        bl[:, 2 * r] = 1.0 if r == g else 0.0
        bl[:, 2 * r + 1] = 0.0 if r == g else 1.0
    d["blend"] = bl
    return d


def _layer_inputs(l, w_in, b_forget, lambda_q1, lambda_k1, lambda_q2, lambda_k2, subln_g, w_out, ln_g, ln_b):
    wk, wq = _weight_layouts(np.asarray(w_in[l], np.float32))
    rep = lambda v: np.ascontiguousarray(np.broadcast_to(np.asarray(v, np.float32)[None, :], (128, len(v))))
    lamv = np.stack([rep(lambda_q1[l]), rep(lambda_k1[l]), rep(lambda_q2[l]), rep(lambda_k2[l])], axis=1)
    return {
        f"wk{l}": wk, f"wq{l}": wq, f"wo{l}": np.ascontiguousarray(np.asarray(w_out[l], np.float32)),
        f"lng{l}": rep(ln_g[l]), f"lnb{l}": rep(ln_b[l]),
        f"bfg{l}": np.asarray(b_forget[l], np.float32).reshape(4, 1),
        f"lamv{l}": np.ascontiguousarray(lamv), f"subg{l}": np.asarray(subln_g[l], np.float32).reshape(128, 1),
    }


LAUNCH_PLAN = [[0, 1]]


def kernel(x, ln_in_g, ln_in_b, w_in, b_forget, lambda_q1, lambda_k1, lambda_q2, lambda_k2, subln_g, w_out, ln_g, ln_b):
    x = np.asarray(x, np.float32)
    B = x.shape[0]
    rep = lambda v: np.ascontiguousarray(np.broadcast_to(np.asarray(v, np.float32)[None, :], (128, len(v))))
    consts = [_consts(g) for g in range(2)]
    gath = np.concatenate([_own_rows(0), _own_rows(1)])
    h = x
    for layers in LAUNCH_PLAN:
        nc = _get_prog(layers)
        lay = {}
        for l in layers:
            lay.update(_layer_inputs(l, w_in, b_forget, lambda_q1, lambda_k1, lambda_q2, lambda_k2, subln_g, w_out, ln_g, ln_b))
        in_maps = []
        for core in range(8):
            b, g = core // 2, core % 2
            m = dict(consts[g])
            m.update(lay)
            m["hin_full"] = np.ascontiguousarray(h[b][gath])
            m["hin_own"] = np.ascontiguousarray(h[b][_own_rows(g)])
            m["hin_oth"] = np.ascontiguousarray(h[b][_own_rows(1 - g)])
            m["lng_in"] = rep(ln_in_g)
            m["lnb_in"] = rep(ln_in_b)
            in_maps.append(m)
        res = run_bass_kernel_spmd(nc, in_maps, core_ids=list(range(8)))
        hn = np.empty_like(x)
        for core in range(8):
            b, g = core // 2, core % 2
            hn[b][_own_rows(g)] = np.asarray(res.results[core]["out"], np.float32)
        h = hn
    return h
```

```python
import contextlib
import math

import numpy as np
import ml_dtypes

import concourse.bass as bass
import concourse.mybir as mybir
from concourse.bass_utils import run_bass_kernel_spmd

F32 = mybir.dt.float32
BF16 = mybir.dt.bfloat16
AF = mybir.ActivationFunctionType
ALU = mybir.AluOpType

S = 8192
D = 1024
OWN = 4096
NB = 16
DEPTH = 2
LN_EPS = 1e-5
SUBLN_EPS = 1e-5
ALPHA = (2 * DEPTH) ** 0.25
ROPE_THETA = 500000.0
MASKV = -30000.0
WKC = 2564
WQC = 2560

ENGS = ("pe", "act", "dve", "pool", "sp")
DEBUG = False
DUMP = False


class Op:
    __slots__ = ("eng", "fn", "deps", "marked", "sig", "dma", "dsem", "dval", "cc")

    def __init__(self, eng, fn, dma):
        self.eng = eng
        self.fn = fn
        self.deps = []
        self.marked = False
        self.sig = None
        self.dma = dma
        self.dsem = None
        self.dval = None
        self.cc = False


class _Rec:
    def __init__(self):
        self.call = None

    def __getattr__(self, name):
        def f(*a, **k):
            self.call = (name, a, k)
            return None
        return f

    def replay(self, eng):
        name, a, k = self.call
        return getattr(eng, name)(*a, **k)


class Prog:
    NDSEM = 24

    def __init__(self, nc):
        self.nc = nc
        self.ops = {e: [] for e in ENGS}
        self.last_writer = {}
        self.readers = {}
        self.dma_ops = {e: [] for e in ENGS}
        self.final = []

    def op(self, eng, fn, reads=(), writes=(), dma=False, extra=()):
        if fn is not None:
            rec = _Rec()
            fn(rec)
            assert rec.call is not None
            fn = rec.replay
        o = Op(eng, fn, dma)
        deps = []
        seen = set()

        def add(d):
            if d is None or id(d) in seen:
                return
            seen.add(id(d))
            deps.append(d)

        for b in reads:
            add(self.last_writer.get(b))
        for b in writes:
            add(self.last_writer.get(b))
            for r in self.readers.get(b, ()):
                add(r)
        for d in extra:
            add(d)
        if dma:
            lst = self.dma_ops[eng]
            n = len(lst)
            if n >= self.NDSEM:
                add(lst[n - self.NDSEM])
            o.dsem = n % self.NDSEM
            o.dval = 16 * (n // self.NDSEM + 1)
            lst.append(o)
        for d in deps:
            if d.dma:
                o.deps.append(d)
            elif d.eng == "pe" and eng == "pe" and not dma and fn is not None:
                continue
            else:
                d.marked = True
                o.deps.append(d)
        for b in reads:
            self.readers.setdefault(b, []).append(o)
        for b in writes:
            self.last_writer[b] = o
            self.readers[b] = []
        self.ops[eng].append(o)
        return o

    def barrier(self, eng, deps):
        return self.op(eng, None, extra=deps)

    def full_barrier(self):
        tails = []
        for e in ENGS:
            for o in reversed(self.ops[e]):
                if o.fn is not None and not o.dma:
                    tails.append(o)
                    break
        dmas = []
        for e in ENGS:
            dmas += self.dma_ops[e][-self.NDSEM:]
        for e in ENGS:
            self.barrier(e, tails + dmas)
        self.last_writer = {}
        self.readers = {}

    def emit(self):
        nc = self.nc
        with contextlib.ExitStack() as st:
            csem = {e: st.enter_context(nc.semaphore(f"c_{e}")) for e in ENGS}
            dsem = {
                e: [st.enter_context(nc.semaphore(f"d_{e}_{i}")) for i in range(self.NDSEM)]
                for e in ENGS if self.dma_ops[e]
            }
            for e in ENGS:
                c = 0
                for o in self.ops[e]:
                    if o.marked and not o.dma:
                        c += 1
                        o.sig = c
            block = st.enter_context(nc.Block())

            def run(e, engobj):
                seen = {}

                def wait(d):
                    if d.dma:
                        key = ("d", d.eng, d.dsem)
                        sem, val = dsem[d.eng][d.dsem], d.dval
                    else:
                        key = ("c", d.eng)
                        sem, val = csem[d.eng], d.sig
                    if seen.get(key, 0) >= val:
                        return
                    seen[key] = val
                    engobj.wait_ge(sem, val)

                for o in self.ops[e]:
                    for d in o.deps:
                        wait(d)
                    if o.fn is None:
                        continue
                    ins = o.fn(engobj)
                    if o.dma:
                        ins.then_inc(dsem[e][o.dsem], 16)
                    elif o.marked:
                        ins.then_inc(csem[e], 1)
                if e == "sp":
                    for d in self.final:
                        wait(d)

            @block.tensor
            def _(eng):
                run("pe", eng)

            @block.scalar
            def _(eng):
                run("act", eng)

            @block.vector
            def _(eng):
                run("dve", eng)

            @block.gpsimd
            def _(eng):
                run("pool", eng)

            @block.sync
            def _(eng):
                run("sp", eng)


SB_BASE = 16512
SB_TOP = 229376 - 1024


class Arena:
    def __init__(self, nc):
        self.nc = nc
        self.ptr = SB_BASE
        self.n = 0

    def mark(self):
        return self.ptr

    def reset(self, p):
        self.ptr = p

    def alloc(self, name, shape, dt):
        esz = 4 if dt == F32 else 2
        nbytes = esz
        for s in shape[1:]:
            nbytes *= s
        nbytes = (nbytes + 63) // 64 * 64
        off = self.ptr
        self.ptr += nbytes
        assert self.ptr <= SB_TOP, (name, self.ptr)
        self.n += 1
        return self.nc.alloc_sbuf_tensor_at(f"{name}_{self.n}", list(shape), dt, offset=off)


def build_program(layers, lam_inits):
    nc = bass.Bass("TRN2", target_bir_lowering=False)
    P = Prog(nc)
    A = Arena(nc)

    def din(name, shape, dt=F32):
        return nc.dram_tensor(name, list(shape), dt, kind="ExternalInput").ap()

    def dscr(name, shape, dt):
        if DEBUG:
            return nc.dram_tensor(name, list(shape), dt, kind="ExternalOutput").ap()
        return nc.dram_tensor(name, list(shape), dt).ap()

    L0 = layers[0]
    first_is_l0 = (L0 == 0)
    hin_full = din("hin_full", [S, D])
    hin_own = din("hin_own", [OWN, D])
    hin_oth = din("hin_oth", [OWN, D])
    blend = din("blend", [128, 4])
    lng_in = din("lng_in", [128, D])
    lnb_in = din("lnb_in", [128, D])
    rk_cos = din("rk_cos", [128, S])
    rk_sin = din("rk_sin", [128, S])
    rq_cos_p = [din(f"rq_cos{p}", [128, OWN]) for p in range(2)]
    rq_sin_p = [din(f"rq_sin{p}", [128, OWN]) for p in range(2)]
    selg_p = [din(f"selg{p}", [4, 2]) for p in range(2)]
    selw_p = [din(f"selw{p}", [128, 2]) for p in range(2)]
    c_ident = din("c_ident", [128, 128], BF16)
    c_tneg = din("c_tneg", [128, 128], BF16)
    c_onesneg = din("c_onesneg", [128, 128], BF16)
    c_onesb = din("c_onesb", [128, 128], BF16)
    c_onesf = din("c_onesf", [128, 128])
    c_mask_ab_p = [din(f"c_mask_ab{p}", [128, 4, 256], BF16) for p in range(2)]
    c_mask_c_p = [din(f"c_mask_c{p}", [128, 4, 256], BF16) for p in range(2)]
    per_layer = {}
    for l in layers:
        per_layer[l] = dict(
            wk=din(f"wk{l}", [D, WKC]), wq=din(f"wq{l}", [D, WQC]), wo=din(f"wo{l}", [D, D]),
            lng=din(f"lng{l}", [128, D]), lnb=din(f"lnb{l}", [128, D]),
            bfg=din(f"bfg{l}", [4, 1]), lamv=din(f"lamv{l}", [128, 4, 64]), subg=din(f"subg{l}", [128, 1]),
        )
    out = nc.dram_tensor("out", [OWN, D], F32, kind="ExternalOutput").ap()

    KTA = dscr("KTA", [4, 128, S], BF16)
    KTB = dscr("KTB", [4, 70, S], BF16)
    KTC = dscr("KTC", [4, 64, S], BF16)
    VS = dscr("VS", [S, 1024], BF16)
    QTA = dscr("QTA", [4, 128, OWN], BF16)
    QTB = dscr("QTB", [4, 70, OWN], BF16)
    QTC = dscr("QTC", [4, 64, OWN], BF16)
    GT = dscr("GT", [1024, OWN], F32)
    HRES = dscr("HRES", [OWN, D], F32)
    YT = dscr("YT", [1024, OWN], BF16)
    NLF = dscr("NLF", [4, S], F32)
    CN = dscr("CN", [4, S], F32)
    HP = [dscr(f"HP{p}", [OWN, D], F32) for p in range(2)]

    if DEBUG == "C0":
        DBG_U = nc.dram_tensor("DBG_U", [128, 1024], F32, kind="ExternalOutput").ap()
        DBG_L = nc.dram_tensor("DBG_L", [128, 1024], BF16, kind="ExternalOutput").ap()
        DBG_X = nc.dram_tensor("DBG_X", [128, 1024], F32, kind="ExternalOutput").ap()
    X0 = nc.alloc_psum_tensor("X0", [128, 1024], F32)
    X1 = nc.alloc_psum_tensor("X1", [128, 1024], F32)
    X2 = nc.alloc_psum_tensor("X2", [128, 1024], F32)
    PA = nc.alloc_psum_tensor("PA", [128, 512], F32)
    PB = nc.alloc_psum_tensor("PB", [128, 512], F32)
    XS = [X0, X1, X2]

    ident = A.alloc("ident", [128, 128], BF16)
    tneg = A.alloc("tneg", [128, 128], BF16)
    onesneg = A.alloc("onesneg", [128, 128], BF16)
    onesb = A.alloc("onesb", [128, 128], BF16)
    onesf = A.alloc("onesf", [128, 128], F32)
    mask_ab = A.alloc("mask_ab", [128, 4, 256], BF16)
    mask_c = A.alloc("mask_c", [128, 4, 256], BF16)
    selg_sb = A.alloc("selg", [4, 2], F32)
    lam_t = A.alloc("lam", [128, 8], F32)
    subg_t = A.alloc("subg", [128, 2], F32)
    negb_t = A.alloc("negb", [4, 2], F32)
    for dst, src, k in ((ident, c_ident, "ident"), (tneg, c_tneg, "tneg"), (onesneg, c_onesneg, "onesneg"),
                        (onesb, c_onesb, "onesb"), (onesf, c_onesf, "onesf")):
        P.op("sp", lambda e, dst=dst, src=src: e.dma_start(out=dst[:], in_=src[:, :]), writes=[k], dma=True)
    blend_sb = A.alloc("blend", [128, 4], F32)
    P.op("sp", lambda e: e.dma_start(out=blend_sb[:], in_=blend[:, :]), writes=["blend"], dma=True)
    persist_mark = A.mark()
    CONST_KEYS = ["ident", "tneg", "onesneg", "onesb", "onesf", "mask_ab", "mask_c", "selg"]

    def reload_const_keys():
        pass

    out_stores = []

    fused = len(layers) > 1
    schedule = []
    for li, l in enumerate(layers):
        npass = 2 if (fused and li < len(layers) - 1) else 1
        for p_ in range(npass):
            schedule.append((li, l, p_))
    for (li, l, pss) in schedule:
        pl = per_layer[l]
        lam_init = lam_inits[l]
        do_ln_in = (l == 0)
        last_layer = (li == len(layers) - 1)
        do_k = (pss == 0)
        from_hp = (li > 0)
        src_own = (HP[0] if from_hp else (hin_own if pss == 0 else hin_oth))
        dest = out if last_layer else HP[pss]
        rq_cos, rq_sin = rq_cos_p[pss], rq_sin_p[pss]
        A.reset(persist_mark)
        P.op("sp", lambda e: e.dma_start(out=mask_ab[:], in_=c_mask_ab_p[pss][:, :, :]), writes=["mask_ab"], dma=True)
        P.op("sp", lambda e: e.dma_start(out=mask_c[:], in_=c_mask_c_p[pss][:, :, :]), writes=["mask_c"], dma=True)
        P.op("sp", lambda e: e.dma_start(out=selg_sb[:], in_=selg_p[pss][:, :]), writes=["selg"], dma=True)

        lamv_sb = A.alloc("lamv", [128, 4, 64], F32)
        lprod = A.alloc("lprod", [128, 2, 64], F32)
        P.op("sp", lambda e: e.dma_start(out=lamv_sb[:], in_=pl["lamv"][:, :, :]), writes=["lamv"], dma=True)
        P.op("sp", lambda e: e.dma_start(out=subg_t[:, 0:1], in_=pl["subg"][:, :]), writes=["subg0"], dma=True)
        P.op("sp", lambda e: e.dma_start(out=negb_t[:, 0:1], in_=pl["bfg"][:, :]), writes=["negb0"], dma=True)
        P.op("dve", lambda e: e.tensor_tensor(out=lprod[:, 0, :], in0=lamv_sb[:, 0, :], in1=lamv_sb[:, 1, :], op=ALU.mult),
             reads=["lamv"], writes=["lprod"])
        P.op("dve", lambda e: e.tensor_tensor(out=lprod[:, 1, :], in0=lamv_sb[:, 2, :], in1=lamv_sb[:, 3, :], op=ALU.mult),
             reads=["lamv", "lprod"], writes=["lprod"])
        P.op("dve", lambda e: e.reduce_sum(out=lam_t[:, 0:1], in_=lprod[:, 0, :], axis=mybir.AxisListType.X),
             reads=["lprod"], writes=["lam01"])
        P.op("dve", lambda e: e.reduce_sum(out=lam_t[:, 1:2], in_=lprod[:, 1, :], axis=mybir.AxisListType.X),
             reads=["lprod", "lam01"], writes=["lam01"])
        P.op("act", lambda e: e.activation(out=lam_t[:, 2:4], in_=lam_t[:, 0:2], func=AF.Exp), reads=["lam01"], writes=["lam23"])
        P.op("dve", lambda e: e.scalar_tensor_tensor(out=lam_t[:, 4:5], in0=lam_t[:, 3:4], scalar=-float(lam_init),
                                                     in1=lam_t[:, 2:3], op0=ALU.add, op1=ALU.subtract),
             reads=["lam23"], writes=["neglam"])
        P.op("dve", lambda e: e.tensor_scalar(out=subg_t[:, 1:2], in0=subg_t[:, 0:1], scalar1=float((1.0 - lam_init) * math.sqrt(128.0)),
                                              scalar2=None, op0=ALU.mult), reads=["subg0"], writes=["gsub"])
        P.op("dve", lambda e: e.tensor_scalar(out=negb_t[:, 1:2], in0=negb_t[:, 0:1], scalar1=-1.0, scalar2=None, op0=ALU.mult),
             reads=["negb0"], writes=["negb"])
        neglam = lam_t[:, 4:5]
        gsub = subg_t[:, 1:2]
        negb = negb_t[:, 1:2]
        layer_mark = A.mark()

        Wk = A.alloc("Wk", [128, 8, WKC], BF16)
        Wq = A.alloc("Wq", [128, 8, WQC], BF16)
        if do_k:
            for c in range(8):
                P.op("pool", lambda e, c=c: e.dma_start(out=Wk[:, c, :], in_=pl["wk"][c * 128:(c + 1) * 128, :]), writes=[("Wk", c)], dma=True)
        for c in range(8):
            P.op("pool", lambda e, c=c: e.dma_start(out=Wq[:, c, :], in_=pl["wq"][c * 128:(c + 1) * 128, :]), writes=[("Wq", c)], dma=True)
        WK_KEYS = [("Wk", c) for c in range(8)]
        WQ_KEYS = [("Wq", c) for c in range(8)]
        if do_ln_in:
            g_in = A.alloc("g_in", [128, D], F32)
            b_in = A.alloc("b_in", [128, D], F32)
            P.op("sp", lambda e: e.dma_start(out=g_in[:], in_=lng_in[:, :]), writes=["g_in"], dma=True)
            P.op("sp", lambda e: e.dma_start(out=b_in[:], in_=lnb_in[:, :]), writes=["b_in"], dma=True)
        xin_t = [A.alloc("xin", [128, D], F32) for _ in range(2)]
        hf_t = [A.alloc("hf", [128, D], F32) for _ in range(2)]
        hb_t = [A.alloc("hb", [128, D], BF16) for _ in range(2)]
        hT_t = [A.alloc("hT", [128, 8, 512], BF16) for _ in range(2)]
        rc_t = [A.alloc("rc", [128, 512], F32) for _ in range(2)]
        rs_t = [A.alloc("rs", [128, 512], F32) for _ in range(2)]
        tmp_t = [A.alloc("tmp", [128, 512], F32) for _ in range(2)]
        stb_t = [A.alloc("stb", [128, 512], BF16) for _ in range(4)]
        stf_t = [A.alloc("stf", [128, 512], F32) for _ in range(2)]
        st6 = A.alloc("st6", [128, 12], F32)
        mv = A.alloc("mv", [128, 4], F32)
        accs = [(X0, 0), (X0, 512), (X1, 0), (X1, 512), (X2, 0), (X2, 512)]
        cnt = dict(tile=0, acc=0, stb=0, stf=0, tmp=0, grp=0)
        stores1 = []

        def layer_norm(src, skey, dst, dkey, gt, bt, gkeys):
            for hh in range(2):
                P.op("dve", lambda e, hh=hh: e.bn_stats(out=st6[:, hh * 6:(hh + 1) * 6], in_=src[:, hh * 512:(hh + 1) * 512]),
                     reads=[skey], writes=[("st6", hh)])
            P.op("dve", lambda e: e.bn_aggr(out=mv[:, 0:2], in_=st6[:, 0:12]), reads=[("st6", 0), ("st6", 1)], writes=["mv"])
            P.op("act", lambda e: e.activation(out=mv[:, 3:4], in_=mv[:, 1:2], func=AF.Ln, bias=float(LN_EPS), scale=1.0),
                 reads=["mv"], writes=["lnv"])
            P.op("act", lambda e: e.activation(out=mv[:, 2:3], in_=mv[:, 3:4], func=AF.Exp, scale=-0.5), reads=["lnv"], writes=["rstd"])
            P.op("dve", lambda e: e.tensor_scalar(out=dst[:], in0=src[:], scalar1=mv[:, 0:1], scalar2=mv[:, 2:3],
                                                  op0=ALU.subtract, op1=ALU.mult), reads=[skey, "mv", "rstd"], writes=[dkey])
            P.op("dve", lambda e: e.tensor_tensor(out=dst[:], in0=dst[:], in1=gt[:], op=ALU.mult), reads=[dkey, gkeys[0]], writes=[dkey])
            P.op("dve", lambda e: e.tensor_tensor(out=dst[:], in0=dst[:], in1=bt[:], op=ALU.add), reads=[dkey, gkeys[1]], writes=[dkey])

        def next_acc():
            a = accs[cnt["acc"] % len(accs)]
            key = ("acc", cnt["acc"] % len(accs))
            cnt["acc"] += 1
            return a[0], a[1], key

        def next_stb():
            i = cnt["stb"] % 4
            cnt["stb"] += 1
            return stb_t[i], ("stb", i)

        def next_stf():
            i = cnt["stf"] % 2
            cnt["stf"] += 1
            return stf_t[i], ("stf", i)

        def proj_pass(src, ngroups, mode):
            W = Wk if mode == "k" else Wq
            WKEYS = WK_KEYS if mode == "k" else WQ_KEYS
            rcos = rk_cos if mode == "k" else rq_cos
            rsin = rk_sin if mode == "k" else rq_sin
            gslot = {}

            def prep_begin(G):
                gs = cnt["grp"] % 2
                cnt["grp"] += 1
                gslot[G] = gs
                rc, rs_ = rc_t[gs], rs_t[gs]
                P.op("sp", lambda e: e.dma_start(out=rc[:], in_=rcos[:, G * 512:(G + 1) * 512]), writes=[("rc", gs)], dma=True)
                P.op("sp", lambda e: e.dma_start(out=rs_[:], in_=rsin[:, G * 512:(G + 1) * 512]), writes=[("rs", gs)], dma=True)

            tstate = {}

            def prep_load(G, tt):
                gs = gslot[G]
                ts_ = cnt["tile"] % 2
                cnt["tile"] += 1
                row0 = G * 512 + tt * 128
                xin = xin_t[ts_]
                tstate[(G, tt)] = ts_
                if src == "blend":
                    idx = row0 % OWN
                    t1 = hf_t[ts_]
                    P.op("sp", lambda e: e.dma_start(out=xin[:], in_=HP[0][idx:idx + 128, :]), writes=[("xin", ts_)], dma=True)
                    P.op("sp", lambda e: e.dma_start(out=t1[:], in_=HP[1][idx:idx + 128, :]), writes=[("hf", ts_)], dma=True)
                else:
                    P.op("sp", lambda e: e.dma_start(out=xin[:], in_=src[row0:row0 + 128, :]), writes=[("xin", ts_)], dma=True)

            def prep_norm(G, tt):
                ts_ = tstate[(G, tt)]
                row0 = G * 512 + tt * 128
                xin = xin_t[ts_]
                if src == "blend":
                    rr = row0 // OWN
                    t1 = hf_t[ts_]
                    P.op("dve", lambda e: e.tensor_scalar(out=xin[:], in0=xin[:], scalar1=blend_sb[:, 2 * rr:2 * rr + 1], scalar2=None,
                                                          op0=ALU.mult), reads=[("xin", ts_)], writes=[("xin", ts_)])
                    P.op("dve", lambda e: e.scalar_tensor_tensor(out=xin[:], in0=t1[:], scalar=blend_sb[:, 2 * rr + 1:2 * rr + 2], in1=xin[:],
                                                                 op0=ALU.mult, op1=ALU.add),
                         reads=[("xin", ts_), ("hf", ts_)], writes=[("xin", ts_)])
                if do_ln_in:
                    hf = hf_t[ts_]
                    hfk = ("hf", ts_)
                    layer_norm(xin, ("xin", ts_), hf, hfk, g_in, b_in, ["g_in", "b_in"])
                else:
                    hf, hfk = xin, ("xin", ts_)
                if mode == "q":
                    stores1.append(P.op("pool", lambda e: e.dma_start(out=HRES[row0:row0 + 128, :], in_=hf[:]), reads=[hfk], dma=True))
                hb = hb_t[ts_]
                P.op("act", lambda e: e.copy(out=hb[:], in_=hf[:]), reads=[hfk], writes=[("hb", ts_)])

            def prep_tr(G, tt):
                gs = gslot[G]
                hT = hT_t[gs]
                ts_ = tstate[(G, tt)]
                hb = hb_t[ts_]
                for half in range(2):
                    ps, o0, akey = next_acc()
                    for c4 in range(4):
                        c = half * 4 + c4
                        P.op("pe", lambda e, c4=c4, c=c: e.matmul(
                            out=ps[:, o0 + c4 * 128:o0 + (c4 + 1) * 128], lhsT=hb[:, c * 128:(c + 1) * 128], rhs=ident[:],
                            start=True, stop=True), reads=[("hb", ts_), "ident"], writes=[akey])
                    P.op("dve", lambda e: e.tensor_copy(
                        out=hT[:, half * 4:(half + 1) * 4, tt * 128:(tt + 1) * 128],
                        in_=ps[:, o0:o0 + 512].rearrange("p (c t) -> p c t", c=4)),
                        reads=[akey], writes=[("hT", gs, tt, half)])

            def chunks(G):
                gs = gslot[G]
                hT = hT_t[gs]
                hTkeys = [("hT", gs, tt_, hf_) for tt_ in range(4) for hf_ in range(2)]
                rc, rs_ = rc_t[gs], rs_t[gs]

                def fm_chunk(col0, M):
                    ps, o0, akey = next_acc()
                    for c in range(8):
                        P.op("pe", lambda e, c=c: e.matmul(
                            out=ps[0:M, o0:o0 + 512], lhsT=W[:, c, col0:col0 + M], rhs=hT[:, c, :], start=(c == 0), stop=(c == 7)),
                            reads=hTkeys + [WKEYS[c]], writes=[akey])
                    return ps, o0, akey

                tok0 = G * 512
                for i in range(4):
                    p1, o1, k1 = fm_chunk(i * 128, 128)
                    p2, o2, k2 = fm_chunk(512 + i * 128, 128)
                    t1 = tmp_t[0]
                    t2 = tmp_t[1]
                    P.op("dve", lambda e: e.tensor_tensor(out=t1[:], in0=p1[:, o1:o1 + 512], in1=rc[:], op=ALU.mult),
                         reads=[k1, ("rc", gs)], writes=[("tmp", 0)])
                    P.op("dve", lambda e: e.tensor_tensor(out=t2[:], in0=p2[:, o2:o2 + 512], in1=rs_[:], op=ALU.mult),
                         reads=[k2, ("rs", gs)], writes=[("tmp", 1)])
                    sb_, sk = next_stb()
                    P.op("dve", lambda e: e.tensor_tensor(out=sb_[:], in0=t1[:], in1=t2[:], op=ALU.add),
                         reads=[("tmp", 0), ("tmp", 1)], writes=[sk])
                    dstT = KTA if mode == "k" else QTA
                    stores1.append(P.op("pool", lambda e: e.dma_start(out=dstT[i, :, tok0:tok0 + 512], in_=sb_[:]), reads=[sk], dma=True))
                    yield
                for kind, cbase, dstT in (("B", 1024, KTB if mode == "k" else QTB), ("C", 1280, KTC if mode == "k" else QTC)):
                    for h in range(4):
                        ps, o0, akey = fm_chunk(cbase + h * 64, 64)
                        sb_, sk = next_stb()
                        if mode == "k":
                            P.op("act", lambda e: e.copy(out=sb_[0:64, :], in_=ps[0:64, o0:o0 + 512]), reads=[akey], writes=[sk])
                        else:
                            P.op("act", lambda e: e.mul(out=sb_[0:64, :], in_=ps[0:64, o0:o0 + 512], mul=0.125), reads=[akey], writes=[sk])
                        stores1.append(P.op("pool", lambda e: e.dma_start(out=dstT[h, 0:64, tok0:tok0 + 512], in_=sb_[0:64, :]),
                                            reads=[sk], dma=True))
                        yield
                if mode == "k":
                    ps, o0, akey = fm_chunk(1536, 4)
                    sf, sfk = next_stf()
                    P.op("act", lambda e: e.activation(out=sf[0:4, :], in_=ps[0:4, o0:o0 + 512], func=AF.Exp, bias=negb, scale=-1.0),
                         reads=[akey, "negb"], writes=[sfk])
                    P.op("act", lambda e: e.activation(out=sf[0:4, :], in_=sf[0:4, :], func=AF.Ln, bias=1.0, scale=1.0),
                         reads=[sfk], writes=[sfk])
                    r = G // 8
                    for bb in range(2):
                        i_blk = (G % 8) * 2 + bb
                        t0 = i_blk * 512 + r * 256
                        stores1.append(P.op("pool", lambda e, bb=bb, t0=t0: e.dma_start(
                            out=NLF[:, t0:t0 + 256], in_=sf[0:4, bb * 256:(bb + 1) * 256]), reads=[sfk], dma=True))
                    yield
                    for tt in range(4):
                        for half in range(2):
                            ps, o0, akey = next_acc()
                            for c in range(8):
                                P.op("pe", lambda e, c=c: e.matmul(
                                    out=ps[:, o0:o0 + 512], lhsT=hT[:, c, tt * 128:(tt + 1) * 128],
                                    rhs=W[:, c, 1540 + half * 512:1540 + (half + 1) * 512], start=(c == 0), stop=(c == 7)),
                                    reads=hTkeys + [WKEYS[c]], writes=[akey])
                            sb_, sk = next_stb()
                            P.op("act", lambda e: e.copy(out=sb_[:], in_=ps[:, o0:o0 + 512]), reads=[akey], writes=[sk])
                            r0 = tok0 + tt * 128
                            stores1.append(P.op("pool", lambda e: e.dma_start(
                                out=VS[r0:r0 + 128, half * 512:(half + 1) * 512], in_=sb_[:]), reads=[sk], dma=True))
                            yield
                else:
                    for gch in range(8):
                        ps, o0, akey = fm_chunk(1536 + gch * 128, 128)
                        sf, sfk = next_stf()
                        P.op("act", lambda e: e.activation(out=sf[:], in_=ps[:, o0:o0 + 512], func=AF.Silu), reads=[akey], writes=[sfk])
                        stores1.append(P.op("pool", lambda e: e.dma_start(
                            out=GT[gch * 128:(gch + 1) * 128, tok0:tok0 + 512], in_=sf[:]), reads=[sfk], dma=True))
                        yield

            prep_begin(0)
            for tt in range(4):
                prep_load(0, tt)
                prep_norm(0, tt)
                prep_tr(0, tt)
            ev_load = {0: 0, 3: 1, 8: 2, 13: 3}
            ev_norm = {1: 0, 6: 1, 11: 2, 16: 3}
            ev_tr = {5: 0, 10: 1, 15: 2, 19: 3}
            for G in range(ngroups):
                more = (G + 1 < ngroups)
                done = set()

                def fire(idx):
                    if not more:
                        return
                    for ev, fn, tag in ((ev_load, prep_load, "l"), (ev_norm, prep_norm, "n"), (ev_tr, prep_tr, "t")):
                        if idx in ev and (tag, ev[idx]) not in done:
                            done.add((tag, ev[idx]))
                            fn(G + 1, ev[idx])
                if more:
                    prep_begin(G + 1)
                fire(0)
                idx = 0
                for _ in chunks(G):
                    idx += 1
                    fire(idx)
                for k in range(idx + 1, 24):
                    fire(k)

        if do_k:
            proj_pass("blend" if from_hp else hin_full, 16, "k")
        proj_pass(src_own, 8, "q")
        P.full_barrier()
        A.reset(layer_mark)

        nlf = A.alloc("nlf", [4, S], F32)
        cn = A.alloc("cn", [4, S], F32)
        kk = A.alloc("kk", [128, 256], F32)
        pkw = [A.alloc("pkw", [128, 256], BF16) for _ in range(3)]
        cqw = [A.alloc("cqw", [64, 256], F32) for _ in range(2)]
        pqw = [A.alloc("pqw", [64, 256], BF16) for _ in range(3)]
        ones_w = A.alloc("ones_w", [128, 256], BF16)
        selw = A.alloc("selw", [128, 2], F32)
        first_pass = (li == 0 and pss == 0)
        P.op("sp", lambda e: e.dma_start(out=selw[:], in_=selw_p[pss][:, :]), writes=["selw"], dma=True)
        if first_pass:
            P.op("pool", lambda e: e.memset(ones_w[:], 1.0), writes=["ones_w"])
        if do_k:
            P.op("sp", lambda e: e.dma_start(out=nlf[:], in_=NLF[:, :]), writes=["nlf"], dma=True)
            P.op("dve", lambda e: e.tensor_tensor_scan(out=cn[:], data0=nlf[:], data1=nlf[:], initial=0.0, op0=ALU.add, op1=ALU.max),
                 reads=["nlf"], writes=["cn"])
            st_cn = P.op("sp", lambda e: e.dma_start(out=CN[:, :], in_=cn[:]), reads=["cn"], dma=True)
            P.barrier("sp", [st_cn])
        CNv = CN.rearrange("h (i r t) -> h r i t", i=16, r=2, t=256)
        for h in range(4):
            for r in range(2):
                if do_k:
                    P.op("sp", lambda e, h=h, r=r: e.dma_start(out=kk[h * 32 + r * 16:h * 32 + (r + 1) * 16, :], in_=CNv[h, r, :, :]),
                         writes=[("kk", h, r)], dma=True)
                P.op("sp", lambda e, h=h, r=r: e.dma_start(out=cqw[r][h * 16:(h + 1) * 16, :], in_=CNv[h, r, :, :]),
                     writes=[("cqw", r, h)], dma=True)
        KK_KEYS = [("kk", h, r) for h in range(4) for r in range(2)]

        def split3(src, skeys, pieces, pkey):
            for p_ in range(3):
                P.op("dve", lambda e, p_=p_: e.tensor_copy(out=pieces[p_][:], in_=src[:]), reads=skeys, writes=[(pkey, p_)])
                if p_ < 2:
                    P.op("dve", lambda e, p_=p_: e.tensor_tensor(out=src[:], in0=src[:], in1=pieces[p_][:], op=ALU.subtract),
                         reads=skeys + [(pkey, p_)], writes=[skeys[0]])

        if do_k:
            split3(kk, KK_KEYS, pkw, "pkw")
        CQ0 = [("cqw", 0, h) for h in range(4)]
        CQ1 = [("cqw", 1, h) for h in range(4)]
        P.op("dve", lambda e: e.tensor_scalar(out=cqw[0][:], in0=cqw[0][:], scalar1=selw[0:64, 0:1], scalar2=-1.0, op0=ALU.mult, op1=ALU.mult),
             reads=CQ0 + ["selw"], writes=[CQ0[0]])
        P.op("dve", lambda e: e.tensor_scalar(out=cqw[1][:], in0=cqw[1][:], scalar1=selw[0:64, 1:2], scalar2=-1.0, op0=ALU.mult, op1=ALU.mult),
             reads=CQ1 + ["selw"], writes=[CQ1[0]])
        P.op("dve", lambda e: e.tensor_tensor(out=cqw[0][:], in0=cqw[0][:], in1=cqw[1][:], op=ALU.add),
             reads=CQ0 + CQ1, writes=[CQ0[0]])
        split3(cqw[0], CQ0, pqw, "pqw")
        bst = []
        for h in range(4):
            for p_ in range(3):
                if do_k:
                    bst.append(P.op("sp", lambda e, h=h, p_=p_: e.dma_start(
                        out=KTB[h, 67 + p_, :].rearrange("(b t) -> b t", t=256), in_=pkw[p_][h * 32:(h + 1) * 32, :]),
                        reads=[("pkw", p_)], dma=True))
                bst.append(P.op("sp", lambda e, h=h, p_=p_: e.dma_start(
                    out=QTB[h, 64 + p_, :].rearrange("(b t) -> b t", t=256), in_=pqw[p_][h * 16:(h + 1) * 16, :]),
                    reads=[("pqw", p_)], dma=True))
                if first_pass:
                    bst.append(P.op("sp", lambda e, h=h, p_=p_: e.dma_start(
                        out=KTB[h, 64 + p_, :].rearrange("(b t) -> b t", t=256), in_=ones_w[0:32, :]), reads=["ones_w"], dma=True))
                    bst.append(P.op("sp", lambda e, h=h, p_=p_: e.dma_start(
                        out=QTB[h, 67 + p_, :].rearrange("(b t) -> b t", t=256), in_=ones_w[0:16, :]), reads=["ones_w"], dma=True))
        P.full_barrier()
        A.reset(layer_mark)

        kt_t = [A.alloc("kt", [128, S], BF16) for _ in range(2)]
        qt_t = [A.alloc("qt", [128, OWN], BF16) for _ in range(2)]
        vt_t = [A.alloc("vt", [128, 64, 128], BF16) for _ in range(2)]
        qz_t = [A.alloc("qz", [128, 2, OWN], BF16) for _ in range(2)]
        for us_ in range(2):
            P.op("pool", lambda e: e.memset(qz_t[us_][64:128, 0, :], 0.0), writes=[("qz0", us_)])
            P.op("pool", lambda e: e.memset(qz_t[us_][0:64, 1, :], 0.0), writes=[("qz1", us_)])
        E_t = [A.alloc("E", [128, 1024], BF16) for _ in range(4)]
        U_t = [A.alloc("U", [128, 1024], F32) for _ in range(3)]
        Lp_t = [A.alloc("Lp", [128, 1024], BF16) for _ in range(3)]
        Y_t = [A.alloc("Y", [128, 1024], F32) for _ in range(3)]
        cb_t = [A.alloc("cb", [128, 256], F32) for _ in range(3)]
        gate_t = [A.alloc("gate", [128, 256], F32) for _ in range(2)]
        r_t = [A.alloc("r", [128, 256], F32) for _ in range(2)]
        o_t = [A.alloc("o", [128, 256], F32) for _ in range(3)]
        sq_t = A.alloc("sq", [128, 256], F32)
        rstd_t = A.alloc("rstd", [128, 256], F32)
        ys_t = [A.alloc("ys", [128, 256], BF16) for _ in range(2)]
        Es_t = [A.alloc("Es", [128, 1024], F32) for _ in range(2)]
        ystores = []
        ucount = [0]
        ycount = [0]

        def kcol(j, m):
            return (m // 2) * OWN + j * 256 + (m % 2) * 128

        def vch(j, m):
            return (m // 2) * 32 + j * 2 + (m % 2)

        def ktk(us):
            return [("kt", us, 0), ("kt", us, 1), ("kt", us, "z")]

        def qtk(us):
            return [("qt", us, 0), ("qt", us, 1), ("qt", us, "z"), ("qz0", us), ("qz1", us)]

        def load_unit(kind, h):
            us = ucount[0] % 2
            ucount[0] += 1
            kt, qt, vt = kt_t[us], qt_t[us], vt_t[us]
            rows = {"A": 128, "B": 70, "C": 64}[kind]
            KT = {"A": KTA, "B": KTB, "C": KTC}[kind]
            QT = {"A": QTA, "B": QTB, "C": QTC}[kind]
            if kind == "C":
                P.op("pool", lambda e: e.memset(kt[64:128, :], 0.0), writes=[("kt", us, "z")])
                P.op("pool", lambda e: e.memset(qt[64:128, :], 0.0), writes=[("qt", us, "z")])
            for r in range(2):
                P.op("pool", lambda e, r=r: e.dma_start(out=kt[0:rows, r * OWN:(r + 1) * OWN], in_=KT[h, :, r * OWN:(r + 1) * OWN]),
                     writes=[("kt", us, r)], dma=True)
            if kind == "A":
                qz = qz_t[us]
                P.op("pool", lambda e: e.dma_start(out=qz[0:64, 0, :], in_=QT[h, 0:64, :]), writes=[("qt", us, 0)], dma=True)
                P.op("pool", lambda e: e.dma_start(out=qz[64:128, 1, :], in_=QT[h, 64:128, :]), writes=[("qt", us, 1)], dma=True)
            else:
                P.op("pool", lambda e: e.dma_start(out=qt[0:rows, :], in_=QT[h, :, :]), writes=[("qt", us, 0)], dma=True)
            if kind == "A":
                c0, cw = h * 128, 128
            elif kind == "B":
                c0, cw = 512 + h * 64, 64
            else:
                c0, cw = 768 + h * 64, 64
            if kind == "B":
                P.op("pool", lambda e: e.memset(vt[:, :, 64:128], 1.0), writes=[("vt", us)])
            for q8 in range(8):
                P.op("pool", lambda e, q8=q8: e.dma_start(
                    out=vt[:, q8 * 8:(q8 + 1) * 8, 0:cw],
                    in_=VS[q8 * 1024:(q8 + 1) * 1024, c0:c0 + cw].rearrange("(c p) w -> p c w", p=128)),
                    writes=[("vt", us, q8)], reads=[("vt", us)], dma=True)
            return us

        def vkeys(us):
            return [("vt", us)] + [("vt", us, q8) for q8 in range(8)]

        def load_gate(row0, nrows, i, slot=None):
            gs = (ycount[0] % 2) if slot is None else slot
            g = gate_t[gs]
            P.op("sp", lambda e: e.dma_start(out=g[0:nrows, :], in_=GT[row0:row0 + nrows, i * 256:(i + 1) * 256]),
                 writes=[("gate", gs)], dma=True)
            return g, ("gate", gs)

        def store_y(ysrc_fn, row0, nrows, i, reads):
            ys = ys_t[ycount[0] % 2]
            yk = ("ys", ycount[0] % 2)
            ycount[0] += 1
            ysrc_fn(ys, yk)
            ystores.append(P.op("sp", lambda e: e.dma_start(out=YT[row0:row0 + nrows, i * 256:(i + 1) * 256], in_=ys[0:nrows, :]),
                                reads=[yk], dma=True))

        def attn_A(h, us):
            kt, qt, vt = kt_t[us], qt_t[us], vt_t[us]
            items = [(i, j, sub) for i in range(NB) for j in range(i + 1) for sub in (0, 1)]
            OP = [(PA, 0), (PA, 256)]
            LP = [(PB, 0), (PB, 256)]
            pending = []
            gate_of = {}
            lsb_t, rsb_t = Es_t[0], Es_t[1]
            x2r = []

            def QK(w):
                i, j, sub = items[w]
                xs = sub
                X = XS[xs]
                r0 = sub * 64
                diag = (j == i)
                for m in range(4):
                    kc = kcol(j, m)
                    P.op("pe", lambda e, m=m, kc=kc: e.matmul(
                        out=X[:, m * 256:(m + 1) * 256], lhsT=kt[:, kc:kc + 128], rhs=qz_t[us][:, sub, i * 256:(i + 1) * 256],
                        start=True, stop=not diag), reads=ktk(us) + qtk(us), writes=[("X", xs)])
                    if diag:
                        P.op("pe", lambda e, m=m: e.matmul(out=X[:, m * 256:(m + 1) * 256], lhsT=ident[:], rhs=mask_ab[:, m, :],
                                                           start=False, stop=True), writes=[("X", xs)])

            def EXP(w):
                i, j, sub = items[w]
                es = (w % 4)
                P.op("act", lambda e: e.activation(out=E_t[es][:], in_=XS[sub][:], func=AF.Exp), reads=[("X", sub)], writes=[("E", es)])

            def PV(w):
                i, j, sub = items[w]
                es = (w % 4)
                E = E_t[es]
                first, last = (j == 0), (j == i)
                if first and sub == 0:
                    gate_of[i] = load_gate(h * 128, 128, i, slot=i % 2)
                for m in range(4):
                    ch = vch(j, m)
                    ot_, oc_ = OP[sub]
                    P.op("pe", lambda e, m=m, ch=ch: e.matmul(out=ot_[:, oc_:oc_ + 256], lhsT=vt[:, ch, :], rhs=E[:, m * 256:(m + 1) * 256],
                                                              start=(first and m == 0 and sub == 0), stop=(last and m == 3),
                                                              skip_group_check=True),
                         reads=[("E", es)] + vkeys(us), writes=["bankPA"])
                    lt_, lc_ = LP[sub]
                    P.op("pe", lambda e, m=m: e.matmul(out=lt_[:, lc_:lc_ + 256], lhsT=onesb[:], rhs=E[:, m * 256:(m + 1) * 256],
                                                       start=(first and m == 0 and sub == 0), stop=(last and m == 3),
                                                       skip_group_check=True),
                         reads=[("E", es)], writes=["bankPB"])
                if last and sub == 1:
                    epi0(i)

            def epi0(i):
                for sub in (0, 1):
                    P.op("dve", lambda e, sub=sub: e.tensor_copy(out=o_t[sub][:], in_=OP[sub][0][:, OP[sub][1]:OP[sub][1] + 256]),
                         reads=["bankPA"], writes=[("o", sub)])
                P.op("dve", lambda e: e.tensor_copy(out=lsb_t[:, 0:512], in_=PB[:, 0:512]), reads=["bankPB"], writes=["lsb"])
                pending.append([2, lambda: epi1(i)])

            def epi1(i):
                P.op("act", lambda e: e.activation(out=rsb_t[:, 0:512], in_=lsb_t[:, 0:512], func=AF.Ln), reads=["lsb"], writes=["rsb"])
                P.op("act", lambda e: e.activation(out=rsb_t[:, 0:512], in_=rsb_t[:, 0:512], func=AF.Exp, scale=-1.0), reads=["rsb"], writes=["rsb"])
                for sub in (0, 1):
                    P.op("dve", lambda e, sub=sub: e.tensor_tensor(out=o_t[sub][:], in0=o_t[sub][:], in1=rsb_t[:, sub * 256:(sub + 1) * 256], op=ALU.mult),
                         reads=[("o", sub), "rsb"], writes=[("o", sub)])
                P.op("dve", lambda e: e.scalar_tensor_tensor(out=o_t[2][:], in0=o_t[1][:], scalar=neglam, in1=o_t[0][:],
                                                             op0=ALU.mult, op1=ALU.add), reads=[("o", 0), ("o", 1)], writes=[("o", 2)])
                P.op("dve", lambda e: e.tensor_tensor(out=sq_t[:], in0=o_t[2][:], in1=o_t[2][:], op=ALU.mult), reads=[("o", 2)], writes=["sq"])
                pending.append([2, lambda: epi2(i)])

            def epi2(i):
                g, gk = gate_of.pop(i)
                P.op("pe", lambda e: e.matmul(out=X2[:, 512:768], lhsT=onesf[:], rhs=sq_t[:], start=True, stop=True),
                     reads=["sq"], writes=["bankX2b"])
                x2r.append(P.op("act", lambda e: e.activation(out=rstd_t[:], in_=X2[:, 512:768], func=AF.Ln, bias=float(128.0 * SUBLN_EPS),
                                                              scale=1.0), reads=["bankX2b"], writes=["rstd2"]))
                P.op("act", lambda e: e.activation(out=rstd_t[:], in_=rstd_t[:], func=AF.Exp, scale=-0.5), reads=["rstd2"], writes=["rstd2"])
                P.op("dve", lambda e: e.scalar_tensor_tensor(out=o_t[2][:], in0=o_t[2][:], scalar=gsub, in1=rstd_t[:],
                                                             op0=ALU.mult, op1=ALU.mult), reads=[("o", 2), "rstd2"], writes=[("o", 2)])

                def fin(ys, yk):
                    P.op("dve", lambda e: e.tensor_tensor(out=ys[:], in0=o_t[2][:], in1=g[:], op=ALU.mult), reads=[("o", 2), gk], writes=[yk])
                store_y(fin, h * 128, 128, i, None)

            W = len(items)
            for w0 in range(min(2, W)):
                QK(w0)
            for w in range(W):
                EXP(w)
                PV(w)
                if w + 2 < W:
                    QK(w + 2)
                for pnd in list(pending):
                    pnd[0] -= 1
                    if pnd[0] <= 0:
                        pending.remove(pnd)
                        pnd[1]()
            while pending:
                pnd = pending.pop(0)
                pnd[1]()
            P.barrier("pe", x2r[-4:])

        def attn_B(h, us):
            kt, qt, vt = kt_t[us], qt_t[us], vt_t[us]
            items = [(i, j) for i in range(NB) for j in range(i + 1)]
            gate_of = {}

            def QK(w):
                i, j = items[w]
                xs = w % 3
                X = XS[xs]
                diag = (j == i)
                for m in range(4):
                    kc = kcol(j, m)
                    P.op("pe", lambda e, m=m, kc=kc: e.matmul(
                        out=X[:, m * 256:(m + 1) * 256], lhsT=kt[0:70, kc:kc + 128], rhs=qt[0:70, i * 256:(i + 1) * 256],
                        start=True, stop=not diag), reads=ktk(us) + qtk(us), writes=[("X", xs)])
                    if diag:
                        P.op("pe", lambda e, m=m: e.matmul(out=X[:, m * 256:(m + 1) * 256], lhsT=ident[:], rhs=mask_ab[:, m, :],
                                                           start=False, stop=True), writes=[("X", xs)])

            def EXP(w):
                es = w % 4
                P.op("act", lambda e: e.activation(out=E_t[es][:], in_=XS[w % 3][:], func=AF.Exp), reads=[("X", w % 3)], writes=[("E", es)])

            def PV(w):
                i, j = items[w]
                es = w % 4
                E = E_t[es]
                first, last = (j == 0), (j == i)
                PO = PA if i % 2 == 0 else PB
                pkey = "bankPA" if i % 2 == 0 else "bankPB"
                if first:
                    gate_of[i] = load_gate(512 + h * 64, 64, i, slot=i % 2)
                for m in range(4):
                    ch = vch(j, m)
                    P.op("pe", lambda e, m=m, ch=ch: e.matmul(out=PO[:, 0:256], lhsT=vt[:, ch, :], rhs=E[:, m * 256:(m + 1) * 256],
                                                              start=(first and m == 0), stop=(last and m == 3)),
                         reads=[("E", es)] + vkeys(us), writes=[pkey])
                if last:
                    g, gk = gate_of.pop(i)
                    P.op("dve", lambda e: e.reciprocal(out=r_t[0][0:64, :], in_=PO[64:128, 0:256]), reads=[pkey], writes=[("r", 0)])
                    P.op("dve", lambda e: e.tensor_tensor(out=o_t[0][0:64, :], in0=PO[0:64, 0:256], in1=r_t[0][0:64, :], op=ALU.mult),
                         reads=[pkey, ("r", 0)], writes=[("o", 0)])

                    def fin(ys, yk):
                        P.op("dve", lambda e: e.tensor_tensor(out=ys[0:64, :], in0=o_t[0][0:64, :], in1=g[0:64, :], op=ALU.mult),
                             reads=[("o", 0), gk], writes=[yk])
                    store_y(fin, 512 + h * 64, 64, i, None)

            W = len(items)
            for w0 in range(min(3, W)):
                QK(w0)
            for w in range(W):
                EXP(w)
                PV(w)
                if w + 3 < W:
                    QK(w + 3)

        def attn_C(h, us):
            kt, qt, vt = kt_t[us], qt_t[us], vt_t[us]
            items = [(i, j) for i in range(NB) for j in range(i, -1, -1)]
            W = len(items)

            def QK(w):
                i, j = items[w]
                xs = w % 3
                X = XS[xs]
                diag = (j == i)
                for m in range(4):
                    kc = kcol(j, m)
                    P.op("pe", lambda e, m=m, kc=kc: e.matmul(
                        out=X[:, m * 256:(m + 1) * 256], lhsT=kt[:, kc:kc + 128], rhs=qt[:, i * 256:(i + 1) * 256],
                        start=(m % 2 == 0), stop=True, skip_group_check=True), reads=ktk(us) + qtk(us), writes=[("X", xs)])
                    if diag:
                        P.op("pe", lambda e, m=m: e.matmul(out=X[:, m * 256:(m + 1) * 256], lhsT=ident[:], rhs=mask_c[:, m, :],
                                                           start=False, stop=True, skip_group_check=True), writes=[("X", xs)])

            def EA(w):
                xs = w % 3
                P.op("act", lambda e: e.activation(out=U_t[xs][:], in_=XS[xs][:], func=AF.Exp), reads=[("X", xs)], writes=[("U", xs)])
                P.op("act", lambda e: e.activation(out=Lp_t[xs][:], in_=U_t[xs][:], func=AF.Ln, bias=1.0, scale=1.0),
                     reads=[("U", xs)], writes=[("Lp", xs)])

            def TM(w):
                i, j = items[w]
                xs = w % 3
                X, Lp = XS[xs], Lp_t[xs]
                for m in range(4):
                    P.op("pe", lambda e, m=m: e.matmul(out=X[:, m * 256:(m + 1) * 256], lhsT=tneg[:], rhs=Lp[:, m * 256:(m + 1) * 256],
                                                       start=False, stop=(m == 3), skip_group_check=True),
                         reads=[("Lp", xs)], writes=[("X", xs)])
                    for m2 in range(m + 1, 4):
                        P.op("pe", lambda e, m=m, m2=m2: e.matmul(out=X[:, m * 256:(m + 1) * 256], lhsT=onesneg[:],
                                                                  rhs=Lp[:, m2 * 256:(m2 + 1) * 256], start=False, stop=True,
                                                                  skip_group_check=True), reads=[("Lp", xs)], writes=[("X", xs)])
                if j > 0:
                    for m in range(4):
                        P.op("pe", lambda e, m=m: e.matmul(out=PB[:, 0:256], lhsT=onesneg[:], rhs=Lp[:, m * 256:(m + 1) * 256],
                                                           start=(j == i and m == 0), stop=(m == 3), skip_group_check=True),
                             reads=[("Lp", xs)], writes=["bankPB"])
                    cs = w % 3
                    P.op("dve", lambda e: e.tensor_copy(out=cb_t[cs][:], in_=PB[:, 0:256]), reads=["bankPB"], writes=[("cb", cs)])

            def ADD(w):
                i, j = items[w]
                xs = w % 3
                Y = Y_t[xs]
                if j == i:
                    for hb_ in range(2):
                        P.op("dve", lambda e, hb_=hb_: e.tensor_copy(out=Y[:, hb_ * 512:(hb_ + 1) * 512], in_=XS[xs][:, hb_ * 512:(hb_ + 1) * 512]),
                             reads=[("X", xs)], writes=[("Y", xs)])
                else:
                    cs = (w - 1) % 3
                    for m in range(4):
                        P.op("dve", lambda e, m=m: e.tensor_tensor(out=Y[:, m * 256:(m + 1) * 256], in0=XS[xs][:, m * 256:(m + 1) * 256],
                                                                   in1=cb_t[cs][:], op=ALU.add),
                             reads=[("X", xs), ("cb", cs)], writes=[("Y", xs)])

            def EB(w):
                xs = w % 3
                es = w % 4
                P.op("act", lambda e: e.activation(out=E_t[es][:], in_=Y_t[xs][:], func=AF.Exp), reads=[("Y", xs)], writes=[("E", es)])

            def PV(w):
                i, j = items[w]
                es = w % 4
                E = E_t[es]
                first, last = (j == i), (j == 0)
                for m in range(4):
                    ch = vch(j, m)
                    P.op("pe", lambda e, m=m, ch=ch, E=E: e.matmul(out=PA[:, 0:256], lhsT=vt[:, ch, :], rhs=E[:, m * 256:(m + 1) * 256],
                                                                 start=(first and m == 0), stop=(last and m == 3)),
                         reads=[("E", es)] + vkeys(us), writes=["bankPA"])
                if last:
                    g, gk = load_gate(768 + h * 64, 64, i)

                    def fin(ys, yk):
                        P.op("dve", lambda e: e.tensor_tensor(out=ys[0:64, :], in0=PA[0:64, 0:256], in1=g[0:64, :], op=ALU.mult),
                             reads=["bankPA", gk], writes=[yk])
                    store_y(fin, 768 + h * 64, 64, i, None)

            for w0 in range(min(3, W)):
                QK(w0)
                EA(w0)
            TM(0)
            for w in range(W):
                ADD(w)
                EB(w)
                if w + 1 < W:
                    TM(w + 1)
                if w + 3 < W:
                    QK(w + 3)
                    EA(w + 3)
                PV(w)

        units = [("A", h) for h in range(4)] + [("B", h) for h in range(4)] + [("C", h) for h in range(4)]
        fns = {"A": attn_A, "B": attn_B, "C": attn_C}
        us_cur = load_unit(*units[0])
        for k_, (kind_, h_) in enumerate(units):
            us_next = load_unit(*units[k_ + 1]) if k_ + 1 < len(units) else None
            fns[kind_](h_, us_cur)
            us_cur = us_next
        P.full_barrier()
        A.reset(layer_mark)

        Wo = A.alloc("Wo", [128, 8, D], BF16)
        for c in range(8):
            P.op("pool", lambda e, c=c: e.dma_start(out=Wo[:, c, :], in_=pl["wo"][c * 128:(c + 1) * 128, :]), writes=[("Wo", c)], dma=True)
        g_o = A.alloc("g_o", [128, D], F32)
        b_o = A.alloc("b_o", [128, D], F32)
        P.op("sp", lambda e: e.dma_start(out=g_o[:], in_=pl["lng"][:, :]), writes=["g_o"], dma=True)
        P.op("sp", lambda e: e.dma_start(out=b_o[:], in_=pl["lnb"][:, :]), writes=["b_o"], dma=True)
        yt_t = [A.alloc("yt", [128, 8, 128], BF16) for _ in range(2)]
        hr_t = [A.alloc("hr", [128, D], F32) for _ in range(2)]
        z_t = [A.alloc("z", [128, D], F32) for _ in range(2)]
        ot_t = [A.alloc("ot", [128, D], F32) for _ in range(2)]
        st6 = A.alloc("st6b", [128, 12], F32)
        mv = A.alloc("mvb", [128, 4], F32)
        for T in range(OWN // 128):
            s_ = T % 2
            yt, hr, z, ot = yt_t[s_], hr_t[s_], z_t[s_], ot_t[s_]
            P.op("sp", lambda e, yt=yt, T=T: e.dma_start(out=yt[:], in_=YT[:, T * 128:(T + 1) * 128].rearrange("(c p) t -> p c t", p=128)),
                 writes=[("yt", s_)], dma=True)
            P.op("sp", lambda e, hr=hr, T=T: e.dma_start(out=hr[:], in_=HRES[T * 128:(T + 1) * 128, :]), writes=[("hr", s_)], dma=True)
            X = XS[s_]
            for half in range(2):
                for c in range(8):
                    P.op("pe", lambda e, X=X, half=half, c=c, yt=yt: e.matmul(
                        out=X[:, half * 512:(half + 1) * 512], lhsT=yt[:, c, :], rhs=Wo[:, c, half * 512:(half + 1) * 512],
                        start=(c == 0), stop=(c == 7)), reads=[("yt", s_), ("Wo", c)], writes=[("X", s_)])
            P.op("dve", lambda e, z=z, hr=hr, X=X: e.scalar_tensor_tensor(out=z[:], in0=hr[:], scalar=float(ALPHA), in1=X[:],
                                                                         op0=ALU.mult, op1=ALU.add),
                 reads=[("hr", s_), ("X", s_)], writes=[("z", s_)])
            layer_norm(z, ("z", s_), ot, ("ot", s_), g_o, b_o, ["g_o", "b_o"])
            o_ = P.op("pool", lambda e, ot=ot, T=T: e.dma_start(out=dest[T * 128:(T + 1) * 128, :], in_=ot[:]),
                      reads=[("ot", s_)], dma=True)
            if last_layer:
                out_stores.append(o_)
        P.full_barrier()

    P.final += out_stores
    P.emit()
    return nc


_BF = ml_dtypes.bfloat16


def _own_rows(g):
    return (np.arange(NB)[:, None] * 512 + g * 256 + np.arange(256)[None, :]).reshape(-1)


def _rope_tables(pos, scale):
    inv = ROPE_THETA ** (-np.arange(0, 16, 2, dtype=np.float32) / 16.0)
    ang = pos.astype(np.float32)[None, :] * inv[:, None].astype(np.float32)
    cos, sin = np.cos(ang), np.sin(ang)
    C = np.ones((128, len(pos)), np.float32)
    Sg = np.zeros((128, len(pos)), np.float32)
    for a in range(2):
        b0 = a * 64
        C[b0:b0 + 8] = cos
        C[b0 + 8:b0 + 16] = cos
        Sg[b0:b0 + 8] = -sin
        Sg[b0 + 8:b0 + 16] = sin
    return (C * scale).astype(np.float32), (Sg * scale).astype(np.float32)


def _swap_cols(cols512):
    idx = np.arange(512)
    out = idx.copy()
    for sh in range(8):
        b = sh * 64
        out[b:b + 8] = idx[b + 8:b + 16]
        out[b + 8:b + 16] = idx[b:b + 8]
    return cols512[out]


def _weight_layouts(w_in_l):
    o = {}
    pos = 0
    for name, n in (("Aq", 512), ("Ak", 512), ("Av", 512), ("Ag", 512), ("Bq", 256), ("Bk", 256), ("Bv", 256), ("Bf", 4),
                    ("Bg", 256), ("Cq", 256), ("Ck", 256), ("Cv", 256), ("Cg", 256)):
        o[name] = np.arange(pos, pos + n)
        pos += n
    kcols = np.concatenate([o["Ak"], _swap_cols(o["Ak"]), o["Bk"], o["Ck"], o["Bf"], o["Av"], o["Bv"], o["Cv"]])
    qcols = np.concatenate([o["Aq"], _swap_cols(o["Aq"]), o["Bq"], o["Cq"], o["Ag"], o["Bg"], o["Cg"]])
    assert len(kcols) == WKC and len(qcols) == WQC
    return np.ascontiguousarray(w_in_l[:, kcols]), np.ascontiguousarray(w_in_l[:, qcols])


def _masks(g):
    p = np.arange(128)[:, None, None]
    m = np.arange(4)[None, :, None]
    t = np.arange(256)[None, None, :]
    kpos = (m // 2) * 256 + (m % 2) * 128 + p
    qpos = g * 256 + t
    mab = np.where(kpos <= qpos, 0.0, MASKV).astype(np.float32)
    mc = np.where(kpos < qpos, 0.0, MASKV).astype(np.float32)
    return mab.astype(_BF), mc.astype(_BF)


_PROG_CACHE = {}


def _get_prog(layers):
    key = tuple(layers)
    if key not in _PROG_CACHE:
        lam_inits = {l: 0.8 - 0.6 * math.exp(-0.3 * l) for l in range(DEPTH)}
        _PROG_CACHE[key] = build_program(list(layers), lam_inits)
    return _PROG_CACHE[key]


def _consts(g):
    j = np.arange(128)[:, None]
    s_ = np.arange(128)[None, :]
    d = {}
    d["c_ident"] = np.eye(128, dtype=np.float32).astype(_BF)
    d["c_tneg"] = np.where(j >= s_, -1.0, 0.0).astype(np.float32).astype(_BF)
    d["c_onesneg"] = np.full((128, 128), -1.0, np.float32).astype(_BF)
    d["c_onesb"] = np.ones((128, 128), np.float32).astype(_BF)
    d["c_onesf"] = np.ones((128, 128), np.float32)
    gath = np.concatenate([_own_rows(0), _own_rows(1)])
    d["rk_cos"], d["rk_sin"] = _rope_tables(gath, 1.0)
    for p in range(2):
        gp = g if p == 0 else 1 - g
        d[f"c_mask_ab{p}"], d[f"c_mask_c{p}"] = _masks(gp)
        d[f"rq_cos{p}"], d[f"rq_sin{p}"] = _rope_tables(_own_rows(gp), 0.125)
        sel = np.zeros((4, 2), np.float32)
        sel[:, gp] = 1.0
        d[f"selg{p}"] = sel
        selw = np.zeros((128, 2), np.float32)
        selw[:, gp] = 1.0
        d[f"selw{p}"] = selw
    bl = np.zeros((128, 4), np.float32)
    for r in range(2):
        bl[:, 2 * r] = 1.0 if r == g else 0.0
        bl[:, 2 * r + 1] = 0.0 if r == g else 1.0
    d["blend"] = bl
    return d


def _layer_inputs(l, w_in, b_forget, lambda_q1, lambda_k1, lambda_q2, lambda_k2, subln_g, w_out, ln_g, ln_b):
    wk, wq = _weight_layouts(np.asarray(w_in[l], np.float32))
    rep = lambda v: np.ascontiguousarray(np.broadcast_to(np.asarray(v, np.float32)[None, :], (128, len(v))))
    lamv = np.stack([rep(lambda_q1[l]), rep(lambda_k1[l]), rep(lambda_q2[l]), rep(lambda_k2[l])], axis=1)
    return {
        f"wk{l}": wk, f"wq{l}": wq, f"wo{l}": np.ascontiguousarray(np.asarray(w_out[l], np.float32)),
        f"lng{l}": rep(ln_g[l]), f"lnb{l}": rep(ln_b[l]),
        f"bfg{l}": np.asarray(b_forget[l], np.float32).reshape(4, 1),
        f"lamv{l}": np.ascontiguousarray(lamv), f"subg{l}": np.asarray(subln_g[l], np.float32).reshape(128, 1),
    }


LAUNCH_PLAN = [[0, 1]]


def kernel(x, ln_in_g, ln_in_b, w_in, b_forget, lambda_q1, lambda_k1, lambda_q2, lambda_k2, subln_g, w_out, ln_g, ln_b):
    x = np.asarray(x, np.float32)
    B = x.shape[0]
    rep = lambda v: np.ascontiguousarray(np.broadcast_to(np.asarray(v, np.float32)[None, :], (128, len(v))))
    consts = [_consts(g) for g in range(2)]
    gath = np.concatenate([_own_rows(0), _own_rows(1)])
    h = x
    for layers in LAUNCH_PLAN:
        nc = _get_prog(layers)
        lay = {}
        for l in layers:
            lay.update(_layer_inputs(l, w_in, b_forget, lambda_q1, lambda_k1, lambda_q2, lambda_k2, subln_g, w_out, ln_g, ln_b))
        in_maps = []
        for core in range(8):
            b, g = core // 2, core % 2
            m = dict(consts[g])
            m.update(lay)
            m["hin_full"] = np.ascontiguousarray(h[b][gath])
            m["hin_own"] = np.ascontiguousarray(h[b][_own_rows(g)])
            m["hin_oth"] = np.ascontiguousarray(h[b][_own_rows(1 - g)])
            m["lng_in"] = rep(ln_in_g)
            m["lnb_in"] = rep(ln_in_b)
            in_maps.append(m)
        res = run_bass_kernel_spmd(nc, in_maps, core_ids=list(range(8)))
        hn = np.empty_like(x)
        for core in range(8):
            b, g = core // 2, core % 2
            hn[b][_own_rows(g)] = np.asarray(res.results[core]["out"], np.float32)
        h = hn
    return h
```

```python
import contextlib
import math

import numpy as np
import ml_dtypes

import concourse.bass as bass
import concourse.mybir as mybir
from concourse.bass_utils import run_bass_kernel_spmd

F32 = mybir.dt.float32
BF16 = mybir.dt.bfloat16
AF = mybir.ActivationFunctionType
ALU = mybir.AluOpType

S = 8192
D = 1024
OWN = 4096
NB = 16
DEPTH = 2
LN_EPS = 1e-5
SUBLN_EPS = 1e-5
ALPHA = (2 * DEPTH) ** 0.25
ROPE_THETA = 500000.0
MASKV = -30000.0
WKC = 2564
WQC = 2560

ENGS = ("pe", "act", "dve", "pool", "sp")
DEBUG = False
DUMP = False


class Op:
    __slots__ = ("eng", "fn", "deps", "marked", "sig", "dma", "dsem", "dval", "cc")

    def __init__(self, eng, fn, dma):
        self.eng = eng
        self.fn = fn
        self.deps = []
        self.marked = False
        self.sig = None
        self.dma = dma
        self.dsem = None
        self.dval = None
        self.cc = False


class _Rec:
    def __init__(self):
        self.call = None

    def __getattr__(self, name):
        def f(*a, **k):
            self.call = (name, a, k)
            return None
        return f

    def replay(self, eng):
        name, a, k = self.call
        return getattr(eng, name)(*a, **k)


class Prog:
    NDSEM = 24

    def __init__(self, nc):
        self.nc = nc
        self.ops = {e: [] for e in ENGS}
        self.last_writer = {}
        self.readers = {}
        self.dma_ops = {e: [] for e in ENGS}
        self.final = []

    def op(self, eng, fn, reads=(), writes=(), dma=False, extra=()):
        if fn is not None:
            rec = _Rec()
            fn(rec)
            assert rec.call is not None
            fn = rec.replay
        o = Op(eng, fn, dma)
        deps = []
        seen = set()

        def add(d):
            if d is None or id(d) in seen:
                return
            seen.add(id(d))
            deps.append(d)

        for b in reads:
            add(self.last_writer.get(b))
        for b in writes:
            add(self.last_writer.get(b))
            for r in self.readers.get(b, ()):
                add(r)
        for d in extra:
            add(d)
        if dma:
            lst = self.dma_ops[eng]
            n = len(lst)
            if n >= self.NDSEM:
                add(lst[n - self.NDSEM])
            o.dsem = n % self.NDSEM
            o.dval = 16 * (n // self.NDSEM + 1)
            lst.append(o)
        for d in deps:
            if d.dma:
                o.deps.append(d)
            elif d.eng == "pe" and eng == "pe" and not dma and fn is not None:
                continue
            else:
                d.marked = True
                o.deps.append(d)
        for b in reads:
            self.readers.setdefault(b, []).append(o)
        for b in writes:
            self.last_writer[b] = o
            self.readers[b] = []
        self.ops[eng].append(o)
        return o

    def barrier(self, eng, deps):
        return self.op(eng, None, extra=deps)

    def full_barrier(self):
        tails = []
        for e in ENGS:
            for o in reversed(self.ops[e]):
                if o.fn is not None and not o.dma:
                    tails.append(o)
                    break
        dmas = []
        for e in ENGS:
            dmas += self.dma_ops[e][-self.NDSEM:]
        for e in ENGS:
            self.barrier(e, tails + dmas)
        self.last_writer = {}
        self.readers = {}

    def emit(self):
        nc = self.nc
        with contextlib.ExitStack() as st:
            csem = {e: st.enter_context(nc.semaphore(f"c_{e}")) for e in ENGS}
            dsem = {
                e: [st.enter_context(nc.semaphore(f"d_{e}_{i}")) for i in range(self.NDSEM)]
                for e in ENGS if self.dma_ops[e]
            }
            for e in ENGS:
                c = 0
                for o in self.ops[e]:
                    if o.marked and not o.dma:
                        c += 1
                        o.sig = c
            block = st.enter_context(nc.Block())

            def run(e, engobj):
                seen = {}

                def wait(d):
                    if d.dma:
                        key = ("d", d.eng, d.dsem)
                        sem, val = dsem[d.eng][d.dsem], d.dval
                    else:
                        key = ("c", d.eng)
                        sem, val = csem[d.eng], d.sig
                    if seen.get(key, 0) >= val:
                        return
                    seen[key] = val
                    engobj.wait_ge(sem, val)

                for o in self.ops[e]:
                    for d in o.deps:
                        wait(d)
                    if o.fn is None:
                        continue
                    ins = o.fn(engobj)
                    if o.dma:
                        ins.then_inc(dsem[e][o.dsem], 16)
                    elif o.marked:
                        ins.then_inc(csem[e], 1)
                if e == "sp":
                    for d in self.final:
                        wait(d)

            @block.tensor
            def _(eng):
                run("pe", eng)

            @block.scalar
            def _(eng):
                run("act", eng)

            @block.vector
            def _(eng):
                run("dve", eng)

            @block.gpsimd
            def _(eng):
                run("pool", eng)

            @block.sync
            def _(eng):
                run("sp", eng)


SB_BASE = 16512
SB_TOP = 229376 - 1024


class Arena:
    def __init__(self, nc):
        self.nc = nc
        self.ptr = SB_BASE
        self.n = 0

    def mark(self):
        return self.ptr

    def reset(self, p):
        self.ptr = p

    def alloc(self, name, shape, dt):
        esz = 4 if dt == F32 else 2
        nbytes = esz
        for s in shape[1:]:
            nbytes *= s
        nbytes = (nbytes + 63) // 64 * 64
        off = self.ptr
        self.ptr += nbytes
        assert self.ptr <= SB_TOP, (name, self.ptr)
        self.n += 1
        return self.nc.alloc_sbuf_tensor_at(f"{name}_{self.n}", list(shape), dt, offset=off)


def build_program(layers, lam_inits):
    nc = bass.Bass("TRN2", target_bir_lowering=False)
    P = Prog(nc)
    A = Arena(nc)

    def din(name, shape, dt=F32):
        return nc.dram_tensor(name, list(shape), dt, kind="ExternalInput").ap()

    def dscr(name, shape, dt):
        if DEBUG:
            return nc.dram_tensor(name, list(shape), dt, kind="ExternalOutput").ap()
        return nc.dram_tensor(name, list(shape), dt).ap()

    L0 = layers[0]
    first_is_l0 = (L0 == 0)
    hin_full = din("hin_full", [S, D])
    hin_own = din("hin_own", [OWN, D])
    hin_oth = din("hin_oth", [OWN, D])
    blend = din("blend", [128, 4])
    lng_in = din("lng_in", [128, D])
    lnb_in = din("lnb_in", [128, D])
    rk_cos = din("rk_cos", [128, S])
    rk_sin = din("rk_sin", [128, S])
    rq_cos_p = [din(f"rq_cos{p}", [128, OWN]) for p in range(2)]
    rq_sin_p = [din(f"rq_sin{p}", [128, OWN]) for p in range(2)]
    selg_p = [din(f"selg{p}", [4, 2]) for p in range(2)]
    selw_p = [din(f"selw{p}", [128, 2]) for p in range(2)]
    c_ident = din("c_ident", [128, 128], BF16)
    c_tneg = din("c_tneg", [128, 128], BF16)
    c_onesneg = din("c_onesneg", [128, 128], BF16)
    c_onesb = din("c_onesb", [128, 128], BF16)
    c_onesf = din("c_onesf", [128, 128])
    c_mask_ab_p = [din(f"c_mask_ab{p}", [128, 4, 256], BF16) for p in range(2)]
    c_mask_c_p = [din(f"c_mask_c{p}", [128, 4, 256], BF16) for p in range(2)]
    per_layer = {}
    for l in layers:
        per_layer[l] = dict(
            wk=din(f"wk{l}", [D, WKC]), wq=din(f"wq{l}", [D, WQC]), wo=din(f"wo{l}", [D, D]),
            lng=din(f"lng{l}", [128, D]), lnb=din(f"lnb{l}", [128, D]),
            bfg=din(f"bfg{l}", [4, 1]), lamv=din(f"lamv{l}", [128, 4, 64]), subg=din(f"subg{l}", [128, 1]),
        )
    out = nc.dram_tensor("out", [OWN, D], F32, kind="ExternalOutput").ap()

    KTA = dscr("KTA", [4, 128, S], BF16)
    KTB = dscr("KTB", [4, 70, S], BF16)
    KTC = dscr("KTC", [4, 64, S], BF16)
    VS = dscr("VS", [S, 1024], BF16)
    QTA = dscr("QTA", [4, 128, OWN], BF16)
    QTB = dscr("QTB", [4, 70, OWN], BF16)
    QTC = dscr("QTC", [4, 64, OWN], BF16)
    GT = dscr("GT", [1024, OWN], F32)
    HRES = dscr("HRES", [OWN, D], F32)
    YT = dscr("YT", [1024, OWN], BF16)
    NLF = dscr("NLF", [4, S], F32)
    CN = dscr("CN", [4, S], F32)
    HP = [dscr(f"HP{p}", [OWN, D], F32) for p in range(2)]

    if DEBUG == "C0":
        DBG_U = nc.dram_tensor("DBG_U", [128, 1024], F32, kind="ExternalOutput").ap()
        DBG_L = nc.dram_tensor("DBG_L", [128, 1024], BF16, kind="ExternalOutput").ap()
        DBG_X = nc.dram_tensor("DBG_X", [128, 1024], F32, kind="ExternalOutput").ap()
    X0 = nc.alloc_psum_tensor("X0", [128, 1024], F32)
    X1 = nc.alloc_psum_tensor("X1", [128, 1024], F32)
    X2 = nc.alloc_psum_tensor("X2", [128, 1024], F32)
    PA = nc.alloc_psum_tensor("PA", [128, 512], F32)
    PB = nc.alloc_psum_tensor("PB", [128, 512], F32)
    XS = [X0, X1, X2]

    ident = A.alloc("ident", [128, 128], BF16)
    tneg = A.alloc("tneg", [128, 128], BF16)
    onesneg = A.alloc("onesneg", [128, 128], BF16)
    onesb = A.alloc("onesb", [128, 128], BF16)
    onesf = A.alloc("onesf", [128, 128], F32)
    mask_ab = A.alloc("mask_ab", [128, 4, 256], BF16)
    mask_c = A.alloc("mask_c", [128, 4, 256], BF16)
    selg_sb = A.alloc("selg", [4, 2], F32)
    lam_t = A.alloc("lam", [128, 8], F32)
    subg_t = A.alloc("subg", [128, 2], F32)
    negb_t = A.alloc("negb", [4, 2], F32)
    for dst, src, k in ((ident, c_ident, "ident"), (tneg, c_tneg, "tneg"), (onesneg, c_onesneg, "onesneg"),
                        (onesb, c_onesb, "onesb"), (onesf, c_onesf, "onesf")):
        P.op("sp", lambda e, dst=dst, src=src: e.dma_start(out=dst[:], in_=src[:, :]), writes=[k], dma=True)
    blend_sb = A.alloc("blend", [128, 4], F32)
    P.op("sp", lambda e: e.dma_start(out=blend_sb[:], in_=blend[:, :]), writes=["blend"], dma=True)
    persist_mark = A.mark()
    CONST_KEYS = ["ident", "tneg", "onesneg", "onesb", "onesf", "mask_ab", "mask_c", "selg"]

    def reload_const_keys():
        pass

    out_stores = []

    fused = len(layers) > 1
    schedule = []
    for li, l in enumerate(layers):
        npass = 2 if (fused and li < len(layers) - 1) else 1
        for p_ in range(npass):
            schedule.append((li, l, p_))
    for (li, l, pss) in schedule:
        pl = per_layer[l]
        lam_init = lam_inits[l]
        do_ln_in = (l == 0)
        last_layer = (li == len(layers) - 1)
        do_k = (pss == 0)
        from_hp = (li > 0)
        src_own = (HP[0] if from_hp else (hin_own if pss == 0 else hin_oth))
        dest = out if last_layer else HP[pss]
        rq_cos, rq_sin = rq_cos_p[pss], rq_sin_p[pss]
        A.reset(persist_mark)
        P.op("sp", lambda e: e.dma_start(out=mask_ab[:], in_=c_mask_ab_p[pss][:, :, :]), writes=["mask_ab"], dma=True)
        P.op("sp", lambda e: e.dma_start(out=mask_c[:], in_=c_mask_c_p[pss][:, :, :]), writes=["mask_c"], dma=True)
        P.op("sp", lambda e: e.dma_start(out=selg_sb[:], in_=selg_p[pss][:, :]), writes=["selg"], dma=True)

        lamv_sb = A.alloc("lamv", [128, 4, 64], F32)
        lprod = A.alloc("lprod", [128, 2, 64], F32)
        P.op("sp", lambda e: e.dma_start(out=lamv_sb[:], in_=pl["lamv"][:, :, :]), writes=["lamv"], dma=True)
        P.op("sp", lambda e: e.dma_start(out=subg_t[:, 0:1], in_=pl["subg"][:, :]), writes=["subg0"], dma=True)
        P.op("sp", lambda e: e.dma_start(out=negb_t[:, 0:1], in_=pl["bfg"][:, :]), writes=["negb0"], dma=True)
        P.op("dve", lambda e: e.tensor_tensor(out=lprod[:, 0, :], in0=lamv_sb[:, 0, :], in1=lamv_sb[:, 1, :], op=ALU.mult),
             reads=["lamv"], writes=["lprod"])
        P.op("dve", lambda e: e.tensor_tensor(out=lprod[:, 1, :], in0=lamv_sb[:, 2, :], in1=lamv_sb[:, 3, :], op=ALU.mult),
             reads=["lamv", "lprod"], writes=["lprod"])
        P.op("dve", lambda e: e.reduce_sum(out=lam_t[:, 0:1], in_=lprod[:, 0, :], axis=mybir.AxisListType.X),
             reads=["lprod"], writes=["lam01"])
        P.op("dve", lambda e: e.reduce_sum(out=lam_t[:, 1:2], in_=lprod[:, 1, :], axis=mybir.AxisListType.X),
             reads=["lprod", "lam01"], writes=["lam01"])
        P.op("act", lambda e: e.activation(out=lam_t[:, 2:4], in_=lam_t[:, 0:2], func=AF.Exp), reads=["lam01"], writes=["lam23"])
        P.op("dve", lambda e: e.scalar_tensor_tensor(out=lam_t[:, 4:5], in0=lam_t[:, 3:4], scalar=-float(lam_init),
                                                     in1=lam_t[:, 2:3], op0=ALU.add, op1=ALU.subtract),
             reads=["lam23"], writes=["neglam"])
        P.op("dve", lambda e: e.tensor_scalar(out=subg_t[:, 1:2], in0=subg_t[:, 0:1], scalar1=float((1.0 - lam_init) * math.sqrt(128.0)),
                                              scalar2=None, op0=ALU.mult), reads=["subg0"], writes=["gsub"])
        P.op("dve", lambda e: e.tensor_scalar(out=negb_t[:, 1:2], in0=negb_t[:, 0:1], scalar1=-1.0, scalar2=None, op0=ALU.mult),
             reads=["negb0"], writes=["negb"])
        neglam = lam_t[:, 4:5]
        gsub = subg_t[:, 1:2]
        negb = negb_t[:, 1:2]
        Wo = A.alloc("Wo", [128, 8, D], BF16)
        layer_mark = A.mark()

        Wk = A.alloc("Wk", [128, 8, WKC], BF16)
        Wq = A.alloc("Wq", [128, 8, WQC], BF16)
        if do_k:
            for c in range(8):
                P.op("pool", lambda e, c=c: e.dma_start(out=Wk[:, c, :], in_=pl["wk"][c * 128:(c + 1) * 128, :]), writes=[("Wk", c)], dma=True)
        for c in range(8):
            P.op("pool", lambda e, c=c: e.dma_start(out=Wq[:, c, :], in_=pl["wq"][c * 128:(c + 1) * 128, :]), writes=[("Wq", c)], dma=True)
        for c in range(8):
            P.op("pool", lambda e, c=c: e.dma_start(out=Wo[:, c, :], in_=pl["wo"][c * 128:(c + 1) * 128, :]), writes=[("Wo", c)], dma=True)
        WK_KEYS = [("Wk", c) for c in range(8)]
        WQ_KEYS = [("Wq", c) for c in range(8)]
        if do_ln_in:
            g_in = A.alloc("g_in", [128, D], F32)
            b_in = A.alloc("b_in", [128, D], F32)
            P.op("sp", lambda e: e.dma_start(out=g_in[:], in_=lng_in[:, :]), writes=["g_in"], dma=True)
            P.op("sp", lambda e: e.dma_start(out=b_in[:], in_=lnb_in[:, :]), writes=["b_in"], dma=True)
        xin_t = [A.alloc("xin", [128, D], F32) for _ in range(2)]
        hf_t = [A.alloc("hf", [128, D], F32) for _ in range(2)]
        hb_t = [A.alloc("hb", [128, D], BF16) for _ in range(2)]
        hT_t = [A.alloc("hT", [128, 8, 512], BF16) for _ in range(2)]
        rc_t = [A.alloc("rc", [128, 512], F32) for _ in range(2)]
        rs_t = [A.alloc("rs", [128, 512], F32) for _ in range(2)]
        tmp_t = [A.alloc("tmp", [128, 512], F32) for _ in range(2)]
        stb_t = [A.alloc("stb", [128, 512], BF16) for _ in range(4)]
        stf_t = [A.alloc("stf", [128, 512], F32) for _ in range(2)]
        st6 = A.alloc("st6", [128, 12], F32)
        mv = A.alloc("mv", [128, 4], F32)
        accs = [(X0, 0), (X0, 512), (X1, 0), (X1, 512), (X2, 0), (X2, 512)]
        cnt = dict(tile=0, acc=0, stb=0, stf=0, tmp=0, grp=0)
        stores1 = []

        def layer_norm(src, skey, dst, dkey, gt, bt, gkeys):
            for hh in range(2):
                P.op("dve", lambda e, hh=hh: e.bn_stats(out=st6[:, hh * 6:(hh + 1) * 6], in_=src[:, hh * 512:(hh + 1) * 512]),
                     reads=[skey], writes=[("st6", hh)])
            P.op("dve", lambda e: e.bn_aggr(out=mv[:, 0:2], in_=st6[:, 0:12]), reads=[("st6", 0), ("st6", 1)], writes=["mv"])
            P.op("act", lambda e: e.activation(out=mv[:, 3:4], in_=mv[:, 1:2], func=AF.Ln, bias=float(LN_EPS), scale=1.0),
                 reads=["mv"], writes=["lnv"])
            P.op("act", lambda e: e.activation(out=mv[:, 2:3], in_=mv[:, 3:4], func=AF.Exp, scale=-0.5), reads=["lnv"], writes=["rstd"])
            P.op("dve", lambda e: e.tensor_scalar(out=dst[:], in0=src[:], scalar1=mv[:, 0:1], scalar2=mv[:, 2:3],
                                                  op0=ALU.subtract, op1=ALU.mult), reads=[skey, "mv", "rstd"], writes=[dkey])
            P.op("dve", lambda e: e.tensor_tensor(out=dst[:], in0=dst[:], in1=gt[:], op=ALU.mult), reads=[dkey, gkeys[0]], writes=[dkey])
            P.op("dve", lambda e: e.tensor_tensor(out=dst[:], in0=dst[:], in1=bt[:], op=ALU.add), reads=[dkey, gkeys[1]], writes=[dkey])

        def next_acc():
            a = accs[cnt["acc"] % len(accs)]
            key = ("acc", cnt["acc"] % len(accs))
            cnt["acc"] += 1
            return a[0], a[1], key

        def next_stb():
            i = cnt["stb"] % 4
            cnt["stb"] += 1
            return stb_t[i], ("stb", i)

        def next_stf():
            i = cnt["stf"] % 2
            cnt["stf"] += 1
            return stf_t[i], ("stf", i)

        def proj_pass(src, ngroups, mode):
            W = Wk if mode == "k" else Wq
            WKEYS = WK_KEYS if mode == "k" else WQ_KEYS
            rcos = rk_cos if mode == "k" else rq_cos
            rsin = rk_sin if mode == "k" else rq_sin
            gslot = {}

            def prep_begin(G):
                gs = cnt["grp"] % 2
                cnt["grp"] += 1
                gslot[G] = gs
                rc, rs_ = rc_t[gs], rs_t[gs]
                P.op("sp", lambda e: e.dma_start(out=rc[:], in_=rcos[:, G * 512:(G + 1) * 512]), writes=[("rc", gs)], dma=True)
                P.op("sp", lambda e: e.dma_start(out=rs_[:], in_=rsin[:, G * 512:(G + 1) * 512]), writes=[("rs", gs)], dma=True)

            tstate = {}

            def prep_load(G, tt):
                gs = gslot[G]
                ts_ = cnt["tile"] % 2
                cnt["tile"] += 1
                row0 = G * 512 + tt * 128
                xin = xin_t[ts_]
                tstate[(G, tt)] = ts_
                if src == "blend":
                    idx = row0 % OWN
                    t1 = hf_t[ts_]
                    P.op("sp", lambda e: e.dma_start(out=xin[:], in_=HP[0][idx:idx + 128, :]), writes=[("xin", ts_)], dma=True)
                    P.op("sp", lambda e: e.dma_start(out=t1[:], in_=HP[1][idx:idx + 128, :]), writes=[("hf", ts_)], dma=True)
                else:
                    P.op("sp", lambda e: e.dma_start(out=xin[:], in_=src[row0:row0 + 128, :]), writes=[("xin", ts_)], dma=True)

            def prep_norm(G, tt):
                ts_ = tstate[(G, tt)]
                row0 = G * 512 + tt * 128
                xin = xin_t[ts_]
                if src == "blend":
                    rr = row0 // OWN
                    t1 = hf_t[ts_]
                    P.op("dve", lambda e: e.tensor_scalar(out=xin[:], in0=xin[:], scalar1=blend_sb[:, 2 * rr:2 * rr + 1], scalar2=None,
                                                          op0=ALU.mult), reads=[("xin", ts_)], writes=[("xin", ts_)])
                    P.op("dve", lambda e: e.scalar_tensor_tensor(out=xin[:], in0=t1[:], scalar=blend_sb[:, 2 * rr + 1:2 * rr + 2], in1=xin[:],
                                                                 op0=ALU.mult, op1=ALU.add),
                         reads=[("xin", ts_), ("hf", ts_)], writes=[("xin", ts_)])
                if do_ln_in:
                    hf = hf_t[ts_]
                    hfk = ("hf", ts_)
                    layer_norm(xin, ("xin", ts_), hf, hfk, g_in, b_in, ["g_in", "b_in"])
                else:
                    hf, hfk = xin, ("xin", ts_)
                if mode == "q":
                    stores1.append(P.op("pool", lambda e: e.dma_start(out=HRES[row0:row0 + 128, :], in_=hf[:]), reads=[hfk], dma=True))
                hb = hb_t[ts_]
                P.op("act", lambda e: e.copy(out=hb[:], in_=hf[:]), reads=[hfk], writes=[("hb", ts_)])

            def prep_tr(G, tt):
                gs = gslot[G]
                hT = hT_t[gs]
                ts_ = tstate[(G, tt)]
                hb = hb_t[ts_]
                for half in range(2):
                    ps, o0, akey = next_acc()
                    for c4 in range(4):
                        c = half * 4 + c4
                        P.op("pe", lambda e, c4=c4, c=c: e.matmul(
                            out=ps[:, o0 + c4 * 128:o0 + (c4 + 1) * 128], lhsT=hb[:, c * 128:(c + 1) * 128], rhs=ident[:],
                            start=True, stop=True), reads=[("hb", ts_), "ident"], writes=[akey])
                    P.op("dve", lambda e: e.tensor_copy(
                        out=hT[:, half * 4:(half + 1) * 4, tt * 128:(tt + 1) * 128],
                        in_=ps[:, o0:o0 + 512].rearrange("p (c t) -> p c t", c=4)),
                        reads=[akey], writes=[("hT", gs, tt, half)])

            def chunks(G):
                gs = gslot[G]
                hT = hT_t[gs]
                hTkeys = [("hT", gs, tt_, hf_) for tt_ in range(4) for hf_ in range(2)]
                rc, rs_ = rc_t[gs], rs_t[gs]

                def fm_chunk(col0, M):
                    ps, o0, akey = next_acc()
                    for c in range(8):
                        P.op("pe", lambda e, c=c: e.matmul(
                            out=ps[0:M, o0:o0 + 512], lhsT=W[:, c, col0:col0 + M], rhs=hT[:, c, :], start=(c == 0), stop=(c == 7)),
                            reads=hTkeys + [WKEYS[c]], writes=[akey])
                    return ps, o0, akey

                tok0 = G * 512
                for i in range(4):
                    p1, o1, k1 = fm_chunk(i * 128, 128)
                    p2, o2, k2 = fm_chunk(512 + i * 128, 128)
                    t1 = tmp_t[0]
                    t2 = tmp_t[1]
                    P.op("dve", lambda e: e.tensor_tensor(out=t1[:], in0=p1[:, o1:o1 + 512], in1=rc[:], op=ALU.mult),
                         reads=[k1, ("rc", gs)], writes=[("tmp", 0)])
                    P.op("dve", lambda e: e.tensor_tensor(out=t2[:], in0=p2[:, o2:o2 + 512], in1=rs_[:], op=ALU.mult),
                         reads=[k2, ("rs", gs)], writes=[("tmp", 1)])
                    sb_, sk = next_stb()
                    P.op("dve", lambda e: e.tensor_tensor(out=sb_[:], in0=t1[:], in1=t2[:], op=ALU.add),
                         reads=[("tmp", 0), ("tmp", 1)], writes=[sk])
                    dstT = KTA if mode == "k" else QTA
                    stores1.append(P.op("pool", lambda e: e.dma_start(out=dstT[i, :, tok0:tok0 + 512], in_=sb_[:]), reads=[sk], dma=True))
                    yield
                for kind, cbase, dstT in (("B", 1024, KTB if mode == "k" else QTB), ("C", 1280, KTC if mode == "k" else QTC)):
                    for h in range(4):
                        ps, o0, akey = fm_chunk(cbase + h * 64, 64)
                        sb_, sk = next_stb()
                        if mode == "k":
                            P.op("act", lambda e: e.copy(out=sb_[0:64, :], in_=ps[0:64, o0:o0 + 512]), reads=[akey], writes=[sk])
                        else:
                            P.op("act", lambda e: e.mul(out=sb_[0:64, :], in_=ps[0:64, o0:o0 + 512], mul=0.125), reads=[akey], writes=[sk])
                        stores1.append(P.op("pool", lambda e: e.dma_start(out=dstT[h, 0:64, tok0:tok0 + 512], in_=sb_[0:64, :]),
                                            reads=[sk], dma=True))
                        yield
                if mode == "k":
                    ps, o0, akey = fm_chunk(1536, 4)
                    sf, sfk = next_stf()
                    P.op("act", lambda e: e.activation(out=sf[0:4, :], in_=ps[0:4, o0:o0 + 512], func=AF.Exp, bias=negb, scale=-1.0),
                         reads=[akey, "negb"], writes=[sfk])
                    P.op("act", lambda e: e.activation(out=sf[0:4, :], in_=sf[0:4, :], func=AF.Ln, bias=1.0, scale=1.0),
                         reads=[sfk], writes=[sfk])
                    r = G // 8
                    for bb in range(2):
                        i_blk = (G % 8) * 2 + bb
                        t0 = i_blk * 512 + r * 256
                        stores1.append(P.op("pool", lambda e, bb=bb, t0=t0: e.dma_start(
                            out=NLF[:, t0:t0 + 256], in_=sf[0:4, bb * 256:(bb + 1) * 256]), reads=[sfk], dma=True))
                    yield
                    for tt in range(4):
                        for half in range(2):
                            ps, o0, akey = next_acc()
                            for c in range(8):
                                P.op("pe", lambda e, c=c: e.matmul(
                                    out=ps[:, o0:o0 + 512], lhsT=hT[:, c, tt * 128:(tt + 1) * 128],
                                    rhs=W[:, c, 1540 + half * 512:1540 + (half + 1) * 512], start=(c == 0), stop=(c == 7)),
                                    reads=hTkeys + [WKEYS[c]], writes=[akey])
                            sb_, sk = next_stb()
                            P.op("act", lambda e: e.copy(out=sb_[:], in_=ps[:, o0:o0 + 512]), reads=[akey], writes=[sk])
                            r0 = tok0 + tt * 128
                            stores1.append(P.op("pool", lambda e: e.dma_start(
                                out=VS[r0:r0 + 128, half * 512:(half + 1) * 512], in_=sb_[:]), reads=[sk], dma=True))
                            yield
                else:
                    for gch in range(8):
                        ps, o0, akey = fm_chunk(1536 + gch * 128, 128)
                        sf, sfk = next_stf()
                        P.op("act", lambda e: e.activation(out=sf[:], in_=ps[:, o0:o0 + 512], func=AF.Silu), reads=[akey], writes=[sfk])
                        stores1.append(P.op("pool", lambda e: e.dma_start(
                            out=GT[gch * 128:(gch + 1) * 128, tok0:tok0 + 512], in_=sf[:]), reads=[sfk], dma=True))
                        yield

            prep_begin(0)
            for tt in range(4):
                prep_load(0, tt)
                prep_norm(0, tt)
                prep_tr(0, tt)
            ev_load = {0: 0, 3: 1, 8: 2, 13: 3}
            ev_norm = {1: 0, 6: 1, 11: 2, 16: 3}
            ev_tr = {5: 0, 10: 1, 15: 2, 19: 3}
            for G in range(ngroups):
                more = (G + 1 < ngroups)
                done = set()

                def fire(idx):
                    if not more:
                        return
                    for ev, fn, tag in ((ev_load, prep_load, "l"), (ev_norm, prep_norm, "n"), (ev_tr, prep_tr, "t")):
                        if idx in ev and (tag, ev[idx]) not in done:
                            done.add((tag, ev[idx]))
                            fn(G + 1, ev[idx])
                if more:
                    prep_begin(G + 1)
                fire(0)
                idx = 0
                for _ in chunks(G):
                    idx += 1
                    fire(idx)
                for k in range(idx + 1, 24):
                    fire(k)

        if do_k:
            proj_pass("blend" if from_hp else hin_full, 16, "k")
        proj_pass(src_own, 8, "q")
        P.full_barrier()
        A.reset(layer_mark)

        nlf = A.alloc("nlf", [4, S], F32)
        cn = A.alloc("cn", [4, S], F32)
        kk = A.alloc("kk", [128, 256], F32)
        pkw = [A.alloc("pkw", [128, 256], BF16) for _ in range(3)]
        cqw = [A.alloc("cqw", [64, 256], F32) for _ in range(2)]
        pqw = [A.alloc("pqw", [64, 256], BF16) for _ in range(3)]
        ones_w = A.alloc("ones_w", [128, 256], BF16)
        selw = A.alloc("selw", [128, 2], F32)
        first_pass = (li == 0 and pss == 0)
        P.op("sp", lambda e: e.dma_start(out=selw[:], in_=selw_p[pss][:, :]), writes=["selw"], dma=True)
        if first_pass:
            P.op("pool", lambda e: e.memset(ones_w[:], 1.0), writes=["ones_w"])
        if do_k:
            P.op("sp", lambda e: e.dma_start(out=nlf[:], in_=NLF[:, :]), writes=["nlf"], dma=True)
            P.op("dve", lambda e: e.tensor_tensor_scan(out=cn[:], data0=nlf[:], data1=nlf[:], initial=0.0, op0=ALU.add, op1=ALU.max),
                 reads=["nlf"], writes=["cn"])
            st_cn = P.op("sp", lambda e: e.dma_start(out=CN[:, :], in_=cn[:]), reads=["cn"], dma=True)
            P.barrier("sp", [st_cn])
        CNv = CN.rearrange("h (i r t) -> h r i t", i=16, r=2, t=256)
        for h in range(4):
            for r in range(2):
                if do_k:
                    P.op("sp", lambda e, h=h, r=r: e.dma_start(out=kk[h * 32 + r * 16:h * 32 + (r + 1) * 16, :], in_=CNv[h, r, :, :]),
                         writes=[("kk", h, r)], dma=True)
                P.op("sp", lambda e, h=h, r=r: e.dma_start(out=cqw[r][h * 16:(h + 1) * 16, :], in_=CNv[h, r, :, :]),
                     writes=[("cqw", r, h)], dma=True)
        KK_KEYS = [("kk", h, r) for h in range(4) for r in range(2)]

        def split3(src, skeys, pieces, pkey):
            for p_ in range(3):
                P.op("dve", lambda e, p_=p_: e.tensor_copy(out=pieces[p_][:], in_=src[:]), reads=skeys, writes=[(pkey, p_)])
                if p_ < 2:
                    P.op("dve", lambda e, p_=p_: e.tensor_tensor(out=src[:], in0=src[:], in1=pieces[p_][:], op=ALU.subtract),
                         reads=skeys + [(pkey, p_)], writes=[skeys[0]])

        if do_k:
            split3(kk, KK_KEYS, pkw, "pkw")
        CQ0 = [("cqw", 0, h) for h in range(4)]
        CQ1 = [("cqw", 1, h) for h in range(4)]
        P.op("dve", lambda e: e.tensor_scalar(out=cqw[0][:], in0=cqw[0][:], scalar1=selw[0:64, 0:1], scalar2=-1.0, op0=ALU.mult, op1=ALU.mult),
             reads=CQ0 + ["selw"], writes=[CQ0[0]])
        P.op("dve", lambda e: e.tensor_scalar(out=cqw[1][:], in0=cqw[1][:], scalar1=selw[0:64, 1:2], scalar2=-1.0, op0=ALU.mult, op1=ALU.mult),
             reads=CQ1 + ["selw"], writes=[CQ1[0]])
        P.op("dve", lambda e: e.tensor_tensor(out=cqw[0][:], in0=cqw[0][:], in1=cqw[1][:], op=ALU.add),
             reads=CQ0 + CQ1, writes=[CQ0[0]])
        split3(cqw[0], CQ0, pqw, "pqw")
        bst = []
        for h in range(4):
            for p_ in range(3):
                if do_k:
                    bst.append(P.op("sp", lambda e, h=h, p_=p_: e.dma_start(
                        out=KTB[h, 67 + p_, :].rearrange("(b t) -> b t", t=256), in_=pkw[p_][h * 32:(h + 1) * 32, :]),
                        reads=[("pkw", p_)], dma=True))
                bst.append(P.op("sp", lambda e, h=h, p_=p_: e.dma_start(
                    out=QTB[h, 64 + p_, :].rearrange("(b t) -> b t", t=256), in_=pqw[p_][h * 16:(h + 1) * 16, :]),
                    reads=[("pqw", p_)], dma=True))
                if first_pass:
                    bst.append(P.op("sp", lambda e, h=h, p_=p_: e.dma_start(
                        out=KTB[h, 64 + p_, :].rearrange("(b t) -> b t", t=256), in_=ones_w[0:32, :]), reads=["ones_w"], dma=True))
                    bst.append(P.op("sp", lambda e, h=h, p_=p_: e.dma_start(
                        out=QTB[h, 67 + p_, :].rearrange("(b t) -> b t", t=256), in_=ones_w[0:16, :]), reads=["ones_w"], dma=True))
        P.full_barrier()
        A.reset(layer_mark)

        kt_t = [A.alloc("kt", [128, S], BF16) for _ in range(2)]
        qt_t = [A.alloc("qt", [128, OWN], BF16) for _ in range(2)]
        vt_t = [A.alloc("vt", [128, 64, 128], BF16) for _ in range(2)]
        qz_t = [A.alloc("qz", [128, 2, OWN], BF16) for _ in range(2)]
        for us_ in range(2):
            P.op("pool", lambda e: e.memset(qz_t[us_][64:128, 0, :], 0.0), writes=[("qz0", us_)])
            P.op("pool", lambda e: e.memset(qz_t[us_][0:64, 1, :], 0.0), writes=[("qz1", us_)])
        E_t = [A.alloc("E", [128, 1024], BF16) for _ in range(4)]
        U_t = [A.alloc("U", [128, 1024], F32) for _ in range(3)]
        Lp_t = [A.alloc("Lp", [128, 1024], BF16) for _ in range(3)]
        Y_t = [A.alloc("Y", [128, 1024], F32) for _ in range(3)]
        cb_t = [A.alloc("cb", [128, 256], F32) for _ in range(3)]
        gate_t = [A.alloc("gate", [128, 256], F32) for _ in range(2)]
        r_t = [A.alloc("r", [128, 256], F32) for _ in range(2)]
        o_t = [A.alloc("o", [128, 256], F32) for _ in range(3)]
        sq_t = A.alloc("sq", [128, 256], F32)
        rstd_t = A.alloc("rstd", [128, 256], F32)
        ys_t = [A.alloc("ys", [128, 256], BF16) for _ in range(2)]
        Es_t = [A.alloc("Es", [128, 1024], F32) for _ in range(2)]
        ystores = []
        ucount = [0]
        ycount = [0]

        def kcol(j, m):
            return (m // 2) * OWN + j * 256 + (m % 2) * 128

        def vch(j, m):
            return (m // 2) * 32 + j * 2 + (m % 2)

        def ktk(us):
            return [("kt", us, 0), ("kt", us, 1), ("kt", us, "z")]

        def qtk(us):
            return [("qt", us, 0), ("qt", us, 1), ("qt", us, "z"), ("qz0", us), ("qz1", us)]

        def load_unit(kind, h):
            us = ucount[0] % 2
            ucount[0] += 1
            kt, qt, vt = kt_t[us], qt_t[us], vt_t[us]
            rows = {"A": 128, "B": 70, "C": 64}[kind]
            KT = {"A": KTA, "B": KTB, "C": KTC}[kind]
            QT = {"A": QTA, "B": QTB, "C": QTC}[kind]
            if kind == "C":
                P.op("pool", lambda e: e.memset(kt[64:128, :], 0.0), writes=[("kt", us, "z")])
                P.op("pool", lambda e: e.memset(qt[64:128, :], 0.0), writes=[("qt", us, "z")])
            for r in range(2):
                P.op("pool", lambda e, r=r: e.dma_start(out=kt[0:rows, r * OWN:(r + 1) * OWN], in_=KT[h, :, r * OWN:(r + 1) * OWN]),
                     writes=[("kt", us, r)], dma=True)
            if kind == "A":
                qz = qz_t[us]
                P.op("pool", lambda e: e.dma_start(out=qz[0:64, 0, :], in_=QT[h, 0:64, :]), writes=[("qt", us, 0)], dma=True)
                P.op("pool", lambda e: e.dma_start(out=qz[64:128, 1, :], in_=QT[h, 64:128, :]), writes=[("qt", us, 1)], dma=True)
            else:
                P.op("pool", lambda e: e.dma_start(out=qt[0:rows, :], in_=QT[h, :, :]), writes=[("qt", us, 0)], dma=True)
            if kind == "A":
                c0, cw = h * 128, 128
            elif kind == "B":
                c0, cw = 512 + h * 64, 64
            else:
                c0, cw = 768 + h * 64, 64
            if kind == "B":
                P.op("pool", lambda e: e.memset(vt[:, :, 64:128], 1.0), writes=[("vt", us)])
            for q8 in range(8):
                P.op("pool", lambda e, q8=q8: e.dma_start(
                    out=vt[:, q8 * 8:(q8 + 1) * 8, 0:cw],
                    in_=VS[q8 * 1024:(q8 + 1) * 1024, c0:c0 + cw].rearrange("(c p) w -> p c w", p=128)),
                    writes=[("vt", us, q8)], reads=[("vt", us)], dma=True)
            return us

        def vkeys(us):
            return [("vt", us)] + [("vt", us, q8) for q8 in range(8)]

        def load_gate(row0, nrows, i, slot=None):
            gs = (ycount[0] % 2) if slot is None else slot
            g = gate_t[gs]
            P.op("sp", lambda e: e.dma_start(out=g[0:nrows, :], in_=GT[row0:row0 + nrows, i * 256:(i + 1) * 256]),
                 writes=[("gate", gs)], dma=True)
            return g, ("gate", gs)

        def store_y(ysrc_fn, row0, nrows, i, reads):
            ys = ys_t[ycount[0] % 2]
            yk = ("ys", ycount[0] % 2)
            ycount[0] += 1
            ysrc_fn(ys, yk)
            ystores.append(P.op("sp", lambda e: e.dma_start(out=YT[row0:row0 + nrows, i * 256:(i + 1) * 256], in_=ys[0:nrows, :]),
                                reads=[yk], dma=True))

        def attn_A(h, us):
            kt, qt, vt = kt_t[us], qt_t[us], vt_t[us]
            items = [(i, j, sub) for i in range(NB) for j in range(i + 1) for sub in (0, 1)]
            OP = [(PA, 0), (PA, 256)]
            LP = [(PB, 0), (PB, 256)]
            pending = []
            gate_of = {}
            lsb_t, rsb_t = Es_t[0], Es_t[1]
            x2r = []

            def QK(w):
                i, j, sub = items[w]
                xs = sub
                X = XS[xs]
                r0 = sub * 64
                diag = (j == i)
                for m in range(4):
                    kc = kcol(j, m)
                    P.op("pe", lambda e, m=m, kc=kc: e.matmul(
                        out=X[:, m * 256:(m + 1) * 256], lhsT=kt[:, kc:kc + 128], rhs=qz_t[us][:, sub, i * 256:(i + 1) * 256],
                        start=True, stop=not diag), reads=ktk(us) + qtk(us), writes=[("X", xs)])
                    if diag:
                        P.op("pe", lambda e, m=m: e.matmul(out=X[:, m * 256:(m + 1) * 256], lhsT=ident[:], rhs=mask_ab[:, m, :],
                                                           start=False, stop=True), writes=[("X", xs)])

            def EXP(w):
                i, j, sub = items[w]
                es = (w % 4)
                P.op("act", lambda e: e.activation(out=E_t[es][:], in_=XS[sub][:], func=AF.Exp), reads=[("X", sub)], writes=[("E", es)])

            def PV(w):
                i, j, sub = items[w]
                es = (w % 4)
                E = E_t[es]
                first, last = (j == 0), (j == i)
                if first and sub == 0:
                    gate_of[i] = load_gate(h * 128, 128, i, slot=i % 2)
                for m in range(4):
                    ch = vch(j, m)
                    ot_, oc_ = OP[sub]
                    P.op("pe", lambda e, m=m, ch=ch: e.matmul(out=ot_[:, oc_:oc_ + 256], lhsT=vt[:, ch, :], rhs=E[:, m * 256:(m + 1) * 256],
                                                              start=(first and m == 0 and sub == 0), stop=(last and m == 3),
                                                              skip_group_check=True),
                         reads=[("E", es)] + vkeys(us), writes=["bankPA"])
                    lt_, lc_ = LP[sub]
                    P.op("pe", lambda e, m=m: e.matmul(out=lt_[:, lc_:lc_ + 256], lhsT=onesb[:], rhs=E[:, m * 256:(m + 1) * 256],
                                                       start=(first and m == 0 and sub == 0), stop=(last and m == 3),
                                                       skip_group_check=True),
                         reads=[("E", es)], writes=["bankPB"])
                if last and sub == 1:
                    epi0(i)

            def epi0(i):
                for sub in (0, 1):
                    P.op("dve", lambda e, sub=sub: e.tensor_copy(out=o_t[sub][:], in_=OP[sub][0][:, OP[sub][1]:OP[sub][1] + 256]),
                         reads=["bankPA"], writes=[("o", sub)])
                P.op("dve", lambda e: e.tensor_copy(out=lsb_t[:, 0:512], in_=PB[:, 0:512]), reads=["bankPB"], writes=["lsb"])
                pending.append([2, lambda: epi1(i)])

            def epi1(i):
                P.op("act", lambda e: e.activation(out=rsb_t[:, 0:512], in_=lsb_t[:, 0:512], func=AF.Ln), reads=["lsb"], writes=["rsb"])
                P.op("act", lambda e: e.activation(out=rsb_t[:, 0:512], in_=rsb_t[:, 0:512], func=AF.Exp, scale=-1.0), reads=["rsb"], writes=["rsb"])
                for sub in (0, 1):
                    P.op("dve", lambda e, sub=sub: e.tensor_tensor(out=o_t[sub][:], in0=o_t[sub][:], in1=rsb_t[:, sub * 256:(sub + 1) * 256], op=ALU.mult),
                         reads=[("o", sub), "rsb"], writes=[("o", sub)])
                P.op("dve", lambda e: e.scalar_tensor_tensor(out=o_t[2][:], in0=o_t[1][:], scalar=neglam, in1=o_t[0][:],
                                                             op0=ALU.mult, op1=ALU.add), reads=[("o", 0), ("o", 1)], writes=[("o", 2)])
                P.op("dve", lambda e: e.tensor_tensor(out=sq_t[:], in0=o_t[2][:], in1=o_t[2][:], op=ALU.mult), reads=[("o", 2)], writes=["sq"])
                pending.append([2, lambda: epi2(i)])

            def epi2(i):
                g, gk = gate_of.pop(i)
                P.op("pe", lambda e: e.matmul(out=X2[:, 512:768], lhsT=onesf[:], rhs=sq_t[:], start=True, stop=True),
                     reads=["sq"], writes=["bankX2b"])
                x2r.append(P.op("act", lambda e: e.activation(out=rstd_t[:], in_=X2[:, 512:768], func=AF.Ln, bias=float(128.0 * SUBLN_EPS),
                                                              scale=1.0), reads=["bankX2b"], writes=["rstd2"]))
                P.op("act", lambda e: e.activation(out=rstd_t[:], in_=rstd_t[:], func=AF.Exp, scale=-0.5), reads=["rstd2"], writes=["rstd2"])
                P.op("dve", lambda e: e.scalar_tensor_tensor(out=o_t[2][:], in0=o_t[2][:], scalar=gsub, in1=rstd_t[:],
                                                             op0=ALU.mult, op1=ALU.mult), reads=[("o", 2), "rstd2"], writes=[("o", 2)])

                def fin(ys, yk):
                    P.op("dve", lambda e: e.tensor_tensor(out=ys[:], in0=o_t[2][:], in1=g[:], op=ALU.mult), reads=[("o", 2), gk], writes=[yk])
                store_y(fin, h * 128, 128, i, None)

            W = len(items)
            for w0 in range(min(2, W)):
                QK(w0)
            for w in range(W):
                EXP(w)
                PV(w)
                if w + 2 < W:
                    QK(w + 2)
                for pnd in list(pending):
                    pnd[0] -= 1
                    if pnd[0] <= 0:
                        pending.remove(pnd)
                        pnd[1]()
            while pending:
                pnd = pending.pop(0)
                pnd[1]()
            P.barrier("pe", x2r[-4:])

        def attn_B(h, us):
            kt, qt, vt = kt_t[us], qt_t[us], vt_t[us]
            items = [(i, j) for i in range(NB) for j in range(i + 1)]
            gate_of = {}

            def QK(w):
                i, j = items[w]
                xs = w % 3
                X = XS[xs]
                diag = (j == i)
                for m in range(4):
                    kc = kcol(j, m)
                    P.op("pe", lambda e, m=m, kc=kc: e.matmul(
                        out=X[:, m * 256:(m + 1) * 256], lhsT=kt[0:70, kc:kc + 128], rhs=qt[0:70, i * 256:(i + 1) * 256],
                        start=True, stop=not diag), reads=ktk(us) + qtk(us), writes=[("X", xs)])
                    if diag:
                        P.op("pe", lambda e, m=m: e.matmul(out=X[:, m * 256:(m + 1) * 256], lhsT=ident[:], rhs=mask_ab[:, m, :],
                                                           start=False, stop=True), writes=[("X", xs)])

            def EXP(w):
                es = w % 4
                P.op("act", lambda e: e.activation(out=E_t[es][:], in_=XS[w % 3][:], func=AF.Exp), reads=[("X", w % 3)], writes=[("E", es)])

            def PV(w):
                i, j = items[w]
                es = w % 4
                E = E_t[es]
                first, last = (j == 0), (j == i)
                PO = PA if i % 2 == 0 else PB
                pkey = "bankPA" if i % 2 == 0 else "bankPB"
                if first:
                    gate_of[i] = load_gate(512 + h * 64, 64, i, slot=i % 2)
                for m in range(4):
                    ch = vch(j, m)
                    P.op("pe", lambda e, m=m, ch=ch: e.matmul(out=PO[:, 0:256], lhsT=vt[:, ch, :], rhs=E[:, m * 256:(m + 1) * 256],
                                                              start=(first and m == 0), stop=(last and m == 3)),
                         reads=[("E", es)] + vkeys(us), writes=[pkey])
                if last:
                    g, gk = gate_of.pop(i)
                    P.op("dve", lambda e: e.reciprocal(out=r_t[0][0:64, :], in_=PO[64:128, 0:256]), reads=[pkey], writes=[("r", 0)])
                    P.op("dve", lambda e: e.tensor_tensor(out=o_t[0][0:64, :], in0=PO[0:64, 0:256], in1=r_t[0][0:64, :], op=ALU.mult),
                         reads=[pkey, ("r", 0)], writes=[("o", 0)])

                    def fin(ys, yk):
                        P.op("dve", lambda e: e.tensor_tensor(out=ys[0:64, :], in0=o_t[0][0:64, :], in1=g[0:64, :], op=ALU.mult),
                             reads=[("o", 0), gk], writes=[yk])
                    store_y(fin, 512 + h * 64, 64, i, None)

            W = len(items)
            for w0 in range(min(3, W)):
                QK(w0)
            for w in range(W):
                EXP(w)
                PV(w)
                if w + 3 < W:
                    QK(w + 3)

        def attn_C(h, us):
            kt, qt, vt = kt_t[us], qt_t[us], vt_t[us]
            items = [(i, j) for i in range(NB) for j in range(i, -1, -1)]
            W = len(items)

            def QK(w):
                i, j = items[w]
                xs = w % 3
                X = XS[xs]
                diag = (j == i)
                for m in range(4):
                    kc = kcol(j, m)
                    P.op("pe", lambda e, m=m, kc=kc: e.matmul(
                        out=X[:, m * 256:(m + 1) * 256], lhsT=kt[:, kc:kc + 128], rhs=qt[:, i * 256:(i + 1) * 256],
                        start=(m % 2 == 0), stop=True, skip_group_check=True), reads=ktk(us) + qtk(us), writes=[("X", xs)])
                    if diag:
                        P.op("pe", lambda e, m=m: e.matmul(out=X[:, m * 256:(m + 1) * 256], lhsT=ident[:], rhs=mask_c[:, m, :],
                                                           start=False, stop=True, skip_group_check=True), writes=[("X", xs)])

            def EA(w):
                xs = w % 3
                P.op("act", lambda e: e.activation(out=U_t[xs][:], in_=XS[xs][:], func=AF.Exp), reads=[("X", xs)], writes=[("U", xs)])
                P.op("act", lambda e: e.activation(out=Lp_t[xs][:], in_=U_t[xs][:], func=AF.Ln, bias=1.0, scale=1.0),
                     reads=[("U", xs)], writes=[("Lp", xs)])

            def TM(w):
                i, j = items[w]
                xs = w % 3
                X, Lp = XS[xs], Lp_t[xs]
                for m in range(4):
                    P.op("pe", lambda e, m=m: e.matmul(out=X[:, m * 256:(m + 1) * 256], lhsT=tneg[:], rhs=Lp[:, m * 256:(m + 1) * 256],
                                                       start=False, stop=(m == 3), skip_group_check=True),
                         reads=[("Lp", xs)], writes=[("X", xs)])
                    for m2 in range(m + 1, 4):
                        P.op("pe", lambda e, m=m, m2=m2: e.matmul(out=X[:, m * 256:(m + 1) * 256], lhsT=onesneg[:],
                                                                  rhs=Lp[:, m2 * 256:(m2 + 1) * 256], start=False, stop=True,
                                                                  skip_group_check=True), reads=[("Lp", xs)], writes=[("X", xs)])
                if j > 0:
                    for m in range(4):
                        P.op("pe", lambda e, m=m: e.matmul(out=PB[:, 0:256], lhsT=onesneg[:], rhs=Lp[:, m * 256:(m + 1) * 256],
                                                           start=(j == i and m == 0), stop=(m == 3), skip_group_check=True),
                             reads=[("Lp", xs)], writes=["bankPB"])
                    cs = w % 3
                    P.op("dve", lambda e: e.tensor_copy(out=cb_t[cs][:], in_=PB[:, 0:256]), reads=["bankPB"], writes=[("cb", cs)])

            def ADD(w):
                i, j = items[w]
                xs = w % 3
                Y = Y_t[xs]
                if j == i:
                    for hb_ in range(2):
                        P.op("dve", lambda e, hb_=hb_: e.tensor_copy(out=Y[:, hb_ * 512:(hb_ + 1) * 512], in_=XS[xs][:, hb_ * 512:(hb_ + 1) * 512]),
                             reads=[("X", xs)], writes=[("Y", xs)])
                else:
                    cs = (w - 1) % 3
                    for m in range(4):
                        P.op("dve", lambda e, m=m: e.tensor_tensor(out=Y[:, m * 256:(m + 1) * 256], in0=XS[xs][:, m * 256:(m + 1) * 256],
                                                                   in1=cb_t[cs][:], op=ALU.add),
                             reads=[("X", xs), ("cb", cs)], writes=[("Y", xs)])

            def EB(w):
                xs = w % 3
                es = w % 4
                P.op("act", lambda e: e.activation(out=E_t[es][:], in_=Y_t[xs][:], func=AF.Exp), reads=[("Y", xs)], writes=[("E", es)])

            def PV(w):
                i, j = items[w]
                es = w % 4
                E = E_t[es]
                first, last = (j == i), (j == 0)
                for m in range(4):
                    ch = vch(j, m)
                    P.op("pe", lambda e, m=m, ch=ch, E=E: e.matmul(out=PA[:, 0:256], lhsT=vt[:, ch, :], rhs=E[:, m * 256:(m + 1) * 256],
                                                                 start=(first and m == 0), stop=(last and m == 3)),
                         reads=[("E", es)] + vkeys(us), writes=["bankPA"])
                if last:
                    g, gk = load_gate(768 + h * 64, 64, i)

                    def fin(ys, yk):
                        P.op("dve", lambda e: e.tensor_tensor(out=ys[0:64, :], in0=PA[0:64, 0:256], in1=g[0:64, :], op=ALU.mult),
                             reads=["bankPA", gk], writes=[yk])
                    store_y(fin, 768 + h * 64, 64, i, None)

            for w0 in range(min(3, W)):
                QK(w0)
                EA(w0)
            TM(0)
            for w in range(W):
                ADD(w)
                EB(w)
                if w + 1 < W:
                    TM(w + 1)
                if w + 3 < W:
                    QK(w + 3)
                    EA(w + 3)
                PV(w)

        units = [("A", h) for h in range(4)] + [("B", h) for h in range(4)] + [("C", h) for h in range(4)]
        fns = {"A": attn_A, "B": attn_B, "C": attn_C}
        us_cur = load_unit(*units[0])
        for k_, (kind_, h_) in enumerate(units):
            us_next = load_unit(*units[k_ + 1]) if k_ + 1 < len(units) else None
            fns[kind_](h_, us_cur)
            us_cur = us_next
        P.full_barrier()
        A.reset(layer_mark)

        g_o = A.alloc("g_o", [128, D], F32)
        b_o = A.alloc("b_o", [128, D], F32)
        P.op("sp", lambda e: e.dma_start(out=g_o[:], in_=pl["lng"][:, :]), writes=["g_o"], dma=True)
        P.op("sp", lambda e: e.dma_start(out=b_o[:], in_=pl["lnb"][:, :]), writes=["b_o"], dma=True)
        yt_t = [A.alloc("yt", [128, 8, 128], BF16) for _ in range(2)]
        hr_t = [A.alloc("hr", [128, D], F32) for _ in range(2)]
        z_t = [A.alloc("z", [128, D], F32) for _ in range(2)]
        ot_t = [A.alloc("ot", [128, D], F32) for _ in range(2)]
        st6 = A.alloc("st6b", [128, 12], F32)
        mv = A.alloc("mvb", [128, 4], F32)
        for T in range(OWN // 128):
            s_ = T % 2
            yt, hr, z, ot = yt_t[s_], hr_t[s_], z_t[s_], ot_t[s_]
            P.op("sp", lambda e, yt=yt, T=T: e.dma_start(out=yt[:], in_=YT[:, T * 128:(T + 1) * 128].rearrange("(c p) t -> p c t", p=128)),
                 writes=[("yt", s_)], dma=True)
            P.op("sp", lambda e, hr=hr, T=T: e.dma_start(out=hr[:], in_=HRES[T * 128:(T + 1) * 128, :]), writes=[("hr", s_)], dma=True)
            X = XS[s_]
            for half in range(2):
                for c in range(8):
                    P.op("pe", lambda e, X=X, half=half, c=c, yt=yt: e.matmul(
                        out=X[:, half * 512:(half + 1) * 512], lhsT=yt[:, c, :], rhs=Wo[:, c, half * 512:(half + 1) * 512],
                        start=(c == 0), stop=(c == 7)), reads=[("yt", s_), ("Wo", c)], writes=[("X", s_)])
            P.op("dve", lambda e, z=z, hr=hr, X=X: e.scalar_tensor_tensor(out=z[:], in0=hr[:], scalar=float(ALPHA), in1=X[:],
                                                                         op0=ALU.mult, op1=ALU.add),
                 reads=[("hr", s_), ("X", s_)], writes=[("z", s_)])
            layer_norm(z, ("z", s_), ot, ("ot", s_), g_o, b_o, ["g_o", "b_o"])
            o_ = P.op("pool", lambda e, ot=ot, T=T: e.dma_start(out=dest[T * 128:(T + 1) * 128, :], in_=ot[:]),
                      reads=[("ot", s_)], dma=True)
            if last_layer:
                out_stores.append(o_)
        P.full_barrier()

    P.final += out_stores
    P.emit()
    return nc


_BF = ml_dtypes.bfloat16


def _own_rows(g):
    return (np.arange(NB)[:, None] * 512 + g * 256 + np.arange(256)[None, :]).reshape(-1)


def _rope_tables(pos, scale):
    inv = ROPE_THETA ** (-np.arange(0, 16, 2, dtype=np.float32) / 16.0)
    ang = pos.astype(np.float32)[None, :] * inv[:, None].astype(np.float32)
    cos, sin = np.cos(ang), np.sin(ang)
    C = np.ones((128, len(pos)), np.float32)
    Sg = np.zeros((128, len(pos)), np.float32)
    for a in range(2):
        b0 = a * 64
        C[b0:b0 + 8] = cos
        C[b0 + 8:b0 + 16] = cos
        Sg[b0:b0 + 8] = -sin
        Sg[b0 + 8:b0 + 16] = sin
    return (C * scale).astype(np.float32), (Sg * scale).astype(np.float32)


def _swap_cols(cols512):
    idx = np.arange(512)
    out = idx.copy()
    for sh in range(8):
        b = sh * 64
        out[b:b + 8] = idx[b + 8:b + 16]
        out[b + 8:b + 16] = idx[b:b + 8]
    return cols512[out]


def _weight_layouts(w_in_l):
    o = {}
    pos = 0
    for name, n in (("Aq", 512), ("Ak", 512), ("Av", 512), ("Ag", 512), ("Bq", 256), ("Bk", 256), ("Bv", 256), ("Bf", 4),
                    ("Bg", 256), ("Cq", 256), ("Ck", 256), ("Cv", 256), ("Cg", 256)):
        o[name] = np.arange(pos, pos + n)
        pos += n
    kcols = np.concatenate([o["Ak"], _swap_cols(o["Ak"]), o["Bk"], o["Ck"], o["Bf"], o["Av"], o["Bv"], o["Cv"]])
    qcols = np.concatenate([o["Aq"], _swap_cols(o["Aq"]), o["Bq"], o["Cq"], o["Ag"], o["Bg"], o["Cg"]])
    assert len(kcols) == WKC and len(qcols) == WQC
    return np.ascontiguousarray(w_in_l[:, kcols]), np.ascontiguousarray(w_in_l[:, qcols])


def _masks(g):
    p = np.arange(128)[:, None, None]
    m = np.arange(4)[None, :, None]
    t = np.arange(256)[None, None, :]
    kpos = (m // 2) * 256 + (m % 2) * 128 + p
    qpos = g * 256 + t
    mab = np.where(kpos <= qpos, 0.0, MASKV).astype(np.float32)
    mc = np.where(kpos < qpos, 0.0, MASKV).astype(np.float32)
    return mab.astype(_BF), mc.astype(_BF)


_PROG_CACHE = {}


def _get_prog(layers):
    key = tuple(layers)
    if key not in _PROG_CACHE:
        lam_inits = {l: 0.8 - 0.6 * math.exp(-0.3 * l) for l in range(DEPTH)}
        _PROG_CACHE[key] = build_program(list(layers), lam_inits)
    return _PROG_CACHE[key]


def _consts(g):
    j = np.arange(128)[:, None]
    s_ = np.arange(128)[None, :]
    d = {}
    d["c_ident"] = np.eye(128, dtype=np.float32).astype(_BF)
    d["c_tneg"] = np.where(j >= s_, -1.0, 0.0).astype(np.float32).astype(_BF)
    d["c_onesneg"] = np.full((128, 128), -1.0, np.float32).astype(_BF)
    d["c_onesb"] = np.ones((128, 128), np.float32).astype(_BF)
    d["c_onesf"] = np.ones((128, 128), np.float32)
    gath = np.concatenate([_own_rows(0), _own_rows(1)])
    d["rk_cos"], d["rk_sin"] = _rope_tables(gath, 1.0)
    for p in range(2):
        gp = g if p == 0 else 1 - g
        d[f"c_mask_ab{p}"], d[f"c_mask_c{p}"] = _masks(gp)
        d[f"rq_cos{p}"], d[f"rq_sin{p}"] = _rope_tables(_own_rows(gp), 0.125)
        sel = np.zeros((4, 2), np.float32)
        sel[:, gp] = 1.0
        d[f"selg{p}"] = sel
        selw = np.zeros((128, 2), np.float32)
        selw[:, gp] = 1.0
        d[f"selw{p}"] = selw
    bl = np.zeros((128, 4), np.float32)
    for r in range(2):
        bl[:, 2 * r] = 1.0 if r == g else 0.0
        bl[:, 2 * r + 1] = 0.0 if r == g else 1.0
    d["blend"] = bl
    return d


def _layer_inputs(l, w_in, b_forget, lambda_q1, lambda_k1, lambda_q2, lambda_k2, subln_g, w_out, ln_g, ln_b):
    wk, wq = _weight_layouts(np.asarray(w_in[l], np.float32))
    rep = lambda v: np.ascontiguousarray(np.broadcast_to(np.asarray(v, np.float32)[None, :], (128, len(v))))
    lamv = np.stack([rep(lambda_q1[l]), rep(lambda_k1[l]), rep(lambda_q2[l]), rep(lambda_k2[l])], axis=1)
    return {
        f"wk{l}": wk, f"wq{l}": wq, f"wo{l}": np.ascontiguousarray(np.asarray(w_out[l], np.float32)),
        f"lng{l}": rep(ln_g[l]), f"lnb{l}": rep(ln_b[l]),
        f"bfg{l}": np.asarray(b_forget[l], np.float32).reshape(4, 1),
        f"lamv{l}": np.ascontiguousarray(lamv), f"subg{l}": np.asarray(subln_g[l], np.float32).reshape(128, 1),
    }


LAUNCH_PLAN = [[0, 1]]


def kernel(x, ln_in_g, ln_in_b, w_in, b_forget, lambda_q1, lambda_k1, lambda_q2, lambda_k2, subln_g, w_out, ln_g, ln_b):
    x = np.asarray(x, np.float32)
    B = x.shape[0]
    rep = lambda v: np.ascontiguousarray(np.broadcast_to(np.asarray(v, np.float32)[None, :], (128, len(v))))
    consts = [_consts(g) for g in range(2)]
    gath = np.concatenate([_own_rows(0), _own_rows(1)])
    h = x
    for layers in LAUNCH_PLAN:
        nc = _get_prog(layers)
        lay = {}
        for l in layers:
            lay.update(_layer_inputs(l, w_in, b_forget, lambda_q1, lambda_k1, lambda_q2, lambda_k2, subln_g, w_out, ln_g, ln_b))
        in_maps = []
        for core in range(8):
            b, g = core // 2, core % 2
            m = dict(consts[g])
            m.update(lay)
            m["hin_full"] = np.ascontiguousarray(h[b][gath])
            m["hin_own"] = np.ascontiguousarray(h[b][_own_rows(g)])
            m["hin_oth"] = np.ascontiguousarray(h[b][_own_rows(1 - g)])
            m["lng_in"] = rep(ln_in_g)
            m["lnb_in"] = rep(ln_in_b)
            in_maps.append(m)
        res = run_bass_kernel_spmd(nc, in_maps, core_ids=list(range(8)))
        hn = np.empty_like(x)
        for core in range(8):
            b, g = core // 2, core % 2
            hn[b][_own_rows(g)] = np.asarray(res.results[core]["out"], np.float32)
        h = hn
    return h
```
